# Optimizing a Trainium2 kernel written in Bass

```python
import jax, jax.numpy as jnp
from jax import lax
import numpy as np

D_MODEL = 1024
BATCH = 32
SEQ = 256
DEPTH = 4
DEC_BATCH = 2
DEC_SEQ = 2048
PAST_LEN = 512

GRID_W = 64
D_FF = 2816
F_GROUPS = 4
F_GC = 128
F_W = F_GROUPS * F_GC
MLA_HEADS = 8
MLA_Q_LORA = 384
MLA_KV_LORA = 256
MLA_NOPE = 64
MLA_ROPE = 32
MLA_V = 64
NA_HEADS = 8
NA_HEAD_DIM = 64
NA_KH = 8
NA_KW = 16
NA_W = NA_HEADS * NA_HEAD_DIM
IN_SPLITS = (F_W, F_W + MLA_Q_LORA, F_W + MLA_Q_LORA + MLA_KV_LORA, F_W + MLA_Q_LORA + MLA_KV_LORA + MLA_ROPE)
IN_W = F_W + MLA_Q_LORA + MLA_KV_LORA + MLA_ROPE + 3 * NA_W
ROPE_BASE = 10000.0
AXIS_DIM = MLA_ROPE // 2
Q_BLOCK = 128
ALPHA = (2.0 * DEPTH) ** 0.25
BETA = (8.0 * DEPTH) ** -0.25
MLA_SCALE = (MLA_NOPE + MLA_ROPE) ** -0.5
NA_SCALE = NA_HEAD_DIM ** -0.5
NEG_INF = -1e30
F32 = jnp.float32

kernel_name = 'hybrid_dit_fourier_mla_natten_step'


def _ln_plain(x, eps=1e-6):
    xf = x.astype(F32)
    mu = jnp.mean(xf, axis=-1, keepdims=True)
    var = jnp.mean(jnp.square(xf - mu), axis=-1, keepdims=True)
    return (xf - mu) * lax.rsqrt(var + eps)


def _layernorm(x, g, b):
    return (_ln_plain(x, 1e-5) * g.astype(F32) + b.astype(F32)).astype(x.dtype)


def _rmsnorm(x, g, eps=1e-6):
    xf = x.astype(F32)
    y = xf * lax.rsqrt(jnp.mean(jnp.square(xf), axis=-1, keepdims=True) + eps)
    return (y * g.astype(F32)).astype(x.dtype)


def _modulate(x, shift, scale):
    return (_ln_plain(x) * (1.0 + scale.astype(F32)) + shift.astype(F32)).astype(x.dtype)


def _adaln(cvec, w_ada, b_ada):
    return jnp.split(jax.nn.silu(cvec) @ w_ada + b_ada, 9, axis=-1)


def _swiglu(h, w1, w3, w2):
    return (jax.nn.silu(h @ w1) * (h @ w3)) @ w2


def _ffn_sublayer(x, shift, scale, gate, w1, w3, w2, g, b):
    y = _swiglu(_modulate(x, shift, scale), w1, w3, w2)
    return _layernorm(ALPHA * x + 0.5 * gate * y, g, b)


def _axial_rope_tables(n):
    t = jnp.arange(n, dtype=jnp.int32)
    pos = jnp.stack([t // GRID_W, t % GRID_W], axis=-1).astype(F32)
    half = AXIS_DIM // 2
    inv_freq = ROPE_BASE ** (-jnp.arange(half, dtype=F32) / half)
    ang = pos[:, :, None] * inv_freq
    ang = jnp.concatenate([ang, ang], axis=-1)
    return jnp.cos(ang), jnp.sin(ang)


def _apply_axial_rope(x, cos, sin):
    B, n, h, d = x.shape
    xr = x.reshape(B, n, h, 2, AXIS_DIM)
    x1 = xr[..., :AXIS_DIM // 2]
    x2 = xr[..., AXIS_DIM // 2:]
    rot = jnp.concatenate([-x2, x1], axis=-1)
    cs = cos[:, None].astype(x.dtype)
    sn = sin[:, None].astype(x.dtype)
    return (xr * cs + rot * sn).reshape(B, n, h, d)


def _block_attention(q, k, v, scale):
    B, Lq, H, dk = q.shape
    dv = v.shape[-1]
    nb = Lq // Q_BLOCK
    qb = jnp.moveaxis(q.reshape(B, nb, Q_BLOCK, H, dk), 1, 0)

    def one_block(qblk):
        s = jnp.einsum('bqhd,bkhd->bhqk', qblk, k).astype(F32) * scale
        p = jax.nn.softmax(s, axis=-1).astype(v.dtype)
        return jnp.einsum('bhqk,bkhd->bqhd', p, v)

    o = lax.map(one_block, qb)
    return jnp.moveaxis(o, 0, 1).reshape(B, Lq, H * dv)


def _neighbourhood_attention(q, k, v, k_ctx, v_ctx, rpb):
    B, N, H, dh = q.shape
    rows = N // GRID_W
    kh = min(NA_KH, rows)
    r = jnp.arange(rows, dtype=jnp.int32)
    col = jnp.arange(GRID_W, dtype=jnp.int32)
    row_start = jnp.clip(r - kh // 2, 0, rows - kh)
    row_idx = row_start[:, None] + jnp.arange(kh, dtype=jnp.int32)[None, :]
    col_start = jnp.clip(col - NA_KW // 2, 0, GRID_W - NA_KW)
    col_ok = (col[None, :] >= col_start[:, None]) & (col[None, :] < col_start[:, None] + NA_KW)
    qg = q.reshape(B, rows, GRID_W, H, dh)
    kg = k.reshape(B, rows, GRID_W, H, dh)[:, row_idx]
    vg = v.reshape(B, rows, GRID_W, H, dh)[:, row_idx]
    s_loc = jnp.einsum('brqhd,brkwhd->bhrqkw', qg, kg).astype(F32) * NA_SCALE
    dr = row_idx - r[:, None] + (NA_KH - 1)
    dc = jnp.clip(col[None, :] - col[:, None], -(NA_KW - 1), NA_KW - 1) + (NA_KW - 1)
    bias = rpb[:, dr[:, None, :, None], dc[None, :, None, :]]
    s_loc = s_loc + bias[None].astype(F32)
    s_loc = jnp.where(col_ok[None, None, None, :, None, :], s_loc, NEG_INF)
    s_ctx = jnp.einsum('brqhd,bchd->bhrqc', qg, k_ctx).astype(F32) * NA_SCALE
    n_loc = kh * GRID_W
    s = jnp.concatenate([s_loc.reshape(B, H, rows, GRID_W, n_loc), s_ctx], axis=-1)
    p = jax.nn.softmax(s, axis=-1).astype(v.dtype)
    p_loc = p[..., :n_loc].reshape(B, H, rows, GRID_W, kh, GRID_W)
    p_ctx = p[..., n_loc:]
    o = (jnp.einsum('bhrqkw,brkwhd->brqhd', p_loc, vg)
         + jnp.einsum('bhrqc,bchd->brqhd', p_ctx, v_ctx))
    return o.reshape(B, N, H * dh)


def _fourier(u_f):
    B, L, _ = u_f.shape
    f = u_f.reshape(B, L, F_GROUPS, F_GC).astype(F32)
    spec = jnp.fft.fft2(f, axes=(1, 3), norm='ortho').real
    return spec.reshape(B, L, F_W).astype(u_f.dtype)


def _mla_q(u_q, g, w_uq):
    B, L, _ = u_q.shape
    q = (_rmsnorm(u_q, g) @ w_uq).reshape(B, L, MLA_HEADS, MLA_NOPE + MLA_ROPE)
    return q[..., :MLA_NOPE], q[..., MLA_NOPE:]


def _mla_kv_up(c_kv, w_ukv):
    B, L, _ = c_kv.shape
    kv = (c_kv @ w_ukv).reshape(B, L, MLA_HEADS, MLA_NOPE + MLA_V)
    return kv[..., :MLA_NOPE], kv[..., MLA_NOPE:]


def _mla_keys(k_nope, k_rope):
    kr = jnp.broadcast_to(k_rope[:, :, None, :], k_nope.shape[:3] + (MLA_ROPE,))
    return jnp.concatenate([k_nope, kr], axis=-1)


def _na_qkv(u_na):
    B, L, _ = u_na.shape
    q, k, v = jnp.split(u_na, 3, axis=-1)
    return (q.reshape(B, L, NA_HEADS, NA_HEAD_DIM), k.reshape(B, L, NA_HEADS, NA_HEAD_DIM),
            v.reshape(B, L, NA_HEADS, NA_HEAD_DIM))


def _merge(h, y_f, y_m, y_n, p):
    g_f, g_m, g_n = jnp.split(jax.nn.sigmoid(h @ p['w_gate'] + p['b_gate']), 3, axis=-1)
    return (g_f * y_f + g_m * y_m + g_n * y_n) @ p['w_out']


def _mixer_context(h, p):
    u_f, u_q, u_kv, k_rope, u_na = jnp.split(h @ p['w_in'], IN_SPLITS, axis=-1)
    y_f = _fourier(u_f) @ p['w_branch_f']
    q_nope, q_rope = _mla_q(u_q, p['mla_q_norm'], p['mla_w_uq'])
    c_kv = _rmsnorm(u_kv, p['mla_kv_norm'])
    k_nope, v_m = _mla_kv_up(c_kv, p['mla_w_ukv'])
    q_m = jnp.concatenate([q_nope, q_rope], axis=-1)
    y_m = _block_attention(q_m, _mla_keys(k_nope, k_rope), v_m, MLA_SCALE) @ p['w_branch_m']
    q_n, k_n, v_n = _na_qkv(u_na)
    y_n = _block_attention(q_n, k_n, v_n, NA_SCALE) @ p['w_branch_n']
    return _merge(h, y_f, y_m, y_n, p), c_kv, k_rope, k_n, v_n


def _mixer_latent(h, p, ckv_ctx, krope_ctx, k_ctx, v_ctx):
    B, N, _ = h.shape
    cos, sin = _axial_rope_tables(N)
    u_f, u_q, u_kv, u_r, u_na = jnp.split(h @ p['w_in'], IN_SPLITS, axis=-1)
    y_f = _fourier(u_f) @ p['w_branch_f']
    q_nope, q_rope = _mla_q(u_q, p['mla_q_norm'], p['mla_w_uq'])
    q_m = jnp.concatenate([q_nope, _apply_axial_rope(q_rope, cos, sin)], axis=-1)
    c_kv = _rmsnorm(u_kv, p['mla_kv_norm'])
    k_nope, v_m = _mla_kv_up(c_kv, p['mla_w_ukv'])
    k_rope = _apply_axial_rope(u_r[:, :, None, :], cos, sin)[:, :, 0, :]
    k_nope_c, v_c = _mla_kv_up(ckv_ctx, p['mla_w_ukv'])
    k_all = jnp.concatenate([_mla_keys(k_nope, k_rope), _mla_keys(k_nope_c, krope_ctx)], axis=1)
    v_all = jnp.concatenate([v_m, v_c], axis=1)
    y_m = _block_attention(q_m, k_all, v_all, MLA_SCALE) @ p['w_branch_m']
    q_n, k_n, v_n = _na_qkv(u_na)
    y_n = _neighbourhood_attention(q_n, k_n, v_n, k_ctx, v_ctx, p['na_rpb']) @ p['w_branch_n']
    return _merge(h, y_f, y_m, y_n, p)


def setup_inputs(seed: int = 0) -> dict:
    key = jax.random.key(seed)
    ks = list(jax.random.split(key, 40))

    def nrm(i, shape, s):
        return jax.random.normal(ks[i], shape, F32) * s

    D = D_MODEL
    return {
        'x_prompt': nrm(0, (BATCH, SEQ, D), 1.0),
        'x_sample': nrm(1, (DEC_BATCH, DEC_SEQ, D), 1.0),
        'cache_mla_ckv': nrm(2, (DEC_BATCH, DEPTH, PAST_LEN, MLA_KV_LORA), 1.0),
        'cache_mla_krope': nrm(3, (DEC_BATCH, DEPTH, PAST_LEN, MLA_ROPE), 1.0),
        'cache_na_k': nrm(4, (DEC_BATCH, DEPTH, PAST_LEN, NA_HEADS, NA_HEAD_DIM), 1.0),
        'cache_na_v': nrm(5, (DEC_BATCH, DEPTH, PAST_LEN, NA_HEADS, NA_HEAD_DIM), 1.0),
        'c': nrm(6, (DEC_BATCH, D), 1.0),
        'c_ctx': nrm(7, (D,), 1.0),
        'w_ada': nrm(8, (DEPTH, D, 9 * D), 0.5 * D ** -0.5),
        'b_ada': nrm(9, (DEPTH, 9 * D), 0.01),
        'ffn1_w1': nrm(10, (DEPTH, D, D_FF), D ** -0.5),
        'ffn1_w3': nrm(11, (DEPTH, D, D_FF), D ** -0.5),
        'ffn1_w2': nrm(12, (DEPTH, D_FF, D), BETA * D_FF ** -0.5),
        'ffn2_w1': nrm(13, (DEPTH, D, D_FF), D ** -0.5),
        'ffn2_w3': nrm(14, (DEPTH, D, D_FF), D ** -0.5),
        'ffn2_w2': nrm(15, (DEPTH, D_FF, D), BETA * D_FF ** -0.5),
        'w_in': nrm(16, (DEPTH, D, IN_W), D ** -0.5),
        'mla_q_norm': 1.0 + nrm(17, (DEPTH, MLA_Q_LORA), 0.01),
        'mla_w_uq': nrm(18, (DEPTH, MLA_Q_LORA, MLA_HEADS * (MLA_NOPE + MLA_ROPE)), MLA_Q_LORA ** -0.5),
        'mla_kv_norm': 1.0 + nrm(19, (DEPTH, MLA_KV_LORA), 0.01),
        'mla_w_ukv': nrm(20, (DEPTH, MLA_KV_LORA, MLA_HEADS * (MLA_NOPE + MLA_V)), MLA_KV_LORA ** -0.5),
        'na_rpb': nrm(21, (DEPTH, NA_HEADS, 2 * NA_KH - 1, 2 * NA_KW - 1), 0.1),
        'w_branch_f': nrm(22, (DEPTH, F_W, D), BETA * F_W ** -0.5),
        'w_branch_m': nrm(23, (DEPTH, MLA_HEADS * MLA_V, D), BETA * (MLA_HEADS * MLA_V) ** -0.5),
        'w_branch_n': nrm(24, (DEPTH, NA_W, D), BETA * NA_W ** -0.5),
        'w_gate': nrm(25, (DEPTH, D, 3 * D), D ** -0.5),
        'b_gate': nrm(26, (DEPTH, 3 * D), 0.01),
        'w_out': nrm(27, (DEPTH, D, D), BETA * D ** -0.5),
        'ln_g': 1.0 + nrm(28, (DEPTH, 3, D), 0.01),
        'ln_b': nrm(29, (DEPTH, 3, D), 0.01),
    }


def reference(x_prompt, x_sample, cache_mla_ckv, cache_mla_krope, cache_na_k, cache_na_v, c, c_ctx,
              w_ada, b_ada, ffn1_w1, ffn1_w3, ffn1_w2, ffn2_w1, ffn2_w3, ffn2_w2, w_in,
              mla_q_norm, mla_w_uq, mla_kv_norm, mla_w_ukv, na_rpb, w_branch_f, w_branch_m, w_branch_n,
              w_gate, b_gate, w_out, ln_g, ln_b):
    xp = x_prompt
    xs = x_sample
    ckv_list, krope_list, nak_list, nav_list = [], [], [], []
    c_lat = c[:, None, :]
    for l in range(DEPTH):
        p = {'w_in': w_in[l], 'mla_q_norm': mla_q_norm[l], 'mla_w_uq': mla_w_uq[l],
             'mla_kv_norm': mla_kv_norm[l], 'mla_w_ukv': mla_w_ukv[l], 'na_rpb': na_rpb[l],
             'w_branch_f': w_branch_f[l], 'w_branch_m': w_branch_m[l], 'w_branch_n': w_branch_n[l],
             'w_gate': w_gate[l], 'b_gate': b_gate[l], 'w_out': w_out[l]}
        m = _adaln(c_ctx, w_ada[l], b_ada[l])
        xp = _ffn_sublayer(xp, m[0], m[1], m[2], ffn1_w1[l], ffn1_w3[l], ffn1_w2[l], ln_g[l, 0], ln_b[l, 0])
        mix, ckv, krope, nak, nav = _mixer_context(_modulate(xp, m[3], m[4]), p)
        xp = _layernorm(ALPHA * xp + m[5] * mix, ln_g[l, 1], ln_b[l, 1])
        xp = _ffn_sublayer(xp, m[6], m[7], m[8], ffn2_w1[l], ffn2_w3[l], ffn2_w2[l], ln_g[l, 2], ln_b[l, 2])
        ckv_list.append(ckv)
        krope_list.append(krope)
        nak_list.append(nak)
        nav_list.append(nav)
        ms = _adaln(c_lat, w_ada[l], b_ada[l])
        xs = _ffn_sublayer(xs, ms[0], ms[1], ms[2], ffn1_w1[l], ffn1_w3[l], ffn1_w2[l], ln_g[l, 0], ln_b[l, 0])
        mix_s = _mixer_latent(_modulate(xs, ms[3], ms[4]), p, cache_mla_ckv[:, l], cache_mla_krope[:, l],
                              cache_na_k[:, l], cache_na_v[:, l])
        xs = _layernorm(ALPHA * xs + ms[5] * mix_s, ln_g[l, 1], ln_b[l, 1])
        xs = _ffn_sublayer(xs, ms[6], ms[7], ms[8], ffn2_w1[l], ffn2_w3[l], ffn2_w2[l], ln_g[l, 2], ln_b[l, 2])
    new_mla_ckv = jnp.stack(ckv_list, axis=1)
    new_mla_krope = jnp.stack(krope_list, axis=1)
    new_na_k = jnp.stack(nak_list, axis=1)
    new_na_v = jnp.stack(nav_list, axis=1)
    return (xp, xs, new_mla_ckv, new_mla_krope, new_na_k, new_na_v)
```

```python
import math
from collections import defaultdict

import numpy as np
import ml_dtypes

import concourse.bass as bass
import concourse.mybir as mybir
from concourse.bass_utils import run_bass_kernel_spmd

F32 = mybir.dt.float32
BF16 = mybir.dt.bfloat16
AF = mybir.ActivationFunctionType
ALU = mybir.AluOpType

D = 1024
KC = 8
DEPTH = 4
NT = 2048
NTILE = 16
NCTX = 512
NK = NT + NCTX
NKT = NK // 128
FF = 2816
FC = FF // 128
IN_W = 2720
C_F, C_Q, C_KV, C_R, C_NQ, C_NK, C_NV = 0, 512, 896, 1152, 1184, 1696, 2208
ALPHA = (2.0 * DEPTH) ** 0.25
MLA_SCALE = 96 ** -0.5
NA_SCALE = 0.125
BIG_MLA = 576.0
BIG_NA = 480.0
NEG = -30000.0
NPAT = 12
RPAD = 64


class _PEProxy:
    def __init__(self, pe):
        self.pe = pe
        self.last_stop = None

    def matmul(self, *a, **kw):
        self.last_stop = kw.get("stop", None)
        return self.pe.matmul(*a, **kw)

    def transpose(self, *a, **kw):
        self.last_stop = True
        return self.pe.transpose(*a, **kw)


class Sched:
    ENGS = ("pe", "act", "dve", "pool", "sp")

    def __init__(self, nc, n_dma_sems=56):
        self.nc = nc
        self.eng = {"pe": nc.tensor, "act": nc.scalar, "dve": nc.vector, "pool": nc.gpsimd, "sp": nc.sync}
        self.esem = {e: nc.alloc_semaphore(f"es_{e}") for e in self.ENGS}
        self.pos = {e: 0 for e in self.ENGS}
        self.sigs = {e: [] for e in self.ENGS}
        self.sigval = {e: 0 for e in self.ENGS}
        self.last = {e: None for e in self.ENGS}
        self.waited = defaultdict(int)
        self.dsems = [nc.alloc_semaphore(f"ds_{i}") for i in range(n_dma_sems)]
        self.dcount = [0] * n_dma_sems
        self.dnext = 0
        self.dnext_pool = 0
        self.W = defaultdict(dict)
        self.R = defaultdict(dict)
        self.peproxy = _PEProxy(nc.tensor)

    def _need(self, eng, tok, raw):
        if tok[0] == "d":
            _, idx, val = tok
            return (("d", idx), self.dsems[idx], val)
        _, f, p = tok
        if f == eng and eng == "pe":
            return None
        val = None
        for (sp_, sv) in reversed(self.sigs[f]):
            if sp_ >= p:
                val = sv
            else:
                break
        if val is None:
            ins, lp = self.last[f]
            assert lp >= p
            self.sigval[f] += 1
            ins.then_inc(self.esem[f], 1)
            self.sigs[f].append((lp, self.sigval[f]))
            val = self.sigval[f]
        return (("e", f), self.esem[f], val)

    def _waits(self, eng, reads, writes):
        needs = {}

        def add(tok, raw):
            n = self._need(eng, tok, raw)
            if n:
                key, sem, val = n
                if key not in needs or needs[key][0] < val:
                    needs[key] = (val, sem)

        for k in reads:
            for t in self.W[k].values():
                add(t, True)
            if k.startswith("ps"):
                for rk, r in self.R[k].items():
                    if rk != eng:
                        add(r, False)
        for k in writes:
            for r in self.R[k].values():
                add(r, False)
        for key, (val, sem) in needs.items():
            if self.waited[(eng, key)] < val:
                self.eng[eng].wait_ge(sem, val)
                self.waited[(eng, key)] = val

    def _record(self, tok, reads, writes, rkey):
        for k in writes:
            self.W[k][rkey] = tok
        for k in reads:
            self.R[k][rkey] = tok

    def op(self, eng, fn, reads=(), writes=(), check_writes=True):
        self._waits(eng, reads, writes if check_writes else ())
        if eng == "pe":
            self.peproxy.last_stop = None
            ins = fn(self.peproxy)
            sig = bool(self.peproxy.last_stop)
        else:
            ins = fn(self.eng[eng])
            sig = True
        self.pos[eng] += 1
        p = self.pos[eng]
        self.last[eng] = (ins, p)
        if sig:
            self.sigval[eng] += 1
            ins.then_inc(self.esem[eng], 1)
            self.sigs[eng].append((p, self.sigval[eng]))
        self._record(("e", eng, p), reads, writes, eng)
        return ins

    def mm(self, fn, reads=(), writes=(), first=True):
        return self.op("pe", fn, reads, writes, check_writes=first)

    def dma(self, q, out, in_, reads=(), writes=(), **kw):
        half = len(self.dsems) // 2
        if q == "pool":
            idx = self.dnext_pool
            self.dnext_pool = (self.dnext_pool + 1) % half
        else:
            idx = half + self.dnext
            self.dnext = (self.dnext + 1) % (len(self.dsems) - half)
        if self.dcount[idx] and self.waited[(q, ("d", idx))] < self.dcount[idx]:
            self.eng[q].wait_ge(self.dsems[idx], self.dcount[idx])
            self.waited[(q, ("d", idx))] = self.dcount[idx]
        self._waits(q, reads, writes)
        self.eng[q].dma_start(out=out, in_=in_, **kw).then_inc(self.dsems[idx], 16)
        self.dcount[idx] += 16
        tok = ("d", idx, self.dcount[idx])
        self._record(tok, reads, writes, ("d", idx))
        return tok

    def wait_all_dma(self, eng="sp"):
        for idx, c in enumerate(self.dcount):
            if c and self.waited[(eng, ("d", idx))] < c:
                self.eng[eng].wait_ge(self.dsems[idx], c)
                self.waited[(eng, ("d", idx))] = c

    def barrier(self):
        toks = []
        for f in self.ENGS:
            if self.last[f] is not None:
                toks.append(("e", f, self.last[f][1]))
        for e in self.ENGS:
            needs = {}
            for t in toks:
                n = self._need(e, t, True)
                if n:
                    key, sem, val = n
                    if key not in needs or needs[key][0] < val:
                        needs[key] = (val, sem)
            for key, (val, sem) in needs.items():
                if self.waited[(e, key)] < val:
                    self.eng[e].wait_ge(sem, val)
                    self.waited[(e, key)] = val
            self.wait_all_dma(e)
        self.W = defaultdict(dict)
        self.R = defaultdict(dict)


def na_window(i):
    if i <= 1:
        tiles, typ = [0, 1, 2, 3], 1
    elif i >= 14:
        tiles, typ = [12, 13, 14, 15], 1
    else:
        tiles, typ = [i - 2, i - 1, i, i + 1, i + 2], 0
    out = []
    for j in tiles:
        dl = j - i
        pat = (dl + 2) if typ == 0 else (5 + dl + 3)
        out.append((j, pat))
    return out


PAT_DELTA = [(-2, 0), (-1, 0), (0, 0), (1, 0), (2, 0)] + [(d, 1) for d in range(-3, 4)]


def build(nstages=3 * DEPTH):
    nc = bass.Bass("TRN2", target_bir_lowering=False)
    S = Sched(nc)

    def din(name, shape, dt=F32):
        return nc.dram_tensor(name, list(shape), dt, kind="ExternalInput")

    x0 = din("x0", [NT, D])
    cvec = din("cvec", [1, D])
    c_ckv = din("c_ckv", [DEPTH, NCTX, 256])
    c_kr = din("c_kr", [DEPTH, NCTX, 32])
    c_nk = din("c_nk", [DEPTH, NCTX, 512])
    c_nv = din("c_nv", [DEPTH, NCTX, 512])
    w_ada = din("w_ada", [DEPTH, D, 9 * D])
    b_ada = din("b_ada", [DEPTH, 9 * D])
    fw = {}
    for nm in ("ffn1_w1", "ffn1_w3", "ffn2_w1", "ffn2_w3"):
        fw[nm] = din(nm, [DEPTH, D, FF])
    for nm in ("ffn1_w2", "ffn2_w2"):
        fw[nm] = din(nm, [DEPTH, FF, D])
    w_in = din("w_in", [DEPTH, D, IN_W])
    q_norm = din("mla_q_norm", [DEPTH, 384])
    w_uq = din("mla_w_uq", [DEPTH, 384, 768])
    kv_norm = din("mla_kv_norm", [DEPTH, 256])
    w_ukv = din("mla_w_ukv", [DEPTH, 256, 1024])
    rpbr = din("rpbr", [2 * RPAD + DEPTH * 8 * 15 * 31])
    w_bf = din("w_branch_f", [DEPTH, 512, D])
    w_bm = din("w_branch_m", [DEPTH, 512, D])
    w_bn = din("w_branch_n", [DEPTH, 512, D])
    w_gate = din("w_gate", [DEPTH, D, 3 * D])
    b_gate = din("b_gate", [DEPTH, 3 * D])
    w_out = din("w_out", [DEPTH, D, D])
    ln_g = din("ln_g", [DEPTH, 3, D])
    ln_b = din("ln_b", [DEPTH, 3, D])
    ropeC = din("ropeC", [32, NT])
    ropeS = din("ropeS", [32, NT])
    protT = din("protT", [32, 32])
    indq_m = din("indq_m", [8, NT])
    indq_n = din("indq_n", [8, NT])
    indk = din("indk", [8, NK])
    dftC = din("dftC", [NT, NT], BF16)
    dftS = din("dftS", [NT, NT], BF16)
    dftCS = din("dftCS", [128, 256])
    m1d = din("m1d", [128, NPAT, 128])
    m2d = din("m2d", [128, NPAT, 128])
    identd = din("identd", [128, 128])

    def dout(name, shape):
        return nc.dram_tensor(name, list(shape), F32, kind="ExternalOutput")

    y = dout("y", [NT, D])
    o_ckv = dout("o_ckv", [DEPTH, NT, 256])
    o_kr = dout("o_kr", [DEPTH, NT, 32])
    o_nk = dout("o_nk", [DEPTH, NT, 512])
    o_nv = dout("o_nv", [DEPTH, NT, 512])

    xa = nc.dram_tensor("xa", [NT, D], F32, kind="Internal")
    xb = nc.dram_tensor("xb", [NT, D], F32, kind="Internal")
    ada_d = nc.dram_tensor("ada_d", [DEPTH, 9 * D], F32, kind="Internal")
    btd = nc.dram_tensor("btd", [DEPTH, 8, NPAT, 128, 128], F32, kind="Internal")

    def AP(t, off, dims):
        return bass.AP(t, off, [list(d) for d in dims])

    _uid = [0]
    _orig_sbuf_tensor = nc.sbuf_tensor

    def _sbuf_tensor(name, shape, dt):
        _uid[0] += 1
        return _orig_sbuf_tensor(f"{name}_{_uid[0]}", shape, dt)

    sb = nc.alloc_sbuf_tensor
    ident = sb("ident", [128, 128], BF16)
    ones = sb("ones", [128, 128], BF16)
    epsc = sb("epsc", [128, 4], F32)
    identf = sb("identf", [128, 128], F32)
    onesf = sb("onesf", [128, 64], F32)
    PS = [nc.alloc_psum_tensor(f"ps{i}", [128, 512], F32) for i in range(7)]
    PSB = nc.alloc_psum_tensor("psb", [128, 1024], BF16)
    PK = [f"ps{i}" for i in range(7)]

    S.dma("sp", identf[:], identd.ap(), writes=["identf"])
    S.op("dve", lambda e: e.tensor_copy(out=ident[:], in_=identf[:]), reads=["identf"], writes=["ident"])
    S.op("dve", lambda e: e.memset(ones[:], 1.0), writes=["ones"])
    S.op("dve", lambda e: e.memset(onesf[:], 1.0), writes=["ones"])
    S.op("dve", lambda e: e.memset(epsc[:, 0:1], 1e-6), writes=["epsc"])
    S.op("dve", lambda e: e.memset(epsc[:, 1:2], 1e-5), writes=["epsc"])

    evac_rr = [0]

    def evac_eng():
        evac_rr[0] += 1
        return "act" if evac_rr[0] % 2 else "dve"

    def copy_op(eng, out, in_):
        if eng == "act":
            return lambda e: e.activation(out=out, in_=in_, func=AF.Copy)
        return lambda e: e.tensor_copy(out=out, in_=in_)

    def prologue():
        with _sbuf_tensor("crow", [8, 128], F32) as crow, \
                _sbuf_tensor("srow", [8, 128], BF16) as srow, \
                _sbuf_tensor("scT", [128, 8], BF16) as scT, \
                _sbuf_tensor("wada", [128, 2, 8, 512], BF16) as wada, \
                _sbuf_tensor("brow", [1, 2, 512], F32) as brow, \
                _sbuf_tensor("orow", [1, 2, 512], F32) as orow:
            S.dma("sp", crow[:], cvec.ap().rearrange("o (k p) -> (o k) p", p=128), writes=["crow"])
            S.op("act", lambda e: e.activation(out=srow[:], in_=crow[:], func=AF.Silu), reads=["crow"], writes=["srow"])
            S.mm(lambda e: e.matmul(PS[0][:, 0:8], lhsT=srow[:], rhs=ident[0:8, 0:8], start=True, stop=True),
                 reads=["srow", "ident"], writes=[PK[0]])
            S.op("dve", lambda e: e.tensor_copy(out=scT[:], in_=PS[0][:, 0:8]), reads=[PK[0]], writes=["scT"])
            it = 0
            for l in range(DEPTH):
                wv = w_ada.ap()[l].rearrange("(k p) n -> p k n", p=128)
                for j in range(18):
                    b = it % 2
                    it += 1
                    S.dma("pool", wada[:, b], wv[:, :, j * 512:(j + 1) * 512], writes=[f"wada{b}"])
                    S.dma("sp", brow[:, b], b_ada.ap()[l:l + 1, j * 512:(j + 1) * 512], writes=[f"brow{b}"])
                    pk = PK[b]
                    for k in range(KC):
                        S.mm(lambda e, k=k, b=b: e.matmul(PS[b][0:1, :], lhsT=scT[:, k:k + 1], rhs=wada[:, b, k, :],
                                                          start=(k == 0), stop=(k == KC - 1)),
                             reads=["scT", f"wada{b}"], writes=[pk], first=(k == 0))
                    S.op("dve", lambda e, b=b: e.tensor_tensor(out=orow[:, b], in0=PS[b][0:1, :], in1=brow[:, b], op=ALU.add),
                         reads=[pk, f"brow{b}"], writes=[f"orow{b}"])
                    S.dma("sp", ada_d.ap()[l:l + 1, j * 512:(j + 1) * 512], orow[:, b], reads=[f"orow{b}"], writes=["ada_d"])
        for l in range(DEPTH):
            for p, (dl, typ) in enumerate(PAT_DELTA):
                for kr in range(2):
                    for qr in range(2):
                        dr = 2 * dl + kr - qr + 7
                        drc = min(max(dr, 0), 14)
                        src = AP(rpbr, RPAD + ((l * 8) * 15 + drc) * 31 + 15, [[465, 8], [-1, 64], [1, 64]])
                        dst = AP(btd, (l * 8 * NPAT + p) * 16384 + kr * 64 * 128 + qr * 64,
                                 [[NPAT * 16384, 8], [128, 64], [1, 64]])
                        S.dma("sp", dst, src, writes=["btd"])
        S.barrier()

    def load_colvec(src_t, off, n, dst, dkey):
        with _sbuf_tensor("cvrow", [32, 128], F32) as row:
            S.dma("sp", row[0:n, :], AP(src_t, off, [[128, n], [1, 128]]), writes=["cvrow"])
            S.mm(lambda e: e.transpose(out=PS[6][:, 0:n], in_=row[0:n, :], identity=identf[0:n, 0:n]),
                 reads=["cvrow", "identf"], writes=[PK[6]])
            S.op("dve", lambda e: e.tensor_copy(out=dst, in_=PS[6][:, 0:n]), reads=[PK[6]], writes=[dkey])
            S.barrier()

    def load_mod(l, shift_idx, scale_idx, V):
        load_colvec(ada_d, l * 9 * D + shift_idx * D, 8, V["shT"][:], "shT")
        load_colvec(ada_d, l * 9 * D + scale_idx * D, 8, V["scT"][:], "scT")
        S.op("dve", lambda e: e.tensor_scalar(out=V["scT"][:], in0=V["scT"][:], scalar1=1.0, scalar2=None, op0=ALU.add),
             reads=["scT"], writes=["scT"])

    def load_epi(l, gate_idx, ln_idx, gate_coef, V):
        S.dma("sp", V["gate_bc"][:], AP(ada_d, l * 9 * D + gate_idx * D, [[0, 128], [1, D]]), reads=["ada_d"], writes=["gate_bc"])
        S.dma("sp", V["lng_bc"][:], AP(ln_g, (l * 3 + ln_idx) * D, [[0, 128], [1, D]]), writes=["lng_bc"])
        S.dma("sp", V["lnb_bc"][:], AP(ln_b, (l * 3 + ln_idx) * D, [[0, 128], [1, D]]), writes=["lnb_bc"])
        if gate_coef != 1.0:
            S.op("pool", lambda e: e.tensor_scalar(out=V["gate_bc"][:], in0=V["gate_bc"][:], scalar1=gate_coef, scalar2=None, op0=ALU.mult),
                 reads=["gate_bc"], writes=["gate_bc"])

    def ln_stats(xt, xkey, T, eps_col, tag):
        st, mv, rs, nb = T["st"], T["mv"], T["rstd"], T["nb"]
        k = f"lnst{tag}"
        S.op("dve", lambda e: e.bn_stats(out=st[:, tag, 0:6], in_=xt[:, 0:512]), reads=[xkey], writes=[k + "a"])
        S.op("dve", lambda e: e.bn_stats(out=st[:, tag, 6:12], in_=xt[:, 512:1024]), reads=[xkey], writes=[k + "b"])
        S.op("dve", lambda e: e.bn_aggr(out=mv[:, tag, :], in_=st[:, tag, :]), reads=[k + "a", k + "b"], writes=[k + "mv"])
        S.op("act", lambda e: e.activation(out=rs[:, tag:tag + 1], in_=mv[:, tag, 1:2], func=AF.Sqrt, bias=epsc[:, eps_col:eps_col + 1], scale=1.0),
             reads=[k + "mv", "epsc"], writes=[k + "rs"])
        S.op("dve", lambda e: e.reciprocal(out=rs[:, tag:tag + 1], in_=rs[:, tag:tag + 1]), reads=[k + "rs"], writes=[k + "rs"])
        S.op("dve", lambda e: e.scalar_tensor_tensor(out=nb[:, tag:tag + 1], in0=mv[:, tag, 0:1], scalar=-1.0, in1=rs[:, tag:tag + 1],
                                                     op0=ALU.mult, op1=ALU.mult),
             reads=[k + "mv", k + "rs"], writes=[k + "nb"])
        return k

    def ln_to_hT(xt, xkey, T, V, hT, hkey, tcol, tag):
        k = ln_stats(xt, xkey, T, 0, tag)
        xn = T["xn"]
        S.op("act", lambda e: e.activation(out=xn[:, tag, :], in_=xt, func=AF.Identity, scale=T["rstd"][:, tag:tag + 1], bias=T["nb"][:, tag:tag + 1]),
             reads=[xkey, k + "rs", k + "nb"], writes=[f"xn{tag}"])
        for kk in range(KC):
            S.mm(lambda e, kk=kk: e.transpose(out=PSB[:, kk * 128:(kk + 1) * 128], in_=xn[:, tag, kk * 128:(kk + 1) * 128], identity=ident[:]),
                 reads=[f"xn{tag}", "ident"], writes=["psb"], first=(kk == 0))
        evac_rr[0] += 1
        teng = "dve" if evac_rr[0] % 2 == 0 else "act"
        for kk in range(KC):
            eng = teng
            if eng == "dve":
                fn = lambda e, kk=kk: e.tensor_scalar(out=hT[:, kk, tcol:tcol + 128], in0=PSB[:, kk * 128:(kk + 1) * 128],
                                                      scalar1=V["scT"][:, kk:kk + 1], scalar2=V["shT"][:, kk:kk + 1], op0=ALU.mult, op1=ALU.add)
            else:
                fn = lambda e, kk=kk: e.activation(out=hT[:, kk, tcol:tcol + 128], in_=PSB[:, kk * 128:(kk + 1) * 128], func=AF.Identity,
                                                   scale=V["scT"][:, kk:kk + 1], bias=V["shT"][:, kk:kk + 1])
            S.op(eng, fn, reads=["psb", "scT", "shT"], writes=[hkey])

    def ln_out(zt, zkey, T, V, ot, okey, dst, tag):
        k = ln_stats(zt, zkey, T, 1, tag)
        S.op("act", lambda e: e.activation(out=ot, in_=zt, func=AF.Identity, scale=T["rstd"][:, tag:tag + 1], bias=T["nb"][:, tag:tag + 1]),
             reads=[zkey, k + "rs", k + "nb"], writes=[okey])
        S.op("dve", lambda e: e.tensor_tensor(out=ot, in0=ot, in1=V["lng_bc"][:], op=ALU.mult), reads=[okey, "lng_bc"], writes=[okey])
        S.op("pool", lambda e: e.tensor_tensor(out=ot, in0=ot, in1=V["lnb_bc"][:], op=ALU.add), reads=[okey, "lnb_bc"], writes=[okey])
        for d_ in dst:
            S.dma("sp", d_, ot, reads=[okey], writes=["xdram"])

    def alloc_ln(stack):
        V = {}
        V["shT"] = stack.enter_context(_sbuf_tensor("shT", [128, 8], F32))
        V["scT"] = stack.enter_context(_sbuf_tensor("scT1", [128, 8], F32))
        T = {}
        T["st"] = stack.enter_context(_sbuf_tensor("st", [128, 2, 12], F32))
        T["mv"] = stack.enter_context(_sbuf_tensor("mv", [128, 2, 2], F32))
        T["rstd"] = stack.enter_context(_sbuf_tensor("rstd", [128, 2], F32))
        T["nb"] = stack.enter_context(_sbuf_tensor("nb", [128, 2], F32))
        T["xn"] = stack.enter_context(_sbuf_tensor("xn", [128, 2, D], BF16))
        return V, T

    def alloc_vec(stack, V):
        for nm in ("gate_bc", "lng_bc", "lnb_bc"):
            V[nm] = stack.enter_context(_sbuf_tensor(nm, [128, D], F32))

    from contextlib import ExitStack

    def ffn(l, which, xin, xouts):
        w1 = fw[f"ffn{which}_w1"].ap()[l].rearrange("(k p) n -> p k n", p=128)
        w3 = fw[f"ffn{which}_w3"].ap()[l].rearrange("(k p) n -> p k n", p=128)
        w2 = fw[f"ffn{which}_w2"].ap()[l].rearrange("(f p) n -> p f n", p=128)
        base = 0 if which == 1 else 6
        with ExitStack() as st:
            V, T = alloc_ln(st)
            alloc_vec(st, V)
            xp = st.enter_context(_sbuf_tensor("xp", [128, 8, D], F32))
            hT = st.enter_context(_sbuf_tensor("hTf", [128, KC, 1024], BF16))
            gT = st.enter_context(_sbuf_tensor("gT", [128, FC, 1024], BF16))
            w13 = st.enter_context(_sbuf_tensor("w13", [128, 2, 2, KC, 256], BF16))
            w2b = st.enter_context(_sbuf_tensor("w2b", [128, 2, FC, 256], BF16))
            sg = st.enter_context(_sbuf_tensor("sg", [128, 2, 512], BF16))
            tmp = st.enter_context(_sbuf_tensor("tmpf", [128, 2, 256], F32))
            zt = st.enter_context(_sbuf_tensor("zt", [128, 2, D], F32))
            import os
            CUT = int(os.environ.get("KDBG_CUT", "99"))
            load_mod(l, base + 0, base + 1, V)
            load_epi(l, base + 2, 0 if which == 1 else 2, 0.5, V)
            if CUT <= 0:
                S.barrier()
                return
            wit = 0
            w2it = 0
            for p in range(2):
                for t in range(8):
                    S.dma("sp", xp[:, t, :], xin[(p * 8 + t) * 128:(p * 8 + t + 1) * 128, :], reads=["xdram"], writes=[f"xp{t}"])
                if CUT <= 1:
                    S.barrier()
                    return
                for t in range(8):
                    ln_to_hT(xp[:, t, :], f"xp{t}", T, V, hT, "hTf", t * 128, t % 2)
                if CUT <= 2:
                    S.barrier()
                    return
                for f2 in range(FC // 2):
                    b = wit % 2
                    wit += 1
                    S.dma("pool", w13[:, b, 0], w1[:, :, f2 * 256:(f2 + 1) * 256], writes=[f"w1b{b}"])
                    S.dma("pool", w13[:, b, 1], w3[:, :, f2 * 256:(f2 + 1) * 256], writes=[f"w3b{b}"])
                    for fi in range(2):
                        f = f2 * 2 + fi
                        for half in range(2):
                            pa, pb_ = (0, 1) if half == 0 else (2, 3)
                            for k in range(KC):
                                S.mm(lambda e, k=k, b=b, fi=fi, half=half, pa=pa: e.matmul(
                                    PS[pa][:, :], lhsT=w13[:, b, 0, k, fi * 128:(fi + 1) * 128], rhs=hT[:, k, half * 512:(half + 1) * 512],
                                    start=(k == 0), stop=(k == KC - 1)),
                                    reads=[f"w1b{b}", "hTf"], writes=[PK[pa]], first=(k == 0))
                            for k in range(KC):
                                S.mm(lambda e, k=k, b=b, fi=fi, half=half, pb_=pb_: e.matmul(
                                    PS[pb_][:, :], lhsT=w13[:, b, 1, k, fi * 128:(fi + 1) * 128], rhs=hT[:, k, half * 512:(half + 1) * 512],
                                    start=(k == 0), stop=(k == KC - 1)),
                                    reads=[f"w3b{b}", "hTf"], writes=[PK[pb_]], first=(k == 0))
                            S.op("act", lambda e, half=half, pa=pa: e.activation(out=sg[:, half, :], in_=PS[pa][:, :], func=AF.Silu),
                                 reads=[PK[pa]], writes=[f"sg{half}"])
                            S.op("dve", lambda e, half=half, pb_=pb_, f=f: e.tensor_tensor(
                                out=gT[:, f, half * 512:(half + 1) * 512], in0=PS[pb_][:, :], in1=sg[:, half, :], op=ALU.mult),
                                reads=[PK[pb_], f"sg{half}"], writes=["gT"])
                if CUT <= 3:
                    S.barrier()
                    return
                for oq in range(4):
                    b = w2it % 2
                    w2it += 1
                    S.dma("pool", w2b[:, b], w2[:, :, oq * 256:(oq + 1) * 256], writes=[f"w2b{b}"])
                    for t in range(8):
                        pi = 4 + (t % 2)
                        for f in range(FC):
                            S.mm(lambda e, f=f, t=t, b=b, pi=pi: e.matmul(
                                PS[pi][:, 0:256], lhsT=gT[:, f, t * 128:(t + 1) * 128], rhs=w2b[:, b, f, :],
                                start=(f == 0), stop=(f == FC - 1)),
                                reads=["gT", f"w2b{b}"], writes=[PK[pi]], first=(f == 0))
                        tb = t % 2
                        S.op("dve", lambda e, pi=pi, tb=tb, oq=oq: e.tensor_tensor(
                            out=tmp[:, tb, :], in0=PS[pi][:, 0:256], in1=V["gate_bc"][:, oq * 256:(oq + 1) * 256], op=ALU.mult),
                            reads=[PK[pi], "gate_bc"], writes=[f"tmpf{tb}"])
                        S.op("dve", lambda e, t=t, tb=tb, oq=oq: e.scalar_tensor_tensor(
                            out=xp[:, t, oq * 256:(oq + 1) * 256], in0=xp[:, t, oq * 256:(oq + 1) * 256], scalar=ALPHA,
                            in1=tmp[:, tb, :], op0=ALU.mult, op1=ALU.add),
                            reads=[f"tmpf{tb}", f"xp{t}"], writes=[f"xp{t}"])
                if CUT <= 4:
                    S.barrier()
                    return
                for t in range(8):
                    r0 = (p * 8 + t) * 128
                    ln_out(xp[:, t, :], f"xp{t}", T, V, zt[:, t % 2, :], f"zt{t % 2}", [xo[r0:r0 + 128, :] for xo in xouts], t % 2)
        S.barrier()

    class _Cut(Exception):
        pass

    def mcut(n):
        import os
        if int(os.environ.get("KDBG_MCUT", "99")) <= n:
            raise _Cut()

    def mixer(l, xin, xouts):
        try:
            mixer_(l, xin, xouts)
        except _Cut:
            pass
        S.barrier()

    def mixer_(l, xin, xouts):
        win = w_in.ap()[l].rearrange("(k p) n -> p k n", p=128)
        with ExitStack() as st:
            V, T = alloc_ln(st)
            hT = st.enter_context(_sbuf_tensor("hTm", [128, KC, NT], BF16))
            specT = st.enter_context(_sbuf_tensor("specT", [128, 4, NT], BF16))
            omT = st.enter_context(_sbuf_tensor("omT", [128, 4, NT], BF16))
            onT = st.enter_context(_sbuf_tensor("onT", [128, 4, NT], BF16))
            load_mod(l, 3, 4, V)
            with _sbuf_tensor("xt2", [128, 2, D], F32) as xt2:
                for t in range(NTILE):
                    b = t % 2
                    S.dma("sp", xt2[:, b, :], xin[t * 128:(t + 1) * 128, :], reads=["xdram"], writes=[f"xt2{b}"])
                    ln_to_hT(xt2[:, b, :], f"xt2{b}", T, V, hT, "hTm", t * 128, b)
            S.barrier()
            mcut(0)

            with ExitStack() as s2:
                wf = s2.enter_context(_sbuf_tensor("wf", [128, KC, 512], BF16))
                cs = s2.enter_context(_sbuf_tensor("cs", [128, 256], BF16))
                AB = s2.enter_context(_sbuf_tensor("AB", [128, NTILE, 4, 256], BF16))
                ufT = s2.enter_context(_sbuf_tensor("ufT", [128, 2, 512], BF16))
                dbuf = s2.enter_context(_sbuf_tensor("dbuf", [128, 2, 2, 8, 512], BF16))
                S.dma("pool", wf[:], win[:, :, C_F:C_F + 512], writes=["wf"])
                S.dma("pool", cs[:], dftCS.ap(), writes=["cs"])
                it = 0
                for c in range(4):
                    for g in range(4):
                        b = it % 2
                        it += 1
                        for k in range(KC):
                            S.mm(lambda e, k=k, g=g, c=c, b=b: e.matmul(PS[b][:, :], lhsT=wf[:, k, g * 128:(g + 1) * 128],
                                                                   rhs=hT[:, k, c * 512:(c + 1) * 512], start=(k == 0), stop=(k == KC - 1)),
                                 reads=["wf", "hTm"], writes=[PK[b]], first=(k == 0))
                        S.op("act", copy_op("act", ufT[:, b, :], PS[b][:, :]), reads=[PK[b]], writes=[f"ufT{b}"])
                        for tt in range(4):
                            t = c * 4 + tt
                            S.mm(lambda e, tt=tt, b=b: e.matmul(PS[2][:, tt * 256:(tt + 1) * 256] if tt < 2 else PS[3][:, (tt - 2) * 256:(tt - 1) * 256],
                                                           lhsT=ufT[:, b, tt * 128:(tt + 1) * 128], rhs=cs[:], start=True, stop=True),
                                 reads=[f"ufT{b}", "cs"], writes=[PK[2] if tt < 2 else PK[3]])
                        for hh in range(2):
                            S.op("dve", lambda e, hh=hh, c=c, g=g: e.tensor_copy(
                                out=AB[:, c * 4 + hh * 2:c * 4 + hh * 2 + 2, g, :],
                                in_=PS[2 + hh][:, :].rearrange("p (t n) -> p t n", n=256)),
                                reads=[PK[2 + hh]], writes=["AB"])
                dC = dftC.ap().rearrange("(t p) n -> p t n", p=128)
                dS = dftS.ap().rearrange("(t p) n -> p t n", p=128)
                dit = 0
                for c in range(4):
                    for half in range(2):
                        b = dit % 2
                        dit += 1
                        S.dma("sp", dbuf[:, b, 0], dC[:, half * 8:(half + 1) * 8, c * 512:(c + 1) * 512], writes=[f"dC{b}"])
                        S.dma("sp", dbuf[:, b, 1], dS[:, half * 8:(half + 1) * 8, c * 512:(c + 1) * 512], writes=[f"dS{b}"])
                        for g in range(4):
                            for lt in range(8):
                                tl = half * 8 + lt
                                S.mm(lambda e, g=g, lt=lt, tl=tl, b=b, half=half: e.matmul(
                                    PS[g][:, :], lhsT=AB[:, tl, g, 0:128], rhs=dbuf[:, b, 0, lt, :],
                                    start=(half == 0 and lt == 0), stop=False),
                                    reads=["AB", f"dC{b}"], writes=[PK[g]], first=(half == 0 and lt == 0))
                                S.mm(lambda e, g=g, lt=lt, tl=tl, b=b, half=half: e.matmul(
                                    PS[g][:, :], lhsT=AB[:, tl, g, 128:256], rhs=dbuf[:, b, 1, lt, :],
                                    start=False, stop=(half == 1 and lt == 7)),
                                    reads=["AB", f"dS{b}"], writes=[PK[g]], first=False)
                    for g in range(4):
                        eng = evac_eng()
                        S.op(eng, copy_op(eng, specT[:, g, c * 512:(c + 1) * 512], PS[g][:, :]), reads=[PK[g]], writes=["specT"])
            S.barrier()

            mcut(1)
            with ExitStack() as s2:
                knT = s2.enter_context(_sbuf_tensor("knT", [128, 4, NK], BF16))
                qnT = s2.enter_context(_sbuf_tensor("qnT", [128, 4, NT], BF16))
                Vn = s2.enter_context(_sbuf_tensor("Vn", [128, NKT, 8, 65], BF16))
                with ExitStack() as s3:
                    wq = s3.enter_context(_sbuf_tensor("wq", [128, KC, 512], BF16))
                    wk = s3.enter_context(_sbuf_tensor("wk", [128, KC, 512], BF16))
                    wv = s3.enter_context(_sbuf_tensor("wv", [128, KC, 512], BF16))
                    ck = s3.enter_context(_sbuf_tensor("ck", [128, 4, 512], BF16))
                    of32 = s3.enter_context(_sbuf_tensor("of32", [128, 2, 512], F32))
                    S.dma("pool", wq[:], win[:, :, C_NQ:C_NQ + 512], writes=["wq"])
                    S.dma("pool", wk[:], win[:, :, C_NK:C_NK + 512], writes=["wk"])
                    S.dma("pool", wv[:], win[:, :, C_NV:C_NV + 512], writes=["wv"])
                    S.dma("pool", ck[:], c_nk.ap()[l].rearrange("(t p) n -> p t n", p=128), writes=["ck"])
                    for j in range(4):
                        S.dma("pool", Vn[:, NTILE + j, :, 0:64], c_nv.ap()[l, j * 128:(j + 1) * 128, :].rearrange("p (h d) -> p h d", d=64), writes=["Vnc"])
                    S.op("pool", lambda e: e.memset(Vn[:, :, :, 64:65], 1.0), writes=["Vn1"])
                    it = 0
                    for t in range(NTILE):
                        for (wsb, wkey, odst, isv) in ((wk, "wk", o_nk, False), (wv, "wv", o_nv, True)):
                            b = it % 2
                            it += 1
                            for k in range(KC):
                                S.mm(lambda e, k=k, t=t, b=b, wsb=wsb: e.matmul(PS[b][:, :], lhsT=hT[:, k, t * 128:(t + 1) * 128], rhs=wsb[:, k, :],
                                                                               start=(k == 0), stop=(k == KC - 1)),
                                     reads=["hTm", wkey], writes=[PK[b]], first=(k == 0))
                            S.op("act", copy_op("act", of32[:, b, :], PS[b][:, :]), reads=[PK[b]], writes=[f"of32{b}"])
                            if isv:
                                S.op("dve", lambda e, t=t, b=b: e.tensor_copy(out=Vn[:, t, :, 0:64], in_=PS[b][:, :].rearrange("p (h d) -> p h d", d=64)),
                                     reads=[PK[b]], writes=["Vn"])
                            S.dma("sp", odst.ap()[l, t * 128:(t + 1) * 128, :], of32[:, b, :], reads=[f"of32{b}"])
                    for pr in range(4):
                        for c in range(4):
                            for (wsb, wkey, dstT, dkey) in ((wk, "wk", knT, "knT"), (wq, "wq", qnT, "qnT")):
                                b = it % 2
                                it += 1
                                for k in range(KC):
                                    S.mm(lambda e, k=k, pr=pr, c=c, b=b, wsb=wsb: e.matmul(
                                        PS[b][:, :], lhsT=wsb[:, k, pr * 128:(pr + 1) * 128], rhs=hT[:, k, c * 512:(c + 1) * 512],
                                        start=(k == 0), stop=(k == KC - 1)),
                                        reads=[wkey, "hTm"], writes=[PK[b]], first=(k == 0))
                                eng = evac_eng()
                                S.op(eng, copy_op(eng, dstT[:, pr, c * 512:(c + 1) * 512], PS[b][:, :]), reads=[PK[b]], writes=[dkey])
                        for j in range(4):
                            S.mm(lambda e, j=j, pr=pr: e.transpose(out=PSB[:, j * 128:(j + 1) * 128], in_=ck[:, j, pr * 128:(pr + 1) * 128], identity=ident[:]),
                                 reads=["ck", "ident"], writes=["psb"], first=(j == 0))
                        S.op("dve", lambda e, pr=pr: e.tensor_copy(out=knT[:, pr, NT:NK], in_=PSB[:, 0:512]), reads=["psb"], writes=["knT"])
                S.barrier()
                mcut(2)
                s3 = s2
                iq = s3.enter_context(_sbuf_tensor("iq", [72, NT], BF16))
                ik = s3.enter_context(_sbuf_tensor("ik", [72, NK], BF16))
                m1 = s3.enter_context(_sbuf_tensor("m1", [128, NPAT, 128], BF16))
                m2 = s3.enter_context(_sbuf_tensor("m2", [128, NPAT, 128], BF16))
                btf = s3.enter_context(_sbuf_tensor("btf", [128, NPAT, 128], F32))
                BT = s3.enter_context(_sbuf_tensor("BT", [128, 2, NPAT, 128], BF16))
                pT = s3.enter_context(_sbuf_tensor("pT", [128, 2, 9, 128], BF16))
                osb = s3.enter_context(_sbuf_tensor("osb", [65, 2, 512], F32))
                otmp = s3.enter_context(_sbuf_tensor("otmp", [64, 2, 512], BF16))
                for pb_ in (0, 64):
                    S.dma("pool", iq[pb_:pb_ + 8, :], indq_n.ap(), writes=["iq"])
                    S.dma("pool", ik[pb_:pb_ + 8, :], indk.ap(), writes=["ik"])
                S.dma("pool", m1[:], m1d.ap(), writes=["m1"])
                S.dma("pool", m2[:], m2d.ap(), writes=["m2"])
                items = [(h, c, qi) for h in range(8) for c in range(4) for qi in range(4)]

                def na_bt(h):
                    hb = h % 2
                    S.dma("sp", btf[:], AP(btd, ((l * 8 + h) * NPAT) * 16384, [[128, 128], [16384, NPAT], [1, 128]]),
                          reads=["btd"], writes=["btf"])
                    S.op("dve", lambda e: e.tensor_tensor(out=btf[:], in0=btf[:], in1=m1[:], op=ALU.mult),
                         reads=["btf", "m1"], writes=["btf"])
                    S.op("dve", lambda e: e.tensor_tensor(out=BT[:, hb], in0=btf[:], in1=m2[:], op=ALU.add),
                         reads=["btf", "m2"], writes=[f"BT{hb}"])

                def na_slots(i):
                    return [(j, pat) for (j, pat) in na_window(i)] + [(NTILE + j, None) for j in range(4)]

                def na_scores(k):
                    h, c, qi = items[k]
                    pr, pb, hb = h // 2, 64 * (h % 2), h % 2
                    i = c * 4 + qi
                    ab = k % 2
                    banks = [ab * 3 + 0, ab * 3 + 1, ab * 3 + 2]
                    for si, (j, pat) in enumerate(na_slots(i)):
                        bk = banks[si // 4]
                        col = (si % 4) * 128
                        S.mm(lambda e, bk=bk, col=col, j=j: e.matmul(
                            PS[bk][:, col:col + 128], lhsT=knT[pb:pb + 64, pr, j * 128:(j + 1) * 128],
                            rhs=qnT[pb:pb + 64, pr, i * 128:(i + 1) * 128], start=True, stop=False),
                            reads=["knT", "qnT"], writes=[PK[bk]], first=(si % 4 == 0))
                        S.mm(lambda e, bk=bk, col=col, j=j, pat=pat: e.matmul(
                            PS[bk][:, col:col + 128], lhsT=ik[pb:pb + 8, j * 128:(j + 1) * 128], rhs=iq[pb:pb + 8, i * 128:(i + 1) * 128],
                            start=False, stop=(pat is None)),
                            reads=["ik", "iq"], writes=[PK[bk]], first=False)
                        if pat is not None:
                            S.mm(lambda e, bk=bk, col=col, pat=pat: e.matmul(
                                PS[bk][:, col:col + 128], lhsT=ident[:], rhs=BT[:, hb, pat, :], start=False, stop=True),
                                reads=["ident", f"BT{hb}"], writes=[PK[bk]], first=False)

                def na_exp(k):
                    h, c, qi = items[k]
                    i = c * 4 + qi
                    ab = k % 2
                    ns = len(na_slots(i))
                    for g in range(3):
                        n_in = min(4, ns - g * 4)
                        if n_in <= 0:
                            continue
                        bk = ab * 3 + g
                        S.op("act", lambda e, bk=bk, g=g, n_in=n_in: e.activation(
                            out=pT[:, ab, g * 4:g * 4 + n_in, :], in_=PS[bk][:, 0:n_in * 128].rearrange("p (s n) -> p s n", n=128),
                            func=AF.Exp, scale=NA_SCALE),
                            reads=[PK[bk]], writes=[f"pT{ab}"])

                def na_pv(k):
                    h, c, qi = items[k]
                    i = c * 4 + qi
                    ab = k % 2
                    po = 6
                    slots = na_slots(i)
                    ns = len(slots)
                    for si, (j, pat) in enumerate(slots):
                        S.mm(lambda e, si=si, j=j: e.matmul(
                            PS[po][0:65, qi * 128:(qi + 1) * 128], lhsT=Vn[:, j, h, :], rhs=pT[:, ab, si, :],
                            start=(si == 0), stop=(si == ns - 1)),
                            reads=["Vn", "Vnc", "Vn1", f"pT{ab}"], writes=[PK[po]], first=(si == 0 and qi == 0))

                def na_norm(k):
                    h, c, qi = items[k]
                    pr, hb = h // 2, h % 2
                    ob = (h * 4 + c) % 2
                    po = 6
                    S.op("act", copy_op("act", osb[:, ob, :], PS[po][0:65, :]), reads=[PK[po]], writes=[f"osb{ob}"])
                    S.op("dve", lambda e: e.reciprocal(out=osb[64:65, ob, :], in_=osb[64:65, ob, :]), reads=[f"osb{ob}"], writes=[f"osb{ob}"])
                    S.mm(lambda e: e.matmul(PS[po][0:64, :], lhsT=onesf[64:65, 0:64], rhs=osb[64:65, ob, :], start=True, stop=True),
                         reads=["ones", f"osb{ob}"], writes=[PK[po]])
                    if hb == 0:
                        S.op("dve", lambda e: e.tensor_tensor(
                            out=onT[0:64, pr, c * 512:(c + 1) * 512], in0=PS[po][0:64, :], in1=osb[0:64, ob, :], op=ALU.mult),
                            reads=[PK[po], f"osb{ob}"], writes=["onT"])
                    else:
                        S.op("dve", lambda e: e.tensor_tensor(
                            out=otmp[:, ob, :], in0=PS[po][0:64, :], in1=osb[0:64, ob, :], op=ALU.mult),
                            reads=[PK[po], f"osb{ob}"], writes=[f"otmp{ob}"])
                        S.dma("sp", onT[64:128, pr, c * 512:(c + 1) * 512], otmp[:, ob, :], reads=[f"otmp{ob}"], writes=["onT"])

                na_bt(0)
                na_scores(0)
                for k in range(len(items)):
                    na_exp(k)
                    if k + 1 < len(items):
                        if items[k + 1][0] != items[k][0]:
                            na_bt(items[k + 1][0])
                        na_scores(k + 1)
                    na_pv(k)
                    if items[k][2] == 3:
                        na_norm(k)
            S.barrier()

            mcut(3)
            with ExitStack() as s2:
                wuq = s2.enter_context(_sbuf_tensor("wuq", [128, 3, 768], BF16))
                wukv = s2.enter_context(_sbuf_tensor("wukv", [128, 2, 2, 8, 64], BF16))
                qn = s2.enter_context(_sbuf_tensor("qn", [128, 3, NT], BF16))
                ckvT = s2.enter_context(_sbuf_tensor("ckvT", [128, 2, NK], BF16))
                kall = s2.enter_context(_sbuf_tensor("kall", [104, NK], BF16))
                qall = s2.enter_context(_sbuf_tensor("qall", [104, NT], BF16))
                Vm = s2.enter_context(_sbuf_tensor("Vm", [128, NKT, 8, 65], BF16))
                rC = s2.enter_context(_sbuf_tensor("rC", [96, NT], BF16))
                rS = s2.enter_context(_sbuf_tensor("rS", [96, NT], BF16))
                prT = s2.enter_context(_sbuf_tensor("prT", [96, 32], BF16))
                xr = s2.enter_context(_sbuf_tensor("xr", [96, 2, 512], BF16))
                t1 = s2.enter_context(_sbuf_tensor("t1", [96, 512], F32))
                t2 = s2.enter_context(_sbuf_tensor("t2", [96, 512], F32))
                S.dma("pool", wuq[:], w_uq.ap()[l].rearrange("(k p) n -> p k n", p=128), writes=["wuq"])
                for k2 in range(2):
                    for tt in range(2):
                        S.dma("pool", wukv[:, k2, tt],
                              w_ukv.ap()[l, k2 * 128:(k2 + 1) * 128, :].rearrange("p (h t d) -> p t h d", h=8, t=2)[:, tt], writes=["wukv"])
                S.dma("pool", rC[64:96, :], ropeC.ap(), writes=["rC"])
                S.dma("pool", rS[64:96, :], ropeS.ap(), writes=["rS"])
                S.dma("pool", prT[64:96, :], protT.ap(), writes=["prT"])
                S.dma("pool", kall[96:104, :], indk.ap(), writes=["krTi"])
                S.dma("pool", qall[96:104, :], indq_m.ap(), writes=["qrTi"])
                S.op("pool", lambda e: e.memset(Vm[:, :, :, 64:65], 1.0), writes=["Vm1"])

                def rope(src_ps, pk, b, dst, dkey, c):
                    S.op("act", copy_op("act", xr[64:96, b, :], src_ps), reads=[pk], writes=[f"xr{b}"])
                    S.mm(lambda e: e.matmul(src_ps, lhsT=prT[64:96, :], rhs=xr[64:96, b, :], start=True, stop=True),
                         reads=["prT", f"xr{b}"], writes=[pk])
                    S.op("dve", lambda e: e.tensor_tensor(out=t1[64:96, :], in0=xr[64:96, b, :], in1=rC[64:96, c * 512:(c + 1) * 512], op=ALU.mult),
                         reads=[f"xr{b}", "rC"], writes=["t1"])
                    S.op("dve", lambda e: e.tensor_tensor(out=t2[64:96, :], in0=src_ps, in1=rS[64:96, c * 512:(c + 1) * 512], op=ALU.mult),
                         reads=[pk, "rS"], writes=["t2"])
                    S.op("pool", lambda e: e.tensor_tensor(out=dst, in0=t1[64:96, :], in1=t2[64:96, :], op=ALU.add),
                         reads=["t1", "t2"], writes=[dkey])

                with ExitStack() as s3:
                    wqa = s3.enter_context(_sbuf_tensor("wqa", [128, KC, 384], BF16))
                    wkr = s3.enter_context(_sbuf_tensor("wkr", [128, KC, 288], BF16))
                    gq = s3.enter_context(_sbuf_tensor("gq", [128, 3], F32))
                    gkv = s3.enter_context(_sbuf_tensor("gkv", [128, 256], F32))
                    cc = s3.enter_context(_sbuf_tensor("cc", [128, 4, 256], BF16))
                    ckr = s3.enter_context(_sbuf_tensor("ckr", [128, 4, 32], BF16))
                    uq = s3.enter_context(_sbuf_tensor("uq", [128, 3, 512], F32))
                    sq = s3.enter_context(_sbuf_tensor("sq", [128, 3, 512], BF16))
                    rq = s3.enter_context(_sbuf_tensor("rq", [128, 512], F32))
                    ukv = s3.enter_context(_sbuf_tensor("ukv", [128, 2, 288], F32))
                    kst = s3.enter_context(_sbuf_tensor("kst", [128, 2, 6], F32))
                    kmv = s3.enter_context(_sbuf_tensor("kmv", [128, 2, 2], F32))
                    ssk = s3.enter_context(_sbuf_tensor("ssk", [128, 2], F32))
                    ckf = s3.enter_context(_sbuf_tensor("ckf", [128, 2, 256], F32))
                    ckb = s3.enter_context(_sbuf_tensor("ckb", [128, 2, 256], BF16))
                    S.dma("pool", wqa[:], win[:, :, C_Q:C_Q + 384], writes=["wqa"])
                    S.dma("pool", wkr[:], win[:, :, C_KV:C_KV + 288], writes=["wkr"])
                    load_colvec(q_norm, l * 384, 3, gq[:], "gq")
                    S.dma("sp", gkv[:], AP(kv_norm, l * 256, [[0, 128], [1, 256]]), writes=["gkv"])
                    S.dma("pool", cc[:], c_ckv.ap()[l].rearrange("(t p) n -> p t n", p=128), writes=["cc"])
                    S.dma("pool", ckr[:], c_kr.ap()[l].rearrange("(t p) n -> p t n", p=128), writes=["ckr"])
                    it = 0
                    for c in range(4):
                        for j in range(3):
                            b = it % 2
                            it += 1
                            for k in range(KC):
                                S.mm(lambda e, k=k, j=j, c=c, b=b: e.matmul(PS[b][:, :], lhsT=wqa[:, k, j * 128:(j + 1) * 128],
                                                                       rhs=hT[:, k, c * 512:(c + 1) * 512], start=(k == 0), stop=(k == KC - 1)),
                                     reads=["wqa", "hTm"], writes=[PK[b]], first=(k == 0))
                            S.op("act", lambda e, j=j, b=b: e.activation(out=sq[:, j, :], in_=PS[b][:, :], func=AF.Square), reads=[PK[b]], writes=[f"sq{j}"])
                            S.op("dve", lambda e, j=j, b=b: e.tensor_copy(out=uq[:, j, :], in_=PS[b][:, :]), reads=[PK[b]], writes=[f"uq{j}"])
                        for j in range(3):
                            S.mm(lambda e, j=j: e.matmul(PS[2][:, :], lhsT=ones[:], rhs=sq[:, j, :], start=(j == 0), stop=(j == 2)),
                                 reads=["ones", f"sq{j}"], writes=[PK[2]], first=(j == 0))
                        S.op("act", lambda e: e.activation(out=rq[:], in_=PS[2][:, :], func=AF.Sqrt, scale=1.0 / 384, bias=epsc[:, 0:1]),
                             reads=[PK[2], "epsc"], writes=["rq"])
                        S.op("dve", lambda e: e.reciprocal(out=rq[:], in_=rq[:]), reads=["rq"], writes=["rq"])
                        for j in range(3):
                            S.op("dve", lambda e, j=j, c=c: e.scalar_tensor_tensor(out=qn[:, j, c * 512:(c + 1) * 512], in0=uq[:, j, :], scalar=gq[:, j:j + 1],
                                                                                  in1=rq[:], op0=ALU.mult, op1=ALU.mult),
                                 reads=[f"uq{j}", "gq", "rq"], writes=["qn"])
                    for t in range(NTILE):
                        b = t % 2
                        for k in range(KC):
                            S.mm(lambda e, k=k, t=t, b=b: e.matmul(PS[b][:, 0:288], lhsT=hT[:, k, t * 128:(t + 1) * 128], rhs=wkr[:, k, :],
                                                                  start=(k == 0), stop=(k == KC - 1)),
                                 reads=["hTm", "wkr"], writes=[PK[b]], first=(k == 0))
                        S.op("act", copy_op("act", ukv[:, b, :], PS[b][:, 0:288]), reads=[PK[b]], writes=[f"ukv{b}"])
                        S.dma("sp", o_kr.ap()[l, t * 128:(t + 1) * 128, :], ukv[:, b, 256:288], reads=[f"ukv{b}"])
                        S.op("dve", lambda e, b=b: e.bn_stats(out=kst[:, b, :], in_=ukv[:, b, 0:256]), reads=[f"ukv{b}"], writes=[f"kst{b}"])
                        S.op("dve", lambda e, b=b: e.bn_aggr(out=kmv[:, b, :], in_=kst[:, b, :]), reads=[f"kst{b}"], writes=[f"kmv{b}"])
                        S.op("dve", lambda e, b=b: e.scalar_tensor_tensor(out=ssk[:, b:b + 1], in0=kmv[:, b, 0:1], scalar=kmv[:, b, 0:1], in1=kmv[:, b, 1:2],
                                                                         op0=ALU.mult, op1=ALU.add),
                             reads=[f"kmv{b}"], writes=[f"ssk{b}"])
                        S.op("act", lambda e, b=b: e.activation(out=ssk[:, b:b + 1], in_=ssk[:, b:b + 1], func=AF.Sqrt, scale=1.0, bias=epsc[:, 0:1]),
                             reads=[f"ssk{b}", "epsc"], writes=[f"ssk{b}"])
                        S.op("dve", lambda e, b=b: e.reciprocal(out=ssk[:, b:b + 1], in_=ssk[:, b:b + 1]), reads=[f"ssk{b}"], writes=[f"ssk{b}"])
                        S.op("dve", lambda e, b=b: e.scalar_tensor_tensor(out=ckf[:, b, :], in0=ukv[:, b, 0:256], scalar=ssk[:, b:b + 1], in1=gkv[:],
                                                                         op0=ALU.mult, op1=ALU.mult),
                             reads=[f"ukv{b}", f"ssk{b}", "gkv"], writes=[f"ckf{b}"])
                        S.dma("sp", o_ckv.ap()[l, t * 128:(t + 1) * 128, :], ckf[:, b, :], reads=[f"ckf{b}"])
                        S.op("pool", lambda e, b=b: e.tensor_copy(out=ckb[:, b, :], in_=ckf[:, b, :]), reads=[f"ckf{b}"], writes=[f"ckb{b}"])
                        for k2 in range(2):
                            S.mm(lambda e, k2=k2, b=b: e.transpose(out=PSB[:, k2 * 128:(k2 + 1) * 128], in_=ckb[:, b, k2 * 128:(k2 + 1) * 128], identity=ident[:]),
                                 reads=[f"ckb{b}", "ident"], writes=["psb"], first=(k2 == 0))
                        S.op("dve", lambda e, t=t: e.tensor_copy(out=ckvT[:, :, t * 128:(t + 1) * 128], in_=PSB[:, 0:256].rearrange("p (k n) -> p k n", n=128)),
                             reads=["psb"], writes=["ckvT"])
                    for j in range(4):
                        for k2 in range(2):
                            S.mm(lambda e, k2=k2, j=j: e.transpose(out=PSB[:, k2 * 128:(k2 + 1) * 128], in_=cc[:, j, k2 * 128:(k2 + 1) * 128], identity=ident[:]),
                                 reads=["cc", "ident"], writes=["psb"], first=(k2 == 0))
                        S.op("dve", lambda e, j=j: e.tensor_copy(out=ckvT[:, :, NT + j * 128:NT + (j + 1) * 128],
                                                                in_=PSB[:, 0:256].rearrange("p (k n) -> p k n", n=128)),
                             reads=["psb"], writes=["ckvT"])
                    for j in range(4):
                        S.mm(lambda e, j=j: e.matmul(PS[3][64:96, j * 128:(j + 1) * 128], lhsT=ckr[:, j, :], rhs=ident[:], start=True, stop=True),
                             reads=["ckr", "ident"], writes=[PK[3]], first=(j == 0))
                    S.op("dve", lambda e: e.tensor_copy(out=kall[64:96, NT:NK], in_=PS[3][64:96, :]), reads=[PK[3]], writes=["krT"])
                    for c in range(4):
                        b = c % 2
                        for k in range(KC):
                            S.mm(lambda e, k=k, c=c, b=b: e.matmul(PS[b][64:96, :], lhsT=wkr[:, k, 256:288], rhs=hT[:, k, c * 512:(c + 1) * 512],
                                                                  start=(k == 0), stop=(k == KC - 1)),
                                 reads=["wkr", "hTm"], writes=[PK[b]], first=(k == 0))
                        rope(PS[b][64:96, :], PK[b], b, kall[64:96, c * 512:(c + 1) * 512], "krT", c)
                    for t in range(NKT):
                        b = t % 2
                        for k2 in range(2):
                            S.mm(lambda e, k2=k2, t=t, b=b: e.matmul(PS[b][:, :], lhsT=ckvT[:, k2, t * 128:(t + 1) * 128],
                                                                    rhs=wukv[:, k2, 1].rearrange("p h d -> p (h d)"),
                                                                    start=(k2 == 0), stop=(k2 == 1)),
                                 reads=["ckvT", "wukv"], writes=[PK[b]], first=(k2 == 0))
                        eng = evac_eng()
                        S.op(eng, copy_op(eng, Vm[:, t, :, 0:64], PS[b][:, :].rearrange("p (h d) -> p h d", d=64)), reads=[PK[b]], writes=["Vm"])
                S.barrier()
                mcut(4)
                s3 = s2
                pTm = s3.enter_context(_sbuf_tensor("pTm", [128, 4, 512], BF16))
                osb = s3.enter_context(_sbuf_tensor("osbm", [65, 2, 512], F32))
                otmp = s3.enter_context(_sbuf_tensor("otmpm", [64, 2, 512], BF16))
                sit = 0
                import os
                mdbg = os.environ.get("KDBG_MLA", "")
                for h in range(8):
                    pr, hh = h // 2, h % 2
                    for c5 in range(5):
                        b = c5 % 2
                        for k2 in range(2):
                            S.mm(lambda e, k2=k2, c5=c5, b=b, h=h: e.matmul(
                                PS[b][0:64, :], lhsT=wukv[:, k2, 0, h, :], rhs=ckvT[:, k2, c5 * 512:(c5 + 1) * 512],
                                start=(k2 == 0), stop=(k2 == 1)),
                                reads=["wukv", "ckvT"], writes=[PK[b]], first=(k2 == 0))
                        eng = evac_eng()
                        S.op(eng, copy_op(eng, kall[0:64, c5 * 512:(c5 + 1) * 512], PS[b][0:64, :]), reads=[PK[b]], writes=["knp"])
                    for c in range(4):
                        b = c % 2
                        for j in range(3):
                            S.mm(lambda e, j=j, c=c, b=b, h=h: e.matmul(
                                PS[b][0:64, :], lhsT=wuq[:, j, h * 96:h * 96 + 64], rhs=qn[:, j, c * 512:(c + 1) * 512],
                                start=(j == 0), stop=(j == 2)),
                                reads=["wuq", "qn"], writes=[PK[b]], first=(j == 0))
                        eng = evac_eng()
                        S.op(eng, copy_op(eng, qall[0:64, c * 512:(c + 1) * 512], PS[b][0:64, :]), reads=[PK[b]], writes=["qnp"])
                        b2 = 2 + (c % 2)
                        for j in range(3):
                            S.mm(lambda e, j=j, c=c, b2=b2, h=h: e.matmul(
                                PS[b2][64:96, :], lhsT=wuq[:, j, h * 96 + 64:h * 96 + 96], rhs=qn[:, j, c * 512:(c + 1) * 512],
                                start=(j == 0), stop=(j == 2)),
                                reads=["wuq", "qn"], writes=[PK[b2]], first=(j == 0))
                        rope(PS[b2][64:96, :], PK[b2], c % 2, qall[64:96, c * 512:(c + 1) * 512], "qrT", c)
                    LA = 2
                    items = [(c, j) for c in range(4) for j in range(NKT)]
                    pend = []

                    def mla_scores(c, j, sidx):
                        bk = sidx % 4
                        S.mm(lambda e: e.matmul(PS[bk][:, :], lhsT=kall[:, j * 128:(j + 1) * 128], rhs=qall[:, c * 512:(c + 1) * 512],
                                                start=True, stop=True),
                             reads=["knp", "qnp", "krT", "krTi", "qrT", "qrTi"], writes=[PK[bk]], first=True)

                    def mla_exp_pv(c, j, sidx, h=h):
                        bk = sidx % 4
                        po = 4 + (c % 2)
                        S.op("act", lambda e: e.activation(out=pTm[:, bk, :], in_=PS[bk][:, :], func=AF.Exp, scale=MLA_SCALE),
                             reads=[PK[bk]], writes=[f"pTm{bk}"])
                        S.mm(lambda e: e.matmul(PS[po][0:65, :], lhsT=Vm[:, j, h, :], rhs=pTm[:, bk, :], start=(j == 0), stop=(j == NKT - 1)),
                             reads=["Vm", "Vm1", f"pTm{bk}"], writes=[PK[po]], first=(j == 0))

                    def mla_norm_a(c, h=h):
                        po = 4 + (c % 2)
                        ob = c % 2
                        S.op("act", copy_op("act", osb[:, ob, :], PS[po][0:65, :]), reads=[PK[po]], writes=[f"osb{ob}"])
                        S.op("dve", lambda e: e.reciprocal(out=osb[64:65, ob, :], in_=osb[64:65, ob, :]), reads=[f"osb{ob}"], writes=[f"osb{ob}"])

                    def mla_norm_b(c, h=h, pr=pr, hh=hh):
                        po = 4 + (c % 2)
                        ob = c % 2
                        S.mm(lambda e: e.matmul(PS[po][0:64, :], lhsT=onesf[64:65, 0:64], rhs=osb[64:65, ob, :], start=True, stop=True),
                             reads=["ones", f"osb{ob}"], writes=[PK[po]])
                        if hh == 0:
                            S.op("dve", lambda e: e.tensor_tensor(
                                out=omT[0:64, pr, c * 512:(c + 1) * 512], in0=PS[po][0:64, :], in1=osb[0:64, ob, :], op=ALU.mult),
                                reads=[PK[po], f"osb{ob}"], writes=["omT"])
                        else:
                            S.op("dve", lambda e: e.tensor_tensor(
                                out=otmp[:, ob, :], in0=PS[po][0:64, :], in1=osb[0:64, ob, :], op=ALU.mult),
                                reads=[PK[po], f"osb{ob}"], writes=[f"otmp{ob}"])
                            S.dma("sp", omT[64:128, pr, c * 512:(c + 1) * 512], otmp[:, ob, :], reads=[f"otmp{ob}"], writes=["omT"])

                    n_it = len(items)
                    for k in range(n_it + LA):
                        if k < n_it:
                            mla_scores(items[k][0], items[k][1], sit + k)
                        for pd in list(pend):
                            if k >= pd[0]:
                                mla_norm_b(pd[1])
                                pend.remove(pd)
                        if k >= LA:
                            c_, j_ = items[k - LA]
                            mla_exp_pv(c_, j_, sit + k - LA)
                            if j_ == NKT - 1:
                                mla_norm_a(c_)
                                pend.append((k + 3, c_))
                    for pd in pend:
                        mla_norm_b(pd[1])
                    sit += n_it
            S.barrier()

            mcut(5)
            with ExitStack() as s2:
                mT = s2.enter_context(_sbuf_tensor("mT", [128, KC, NT], BF16))
                wg = s2.enter_context(_sbuf_tensor("wg", [128, 2, 3, KC, 128], BF16))
                wb_ = s2.enter_context(_sbuf_tensor("wbr", [128, 2, 3, 4, 128], BF16))
                bgT = s2.enter_context(_sbuf_tensor("bgT", [128, 24], F32))
                gsb = s2.enter_context(_sbuf_tensor("gsb", [128, 2, 512], BF16))
                acc = s2.enter_context(_sbuf_tensor("acc", [128, 2, 512], F32))
                tm = s2.enter_context(_sbuf_tensor("tm", [128, 2, 512], F32))
                wo = s2.enter_context(_sbuf_tensor("wo", [128, KC, D], BF16))
                zt = s2.enter_context(_sbuf_tensor("ztm", [128, 2, D], F32))
                tmp = s2.enter_context(_sbuf_tensor("tmpm", [128, 2, D], F32))
                xt2 = s2.enter_context(_sbuf_tensor("xt2m", [128, 2, D], F32))
                alloc_vec(s2, V)
                load_epi(l, 5, 1, 1.0, V)
                load_colvec(b_gate, l * 3 * D, 24, bgT[:], "bgT")
                S.dma("pool", wo[:], w_out.ap()[l].rearrange("(k p) n -> p k n", p=128), writes=["wo"])
                wgv = w_gate.ap()[l].rearrange("(k p) (g n) -> p g k n", p=128, g=3)
                brs = [w.ap()[l].rearrange("(k p) n -> p k n", p=128) for w in (w_bf, w_bm, w_bn)]
                bins = [(specT, "specT"), (omT, "omT"), (onT, "onT")]
                git = 0
                for oc in range(KC):
                    wbuf = oc % 2
                    S.dma("pool", wg[:, wbuf], wgv[:, :, :, oc * 128:(oc + 1) * 128], writes=[f"wg{wbuf}"])
                    for gi in range(3):
                        S.dma("pool", wb_[:, wbuf, gi], brs[gi][:, :, oc * 128:(oc + 1) * 128], writes=[f"wbr{wbuf}"])
                    for c in range(4):
                        ab = (oc * 4 + c) % 2
                        for gi in range(3):
                            pg = (git % 2)
                            py = 2 + (git % 2)
                            git += 1
                            for k in range(KC):
                                S.mm(lambda e, k=k, gi=gi, c=c, pg=pg, wbuf=wbuf: e.matmul(
                                    PS[pg][:, :], lhsT=wg[:, wbuf, gi, k, :], rhs=hT[:, k, c * 512:(c + 1) * 512], start=(k == 0), stop=(k == KC - 1)),
                                    reads=[f"wg{wbuf}", "hTm"], writes=[PK[pg]], first=(k == 0))
                            bsrc, bkey = bins[gi]
                            for k4 in range(4):
                                S.mm(lambda e, k4=k4, gi=gi, c=c, py=py, wbuf=wbuf, bsrc=bsrc: e.matmul(
                                    PS[py][:, :], lhsT=wb_[:, wbuf, gi, k4, :], rhs=bsrc[:, k4, c * 512:(c + 1) * 512], start=(k4 == 0), stop=(k4 == 3)),
                                    reads=[f"wbr{wbuf}", bkey], writes=[PK[py]], first=(k4 == 0))
                            gb = git % 2
                            S.op("act", lambda e, pg=pg, gi=gi, oc=oc, gb=gb: e.activation(
                                out=gsb[:, gb, :], in_=PS[pg][:, :], func=AF.Sigmoid, bias=bgT[:, gi * 8 + oc:gi * 8 + oc + 1], scale=1.0),
                                reads=[PK[pg], "bgT"], writes=[f"gsb{gb}"])
                            if gi == 0:
                                S.op("dve", lambda e, py=py, gb=gb, ab=ab: e.tensor_tensor(out=acc[:, ab, :], in0=PS[py][:, :], in1=gsb[:, gb, :], op=ALU.mult),
                                     reads=[PK[py], f"gsb{gb}"], writes=[f"acc{ab}"])
                            else:
                                S.op("dve", lambda e, py=py, gb=gb, ab=ab: e.tensor_tensor(out=tm[:, ab, :], in0=PS[py][:, :], in1=gsb[:, gb, :], op=ALU.mult),
                                     reads=[PK[py], f"gsb{gb}"], writes=[f"tm{ab}"])
                                if gi == 1:
                                    S.op("pool", lambda e, ab=ab: e.tensor_tensor(out=acc[:, ab, :], in0=acc[:, ab, :], in1=tm[:, ab, :], op=ALU.add),
                                         reads=[f"acc{ab}", f"tm{ab}"], writes=[f"acc{ab}"])
                                else:
                                    S.op("pool", lambda e, ab=ab, oc=oc, c=c: e.tensor_tensor(out=mT[:, oc, c * 512:(c + 1) * 512], in0=acc[:, ab, :], in1=tm[:, ab, :], op=ALU.add),
                                         reads=[f"acc{ab}", f"tm{ab}"], writes=["mT"])
                for t in range(NTILE):
                    b = t % 2
                    S.dma("sp", xt2[:, b, :], xin[t * 128:(t + 1) * 128, :], reads=["xdram"], writes=[f"xt2{b}"])
                    for half in range(2):
                        pi = 4 + half
                        for k in range(KC):
                            S.mm(lambda e, k=k, t=t, half=half, pi=pi: e.matmul(
                                PS[pi][:, :], lhsT=mT[:, k, t * 128:(t + 1) * 128], rhs=wo[:, k, half * 512:(half + 1) * 512],
                                start=(k == 0), stop=(k == KC - 1)),
                                reads=["mT", "wo"], writes=[PK[pi]], first=(k == 0))
                        S.op("dve", lambda e, pi=pi, b=b, half=half: e.tensor_tensor(
                            out=tmp[:, b, half * 512:(half + 1) * 512], in0=PS[pi][:, :], in1=V["gate_bc"][:, half * 512:(half + 1) * 512], op=ALU.mult),
                            reads=[PK[pi], "gate_bc"], writes=[f"tmpm{b}"])
                    S.op("dve", lambda e, b=b: e.scalar_tensor_tensor(out=tmp[:, b, :], in0=xt2[:, b, :], scalar=ALPHA, in1=tmp[:, b, :],
                                                                     op0=ALU.mult, op1=ALU.add),
                         reads=[f"xt2{b}", f"tmpm{b}"], writes=[f"tmpm{b}"])
                    ln_out(tmp[:, b, :], f"tmpm{b}", T, V, zt[:, b, :], f"ztm{b}", [xo[t * 128:(t + 1) * 128, :] for xo in xouts], b)
        S.barrier()

    prologue()
    bufs = [xa.ap(), xb.ap()]
    cur = x0.ap()
    stage = 0
    for l in range(DEPTH):
        for kind in ("ffn1", "mix", "ffn2"):
            if stage >= nstages:
                break
            last = (stage == nstages - 1)
            dst = y.ap() if last else bufs[stage % 2]
            if kind == "ffn1":
                ffn(l, 1, cur, [dst])
            elif kind == "mix":
                mixer(l, cur, [dst])
            else:
                ffn(l, 2, cur, [dst])
            cur = dst
            stage += 1
    for e in ("sp", "pool", "act", "dve", "pe"):
        S.wait_all_dma(e)
    import os
    if os.environ.get("KDBG_STATS"):
        print("SIGVALS", S.sigval, "POS", S.pos, "DMA", max(S.dcount), flush=True)
    return nc


def _bf16(a):
    return np.asarray(a, dtype=np.float32).astype(ml_dtypes.bfloat16)


def _role_consts(role):
    c = {}
    t = np.arange(NT)
    if role == "sample":
        pos = np.stack([t // 64, t % 64], -1).astype(np.float32)
        inv = (10000.0 ** (-np.arange(8, dtype=np.float32) / 8)).astype(np.float32)
        ang = pos[:, :, None] * inv
        ang = np.concatenate([ang, ang], -1)
        cos = np.cos(ang).reshape(NT, 32).T
        sin = np.sin(ang).reshape(NT, 32).T
        c["ropeC"] = np.ascontiguousarray(cos, dtype=np.float32)
        c["ropeS"] = np.ascontiguousarray(sin, dtype=np.float32)
        c["indq_m"] = np.zeros((8, NT), np.float32)
        c["indq_n"] = np.zeros((8, NT), np.float32)
        c["indk"] = np.zeros((8, NK), np.float32)
        L = NT
        blk = np.zeros(NT, np.int64)
        loc = t
    else:
        c["ropeC"] = np.ones((32, NT), np.float32)
        c["ropeS"] = np.zeros((32, NT), np.float32)
        oh = (t[None, :] // 256 == np.arange(8)[:, None]).astype(np.float32)
        c["indq_m"] = oh * BIG_MLA
        c["indq_n"] = oh * BIG_NA
        ik = np.zeros((8, NK), np.float32)
        ik[:, :NT] = oh
        c["indk"] = ik
        L = 256
        blk = t // 256
        loc = t % 256
    norm = 1.0 / math.sqrt(L * 128.0)
    same = (blk[:, None] == blk[None, :])
    ph = (2.0 * np.pi / L) * ((loc[:, None] * loc[None, :]) % L).astype(np.float64)
    c["dftC"] = _bf16(np.where(same, np.cos(ph) * norm, 0.0))
    c["dftS"] = _bf16(np.where(same, -np.sin(ph) * norm, 0.0))
    cc = np.arange(128)
    ph2 = (2.0 * np.pi / 128) * ((cc[:, None] * cc[None, :]) % 128).astype(np.float64)
    c["dftCS"] = np.concatenate([np.cos(ph2), np.sin(ph2)], 1).astype(np.float32)
    m1 = np.zeros((128, NPAT, 128), np.float32)
    m2 = np.zeros((128, NPAT, 128), np.float32)
    if role == "sample":
        kk = np.arange(128)
        kr, kc = kk // 64, kk % 64
        qr, qc = kk // 64, kk % 64
        cstart = np.clip(qc - 8, 0, 48)
        col_ok = (kc[:, None] >= cstart[None, :]) & (kc[:, None] < cstart[None, :] + 16)
        for p, (dl, typ) in enumerate(PAT_DELTA):
            rel = 2 * dl + kr[:, None] - qr[None, :]
            row_ok = ((rel >= -4) & (rel <= 3)) if typ == 0 else np.ones_like(rel, bool)
            ok = col_ok & row_ok
            m1[:, p, :] = np.where(ok, 1.0 / NA_SCALE, 0.0)
            m2[:, p, :] = np.where(ok, 0.0, NEG)
    c["m1d"] = m1
    c["m2d"] = m2
    pr = np.zeros((32, 32), np.float32)
    for a in range(2):
        for j in range(16):
            d = a * 16 + j
            if j < 8:
                pr[d, d + 8] = -1.0
            else:
                pr[d, d - 8] = 1.0
    c["protT"] = np.ascontiguousarray(pr.T)
    c["identd"] = np.eye(128, dtype=np.float32)
    return c


_CACHE = {}


def _get_nc(nstages):
    if nstages not in _CACHE:
        _CACHE[nstages] = build(nstages)
    return _CACHE[nstages]


def run_units(inputs, nstages=3 * DEPTH, cores=None):
    f32 = lambda a: np.ascontiguousarray(np.asarray(a), dtype=np.float32)
    xp = f32(inputs["x_prompt"])
    xs = f32(inputs["x_sample"])
    shared = {}
    for nm in ("w_ada", "b_ada", "ffn1_w1", "ffn1_w3", "ffn1_w2", "ffn2_w1", "ffn2_w3", "ffn2_w2", "w_in",
               "mla_q_norm", "mla_w_uq", "mla_kv_norm", "mla_w_ukv", "w_branch_f", "w_branch_m", "w_branch_n",
               "w_gate", "b_gate", "w_out", "ln_g", "ln_b"):
        shared[nm] = f32(inputs[nm])
    rp = f32(inputs["na_rpb"])[..., ::-1].reshape(-1)
    shared["rpbr"] = np.concatenate([np.zeros(RPAD, np.float32), rp, np.zeros(RPAD, np.float32)])
    cp = _role_consts("prompt")
    cs = _role_consts("sample")
    zc = {"c_ckv": np.zeros((DEPTH, NCTX, 256), np.float32), "c_kr": np.zeros((DEPTH, NCTX, 32), np.float32),
          "c_nk": np.zeros((DEPTH, NCTX, 512), np.float32), "c_nv": np.zeros((DEPTH, NCTX, 512), np.float32)}
    in_maps = []
    for core in range(8):
        m = dict(shared)
        if core < 4 or core >= 6:
            u = core if core < 4 else core - 6
            m["x0"] = xp[u * 8:(u + 1) * 8].reshape(NT, D)
            m["cvec"] = f32(inputs["c_ctx"]).reshape(1, D)
            m.update(zc)
            m.update(cp)
        else:
            b = core - 4
            m["x0"] = xs[b]
            m["cvec"] = f32(inputs["c"])[b].reshape(1, D)
            m["c_ckv"] = f32(inputs["cache_mla_ckv"])[b]
            m["c_kr"] = f32(inputs["cache_mla_krope"])[b]
            m["c_nk"] = f32(inputs["cache_na_k"])[b].reshape(DEPTH, NCTX, 512)
            m["c_nv"] = f32(inputs["cache_na_v"])[b].reshape(DEPTH, NCTX, 512)
            m.update(cs)
        in_maps.append(m)
    nc = _get_nc(nstages)
    if cores is not None:
        res = run_bass_kernel_spmd(nc, [in_maps[c] for c in cores], core_ids=list(range(len(cores))))
        return {c: res.results[i] for i, c in enumerate(cores)}
    res = run_bass_kernel_spmd(nc, in_maps, core_ids=list(range(8)))
    return res.results


def kernel(**inputs):
    r = run_units(inputs)
    yp = np.concatenate([r[u]["y"].reshape(8, 256, D) for u in range(4)], 0)
    ys = np.stack([r[4]["y"], r[5]["y"]], 0)

    def gather(name, tail):
        a = np.concatenate([r[u][name].reshape(DEPTH, 8, 256, -1).transpose(1, 0, 2, 3) for u in range(4)], 0)
        return np.ascontiguousarray(a.reshape((32, DEPTH, 256) + tail), dtype=np.float32)

    return (np.ascontiguousarray(yp, dtype=np.float32), np.ascontiguousarray(ys, dtype=np.float32),
            gather("o_ckv", (256,)), gather("o_kr", (32,)), gather("o_nk", (8, 64)), gather("o_nv", (8, 64)))
```

```python
import math
from collections import defaultdict

import numpy as np
import ml_dtypes

import concourse.bass as bass
import concourse.mybir as mybir
from concourse.bass_utils import run_bass_kernel_spmd

F32 = mybir.dt.float32
BF16 = mybir.dt.bfloat16
AF = mybir.ActivationFunctionType
ALU = mybir.AluOpType

D = 1024
KC = 8
DEPTH = 4
NT = 2048
NTILE = 16
NCTX = 512
NK = NT + NCTX
NKT = NK // 128
FF = 2816
FC = FF // 128
IN_W = 2720
C_F, C_Q, C_KV, C_R, C_NQ, C_NK, C_NV = 0, 512, 896, 1152, 1184, 1696, 2208
ALPHA = (2.0 * DEPTH) ** 0.25
MLA_SCALE = 96 ** -0.5
NA_SCALE = 0.125
BIG_MLA = 576.0
BIG_NA = 480.0
NEG = -30000.0
NPAT = 12
RPAD = 64


class _PEProxy:
    def __init__(self, pe):
        self.pe = pe
        self.last_stop = None

    def matmul(self, *a, **kw):
        self.last_stop = kw.get("stop", None)
        return self.pe.matmul(*a, **kw)

    def transpose(self, *a, **kw):
        self.last_stop = True
        return self.pe.transpose(*a, **kw)


class Sched:
    ENGS = ("pe", "act", "dve", "pool", "sp")

    def __init__(self, nc, n_dma_sems=56):
        self.nc = nc
        self.eng = {"pe": nc.tensor, "act": nc.scalar, "dve": nc.vector, "pool": nc.gpsimd, "sp": nc.sync}
        self.esem = {e: nc.alloc_semaphore(f"es_{e}") for e in self.ENGS}
        self.pos = {e: 0 for e in self.ENGS}
        self.sigs = {e: [] for e in self.ENGS}
        self.sigval = {e: 0 for e in self.ENGS}
        self.last = {e: None for e in self.ENGS}
        self.waited = defaultdict(int)
        self.dsems = [nc.alloc_semaphore(f"ds_{i}") for i in range(n_dma_sems)]
        self.dcount = [0] * n_dma_sems
        self.dnext = 0
        self.dnext_pool = 0
        self.W = defaultdict(dict)
        self.R = defaultdict(dict)
        self.peproxy = _PEProxy(nc.tensor)

    def _need(self, eng, tok, raw):
        if tok[0] == "d":
            _, idx, val = tok
            return (("d", idx), self.dsems[idx], val)
        _, f, p = tok
        if f == eng and eng == "pe":
            return None
        val = None
        for (sp_, sv) in reversed(self.sigs[f]):
            if sp_ >= p:
                val = sv
            else:
                break
        if val is None:
            ins, lp = self.last[f]
            assert lp >= p
            self.sigval[f] += 1
            ins.then_inc(self.esem[f], 1)
            self.sigs[f].append((lp, self.sigval[f]))
            val = self.sigval[f]
        return (("e", f), self.esem[f], val)

    def _waits(self, eng, reads, writes):
        needs = {}

        def add(tok, raw):
            n = self._need(eng, tok, raw)
            if n:
                key, sem, val = n
                if key not in needs or needs[key][0] < val:
                    needs[key] = (val, sem)

        for k in reads:
            for t in self.W[k].values():
                add(t, True)
            if k.startswith("ps"):
                for rk, r in self.R[k].items():
                    if rk != eng:
                        add(r, False)
        for k in writes:
            for r in self.R[k].values():
                add(r, False)
        for key, (val, sem) in needs.items():
            if self.waited[(eng, key)] < val:
                self.eng[eng].wait_ge(sem, val)
                self.waited[(eng, key)] = val

    def _record(self, tok, reads, writes, rkey):
        for k in writes:
            self.W[k][rkey] = tok
        for k in reads:
            self.R[k][rkey] = tok

    def op(self, eng, fn, reads=(), writes=(), check_writes=True):
        self._waits(eng, reads, writes if check_writes else ())
        if eng == "pe":
            self.peproxy.last_stop = None
            ins = fn(self.peproxy)
            sig = bool(self.peproxy.last_stop)
        else:
            ins = fn(self.eng[eng])
            sig = True
        self.pos[eng] += 1
        p = self.pos[eng]
        self.last[eng] = (ins, p)
        if sig:
            self.sigval[eng] += 1
            ins.then_inc(self.esem[eng], 1)
            self.sigs[eng].append((p, self.sigval[eng]))
        self._record(("e", eng, p), reads, writes, eng)
        return ins

    def mm(self, fn, reads=(), writes=(), first=True):
        return self.op("pe", fn, reads, writes, check_writes=first)

    def dma(self, q, out, in_, reads=(), writes=(), **kw):
        half = len(self.dsems) // 2
        if q == "pool":
            idx = self.dnext_pool
            self.dnext_pool = (self.dnext_pool + 1) % half
        else:
            idx = half + self.dnext
            self.dnext = (self.dnext + 1) % (len(self.dsems) - half)
        if self.dcount[idx] and self.waited[(q, ("d", idx))] < self.dcount[idx]:
            self.eng[q].wait_ge(self.dsems[idx], self.dcount[idx])
            self.waited[(q, ("d", idx))] = self.dcount[idx]
        self._waits(q, reads, writes)
        self.eng[q].dma_start(out=out, in_=in_, **kw).then_inc(self.dsems[idx], 16)
        self.dcount[idx] += 16
        tok = ("d", idx, self.dcount[idx])
        self._record(tok, reads, writes, ("d", idx))
        return tok

    def wait_all_dma(self, eng="sp"):
        for idx, c in enumerate(self.dcount):
            if c and self.waited[(eng, ("d", idx))] < c:
                self.eng[eng].wait_ge(self.dsems[idx], c)
                self.waited[(eng, ("d", idx))] = c

    def barrier(self):
        toks = []
        for f in self.ENGS:
            if self.last[f] is not None:
                toks.append(("e", f, self.last[f][1]))
        for e in self.ENGS:
            needs = {}
            for t in toks:
                n = self._need(e, t, True)
                if n:
                    key, sem, val = n
                    if key not in needs or needs[key][0] < val:
                        needs[key] = (val, sem)
            for key, (val, sem) in needs.items():
                if self.waited[(e, key)] < val:
                    self.eng[e].wait_ge(sem, val)
                    self.waited[(e, key)] = val
            self.wait_all_dma(e)
        self.W = defaultdict(dict)
        self.R = defaultdict(dict)


def na_window(i):
    if i <= 1:
        tiles, typ = [0, 1, 2, 3], 1
    elif i >= 14:
        tiles, typ = [12, 13, 14, 15], 1
    else:
        tiles, typ = [i - 2, i - 1, i, i + 1, i + 2], 0
    out = []
    for j in tiles:
        dl = j - i
        pat = (dl + 2) if typ == 0 else (5 + dl + 3)
        out.append((j, pat))
    return out


PAT_DELTA = [(-2, 0), (-1, 0), (0, 0), (1, 0), (2, 0)] + [(d, 1) for d in range(-3, 4)]


def build(nstages=3 * DEPTH):
    nc = bass.Bass("TRN2", target_bir_lowering=False)
    S = Sched(nc)

    def din(name, shape, dt=F32):
        return nc.dram_tensor(name, list(shape), dt, kind="ExternalInput")

    x0 = din("x0", [NT, D])
    cvec = din("cvec", [1, D])
    c_ckv = din("c_ckv", [DEPTH, NCTX, 256])
    c_kr = din("c_kr", [DEPTH, NCTX, 32])
    c_nk = din("c_nk", [DEPTH, NCTX, 512])
    c_nv = din("c_nv", [DEPTH, NCTX, 512])
    w_ada = din("w_ada", [DEPTH, D, 9 * D])
    b_ada = din("b_ada", [DEPTH, 9 * D])
    fw = {}
    for nm in ("ffn1_w1", "ffn1_w3", "ffn2_w1", "ffn2_w3"):
        fw[nm] = din(nm, [DEPTH, D, FF])
    for nm in ("ffn1_w2", "ffn2_w2"):
        fw[nm] = din(nm, [DEPTH, FF, D])
    w_in = din("w_in", [DEPTH, D, IN_W])
    q_norm = din("mla_q_norm", [DEPTH, 384])
    w_uq = din("mla_w_uq", [DEPTH, 384, 768])
    kv_norm = din("mla_kv_norm", [DEPTH, 256])
    w_ukv = din("mla_w_ukv", [DEPTH, 256, 1024])
    rpbr = din("rpbr", [2 * RPAD + DEPTH * 8 * 15 * 31])
    w_bf = din("w_branch_f", [DEPTH, 512, D])
    w_bm = din("w_branch_m", [DEPTH, 512, D])
    w_bn = din("w_branch_n", [DEPTH, 512, D])
    w_gate = din("w_gate", [DEPTH, D, 3 * D])
    b_gate = din("b_gate", [DEPTH, 3 * D])
    w_out = din("w_out", [DEPTH, D, D])
    ln_g = din("ln_g", [DEPTH, 3, D])
    ln_b = din("ln_b", [DEPTH, 3, D])
    ropeC = din("ropeC", [32, NT])
    ropeS = din("ropeS", [32, NT])
    protT = din("protT", [32, 32])
    indq_m = din("indq_m", [8, NT])
    indq_n = din("indq_n", [8, NT])
    indk = din("indk", [8, NK])
    dftC = din("dftC", [NT, NT], BF16)
    dftS = din("dftS", [NT, NT], BF16)
    dftCS = din("dftCS", [128, 256])
    m1d = din("m1d", [128, NPAT, 128])
    m2d = din("m2d", [128, NPAT, 128])
    identd = din("identd", [128, 128])

    def dout(name, shape):
        return nc.dram_tensor(name, list(shape), F32, kind="ExternalOutput")

    y = dout("y", [NT, D])
    o_ckv = dout("o_ckv", [DEPTH, NT, 256])
    o_kr = dout("o_kr", [DEPTH, NT, 32])
    o_nk = dout("o_nk", [DEPTH, NT, 512])
    o_nv = dout("o_nv", [DEPTH, NT, 512])

    xa = nc.dram_tensor("xa", [NT, D], F32, kind="Internal")
    xb = nc.dram_tensor("xb", [NT, D], F32, kind="Internal")
    ada_d = nc.dram_tensor("ada_d", [DEPTH, 9 * D], F32, kind="Internal")
    btd = nc.dram_tensor("btd", [DEPTH, 8, NPAT, 128, 128], F32, kind="Internal")

    def AP(t, off, dims):
        return bass.AP(t, off, [list(d) for d in dims])

    _uid = [0]
    _orig_sbuf_tensor = nc.sbuf_tensor

    def _sbuf_tensor(name, shape, dt):
        _uid[0] += 1
        return _orig_sbuf_tensor(f"{name}_{_uid[0]}", shape, dt)

    sb = nc.alloc_sbuf_tensor
    ident = sb("ident", [128, 128], BF16)
    ones = sb("ones", [128, 128], BF16)
    epsc = sb("epsc", [128, 4], F32)
    identf = sb("identf", [128, 128], F32)
    onesf = sb("onesf", [128, 64], F32)
    PS = [nc.alloc_psum_tensor(f"ps{i}", [128, 512], F32) for i in range(7)]
    PSB = nc.alloc_psum_tensor("psb", [128, 1024], BF16)
    PK = [f"ps{i}" for i in range(7)]

    S.dma("sp", identf[:], identd.ap(), writes=["identf"])
    S.op("dve", lambda e: e.tensor_copy(out=ident[:], in_=identf[:]), reads=["identf"], writes=["ident"])
    S.op("dve", lambda e: e.memset(ones[:], 1.0), writes=["ones"])
    S.op("dve", lambda e: e.memset(onesf[:], 1.0), writes=["ones"])
    S.op("dve", lambda e: e.memset(epsc[:, 0:1], 1e-6), writes=["epsc"])
    S.op("dve", lambda e: e.memset(epsc[:, 1:2], 1e-5), writes=["epsc"])

    evac_rr = [0]

    def evac_eng():
        evac_rr[0] += 1
        return "act" if evac_rr[0] % 2 else "dve"

    def copy_op(eng, out, in_):
        if eng == "act":
            return lambda e: e.activation(out=out, in_=in_, func=AF.Copy)
        return lambda e: e.tensor_copy(out=out, in_=in_)

    def prologue():
        with _sbuf_tensor("crow", [8, 128], F32) as crow, \
                _sbuf_tensor("srow", [8, 128], BF16) as srow, \
                _sbuf_tensor("scT", [128, 8], BF16) as scT, \
                _sbuf_tensor("wada", [128, 2, 8, 512], BF16) as wada, \
                _sbuf_tensor("brow", [1, 2, 512], F32) as brow, \
                _sbuf_tensor("orow", [1, 2, 512], F32) as orow:
            S.dma("sp", crow[:], cvec.ap().rearrange("o (k p) -> (o k) p", p=128), writes=["crow"])
            S.op("act", lambda e: e.activation(out=srow[:], in_=crow[:], func=AF.Silu), reads=["crow"], writes=["srow"])
            S.mm(lambda e: e.matmul(PS[0][:, 0:8], lhsT=srow[:], rhs=ident[0:8, 0:8], start=True, stop=True),
                 reads=["srow", "ident"], writes=[PK[0]])
            S.op("dve", lambda e: e.tensor_copy(out=scT[:], in_=PS[0][:, 0:8]), reads=[PK[0]], writes=["scT"])
            it = 0
            for l in range(DEPTH):
                wv = w_ada.ap()[l].rearrange("(k p) n -> p k n", p=128)
                for j in range(18):
                    b = it % 2
                    it += 1
                    S.dma("pool", wada[:, b], wv[:, :, j * 512:(j + 1) * 512], writes=[f"wada{b}"])
                    S.dma("sp", brow[:, b], b_ada.ap()[l:l + 1, j * 512:(j + 1) * 512], writes=[f"brow{b}"])
                    pk = PK[b]
                    for k in range(KC):
                        S.mm(lambda e, k=k, b=b: e.matmul(PS[b][0:1, :], lhsT=scT[:, k:k + 1], rhs=wada[:, b, k, :],
                                                          start=(k == 0), stop=(k == KC - 1)),
                             reads=["scT", f"wada{b}"], writes=[pk], first=(k == 0))
                    S.op("dve", lambda e, b=b: e.tensor_tensor(out=orow[:, b], in0=PS[b][0:1, :], in1=brow[:, b], op=ALU.add),
                         reads=[pk, f"brow{b}"], writes=[f"orow{b}"])
                    S.dma("sp", ada_d.ap()[l:l + 1, j * 512:(j + 1) * 512], orow[:, b], reads=[f"orow{b}"], writes=["ada_d"])
        for l in range(DEPTH):
            for p, (dl, typ) in enumerate(PAT_DELTA):
                for kr in range(2):
                    for qr in range(2):
                        dr = 2 * dl + kr - qr + 7
                        drc = min(max(dr, 0), 14)
                        src = AP(rpbr, RPAD + ((l * 8) * 15 + drc) * 31 + 15, [[465, 8], [-1, 64], [1, 64]])
                        dst = AP(btd, (l * 8 * NPAT + p) * 16384 + kr * 64 * 128 + qr * 64,
                                 [[NPAT * 16384, 8], [128, 64], [1, 64]])
                        S.dma("sp", dst, src, writes=["btd"])
        S.barrier()

    def load_colvec(src_t, off, n, dst, dkey):
        with _sbuf_tensor("cvrow", [32, 128], F32) as row:
            S.dma("sp", row[0:n, :], AP(src_t, off, [[128, n], [1, 128]]), writes=["cvrow"])
            S.mm(lambda e: e.transpose(out=PS[6][:, 0:n], in_=row[0:n, :], identity=identf[0:n, 0:n]),
                 reads=["cvrow", "identf"], writes=[PK[6]])
            S.op("dve", lambda e: e.tensor_copy(out=dst, in_=PS[6][:, 0:n]), reads=[PK[6]], writes=[dkey])
            S.barrier()

    def load_mod(l, shift_idx, scale_idx, V):
        load_colvec(ada_d, l * 9 * D + shift_idx * D, 8, V["shT"][:], "shT")
        load_colvec(ada_d, l * 9 * D + scale_idx * D, 8, V["scT"][:], "scT")
        S.op("dve", lambda e: e.tensor_scalar(out=V["scT"][:], in0=V["scT"][:], scalar1=1.0, scalar2=None, op0=ALU.add),
             reads=["scT"], writes=["scT"])

    def load_epi(l, gate_idx, ln_idx, gate_coef, V):
        S.dma("sp", V["gate_bc"][:], AP(ada_d, l * 9 * D + gate_idx * D, [[0, 128], [1, D]]), reads=["ada_d"], writes=["gate_bc"])
        S.dma("sp", V["lng_bc"][:], AP(ln_g, (l * 3 + ln_idx) * D, [[0, 128], [1, D]]), writes=["lng_bc"])
        S.dma("sp", V["lnb_bc"][:], AP(ln_b, (l * 3 + ln_idx) * D, [[0, 128], [1, D]]), writes=["lnb_bc"])
        if gate_coef != 1.0:
            S.op("pool", lambda e: e.tensor_scalar(out=V["gate_bc"][:], in0=V["gate_bc"][:], scalar1=gate_coef, scalar2=None, op0=ALU.mult),
                 reads=["gate_bc"], writes=["gate_bc"])

    NTAG = 5

    def run_staged(n, stages):
        k = len(stages)
        for step in range(n + k - 1):
            for s_ in reversed(range(k)):
                t = step - s_
                if 0 <= t < n:
                    stages[s_](t)

    def ln_stage_fns(x_of, xkey_of, T, eps_col):
        st, mv, rs, nb = T["st"], T["mv"], T["rstd"], T["nb"]

        def A(t):
            g = t % NTAG
            xt, xkey = x_of(t), xkey_of(t)
            S.op("dve", lambda e: e.bn_stats(out=st[:, g, 0:6], in_=xt[:, 0:512]), reads=[xkey], writes=[f"lnsa{g}"])
            S.op("dve", lambda e: e.bn_stats(out=st[:, g, 6:12], in_=xt[:, 512:1024]), reads=[xkey], writes=[f"lnsb{g}"])
            S.op("dve", lambda e: e.bn_aggr(out=mv[:, g, :], in_=st[:, g, :]), reads=[f"lnsa{g}", f"lnsb{g}"], writes=[f"lnmv{g}"])

        def B(t):
            g = t % NTAG
            S.op("act", lambda e: e.activation(out=rs[:, g:g + 1], in_=mv[:, g, 1:2], func=AF.Sqrt, bias=epsc[:, eps_col:eps_col + 1], scale=1.0),
                 reads=[f"lnmv{g}", "epsc"], writes=[f"lnrs{g}"])

        def C(t):
            g = t % NTAG
            S.op("dve", lambda e: e.reciprocal(out=rs[:, g:g + 1], in_=rs[:, g:g + 1]), reads=[f"lnrs{g}"], writes=[f"lnrs{g}"])
            S.op("dve", lambda e: e.scalar_tensor_tensor(out=nb[:, g:g + 1], in0=mv[:, g, 0:1], scalar=-1.0, in1=rs[:, g:g + 1],
                                                         op0=ALU.mult, op1=ALU.mult),
                 reads=[f"lnmv{g}", f"lnrs{g}"], writes=[f"lnnb{g}"])

        return [A, B, C]

    def ln_in_stages(x_of, xkey_of, T, V, hT, hkey, tcol_of):
        xn = T["xn"]

        def D(t):
            g = t % NTAG
            S.op("act", lambda e: e.activation(out=xn[:, g, :], in_=x_of(t), func=AF.Identity, scale=T["rstd"][:, g:g + 1], bias=T["nb"][:, g:g + 1]),
                 reads=[xkey_of(t), f"lnrs{g}", f"lnnb{g}"], writes=[f"xn{g}"])

        def E(t):
            g = t % NTAG
            tcol = tcol_of(t)
            for kk in range(KC):
                S.mm(lambda e, kk=kk: e.transpose(out=PSB[:, kk * 128:(kk + 1) * 128], in_=xn[:, g, kk * 128:(kk + 1) * 128], identity=ident[:]),
                     reads=[f"xn{g}", "ident"], writes=["psb"], first=(kk == 0))
            evac_rr[0] += 1
            teng = "dve" if evac_rr[0] % 2 == 0 else "act"
            for kk in range(KC):
                if teng == "dve":
                    fn = lambda e, kk=kk: e.tensor_scalar(out=hT[:, kk, tcol:tcol + 128], in0=PSB[:, kk * 128:(kk + 1) * 128],
                                                          scalar1=V["scT"][:, kk:kk + 1], scalar2=V["shT"][:, kk:kk + 1], op0=ALU.mult, op1=ALU.add)
                else:
                    fn = lambda e, kk=kk: e.activation(out=hT[:, kk, tcol:tcol + 128], in_=PSB[:, kk * 128:(kk + 1) * 128], func=AF.Identity,
                                                       scale=V["scT"][:, kk:kk + 1], bias=V["shT"][:, kk:kk + 1])
                S.op(teng, fn, reads=["psb", "scT", "shT"], writes=[hkey])

        return ln_stage_fns(x_of, xkey_of, T, 0) + [D, E]

    def ln_out_stages(z_of, zkey_of, T, V, dst_of):
        def D(t):
            g = t % NTAG
            z, zk = z_of(t), zkey_of(t)
            S.op("act", lambda e: e.activation(out=z, in_=z, func=AF.Identity, scale=T["rstd"][:, g:g + 1], bias=T["nb"][:, g:g + 1]),
                 reads=[zk, f"lnrs{g}", f"lnnb{g}"], writes=[zk])

        def E(t):
            z, zk = z_of(t), zkey_of(t)
            S.op("dve", lambda e: e.tensor_tensor(out=z, in0=z, in1=V["lng_bc"][:], op=ALU.mult), reads=[zk, "lng_bc"], writes=[zk])
            S.op("pool", lambda e: e.tensor_tensor(out=z, in0=z, in1=V["lnb_bc"][:], op=ALU.add), reads=[zk, "lnb_bc"], writes=[zk])
            for d_ in dst_of(t):
                S.dma("sp", d_, z, reads=[zk], writes=["xdram"])

        return ln_stage_fns(z_of, zkey_of, T, 1) + [D, E]

    def alloc_ln(stack):
        V = {}
        V["shT"] = stack.enter_context(_sbuf_tensor("shT", [128, 8], F32))
        V["scT"] = stack.enter_context(_sbuf_tensor("scT1", [128, 8], F32))
        T = {}
        T["st"] = stack.enter_context(_sbuf_tensor("st", [128, NTAG, 12], F32))
        T["mv"] = stack.enter_context(_sbuf_tensor("mv", [128, NTAG, 2], F32))
        T["rstd"] = stack.enter_context(_sbuf_tensor("rstd", [128, NTAG], F32))
        T["nb"] = stack.enter_context(_sbuf_tensor("nb", [128, NTAG], F32))
        return V, T

    def alloc_vec(stack, V):
        for nm in ("gate_bc", "lng_bc", "lnb_bc"):
            V[nm] = stack.enter_context(_sbuf_tensor(nm, [128, D], F32))

    from contextlib import ExitStack

    def ffn(l, which, xin, xouts):
        w1 = fw[f"ffn{which}_w1"].ap()[l].rearrange("(k p) n -> p k n", p=128)
        w3 = fw[f"ffn{which}_w3"].ap()[l].rearrange("(k p) n -> p k n", p=128)
        w2 = fw[f"ffn{which}_w2"].ap()[l].rearrange("(f p) n -> p f n", p=128)
        base = 0 if which == 1 else 6
        with ExitStack() as st:
            V, T = alloc_ln(st)
            T["xn"] = st.enter_context(_sbuf_tensor("xn", [128, NTAG, D], BF16))
            alloc_vec(st, V)
            xp = st.enter_context(_sbuf_tensor("xp", [128, 8, D], F32))
            hT = st.enter_context(_sbuf_tensor("hTf", [128, KC, 1024], BF16))
            gT = st.enter_context(_sbuf_tensor("gT", [128, FC, 1024], BF16))
            w13 = st.enter_context(_sbuf_tensor("w13", [128, 2, 2, KC, 256], BF16))
            w2b = st.enter_context(_sbuf_tensor("w2b", [128, 2, FC, 256], BF16))
            sg = st.enter_context(_sbuf_tensor("sg", [128, 2, 512], BF16))
            tmp = st.enter_context(_sbuf_tensor("tmpf", [128, 2, 256], F32))
            import os
            CUT = int(os.environ.get("KDBG_CUT", "99"))
            load_mod(l, base + 0, base + 1, V)
            load_epi(l, base + 2, 0 if which == 1 else 2, 0.5, V)
            if CUT <= 0:
                S.barrier()
                return
            wit = 0
            w2it = 0
            for p in range(2):
                for t in range(8):
                    S.dma("sp", xp[:, t, :], xin[(p * 8 + t) * 128:(p * 8 + t + 1) * 128, :], reads=["xdram"], writes=[f"xp{t}"])
                if CUT <= 1:
                    S.barrier()
                    return
                run_staged(8, ln_in_stages(lambda t: xp[:, t, :], lambda t: f"xp{t}", T, V, hT, "hTf", lambda t: t * 128))
                if CUT <= 2:
                    S.barrier()
                    return
                for f2 in range(FC // 2):
                    b = wit % 2
                    wit += 1
                    S.dma("pool", w13[:, b, 0], w1[:, :, f2 * 256:(f2 + 1) * 256], writes=[f"w1b{b}"])
                    S.dma("pool", w13[:, b, 1], w3[:, :, f2 * 256:(f2 + 1) * 256], writes=[f"w3b{b}"])
                    for fi in range(2):
                        f = f2 * 2 + fi
                        for half in range(2):
                            pa, pb_ = (0, 1) if half == 0 else (2, 3)
                            for k in range(KC):
                                S.mm(lambda e, k=k, b=b, fi=fi, half=half, pa=pa: e.matmul(
                                    PS[pa][:, :], lhsT=w13[:, b, 0, k, fi * 128:(fi + 1) * 128], rhs=hT[:, k, half * 512:(half + 1) * 512],
                                    start=(k == 0), stop=(k == KC - 1)),
                                    reads=[f"w1b{b}", "hTf"], writes=[PK[pa]], first=(k == 0))
                            for k in range(KC):
                                S.mm(lambda e, k=k, b=b, fi=fi, half=half, pb_=pb_: e.matmul(
                                    PS[pb_][:, :], lhsT=w13[:, b, 1, k, fi * 128:(fi + 1) * 128], rhs=hT[:, k, half * 512:(half + 1) * 512],
                                    start=(k == 0), stop=(k == KC - 1)),
                                    reads=[f"w3b{b}", "hTf"], writes=[PK[pb_]], first=(k == 0))
                            S.op("act", lambda e, half=half, pa=pa: e.activation(out=sg[:, half, :], in_=PS[pa][:, :], func=AF.Silu),
                                 reads=[PK[pa]], writes=[f"sg{half}"])
                            S.op("dve", lambda e, half=half, pb_=pb_, f=f: e.tensor_tensor(
                                out=gT[:, f, half * 512:(half + 1) * 512], in0=PS[pb_][:, :], in1=sg[:, half, :], op=ALU.mult),
                                reads=[PK[pb_], f"sg{half}"], writes=["gT"])
                if CUT <= 3:
                    S.barrier()
                    return
                for oq in range(4):
                    b = w2it % 2
                    w2it += 1
                    S.dma("pool", w2b[:, b], w2[:, :, oq * 256:(oq + 1) * 256], writes=[f"w2b{b}"])
                    for t in range(8):
                        pi = 4 + (t % 2)
                        for f in range(FC):
                            S.mm(lambda e, f=f, t=t, b=b, pi=pi: e.matmul(
                                PS[pi][:, 0:256], lhsT=gT[:, f, t * 128:(t + 1) * 128], rhs=w2b[:, b, f, :],
                                start=(f == 0), stop=(f == FC - 1)),
                                reads=["gT", f"w2b{b}"], writes=[PK[pi]], first=(f == 0))
                        tb = t % 2
                        S.op("dve", lambda e, pi=pi, tb=tb, oq=oq: e.tensor_tensor(
                            out=tmp[:, tb, :], in0=PS[pi][:, 0:256], in1=V["gate_bc"][:, oq * 256:(oq + 1) * 256], op=ALU.mult),
                            reads=[PK[pi], "gate_bc"], writes=[f"tmpf{tb}"])
                        S.op("dve", lambda e, t=t, tb=tb, oq=oq: e.scalar_tensor_tensor(
                            out=xp[:, t, oq * 256:(oq + 1) * 256], in0=xp[:, t, oq * 256:(oq + 1) * 256], scalar=ALPHA,
                            in1=tmp[:, tb, :], op0=ALU.mult, op1=ALU.add),
                            reads=[f"tmpf{tb}", f"xp{t}"], writes=[f"xp{t}"])
                if CUT <= 4:
                    S.barrier()
                    return
                run_staged(8, ln_out_stages(lambda t: xp[:, t, :], lambda t: f"xp{t}", T, V,
                                            lambda t, p=p: [xo[(p * 8 + t) * 128:(p * 8 + t + 1) * 128, :] for xo in xouts]))
        S.barrier()

    class _Cut(Exception):
        pass

    def mcut(n):
        import os
        if int(os.environ.get("KDBG_MCUT", "99")) <= n:
            raise _Cut()

    def mixer(l, xin, xouts):
        try:
            mixer_(l, xin, xouts)
        except _Cut:
            pass
        S.barrier()

    def mixer_(l, xin, xouts):
        win = w_in.ap()[l].rearrange("(k p) n -> p k n", p=128)
        with ExitStack() as st:
            V, T = alloc_ln(st)
            hT = st.enter_context(_sbuf_tensor("hTm", [128, KC, NT], BF16))
            specT = st.enter_context(_sbuf_tensor("specT", [128, 4, NT], BF16))
            omT = st.enter_context(_sbuf_tensor("omT", [128, 4, NT], BF16))
            onT = st.enter_context(_sbuf_tensor("onT", [128, 4, NT], BF16))
            load_mod(l, 3, 4, V)
            with _sbuf_tensor("xt2", [128, 6, D], F32) as xt2, _sbuf_tensor("xn", [128, NTAG, D], BF16) as xn_:
                T["xn"] = xn_

                def m0_load(t):
                    S.dma("sp", xt2[:, t % 6, :], xin[t * 128:(t + 1) * 128, :], reads=["xdram"], writes=[f"xt2{t % 6}"])
                run_staged(NTILE, [m0_load] + ln_in_stages(lambda t: xt2[:, t % 6, :], lambda t: f"xt2{t % 6}", T, V, hT, "hTm",
                                                           lambda t: t * 128))
            S.barrier()
            mcut(0)

            with ExitStack() as s2:
                wf = s2.enter_context(_sbuf_tensor("wf", [128, KC, 512], BF16))
                cs = s2.enter_context(_sbuf_tensor("cs", [128, 256], BF16))
                AB = s2.enter_context(_sbuf_tensor("AB", [128, NTILE, 4, 256], BF16))
                ufT = s2.enter_context(_sbuf_tensor("ufT", [128, 2, 512], BF16))
                dbuf = s2.enter_context(_sbuf_tensor("dbuf", [128, 2, 2, 8, 512], BF16))
                S.dma("pool", wf[:], win[:, :, C_F:C_F + 512], writes=["wf"])
                S.dma("pool", cs[:], dftCS.ap(), writes=["cs"])
                it = 0
                for c in range(4):
                    for g in range(4):
                        b = it % 2
                        it += 1
                        for k in range(KC):
                            S.mm(lambda e, k=k, g=g, c=c, b=b: e.matmul(PS[b][:, :], lhsT=wf[:, k, g * 128:(g + 1) * 128],
                                                                   rhs=hT[:, k, c * 512:(c + 1) * 512], start=(k == 0), stop=(k == KC - 1)),
                                 reads=["wf", "hTm"], writes=[PK[b]], first=(k == 0))
                        S.op("act", copy_op("act", ufT[:, b, :], PS[b][:, :]), reads=[PK[b]], writes=[f"ufT{b}"])
                        for tt in range(4):
                            t = c * 4 + tt
                            S.mm(lambda e, tt=tt, b=b: e.matmul(PS[2][:, tt * 256:(tt + 1) * 256] if tt < 2 else PS[3][:, (tt - 2) * 256:(tt - 1) * 256],
                                                           lhsT=ufT[:, b, tt * 128:(tt + 1) * 128], rhs=cs[:], start=True, stop=True),
                                 reads=[f"ufT{b}", "cs"], writes=[PK[2] if tt < 2 else PK[3]])
                        for hh in range(2):
                            S.op("dve", lambda e, hh=hh, c=c, g=g: e.tensor_copy(
                                out=AB[:, c * 4 + hh * 2:c * 4 + hh * 2 + 2, g, :],
                                in_=PS[2 + hh][:, :].rearrange("p (t n) -> p t n", n=256)),
                                reads=[PK[2 + hh]], writes=["AB"])
                dC = dftC.ap().rearrange("(t p) n -> p t n", p=128)
                dS = dftS.ap().rearrange("(t p) n -> p t n", p=128)
                dit = 0
                for c in range(4):
                    for half in range(2):
                        b = dit % 2
                        dit += 1
                        S.dma("sp", dbuf[:, b, 0], dC[:, half * 8:(half + 1) * 8, c * 512:(c + 1) * 512], writes=[f"dC{b}"])
                        S.dma("sp", dbuf[:, b, 1], dS[:, half * 8:(half + 1) * 8, c * 512:(c + 1) * 512], writes=[f"dS{b}"])
                        for g in range(4):
                            for lt in range(8):
                                tl = half * 8 + lt
                                S.mm(lambda e, g=g, lt=lt, tl=tl, b=b, half=half: e.matmul(
                                    PS[g][:, :], lhsT=AB[:, tl, g, 0:128], rhs=dbuf[:, b, 0, lt, :],
                                    start=(half == 0 and lt == 0), stop=False),
                                    reads=["AB", f"dC{b}"], writes=[PK[g]], first=(half == 0 and lt == 0))
                                S.mm(lambda e, g=g, lt=lt, tl=tl, b=b, half=half: e.matmul(
                                    PS[g][:, :], lhsT=AB[:, tl, g, 128:256], rhs=dbuf[:, b, 1, lt, :],
                                    start=False, stop=(half == 1 and lt == 7)),
                                    reads=["AB", f"dS{b}"], writes=[PK[g]], first=False)
                    for g in range(4):
                        eng = evac_eng()
                        S.op(eng, copy_op(eng, specT[:, g, c * 512:(c + 1) * 512], PS[g][:, :]), reads=[PK[g]], writes=["specT"])
            S.barrier()

            mcut(1)
            with ExitStack() as s2:
                knT = s2.enter_context(_sbuf_tensor("knT", [128, 4, NK], BF16))
                qnT = s2.enter_context(_sbuf_tensor("qnT", [128, 4, NT], BF16))
                Vn = s2.enter_context(_sbuf_tensor("Vn", [128, NKT, 8, 65], BF16))
                with ExitStack() as s3:
                    wq = s3.enter_context(_sbuf_tensor("wq", [128, KC, 512], BF16))
                    wk = s3.enter_context(_sbuf_tensor("wk", [128, KC, 512], BF16))
                    wv = s3.enter_context(_sbuf_tensor("wv", [128, KC, 512], BF16))
                    ck = s3.enter_context(_sbuf_tensor("ck", [128, 4, 512], BF16))
                    of32 = s3.enter_context(_sbuf_tensor("of32", [128, 2, 512], F32))
                    S.dma("pool", wq[:], win[:, :, C_NQ:C_NQ + 512], writes=["wq"])
                    S.dma("pool", wk[:], win[:, :, C_NK:C_NK + 512], writes=["wk"])
                    S.dma("pool", wv[:], win[:, :, C_NV:C_NV + 512], writes=["wv"])
                    S.dma("pool", ck[:], c_nk.ap()[l].rearrange("(t p) n -> p t n", p=128), writes=["ck"])
                    for j in range(4):
                        S.dma("pool", Vn[:, NTILE + j, :, 0:64], c_nv.ap()[l, j * 128:(j + 1) * 128, :].rearrange("p (h d) -> p h d", d=64), writes=["Vnc"])
                    S.op("pool", lambda e: e.memset(Vn[:, :, :, 64:65], 1.0), writes=["Vn1"])
                    it = 0
                    for t in range(NTILE):
                        for (wsb, wkey, odst, isv) in ((wk, "wk", o_nk, False), (wv, "wv", o_nv, True)):
                            b = it % 2
                            it += 1
                            for k in range(KC):
                                S.mm(lambda e, k=k, t=t, b=b, wsb=wsb: e.matmul(PS[b][:, :], lhsT=hT[:, k, t * 128:(t + 1) * 128], rhs=wsb[:, k, :],
                                                                               start=(k == 0), stop=(k == KC - 1)),
                                     reads=["hTm", wkey], writes=[PK[b]], first=(k == 0))
                            S.op("act", copy_op("act", of32[:, b, :], PS[b][:, :]), reads=[PK[b]], writes=[f"of32{b}"])
                            if isv:
                                S.op("dve", lambda e, t=t, b=b: e.tensor_copy(out=Vn[:, t, :, 0:64], in_=PS[b][:, :].rearrange("p (h d) -> p h d", d=64)),
                                     reads=[PK[b]], writes=["Vn"])
                            S.dma("sp", odst.ap()[l, t * 128:(t + 1) * 128, :], of32[:, b, :], reads=[f"of32{b}"])
                    for pr in range(4):
                        for c in range(4):
                            for (wsb, wkey, dstT, dkey) in ((wk, "wk", knT, "knT"), (wq, "wq", qnT, "qnT")):
                                b = it % 2
                                it += 1
                                for k in range(KC):
                                    S.mm(lambda e, k=k, pr=pr, c=c, b=b, wsb=wsb: e.matmul(
                                        PS[b][:, :], lhsT=wsb[:, k, pr * 128:(pr + 1) * 128], rhs=hT[:, k, c * 512:(c + 1) * 512],
                                        start=(k == 0), stop=(k == KC - 1)),
                                        reads=[wkey, "hTm"], writes=[PK[b]], first=(k == 0))
                                eng = evac_eng()
                                S.op(eng, copy_op(eng, dstT[:, pr, c * 512:(c + 1) * 512], PS[b][:, :]), reads=[PK[b]], writes=[dkey])
                        for j in range(4):
                            S.mm(lambda e, j=j, pr=pr: e.transpose(out=PSB[:, j * 128:(j + 1) * 128], in_=ck[:, j, pr * 128:(pr + 1) * 128], identity=ident[:]),
                                 reads=["ck", "ident"], writes=["psb"], first=(j == 0))
                        S.op("dve", lambda e, pr=pr: e.tensor_copy(out=knT[:, pr, NT:NK], in_=PSB[:, 0:512]), reads=["psb"], writes=["knT"])
                S.barrier()
                mcut(2)
                s3 = s2
                iq = s3.enter_context(_sbuf_tensor("iq", [72, NT], BF16))
                ik = s3.enter_context(_sbuf_tensor("ik", [72, NK], BF16))
                m1 = s3.enter_context(_sbuf_tensor("m1", [128, NPAT, 128], BF16))
                m2 = s3.enter_context(_sbuf_tensor("m2", [128, NPAT, 128], BF16))
                btf = s3.enter_context(_sbuf_tensor("btf", [128, NPAT, 128], F32))
                BT = s3.enter_context(_sbuf_tensor("BT", [128, 2, NPAT, 128], BF16))
                pT = s3.enter_context(_sbuf_tensor("pT", [128, 2, 9, 128], BF16))
                osb = s3.enter_context(_sbuf_tensor("osb", [65, 2, 512], F32))
                otmp = s3.enter_context(_sbuf_tensor("otmp", [64, 2, 512], BF16))
                for pb_ in (0, 64):
                    S.dma("pool", iq[pb_:pb_ + 8, :], indq_n.ap(), writes=["iq"])
                    S.dma("pool", ik[pb_:pb_ + 8, :], indk.ap(), writes=["ik"])
                S.dma("pool", m1[:], m1d.ap(), writes=["m1"])
                S.dma("pool", m2[:], m2d.ap(), writes=["m2"])
                items = [(h, c, qi) for h in range(8) for c in range(4) for qi in range(4)]

                def na_bt(h):
                    hb = h % 2
                    S.dma("sp", btf[:], AP(btd, ((l * 8 + h) * NPAT) * 16384, [[128, 128], [16384, NPAT], [1, 128]]),
                          reads=["btd"], writes=["btf"])
                    S.op("dve", lambda e: e.tensor_tensor(out=btf[:], in0=btf[:], in1=m1[:], op=ALU.mult),
                         reads=["btf", "m1"], writes=["btf"])
                    S.op("dve", lambda e: e.tensor_tensor(out=BT[:, hb], in0=btf[:], in1=m2[:], op=ALU.add),
                         reads=["btf", "m2"], writes=[f"BT{hb}"])

                def na_slots(i):
                    return [(j, pat) for (j, pat) in na_window(i)] + [(NTILE + j, None) for j in range(4)]

                def na_scores(k):
                    h, c, qi = items[k]
                    pr, pb, hb = h // 2, 64 * (h % 2), h % 2
                    i = c * 4 + qi
                    ab = k % 2
                    banks = [ab * 3 + 0, ab * 3 + 1, ab * 3 + 2]
                    for si, (j, pat) in enumerate(na_slots(i)):
                        bk = banks[si // 4]
                        col = (si % 4) * 128
                        S.mm(lambda e, bk=bk, col=col, j=j: e.matmul(
                            PS[bk][:, col:col + 128], lhsT=knT[pb:pb + 64, pr, j * 128:(j + 1) * 128],
                            rhs=qnT[pb:pb + 64, pr, i * 128:(i + 1) * 128], start=True, stop=False),
                            reads=["knT", "qnT"], writes=[PK[bk]], first=(si % 4 == 0))
                        S.mm(lambda e, bk=bk, col=col, j=j, pat=pat: e.matmul(
                            PS[bk][:, col:col + 128], lhsT=ik[pb:pb + 8, j * 128:(j + 1) * 128], rhs=iq[pb:pb + 8, i * 128:(i + 1) * 128],
                            start=False, stop=(pat is None)),
                            reads=["ik", "iq"], writes=[PK[bk]], first=False)
                        if pat is not None:
                            S.mm(lambda e, bk=bk, col=col, pat=pat: e.matmul(
                                PS[bk][:, col:col + 128], lhsT=ident[:], rhs=BT[:, hb, pat, :], start=False, stop=True),
                                reads=["ident", f"BT{hb}"], writes=[PK[bk]], first=False)

                def na_exp(k):
                    h, c, qi = items[k]
                    i = c * 4 + qi
                    ab = k % 2
                    ns = len(na_slots(i))
                    for g in range(3):
                        n_in = min(4, ns - g * 4)
                        if n_in <= 0:
                            continue
                        bk = ab * 3 + g
                        S.op("act", lambda e, bk=bk, g=g, n_in=n_in: e.activation(
                            out=pT[:, ab, g * 4:g * 4 + n_in, :], in_=PS[bk][:, 0:n_in * 128].rearrange("p (s n) -> p s n", n=128),
                            func=AF.Exp, scale=NA_SCALE),
                            reads=[PK[bk]], writes=[f"pT{ab}"])

                def na_pv(k):
                    h, c, qi = items[k]
                    i = c * 4 + qi
                    ab = k % 2
                    po = 6
                    slots = na_slots(i)
                    ns = len(slots)
                    for si, (j, pat) in enumerate(slots):
                        S.mm(lambda e, si=si, j=j: e.matmul(
                            PS[po][0:65, qi * 128:(qi + 1) * 128], lhsT=Vn[:, j, h, :], rhs=pT[:, ab, si, :],
                            start=(si == 0), stop=(si == ns - 1)),
                            reads=["Vn", "Vnc", "Vn1", f"pT{ab}"], writes=[PK[po]], first=(si == 0 and qi == 0))

                def na_norm(k):
                    h, c, qi = items[k]
                    pr, hb = h // 2, h % 2
                    ob = (h * 4 + c) % 2
                    po = 6
                    S.op("act", copy_op("act", osb[:, ob, :], PS[po][0:65, :]), reads=[PK[po]], writes=[f"osb{ob}"])
                    S.op("dve", lambda e: e.reciprocal(out=osb[64:65, ob, :], in_=osb[64:65, ob, :]), reads=[f"osb{ob}"], writes=[f"osb{ob}"])
                    S.mm(lambda e: e.matmul(PS[po][0:64, :], lhsT=onesf[64:65, 0:64], rhs=osb[64:65, ob, :], start=True, stop=True),
                         reads=["ones", f"osb{ob}"], writes=[PK[po]])
                    if hb == 0:
                        S.op("dve", lambda e: e.tensor_tensor(
                            out=onT[0:64, pr, c * 512:(c + 1) * 512], in0=PS[po][0:64, :], in1=osb[0:64, ob, :], op=ALU.mult),
                            reads=[PK[po], f"osb{ob}"], writes=["onT"])
                    else:
                        S.op("dve", lambda e: e.tensor_tensor(
                            out=otmp[:, ob, :], in0=PS[po][0:64, :], in1=osb[0:64, ob, :], op=ALU.mult),
                            reads=[PK[po], f"osb{ob}"], writes=[f"otmp{ob}"])
                        S.dma("sp", onT[64:128, pr, c * 512:(c + 1) * 512], otmp[:, ob, :], reads=[f"otmp{ob}"], writes=["onT"])

                na_bt(0)
                na_scores(0)
                for k in range(len(items)):
                    na_exp(k)
                    if k + 1 < len(items):
                        if items[k + 1][0] != items[k][0]:
                            na_bt(items[k + 1][0])
                        na_scores(k + 1)
                    na_pv(k)
                    if items[k][2] == 3:
                        na_norm(k)
            S.barrier()

            mcut(3)
            with ExitStack() as s2:
                wuq = s2.enter_context(_sbuf_tensor("wuq", [128, 3, 768], BF16))
                wukv = s2.enter_context(_sbuf_tensor("wukv", [128, 2, 2, 8, 64], BF16))
                qn = s2.enter_context(_sbuf_tensor("qn", [128, 3, NT], BF16))
                ckvT = s2.enter_context(_sbuf_tensor("ckvT", [128, 2, NK], BF16))
                kall = s2.enter_context(_sbuf_tensor("kall", [104, NK], BF16))
                qall = s2.enter_context(_sbuf_tensor("qall", [104, NT], BF16))
                Vm = s2.enter_context(_sbuf_tensor("Vm", [128, NKT, 8, 65], BF16))
                rC = s2.enter_context(_sbuf_tensor("rC", [96, NT], BF16))
                rS = s2.enter_context(_sbuf_tensor("rS", [96, NT], BF16))
                prT = s2.enter_context(_sbuf_tensor("prT", [96, 32], BF16))
                xr = s2.enter_context(_sbuf_tensor("xr", [96, 2, 512], BF16))
                t1 = s2.enter_context(_sbuf_tensor("t1", [96, 512], F32))
                t2 = s2.enter_context(_sbuf_tensor("t2", [96, 512], F32))
                S.dma("pool", wuq[:], w_uq.ap()[l].rearrange("(k p) n -> p k n", p=128), writes=["wuq"])
                for k2 in range(2):
                    for tt in range(2):
                        S.dma("pool", wukv[:, k2, tt],
                              w_ukv.ap()[l, k2 * 128:(k2 + 1) * 128, :].rearrange("p (h t d) -> p t h d", h=8, t=2)[:, tt], writes=["wukv"])
                S.dma("pool", rC[64:96, :], ropeC.ap(), writes=["rC"])
                S.dma("pool", rS[64:96, :], ropeS.ap(), writes=["rS"])
                S.dma("pool", prT[64:96, :], protT.ap(), writes=["prT"])
                S.dma("pool", kall[96:104, :], indk.ap(), writes=["krTi"])
                S.dma("pool", qall[96:104, :], indq_m.ap(), writes=["qrTi"])
                S.op("pool", lambda e: e.memset(Vm[:, :, :, 64:65], 1.0), writes=["Vm1"])

                def rope(src_ps, pk, b, dst, dkey, c):
                    S.op("act", copy_op("act", xr[64:96, b, :], src_ps), reads=[pk], writes=[f"xr{b}"])
                    S.mm(lambda e: e.matmul(src_ps, lhsT=prT[64:96, :], rhs=xr[64:96, b, :], start=True, stop=True),
                         reads=["prT", f"xr{b}"], writes=[pk])
                    S.op("dve", lambda e: e.tensor_tensor(out=t1[64:96, :], in0=xr[64:96, b, :], in1=rC[64:96, c * 512:(c + 1) * 512], op=ALU.mult),
                         reads=[f"xr{b}", "rC"], writes=["t1"])
                    S.op("dve", lambda e: e.tensor_tensor(out=t2[64:96, :], in0=src_ps, in1=rS[64:96, c * 512:(c + 1) * 512], op=ALU.mult),
                         reads=[pk, "rS"], writes=["t2"])
                    S.op("pool", lambda e: e.tensor_tensor(out=dst, in0=t1[64:96, :], in1=t2[64:96, :], op=ALU.add),
                         reads=["t1", "t2"], writes=[dkey])

                with ExitStack() as s3:
                    wqa = s3.enter_context(_sbuf_tensor("wqa", [128, KC, 384], BF16))
                    wkr = s3.enter_context(_sbuf_tensor("wkr", [128, KC, 288], BF16))
                    gq = s3.enter_context(_sbuf_tensor("gq", [128, 3], F32))
                    gkv = s3.enter_context(_sbuf_tensor("gkv", [128, 256], F32))
                    cc = s3.enter_context(_sbuf_tensor("cc", [128, 4, 256], BF16))
                    ckr = s3.enter_context(_sbuf_tensor("ckr", [128, 4, 32], BF16))
                    uq = s3.enter_context(_sbuf_tensor("uq", [128, 3, 512], F32))
                    sq = s3.enter_context(_sbuf_tensor("sq", [128, 3, 512], BF16))
                    rq = s3.enter_context(_sbuf_tensor("rq", [128, 512], F32))
                    ukv = s3.enter_context(_sbuf_tensor("ukv", [128, 2, 288], F32))
                    kst = s3.enter_context(_sbuf_tensor("kst", [128, 2, 6], F32))
                    kmv = s3.enter_context(_sbuf_tensor("kmv", [128, 2, 2], F32))
                    ssk = s3.enter_context(_sbuf_tensor("ssk", [128, 2], F32))
                    ckf = s3.enter_context(_sbuf_tensor("ckf", [128, 2, 256], F32))
                    ckb = s3.enter_context(_sbuf_tensor("ckb", [128, 2, 256], BF16))
                    S.dma("pool", wqa[:], win[:, :, C_Q:C_Q + 384], writes=["wqa"])
                    S.dma("pool", wkr[:], win[:, :, C_KV:C_KV + 288], writes=["wkr"])
                    load_colvec(q_norm, l * 384, 3, gq[:], "gq")
                    S.dma("sp", gkv[:], AP(kv_norm, l * 256, [[0, 128], [1, 256]]), writes=["gkv"])
                    S.dma("pool", cc[:], c_ckv.ap()[l].rearrange("(t p) n -> p t n", p=128), writes=["cc"])
                    S.dma("pool", ckr[:], c_kr.ap()[l].rearrange("(t p) n -> p t n", p=128), writes=["ckr"])
                    it = 0
                    for c in range(4):
                        for j in range(3):
                            b = it % 2
                            it += 1
                            for k in range(KC):
                                S.mm(lambda e, k=k, j=j, c=c, b=b: e.matmul(PS[b][:, :], lhsT=wqa[:, k, j * 128:(j + 1) * 128],
                                                                       rhs=hT[:, k, c * 512:(c + 1) * 512], start=(k == 0), stop=(k == KC - 1)),
                                     reads=["wqa", "hTm"], writes=[PK[b]], first=(k == 0))
                            S.op("act", lambda e, j=j, b=b: e.activation(out=sq[:, j, :], in_=PS[b][:, :], func=AF.Square), reads=[PK[b]], writes=[f"sq{j}"])
                            S.op("dve", lambda e, j=j, b=b: e.tensor_copy(out=uq[:, j, :], in_=PS[b][:, :]), reads=[PK[b]], writes=[f"uq{j}"])
                        for j in range(3):
                            S.mm(lambda e, j=j: e.matmul(PS[2][:, :], lhsT=ones[:], rhs=sq[:, j, :], start=(j == 0), stop=(j == 2)),
                                 reads=["ones", f"sq{j}"], writes=[PK[2]], first=(j == 0))
                        S.op("act", lambda e: e.activation(out=rq[:], in_=PS[2][:, :], func=AF.Sqrt, scale=1.0 / 384, bias=epsc[:, 0:1]),
                             reads=[PK[2], "epsc"], writes=["rq"])
                        S.op("dve", lambda e: e.reciprocal(out=rq[:], in_=rq[:]), reads=["rq"], writes=["rq"])
                        for j in range(3):
                            S.op("dve", lambda e, j=j, c=c: e.scalar_tensor_tensor(out=qn[:, j, c * 512:(c + 1) * 512], in0=uq[:, j, :], scalar=gq[:, j:j + 1],
                                                                                  in1=rq[:], op0=ALU.mult, op1=ALU.mult),
                                 reads=[f"uq{j}", "gq", "rq"], writes=["qn"])
                    for t in range(NTILE):
                        b = t % 2
                        for k in range(KC):
                            S.mm(lambda e, k=k, t=t, b=b: e.matmul(PS[b][:, 0:288], lhsT=hT[:, k, t * 128:(t + 1) * 128], rhs=wkr[:, k, :],
                                                                  start=(k == 0), stop=(k == KC - 1)),
                                 reads=["hTm", "wkr"], writes=[PK[b]], first=(k == 0))
                        S.op("act", copy_op("act", ukv[:, b, :], PS[b][:, 0:288]), reads=[PK[b]], writes=[f"ukv{b}"])
                        S.dma("sp", o_kr.ap()[l, t * 128:(t + 1) * 128, :], ukv[:, b, 256:288], reads=[f"ukv{b}"])
                        S.op("dve", lambda e, b=b: e.bn_stats(out=kst[:, b, :], in_=ukv[:, b, 0:256]), reads=[f"ukv{b}"], writes=[f"kst{b}"])
                        S.op("dve", lambda e, b=b: e.bn_aggr(out=kmv[:, b, :], in_=kst[:, b, :]), reads=[f"kst{b}"], writes=[f"kmv{b}"])
                        S.op("dve", lambda e, b=b: e.scalar_tensor_tensor(out=ssk[:, b:b + 1], in0=kmv[:, b, 0:1], scalar=kmv[:, b, 0:1], in1=kmv[:, b, 1:2],
                                                                         op0=ALU.mult, op1=ALU.add),
                             reads=[f"kmv{b}"], writes=[f"ssk{b}"])
                        S.op("act", lambda e, b=b: e.activation(out=ssk[:, b:b + 1], in_=ssk[:, b:b + 1], func=AF.Sqrt, scale=1.0, bias=epsc[:, 0:1]),
                             reads=[f"ssk{b}", "epsc"], writes=[f"ssk{b}"])
                        S.op("dve", lambda e, b=b: e.reciprocal(out=ssk[:, b:b + 1], in_=ssk[:, b:b + 1]), reads=[f"ssk{b}"], writes=[f"ssk{b}"])
                        S.op("dve", lambda e, b=b: e.scalar_tensor_tensor(out=ckf[:, b, :], in0=ukv[:, b, 0:256], scalar=ssk[:, b:b + 1], in1=gkv[:],
                                                                         op0=ALU.mult, op1=ALU.mult),
                             reads=[f"ukv{b}", f"ssk{b}", "gkv"], writes=[f"ckf{b}"])
                        S.dma("sp", o_ckv.ap()[l, t * 128:(t + 1) * 128, :], ckf[:, b, :], reads=[f"ckf{b}"])
                        S.op("pool", lambda e, b=b: e.tensor_copy(out=ckb[:, b, :], in_=ckf[:, b, :]), reads=[f"ckf{b}"], writes=[f"ckb{b}"])
                        for k2 in range(2):
                            S.mm(lambda e, k2=k2, b=b: e.transpose(out=PSB[:, k2 * 128:(k2 + 1) * 128], in_=ckb[:, b, k2 * 128:(k2 + 1) * 128], identity=ident[:]),
                                 reads=[f"ckb{b}", "ident"], writes=["psb"], first=(k2 == 0))
                        S.op("dve", lambda e, t=t: e.tensor_copy(out=ckvT[:, :, t * 128:(t + 1) * 128], in_=PSB[:, 0:256].rearrange("p (k n) -> p k n", n=128)),
                             reads=["psb"], writes=["ckvT"])
                    for j in range(4):
                        for k2 in range(2):
                            S.mm(lambda e, k2=k2, j=j: e.transpose(out=PSB[:, k2 * 128:(k2 + 1) * 128], in_=cc[:, j, k2 * 128:(k2 + 1) * 128], identity=ident[:]),
                                 reads=["cc", "ident"], writes=["psb"], first=(k2 == 0))
                        S.op("dve", lambda e, j=j: e.tensor_copy(out=ckvT[:, :, NT + j * 128:NT + (j + 1) * 128],
                                                                in_=PSB[:, 0:256].rearrange("p (k n) -> p k n", n=128)),
                             reads=["psb"], writes=["ckvT"])
                    for j in range(4):
                        S.mm(lambda e, j=j: e.matmul(PS[3][64:96, j * 128:(j + 1) * 128], lhsT=ckr[:, j, :], rhs=ident[:], start=True, stop=True),
                             reads=["ckr", "ident"], writes=[PK[3]], first=(j == 0))
                    S.op("dve", lambda e: e.tensor_copy(out=kall[64:96, NT:NK], in_=PS[3][64:96, :]), reads=[PK[3]], writes=["krT"])
                    for c in range(4):
                        b = c % 2
                        for k in range(KC):
                            S.mm(lambda e, k=k, c=c, b=b: e.matmul(PS[b][64:96, :], lhsT=wkr[:, k, 256:288], rhs=hT[:, k, c * 512:(c + 1) * 512],
                                                                  start=(k == 0), stop=(k == KC - 1)),
                                 reads=["wkr", "hTm"], writes=[PK[b]], first=(k == 0))
                        rope(PS[b][64:96, :], PK[b], b, kall[64:96, c * 512:(c + 1) * 512], "krT", c)
                    for t in range(NKT):
                        b = t % 2
                        for k2 in range(2):
                            S.mm(lambda e, k2=k2, t=t, b=b: e.matmul(PS[b][:, :], lhsT=ckvT[:, k2, t * 128:(t + 1) * 128],
                                                                    rhs=wukv[:, k2, 1].rearrange("p h d -> p (h d)"),
                                                                    start=(k2 == 0), stop=(k2 == 1)),
                                 reads=["ckvT", "wukv"], writes=[PK[b]], first=(k2 == 0))
                        eng = evac_eng()
                        S.op(eng, copy_op(eng, Vm[:, t, :, 0:64], PS[b][:, :].rearrange("p (h d) -> p h d", d=64)), reads=[PK[b]], writes=["Vm"])
                S.barrier()
                mcut(4)
                s3 = s2
                pTm = s3.enter_context(_sbuf_tensor("pTm", [128, 4, 512], BF16))
                osb = s3.enter_context(_sbuf_tensor("osbm", [65, 2, 512], F32))
                otmp = s3.enter_context(_sbuf_tensor("otmpm", [64, 2, 512], BF16))
                sit = 0
                import os
                mdbg = os.environ.get("KDBG_MLA", "")
                for h in range(8):
                    pr, hh = h // 2, h % 2
                    for c5 in range(5):
                        b = c5 % 2
                        for k2 in range(2):
                            S.mm(lambda e, k2=k2, c5=c5, b=b, h=h: e.matmul(
                                PS[b][0:64, :], lhsT=wukv[:, k2, 0, h, :], rhs=ckvT[:, k2, c5 * 512:(c5 + 1) * 512],
                                start=(k2 == 0), stop=(k2 == 1)),
                                reads=["wukv", "ckvT"], writes=[PK[b]], first=(k2 == 0))
                        eng = evac_eng()
                        S.op(eng, copy_op(eng, kall[0:64, c5 * 512:(c5 + 1) * 512], PS[b][0:64, :]), reads=[PK[b]], writes=["knp"])
                    for c in range(4):
                        b = c % 2
                        for j in range(3):
                            S.mm(lambda e, j=j, c=c, b=b, h=h: e.matmul(
                                PS[b][0:64, :], lhsT=wuq[:, j, h * 96:h * 96 + 64], rhs=qn[:, j, c * 512:(c + 1) * 512],
                                start=(j == 0), stop=(j == 2)),
                                reads=["wuq", "qn"], writes=[PK[b]], first=(j == 0))
                        eng = evac_eng()
                        S.op(eng, copy_op(eng, qall[0:64, c * 512:(c + 1) * 512], PS[b][0:64, :]), reads=[PK[b]], writes=["qnp"])
                        b2 = 2 + (c % 2)
                        for j in range(3):
                            S.mm(lambda e, j=j, c=c, b2=b2, h=h: e.matmul(
                                PS[b2][64:96, :], lhsT=wuq[:, j, h * 96 + 64:h * 96 + 96], rhs=qn[:, j, c * 512:(c + 1) * 512],
                                start=(j == 0), stop=(j == 2)),
                                reads=["wuq", "qn"], writes=[PK[b2]], first=(j == 0))
                        rope(PS[b2][64:96, :], PK[b2], c % 2, qall[64:96, c * 512:(c + 1) * 512], "qrT", c)
                    LA = 2
                    items = [(c, j) for c in range(4) for j in range(NKT)]
                    pend = []

                    def mla_scores(c, j, sidx):
                        bk = sidx % 4
                        S.mm(lambda e: e.matmul(PS[bk][:, :], lhsT=kall[:, j * 128:(j + 1) * 128], rhs=qall[:, c * 512:(c + 1) * 512],
                                                start=True, stop=True),
                             reads=["knp", "qnp", "krT", "krTi", "qrT", "qrTi"], writes=[PK[bk]], first=True)

                    def mla_exp_pv(c, j, sidx, h=h):
                        bk = sidx % 4
                        po = 4 + (c % 2)
                        S.op("act", lambda e: e.activation(out=pTm[:, bk, :], in_=PS[bk][:, :], func=AF.Exp, scale=MLA_SCALE),
                             reads=[PK[bk]], writes=[f"pTm{bk}"])
                        S.mm(lambda e: e.matmul(PS[po][0:65, :], lhsT=Vm[:, j, h, :], rhs=pTm[:, bk, :], start=(j == 0), stop=(j == NKT - 1)),
                             reads=["Vm", "Vm1", f"pTm{bk}"], writes=[PK[po]], first=(j == 0))

                    def mla_norm_a(c, h=h):
                        po = 4 + (c % 2)
                        ob = c % 2
                        S.op("act", copy_op("act", osb[:, ob, :], PS[po][0:65, :]), reads=[PK[po]], writes=[f"osb{ob}"])
                        S.op("dve", lambda e: e.reciprocal(out=osb[64:65, ob, :], in_=osb[64:65, ob, :]), reads=[f"osb{ob}"], writes=[f"osb{ob}"])

                    def mla_norm_b(c, h=h, pr=pr, hh=hh):
                        po = 4 + (c % 2)
                        ob = c % 2
                        S.mm(lambda e: e.matmul(PS[po][0:64, :], lhsT=onesf[64:65, 0:64], rhs=osb[64:65, ob, :], start=True, stop=True),
                             reads=["ones", f"osb{ob}"], writes=[PK[po]])
                        if hh == 0:
                            S.op("dve", lambda e: e.tensor_tensor(
                                out=omT[0:64, pr, c * 512:(c + 1) * 512], in0=PS[po][0:64, :], in1=osb[0:64, ob, :], op=ALU.mult),
                                reads=[PK[po], f"osb{ob}"], writes=["omT"])
                        else:
                            S.op("dve", lambda e: e.tensor_tensor(
                                out=otmp[:, ob, :], in0=PS[po][0:64, :], in1=osb[0:64, ob, :], op=ALU.mult),
                                reads=[PK[po], f"osb{ob}"], writes=[f"otmp{ob}"])
                            S.dma("sp", omT[64:128, pr, c * 512:(c + 1) * 512], otmp[:, ob, :], reads=[f"otmp{ob}"], writes=["omT"])

                    n_it = len(items)
                    for k in range(n_it + LA):
                        if k < n_it:
                            mla_scores(items[k][0], items[k][1], sit + k)
                        for pd in list(pend):
                            if k >= pd[0]:
                                mla_norm_b(pd[1])
                                pend.remove(pd)
                        if k >= LA:
                            c_, j_ = items[k - LA]
                            mla_exp_pv(c_, j_, sit + k - LA)
                            if j_ == NKT - 1:
                                mla_norm_a(c_)
                                pend.append((k + 3, c_))
                    for pd in pend:
                        mla_norm_b(pd[1])
                    sit += n_it
            S.barrier()

            mcut(5)
            with ExitStack() as s2:
                mT = s2.enter_context(_sbuf_tensor("mT", [128, KC, NT], BF16))
                wg = s2.enter_context(_sbuf_tensor("wg", [128, 2, 3, KC, 128], BF16))
                wb_ = s2.enter_context(_sbuf_tensor("wbr", [128, 2, 3, 4, 128], BF16))
                bgT = s2.enter_context(_sbuf_tensor("bgT", [128, 24], F32))
                gsb = s2.enter_context(_sbuf_tensor("gsb", [128, 2, 512], BF16))
                acc = s2.enter_context(_sbuf_tensor("acc", [128, 2, 512], F32))
                tm = s2.enter_context(_sbuf_tensor("tm", [128, 2, 512], F32))
                wo = s2.enter_context(_sbuf_tensor("wo", [128, KC, D], BF16))
                tmp = s2.enter_context(_sbuf_tensor("tmpm", [128, 5, D], F32))
                xt2 = s2.enter_context(_sbuf_tensor("xt2m", [128, 2, D], F32))
                alloc_vec(s2, V)
                load_epi(l, 5, 1, 1.0, V)
                load_colvec(b_gate, l * 3 * D, 24, bgT[:], "bgT")
                S.dma("pool", wo[:], w_out.ap()[l].rearrange("(k p) n -> p k n", p=128), writes=["wo"])
                wgv = w_gate.ap()[l].rearrange("(k p) (g n) -> p g k n", p=128, g=3)
                brs = [w.ap()[l].rearrange("(k p) n -> p k n", p=128) for w in (w_bf, w_bm, w_bn)]
                bins = [(specT, "specT"), (omT, "omT"), (onT, "onT")]
                git = 0
                for oc in range(KC):
                    wbuf = oc % 2
                    S.dma("pool", wg[:, wbuf], wgv[:, :, :, oc * 128:(oc + 1) * 128], writes=[f"wg{wbuf}"])
                    for gi in range(3):
                        S.dma("pool", wb_[:, wbuf, gi], brs[gi][:, :, oc * 128:(oc + 1) * 128], writes=[f"wbr{wbuf}"])
                    for c in range(4):
                        ab = (oc * 4 + c) % 2
                        for gi in range(3):
                            pg = (git % 2)
                            py = 2 + (git % 2)
                            git += 1
                            for k in range(KC):
                                S.mm(lambda e, k=k, gi=gi, c=c, pg=pg, wbuf=wbuf: e.matmul(
                                    PS[pg][:, :], lhsT=wg[:, wbuf, gi, k, :], rhs=hT[:, k, c * 512:(c + 1) * 512], start=(k == 0), stop=(k == KC - 1)),
                                    reads=[f"wg{wbuf}", "hTm"], writes=[PK[pg]], first=(k == 0))
                            bsrc, bkey = bins[gi]
                            for k4 in range(4):
                                S.mm(lambda e, k4=k4, gi=gi, c=c, py=py, wbuf=wbuf, bsrc=bsrc: e.matmul(
                                    PS[py][:, :], lhsT=wb_[:, wbuf, gi, k4, :], rhs=bsrc[:, k4, c * 512:(c + 1) * 512], start=(k4 == 0), stop=(k4 == 3)),
                                    reads=[f"wbr{wbuf}", bkey], writes=[PK[py]], first=(k4 == 0))
                            gb = git % 2
                            S.op("act", lambda e, pg=pg, gi=gi, oc=oc, gb=gb: e.activation(
                                out=gsb[:, gb, :], in_=PS[pg][:, :], func=AF.Sigmoid, bias=bgT[:, gi * 8 + oc:gi * 8 + oc + 1], scale=1.0),
                                reads=[PK[pg], "bgT"], writes=[f"gsb{gb}"])
                            if gi == 0:
                                S.op("dve", lambda e, py=py, gb=gb, ab=ab: e.tensor_tensor(out=acc[:, ab, :], in0=PS[py][:, :], in1=gsb[:, gb, :], op=ALU.mult),
                                     reads=[PK[py], f"gsb{gb}"], writes=[f"acc{ab}"])
                            else:
                                S.op("dve", lambda e, py=py, gb=gb, ab=ab: e.tensor_tensor(out=tm[:, ab, :], in0=PS[py][:, :], in1=gsb[:, gb, :], op=ALU.mult),
                                     reads=[PK[py], f"gsb{gb}"], writes=[f"tm{ab}"])
                                if gi == 1:
                                    S.op("pool", lambda e, ab=ab: e.tensor_tensor(out=acc[:, ab, :], in0=acc[:, ab, :], in1=tm[:, ab, :], op=ALU.add),
                                         reads=[f"acc{ab}", f"tm{ab}"], writes=[f"acc{ab}"])
                                else:
                                    S.op("pool", lambda e, ab=ab, oc=oc, c=c: e.tensor_tensor(out=mT[:, oc, c * 512:(c + 1) * 512], in0=acc[:, ab, :], in1=tm[:, ab, :], op=ALU.add),
                                         reads=[f"acc{ab}", f"tm{ab}"], writes=["mT"])
                NB = 5

                def mg_load(t):
                    b = t % 2
                    S.dma("sp", xt2[:, b, :], xin[t * 128:(t + 1) * 128, :], reads=["xdram"], writes=[f"xt2{b}"])

                def mg_mm(t):
                    b = t % 2
                    zb = t % NB
                    for half in range(2):
                        pi = 4 + half
                        for k in range(KC):
                            S.mm(lambda e, k=k, half=half, pi=pi: e.matmul(
                                PS[pi][:, :], lhsT=mT[:, k, t * 128:(t + 1) * 128], rhs=wo[:, k, half * 512:(half + 1) * 512],
                                start=(k == 0), stop=(k == KC - 1)),
                                reads=["mT", "wo"], writes=[PK[pi]], first=(k == 0))
                        S.op("dve", lambda e, pi=pi, half=half: e.tensor_tensor(
                            out=tmp[:, zb, half * 512:(half + 1) * 512], in0=PS[pi][:, :], in1=V["gate_bc"][:, half * 512:(half + 1) * 512], op=ALU.mult),
                            reads=[PK[pi], "gate_bc"], writes=[f"tmpm{zb}"])
                    S.op("dve", lambda e: e.scalar_tensor_tensor(out=tmp[:, zb, :], in0=xt2[:, b, :], scalar=ALPHA, in1=tmp[:, zb, :],
                                                                 op0=ALU.mult, op1=ALU.add),
                         reads=[f"xt2{b}", f"tmpm{zb}"], writes=[f"tmpm{zb}"])

                lo = ln_out_stages(lambda t: tmp[:, t % NB, :], lambda t: f"tmpm{t % NB}", T, V,
                                   lambda t: [xo[t * 128:(t + 1) * 128, :] for xo in xouts])

                def mg_mm_stats(t):
                    mg_mm(t)
                    lo[0](t)

                run_staged(NTILE, [mg_load, mg_mm_stats] + lo[1:])
        S.barrier()

    prologue()
    bufs = [xa.ap(), xb.ap()]
    cur = x0.ap()
    stage = 0
    for l in range(DEPTH):
        for kind in ("ffn1", "mix", "ffn2"):
            if stage >= nstages:
                break
            last = (stage == nstages - 1)
            dst = y.ap() if last else bufs[stage % 2]
            if kind == "ffn1":
                ffn(l, 1, cur, [dst])
            elif kind == "mix":
                mixer(l, cur, [dst])
            else:
                ffn(l, 2, cur, [dst])
            cur = dst
            stage += 1
    for e in ("sp", "pool", "act", "dve", "pe"):
        S.wait_all_dma(e)
    import os
    if os.environ.get("KDBG_STATS"):
        print("SIGVALS", S.sigval, "POS", S.pos, "DMA", max(S.dcount), flush=True)
    return nc


def _bf16(a):
    return np.asarray(a, dtype=np.float32).astype(ml_dtypes.bfloat16)


def _role_consts(role):
    c = {}
    t = np.arange(NT)
    if role == "sample":
        pos = np.stack([t // 64, t % 64], -1).astype(np.float32)
        inv = (10000.0 ** (-np.arange(8, dtype=np.float32) / 8)).astype(np.float32)
        ang = pos[:, :, None] * inv
        ang = np.concatenate([ang, ang], -1)
        cos = np.cos(ang).reshape(NT, 32).T
        sin = np.sin(ang).reshape(NT, 32).T
        c["ropeC"] = np.ascontiguousarray(cos, dtype=np.float32)
        c["ropeS"] = np.ascontiguousarray(sin, dtype=np.float32)
        c["indq_m"] = np.zeros((8, NT), np.float32)
        c["indq_n"] = np.zeros((8, NT), np.float32)
        c["indk"] = np.zeros((8, NK), np.float32)
        L = NT
        blk = np.zeros(NT, np.int64)
        loc = t
    else:
        c["ropeC"] = np.ones((32, NT), np.float32)
        c["ropeS"] = np.zeros((32, NT), np.float32)
        oh = (t[None, :] // 256 == np.arange(8)[:, None]).astype(np.float32)
        c["indq_m"] = oh * BIG_MLA
        c["indq_n"] = oh * BIG_NA
        ik = np.zeros((8, NK), np.float32)
        ik[:, :NT] = oh
        c["indk"] = ik
        L = 256
        blk = t // 256
        loc = t % 256
    norm = 1.0 / math.sqrt(L * 128.0)
    same = (blk[:, None] == blk[None, :])
    ph = (2.0 * np.pi / L) * ((loc[:, None] * loc[None, :]) % L).astype(np.float64)
    c["dftC"] = _bf16(np.where(same, np.cos(ph) * norm, 0.0))
    c["dftS"] = _bf16(np.where(same, -np.sin(ph) * norm, 0.0))
    cc = np.arange(128)
    ph2 = (2.0 * np.pi / 128) * ((cc[:, None] * cc[None, :]) % 128).astype(np.float64)
    c["dftCS"] = np.concatenate([np.cos(ph2), np.sin(ph2)], 1).astype(np.float32)
    m1 = np.zeros((128, NPAT, 128), np.float32)
    m2 = np.zeros((128, NPAT, 128), np.float32)
    if role == "sample":
        kk = np.arange(128)
        kr, kc = kk // 64, kk % 64
        qr, qc = kk // 64, kk % 64
        cstart = np.clip(qc - 8, 0, 48)
        col_ok = (kc[:, None] >= cstart[None, :]) & (kc[:, None] < cstart[None, :] + 16)
        for p, (dl, typ) in enumerate(PAT_DELTA):
            rel = 2 * dl + kr[:, None] - qr[None, :]
            row_ok = ((rel >= -4) & (rel <= 3)) if typ == 0 else np.ones_like(rel, bool)
            ok = col_ok & row_ok
            m1[:, p, :] = np.where(ok, 1.0 / NA_SCALE, 0.0)
            m2[:, p, :] = np.where(ok, 0.0, NEG)
    c["m1d"] = m1
    c["m2d"] = m2
    pr = np.zeros((32, 32), np.float32)
    for a in range(2):
        for j in range(16):
            d = a * 16 + j
            if j < 8:
                pr[d, d + 8] = -1.0
            else:
                pr[d, d - 8] = 1.0
    c["protT"] = np.ascontiguousarray(pr.T)
    c["identd"] = np.eye(128, dtype=np.float32)
    return c


_CACHE = {}


def _get_nc(nstages):
    if nstages not in _CACHE:
        _CACHE[nstages] = build(nstages)
    return _CACHE[nstages]


def run_units(inputs, nstages=3 * DEPTH, cores=None):
    f32 = lambda a: np.ascontiguousarray(np.asarray(a), dtype=np.float32)
    xp = f32(inputs["x_prompt"])
    xs = f32(inputs["x_sample"])
    shared = {}
    for nm in ("w_ada", "b_ada", "ffn1_w1", "ffn1_w3", "ffn1_w2", "ffn2_w1", "ffn2_w3", "ffn2_w2", "w_in",
               "mla_q_norm", "mla_w_uq", "mla_kv_norm", "mla_w_ukv", "w_branch_f", "w_branch_m", "w_branch_n",
               "w_gate", "b_gate", "w_out", "ln_g", "ln_b"):
        shared[nm] = f32(inputs[nm])
    rp = f32(inputs["na_rpb"])[..., ::-1].reshape(-1)
    shared["rpbr"] = np.concatenate([np.zeros(RPAD, np.float32), rp, np.zeros(RPAD, np.float32)])
    cp = _role_consts("prompt")
    cs = _role_consts("sample")
    zc = {"c_ckv": np.zeros((DEPTH, NCTX, 256), np.float32), "c_kr": np.zeros((DEPTH, NCTX, 32), np.float32),
          "c_nk": np.zeros((DEPTH, NCTX, 512), np.float32), "c_nv": np.zeros((DEPTH, NCTX, 512), np.float32)}
    in_maps = []
    for core in range(8):
        m = dict(shared)
        if core < 4 or core >= 6:
            u = core if core < 4 else core - 6
            m["x0"] = xp[u * 8:(u + 1) * 8].reshape(NT, D)
            m["cvec"] = f32(inputs["c_ctx"]).reshape(1, D)
            m.update(zc)
            m.update(cp)
        else:
            b = core - 4
            m["x0"] = xs[b]
            m["cvec"] = f32(inputs["c"])[b].reshape(1, D)
            m["c_ckv"] = f32(inputs["cache_mla_ckv"])[b]
            m["c_kr"] = f32(inputs["cache_mla_krope"])[b]
            m["c_nk"] = f32(inputs["cache_na_k"])[b].reshape(DEPTH, NCTX, 512)
            m["c_nv"] = f32(inputs["cache_na_v"])[b].reshape(DEPTH, NCTX, 512)
            m.update(cs)
        in_maps.append(m)
    nc = _get_nc(nstages)
    if cores is not None:
        res = run_bass_kernel_spmd(nc, [in_maps[c] for c in cores], core_ids=list(range(len(cores))))
        return {c: res.results[i] for i, c in enumerate(cores)}
    res = run_bass_kernel_spmd(nc, in_maps, core_ids=list(range(8)))
    return res.results


def kernel(**inputs):
    r = run_units(inputs)
    yp = np.concatenate([r[u]["y"].reshape(8, 256, D) for u in range(4)], 0)
    ys = np.stack([r[4]["y"], r[5]["y"]], 0)

    def gather(name, tail):
        a = np.concatenate([r[u][name].reshape(DEPTH, 8, 256, -1).transpose(1, 0, 2, 3) for u in range(4)], 0)
        return np.ascontiguousarray(a.reshape((32, DEPTH, 256) + tail), dtype=np.float32)

    return (np.ascontiguousarray(yp, dtype=np.float32), np.ascontiguousarray(ys, dtype=np.float32),
            gather("o_ckv", (256,)), gather("o_kr", (32,)), gather("o_nk", (8, 64)), gather("o_nv", (8, 64)))
```

```python
import math
from collections import defaultdict

import numpy as np
import ml_dtypes

import concourse.bass as bass
import concourse.mybir as mybir
from concourse.bass_utils import run_bass_kernel_spmd

F32 = mybir.dt.float32
BF16 = mybir.dt.bfloat16
AF = mybir.ActivationFunctionType
ALU = mybir.AluOpType

D = 1024
KC = 8
DEPTH = 4
NT = 2048
NTILE = 16
NCTX = 512
NK = NT + NCTX
NKT = NK // 128
FF = 2816
FC = FF // 128
IN_W = 2720
C_F, C_Q, C_KV, C_R, C_NQ, C_NK, C_NV = 0, 512, 896, 1152, 1184, 1696, 2208
ALPHA = (2.0 * DEPTH) ** 0.25
MLA_SCALE = 96 ** -0.5
NA_SCALE = 0.125
BIG_MLA = 576.0
BIG_NA = 480.0
NEG = -30000.0
NPAT = 12
RPAD = 64


class _PEProxy:
    def __init__(self, pe):
        self.pe = pe
        self.last_stop = None

    def matmul(self, *a, **kw):
        self.last_stop = kw.get("stop", None)
        return self.pe.matmul(*a, **kw)

    def transpose(self, *a, **kw):
        self.last_stop = True
        return self.pe.transpose(*a, **kw)


class Sched:
    ENGS = ("pe", "act", "dve", "pool", "sp")

    def __init__(self, nc, n_dma_sems=56):
        self.nc = nc
        self.eng = {"pe": nc.tensor, "act": nc.scalar, "dve": nc.vector, "pool": nc.gpsimd, "sp": nc.sync}
        self.esem = {e: nc.alloc_semaphore(f"es_{e}") for e in self.ENGS}
        self.pos = {e: 0 for e in self.ENGS}
        self.sigs = {e: [] for e in self.ENGS}
        self.sigval = {e: 0 for e in self.ENGS}
        self.last = {e: None for e in self.ENGS}
        self.waited = defaultdict(int)
        self.dsems = [nc.alloc_semaphore(f"ds_{i}") for i in range(n_dma_sems)]
        self.dcount = [0] * n_dma_sems
        self.dnext = 0
        self.dnext_pool = 0
        self.W = defaultdict(dict)
        self.R = defaultdict(dict)
        self.peproxy = _PEProxy(nc.tensor)

    def _need(self, eng, tok, raw):
        if tok[0] == "d":
            _, idx, val = tok
            return (("d", idx), self.dsems[idx], val)
        _, f, p = tok
        if f == eng and eng == "pe":
            return None
        val = None
        for (sp_, sv) in reversed(self.sigs[f]):
            if sp_ >= p:
                val = sv
            else:
                break
        if val is None:
            ins, lp = self.last[f]
            assert lp >= p
            self.sigval[f] += 1
            ins.then_inc(self.esem[f], 1)
            self.sigs[f].append((lp, self.sigval[f]))
            val = self.sigval[f]
        return (("e", f), self.esem[f], val)

    def _waits(self, eng, reads, writes):
        needs = {}

        def add(tok, raw):
            n = self._need(eng, tok, raw)
            if n:
                key, sem, val = n
                if key not in needs or needs[key][0] < val:
                    needs[key] = (val, sem)

        for k in reads:
            for t in self.W[k].values():
                add(t, True)
            if k.startswith("ps"):
                for rk, r in self.R[k].items():
                    if rk != eng:
                        add(r, False)
        for k in writes:
            for r in self.R[k].values():
                add(r, False)
        for key, (val, sem) in needs.items():
            if self.waited[(eng, key)] < val:
                self.eng[eng].wait_ge(sem, val)
                self.waited[(eng, key)] = val

    def _record(self, tok, reads, writes, rkey):
        for k in writes:
            self.W[k][rkey] = tok
        for k in reads:
            self.R[k][rkey] = tok

    def op(self, eng, fn, reads=(), writes=(), check_writes=True):
        self._waits(eng, reads, writes if check_writes else ())
        if eng == "pe":
            self.peproxy.last_stop = None
            ins = fn(self.peproxy)
            sig = bool(self.peproxy.last_stop)
        else:
            ins = fn(self.eng[eng])
            sig = True
        self.pos[eng] += 1
        p = self.pos[eng]
        self.last[eng] = (ins, p)
        if sig:
            self.sigval[eng] += 1
            ins.then_inc(self.esem[eng], 1)
            self.sigs[eng].append((p, self.sigval[eng]))
        self._record(("e", eng, p), reads, writes, eng)
        return ins

    def mm(self, fn, reads=(), writes=(), first=True):
        return self.op("pe", fn, reads, writes, check_writes=first)

    def dma(self, q, out, in_, reads=(), writes=(), **kw):
        half = len(self.dsems) // 2
        if q == "pool":
            idx = self.dnext_pool
            self.dnext_pool = (self.dnext_pool + 1) % half
        else:
            idx = half + self.dnext
            self.dnext = (self.dnext + 1) % (len(self.dsems) - half)
        if self.dcount[idx] and self.waited[(q, ("d", idx))] < self.dcount[idx]:
            self.eng[q].wait_ge(self.dsems[idx], self.dcount[idx])
            self.waited[(q, ("d", idx))] = self.dcount[idx]
        self._waits(q, reads, writes)
        self.eng[q].dma_start(out=out, in_=in_, **kw).then_inc(self.dsems[idx], 16)
        self.dcount[idx] += 16
        tok = ("d", idx, self.dcount[idx])
        self._record(tok, reads, writes, ("d", idx))
        return tok

    def wait_all_dma(self, eng="sp"):
        for idx, c in enumerate(self.dcount):
            if c and self.waited[(eng, ("d", idx))] < c:
                self.eng[eng].wait_ge(self.dsems[idx], c)
                self.waited[(eng, ("d", idx))] = c

    def barrier(self):
        toks = []
        for f in self.ENGS:
            if self.last[f] is not None:
                toks.append(("e", f, self.last[f][1]))
        for e in self.ENGS:
            needs = {}
            for t in toks:
                n = self._need(e, t, True)
                if n:
                    key, sem, val = n
                    if key not in needs or needs[key][0] < val:
                        needs[key] = (val, sem)
            for key, (val, sem) in needs.items():
                if self.waited[(e, key)] < val:
                    self.eng[e].wait_ge(sem, val)
                    self.waited[(e, key)] = val
            self.wait_all_dma(e)
        self.W = defaultdict(dict)
        self.R = defaultdict(dict)


def na_window(i):
    if i <= 1:
        tiles, typ = [0, 1, 2, 3], 1
    elif i >= 14:
        tiles, typ = [12, 13, 14, 15], 1
    else:
        tiles, typ = [i - 2, i - 1, i, i + 1, i + 2], 0
    out = []
    for j in tiles:
        dl = j - i
        pat = (dl + 2) if typ == 0 else (5 + dl + 3)
        out.append((j, pat))
    return out


PAT_DELTA = [(-2, 0), (-1, 0), (0, 0), (1, 0), (2, 0)] + [(d, 1) for d in range(-3, 4)]


def build(nstages=3 * DEPTH):
    nc = bass.Bass("TRN2", target_bir_lowering=False)
    S = Sched(nc)

    def din(name, shape, dt=F32):
        return nc.dram_tensor(name, list(shape), dt, kind="ExternalInput")

    x0 = din("x0", [NT, D])
    cvec = din("cvec", [1, D])
    c_ckv = din("c_ckv", [DEPTH, NCTX, 256])
    c_kr = din("c_kr", [DEPTH, NCTX, 32])
    c_nk = din("c_nk", [DEPTH, NCTX, 512])
    c_nv = din("c_nv", [DEPTH, NCTX, 512])
    w_ada = din("w_ada", [DEPTH, D, 9 * D])
    b_ada = din("b_ada", [DEPTH, 9 * D])
    fw = {}
    for nm in ("ffn1_w1", "ffn1_w3", "ffn2_w1", "ffn2_w3"):
        fw[nm] = din(nm, [DEPTH, D, FF])
    for nm in ("ffn1_w2", "ffn2_w2"):
        fw[nm] = din(nm, [DEPTH, FF, D])
    w_in = din("w_in", [DEPTH, D, IN_W])
    q_norm = din("mla_q_norm", [DEPTH, 384])
    w_uq = din("mla_w_uq", [DEPTH, 384, 768])
    kv_norm = din("mla_kv_norm", [DEPTH, 256])
    w_ukv = din("mla_w_ukv", [DEPTH, 256, 1024])
    rpbr = din("rpbr", [2 * RPAD + DEPTH * 8 * 15 * 31])
    w_bf = din("w_branch_f", [DEPTH, 512, D])
    w_bm = din("w_branch_m", [DEPTH, 512, D])
    w_bn = din("w_branch_n", [DEPTH, 512, D])
    w_gate = din("w_gate", [DEPTH, D, 3 * D])
    b_gate = din("b_gate", [DEPTH, 3 * D])
    w_out = din("w_out", [DEPTH, D, D])
    ln_g = din("ln_g", [DEPTH, 3, D])
    ln_b = din("ln_b", [DEPTH, 3, D])
    ropeC = din("ropeC", [32, NT])
    ropeS = din("ropeS", [32, NT])
    protT = din("protT", [32, 32])
    indq_m = din("indq_m", [8, NT])
    indq_n = din("indq_n", [8, NT])
    indk = din("indk", [8, NK])
    dftC = din("dftC", [NT, NT], BF16)
    dftS = din("dftS", [NT, NT], BF16)
    dftCS = din("dftCS", [128, 256])
    m1d = din("m1d", [128, NPAT, 128])
    m2d = din("m2d", [128, NPAT, 128])
    identd = din("identd", [128, 128])

    def dout(name, shape):
        return nc.dram_tensor(name, list(shape), F32, kind="ExternalOutput")

    y = dout("y", [NT, D])
    o_ckv = dout("o_ckv", [DEPTH, NT, 256])
    o_kr = dout("o_kr", [DEPTH, NT, 32])
    o_nk = dout("o_nk", [DEPTH, NT, 512])
    o_nv = dout("o_nv", [DEPTH, NT, 512])

    xa = nc.dram_tensor("xa", [NT, D], F32, kind="Internal")
    xb = nc.dram_tensor("xb", [NT, D], F32, kind="Internal")
    ada_d = nc.dram_tensor("ada_d", [DEPTH, 9 * D], F32, kind="Internal")
    btd = nc.dram_tensor("btd", [DEPTH, 8, NPAT, 128, 128], F32, kind="Internal")

    def AP(t, off, dims):
        return bass.AP(t, off, [list(d) for d in dims])

    _uid = [0]
    _orig_sbuf_tensor = nc.sbuf_tensor

    def _sbuf_tensor(name, shape, dt):
        _uid[0] += 1
        return _orig_sbuf_tensor(f"{name}_{_uid[0]}", shape, dt)

    sb = nc.alloc_sbuf_tensor
    ident = sb("ident", [128, 128], BF16)
    ones = sb("ones", [128, 128], BF16)
    epsc = sb("epsc", [128, 4], F32)
    identf = sb("identf", [128, 128], F32)
    onesf = sb("onesf", [128, 64], F32)
    PS = [nc.alloc_psum_tensor(f"ps{i}", [128, 512], F32) for i in range(8)]
    PSB = PS[7].bitcast(BF16)
    PK = [f"ps{i}" for i in range(8)]

    S.dma("sp", identf[:], identd.ap(), writes=["identf"])
    S.op("dve", lambda e: e.tensor_copy(out=ident[:], in_=identf[:]), reads=["identf"], writes=["ident"])
    S.op("dve", lambda e: e.memset(ones[:], 1.0), writes=["ones"])
    S.op("dve", lambda e: e.memset(onesf[:], 1.0), writes=["ones"])
    S.op("dve", lambda e: e.memset(epsc[:, 0:1], 1e-6), writes=["epsc"])
    S.op("dve", lambda e: e.memset(epsc[:, 1:2], 1e-5), writes=["epsc"])

    evac_rr = [0]

    def evac_eng():
        evac_rr[0] += 1
        return "act" if evac_rr[0] % 2 else "dve"

    def copy_op(eng, out, in_):
        if eng == "act":
            return lambda e: e.activation(out=out, in_=in_, func=AF.Copy)
        return lambda e: e.tensor_copy(out=out, in_=in_)

    def prologue():
        with _sbuf_tensor("crow", [8, 128], F32) as crow, \
                _sbuf_tensor("srow", [8, 128], BF16) as srow, \
                _sbuf_tensor("scT", [128, 8], BF16) as scT, \
                _sbuf_tensor("wada", [128, 2, 8, 512], BF16) as wada, \
                _sbuf_tensor("brow", [1, 2, 512], F32) as brow, \
                _sbuf_tensor("orow", [1, 2, 512], F32) as orow:
            S.dma("sp", crow[:], cvec.ap().rearrange("o (k p) -> (o k) p", p=128), writes=["crow"])
            S.op("act", lambda e: e.activation(out=srow[:], in_=crow[:], func=AF.Silu), reads=["crow"], writes=["srow"])
            S.mm(lambda e: e.matmul(PS[0][:, 0:8], lhsT=srow[:], rhs=ident[0:8, 0:8], start=True, stop=True),
                 reads=["srow", "ident"], writes=[PK[0]])
            S.op("dve", lambda e: e.tensor_copy(out=scT[:], in_=PS[0][:, 0:8]), reads=[PK[0]], writes=["scT"])
            it = 0
            for l in range(DEPTH):
                wv = w_ada.ap()[l].rearrange("(k p) n -> p k n", p=128)
                for j in range(18):
                    b = it % 2
                    it += 1
                    S.dma("pool", wada[:, b], wv[:, :, j * 512:(j + 1) * 512], writes=[f"wada{b}"])
                    S.dma("sp", brow[:, b], b_ada.ap()[l:l + 1, j * 512:(j + 1) * 512], writes=[f"brow{b}"])
                    pk = PK[b]
                    for k in range(KC):
                        S.mm(lambda e, k=k, b=b: e.matmul(PS[b][0:1, :], lhsT=scT[:, k:k + 1], rhs=wada[:, b, k, :],
                                                          start=(k == 0), stop=(k == KC - 1)),
                             reads=["scT", f"wada{b}"], writes=[pk], first=(k == 0))
                    S.op("dve", lambda e, b=b: e.tensor_tensor(out=orow[:, b], in0=PS[b][0:1, :], in1=brow[:, b], op=ALU.add),
                         reads=[pk, f"brow{b}"], writes=[f"orow{b}"])
                    S.dma("sp", ada_d.ap()[l:l + 1, j * 512:(j + 1) * 512], orow[:, b], reads=[f"orow{b}"], writes=["ada_d"])
        for l in range(DEPTH):
            for p, (dl, typ) in enumerate(PAT_DELTA):
                for kr in range(2):
                    for qr in range(2):
                        dr = 2 * dl + kr - qr + 7
                        drc = min(max(dr, 0), 14)
                        src = AP(rpbr, RPAD + ((l * 8) * 15 + drc) * 31 + 15, [[465, 8], [-1, 64], [1, 64]])
                        dst = AP(btd, (l * 8 * NPAT + p) * 16384 + kr * 64 * 128 + qr * 64,
                                 [[NPAT * 16384, 8], [128, 64], [1, 64]])
                        S.dma("sp", dst, src, writes=["btd"])
        S.barrier()

    def load_colvec(src_t, off, n, dst, dkey):
        with _sbuf_tensor("cvrow", [32, 128], F32) as row:
            S.dma("sp", row[0:n, :], AP(src_t, off, [[128, n], [1, 128]]), writes=["cvrow"])
            S.mm(lambda e: e.transpose(out=PS[6][:, 0:n], in_=row[0:n, :], identity=identf[0:n, 0:n]),
                 reads=["cvrow", "identf"], writes=[PK[6]])
            S.op("dve", lambda e: e.tensor_copy(out=dst, in_=PS[6][:, 0:n]), reads=[PK[6]], writes=[dkey])
            S.barrier()

    def load_mod(l, shift_idx, scale_idx, V):
        load_colvec(ada_d, l * 9 * D + shift_idx * D, 8, V["shT"][:], "shT")
        load_colvec(ada_d, l * 9 * D + scale_idx * D, 8, V["scT"][:], "scT")
        S.op("dve", lambda e: e.tensor_scalar(out=V["scT"][:], in0=V["scT"][:], scalar1=1.0, scalar2=None, op0=ALU.add),
             reads=["scT"], writes=["scT"])

    def load_epi(l, gate_idx, ln_idx, gate_coef, V):
        S.dma("sp", V["gate_bc"][:], AP(ada_d, l * 9 * D + gate_idx * D, [[0, 128], [1, D]]), reads=["ada_d"], writes=["gate_bc"])
        S.dma("sp", V["lng_bc"][:], AP(ln_g, (l * 3 + ln_idx) * D, [[0, 128], [1, D]]), writes=["lng_bc"])
        S.dma("sp", V["lnb_bc"][:], AP(ln_b, (l * 3 + ln_idx) * D, [[0, 128], [1, D]]), writes=["lnb_bc"])
        if gate_coef != 1.0:
            S.op("pool", lambda e: e.tensor_scalar(out=V["gate_bc"][:], in0=V["gate_bc"][:], scalar1=gate_coef, scalar2=None, op0=ALU.mult),
                 reads=["gate_bc"], writes=["gate_bc"])

    NTAG = 5

    def run_staged(n, stages):
        k = len(stages)
        for step in range(n + k - 1):
            for s_ in reversed(range(k)):
                t = step - s_
                if 0 <= t < n:
                    stages[s_](t)

    def ln_stage_fns(x_of, xkey_of, T, eps_col):
        st, mv, rs, nb = T["st"], T["mv"], T["rstd"], T["nb"]

        def A(t):
            g = t % NTAG
            xt, xkey = x_of(t), xkey_of(t)
            S.op("dve", lambda e: e.bn_stats(out=st[:, g, 0:6], in_=xt[:, 0:512]), reads=[xkey], writes=[f"lnsa{g}"])
            S.op("dve", lambda e: e.bn_stats(out=st[:, g, 6:12], in_=xt[:, 512:1024]), reads=[xkey], writes=[f"lnsb{g}"])
            S.op("dve", lambda e: e.bn_aggr(out=mv[:, g, :], in_=st[:, g, :]), reads=[f"lnsa{g}", f"lnsb{g}"], writes=[f"lnmv{g}"])

        def B(t):
            g = t % NTAG
            S.op("act", lambda e: e.activation(out=rs[:, g:g + 1], in_=mv[:, g, 1:2], func=AF.Sqrt, bias=epsc[:, eps_col:eps_col + 1], scale=1.0),
                 reads=[f"lnmv{g}", "epsc"], writes=[f"lnrs{g}"])

        def C(t):
            g = t % NTAG
            S.op("dve", lambda e: e.reciprocal(out=rs[:, g:g + 1], in_=rs[:, g:g + 1]), reads=[f"lnrs{g}"], writes=[f"lnrs{g}"])
            S.op("dve", lambda e: e.scalar_tensor_tensor(out=nb[:, g:g + 1], in0=mv[:, g, 0:1], scalar=-1.0, in1=rs[:, g:g + 1],
                                                         op0=ALU.mult, op1=ALU.mult),
                 reads=[f"lnmv{g}", f"lnrs{g}"], writes=[f"lnnb{g}"])

        return [A, B, C]

    def ln_in_stages(x_of, xkey_of, T, V, hT, hkey, tcol_of):
        xn = T["xn"]

        def D(t):
            g = t % NTAG
            S.op("act", lambda e: e.activation(out=xn[:, g, :], in_=x_of(t), func=AF.Identity, scale=T["rstd"][:, g:g + 1], bias=T["nb"][:, g:g + 1]),
                 reads=[xkey_of(t), f"lnrs{g}", f"lnnb{g}"], writes=[f"xn{g}"])

        def E(t):
            g = t % NTAG
            tcol = tcol_of(t)
            for kk in range(KC):
                S.mm(lambda e, kk=kk: e.transpose(out=PSB[:, kk * 128:(kk + 1) * 128], in_=xn[:, g, kk * 128:(kk + 1) * 128], identity=ident[:]),
                     reads=[f"xn{g}", "ident"], writes=["ps7"], first=(kk == 0))
            evac_rr[0] += 1
            teng = "dve" if evac_rr[0] % 2 == 0 else "act"
            for kk in range(KC):
                if teng == "dve":
                    fn = lambda e, kk=kk: e.tensor_scalar(out=hT[:, kk, tcol:tcol + 128], in0=PSB[:, kk * 128:(kk + 1) * 128],
                                                          scalar1=V["scT"][:, kk:kk + 1], scalar2=V["shT"][:, kk:kk + 1], op0=ALU.mult, op1=ALU.add)
                else:
                    fn = lambda e, kk=kk: e.activation(out=hT[:, kk, tcol:tcol + 128], in_=PSB[:, kk * 128:(kk + 1) * 128], func=AF.Identity,
                                                       scale=V["scT"][:, kk:kk + 1], bias=V["shT"][:, kk:kk + 1])
                S.op(teng, fn, reads=["ps7", "scT", "shT"], writes=[hkey])

        return ln_stage_fns(x_of, xkey_of, T, 0) + [D, E]

    def ln_out_stages(z_of, zkey_of, T, V, dst_of):
        def D(t):
            g = t % NTAG
            z, zk = z_of(t), zkey_of(t)
            S.op("act", lambda e: e.activation(out=z, in_=z, func=AF.Identity, scale=T["rstd"][:, g:g + 1], bias=T["nb"][:, g:g + 1]),
                 reads=[zk, f"lnrs{g}", f"lnnb{g}"], writes=[zk])

        def E(t):
            z, zk = z_of(t), zkey_of(t)
            S.op("dve", lambda e: e.tensor_tensor(out=z, in0=z, in1=V["lng_bc"][:], op=ALU.mult), reads=[zk, "lng_bc"], writes=[zk])
            S.op("pool", lambda e: e.tensor_tensor(out=z, in0=z, in1=V["lnb_bc"][:], op=ALU.add), reads=[zk, "lnb_bc"], writes=[zk])
            for d_ in dst_of(t):
                S.dma("sp", d_, z, reads=[zk], writes=["xdram"])

        return ln_stage_fns(z_of, zkey_of, T, 1) + [D, E]

    def alloc_ln(stack):
        V = {}
        V["shT"] = stack.enter_context(_sbuf_tensor("shT", [128, 8], F32))
        V["scT"] = stack.enter_context(_sbuf_tensor("scT1", [128, 8], F32))
        T = {}
        T["st"] = stack.enter_context(_sbuf_tensor("st", [128, NTAG, 12], F32))
        T["mv"] = stack.enter_context(_sbuf_tensor("mv", [128, NTAG, 2], F32))
        T["rstd"] = stack.enter_context(_sbuf_tensor("rstd", [128, NTAG], F32))
        T["nb"] = stack.enter_context(_sbuf_tensor("nb", [128, NTAG], F32))
        return V, T

    def alloc_vec(stack, V):
        for nm in ("gate_bc", "lng_bc", "lnb_bc"):
            V[nm] = stack.enter_context(_sbuf_tensor(nm, [128, D], F32))

    from contextlib import ExitStack

    def ffn(l, which, xin, xouts):
        w1 = fw[f"ffn{which}_w1"].ap()[l].rearrange("(k p) n -> p k n", p=128)
        w3 = fw[f"ffn{which}_w3"].ap()[l].rearrange("(k p) n -> p k n", p=128)
        w2 = fw[f"ffn{which}_w2"].ap()[l].rearrange("(f p) n -> p f n", p=128)
        base = 0 if which == 1 else 6
        with ExitStack() as st:
            V, T = alloc_ln(st)
            T["xn"] = st.enter_context(_sbuf_tensor("xn", [128, NTAG, D], BF16))
            alloc_vec(st, V)
            xp = st.enter_context(_sbuf_tensor("xp", [128, 8, D], F32))
            hT = st.enter_context(_sbuf_tensor("hTf", [128, KC, 1024], BF16))
            gT = st.enter_context(_sbuf_tensor("gT", [128, FC, 1024], BF16))
            w13 = st.enter_context(_sbuf_tensor("w13", [128, 2, 2, KC, 256], BF16))
            w2b = st.enter_context(_sbuf_tensor("w2b", [128, 2, FC, 256], BF16))
            sg = st.enter_context(_sbuf_tensor("sg", [128, 2, 512], BF16))
            tmp = st.enter_context(_sbuf_tensor("tmpf", [128, 2, 256], F32))
            import os
            CUT = int(os.environ.get("KDBG_CUT", "99"))
            load_mod(l, base + 0, base + 1, V)
            load_epi(l, base + 2, 0 if which == 1 else 2, 0.5, V)
            if CUT <= 0:
                S.barrier()
                return
            wit = 0
            w2it = 0
            for p in range(2):
                for t in range(8):
                    S.dma("sp", xp[:, t, :], xin[(p * 8 + t) * 128:(p * 8 + t + 1) * 128, :], reads=["xdram"], writes=[f"xp{t}"])
                if CUT <= 1:
                    S.barrier()
                    return
                run_staged(8, ln_in_stages(lambda t: xp[:, t, :], lambda t: f"xp{t}", T, V, hT, "hTf", lambda t: t * 128))
                if CUT <= 2:
                    S.barrier()
                    return
                for f2 in range(FC // 2):
                    b = wit % 2
                    wit += 1
                    S.dma("pool", w13[:, b, 0], w1[:, :, f2 * 256:(f2 + 1) * 256], writes=[f"w1b{b}"])
                    S.dma("pool", w13[:, b, 1], w3[:, :, f2 * 256:(f2 + 1) * 256], writes=[f"w3b{b}"])
                    for fi in range(2):
                        f = f2 * 2 + fi
                        for half in range(2):
                            pa, pb_ = (0, 1) if half == 0 else (2, 3)
                            for k in range(KC):
                                S.mm(lambda e, k=k, b=b, fi=fi, half=half, pa=pa: e.matmul(
                                    PS[pa][:, :], lhsT=w13[:, b, 0, k, fi * 128:(fi + 1) * 128], rhs=hT[:, k, half * 512:(half + 1) * 512],
                                    start=(k == 0), stop=(k == KC - 1)),
                                    reads=[f"w1b{b}", "hTf"], writes=[PK[pa]], first=(k == 0))
                            for k in range(KC):
                                S.mm(lambda e, k=k, b=b, fi=fi, half=half, pb_=pb_: e.matmul(
                                    PS[pb_][:, :], lhsT=w13[:, b, 1, k, fi * 128:(fi + 1) * 128], rhs=hT[:, k, half * 512:(half + 1) * 512],
                                    start=(k == 0), stop=(k == KC - 1)),
                                    reads=[f"w3b{b}", "hTf"], writes=[PK[pb_]], first=(k == 0))
                            S.op("act", lambda e, half=half, pa=pa: e.activation(out=sg[:, half, :], in_=PS[pa][:, :], func=AF.Silu),
                                 reads=[PK[pa]], writes=[f"sg{half}"])
                            S.op("dve", lambda e, half=half, pb_=pb_, f=f: e.tensor_tensor(
                                out=gT[:, f, half * 512:(half + 1) * 512], in0=PS[pb_][:, :], in1=sg[:, half, :], op=ALU.mult),
                                reads=[PK[pb_], f"sg{half}"], writes=["gT"])
                if CUT <= 3:
                    S.barrier()
                    return
                for oq in range(4):
                    b = w2it % 2
                    w2it += 1
                    S.dma("pool", w2b[:, b], w2[:, :, oq * 256:(oq + 1) * 256], writes=[f"w2b{b}"])
                    for t in range(8):
                        pi = 4 + (t % 2)
                        for f in range(FC):
                            S.mm(lambda e, f=f, t=t, b=b, pi=pi: e.matmul(
                                PS[pi][:, 0:256], lhsT=gT[:, f, t * 128:(t + 1) * 128], rhs=w2b[:, b, f, :],
                                start=(f == 0), stop=(f == FC - 1)),
                                reads=["gT", f"w2b{b}"], writes=[PK[pi]], first=(f == 0))
                        tb = t % 2
                        S.op("dve", lambda e, pi=pi, tb=tb, oq=oq: e.tensor_tensor(
                            out=tmp[:, tb, :], in0=PS[pi][:, 0:256], in1=V["gate_bc"][:, oq * 256:(oq + 1) * 256], op=ALU.mult),
                            reads=[PK[pi], "gate_bc"], writes=[f"tmpf{tb}"])
                        S.op("dve", lambda e, t=t, tb=tb, oq=oq: e.scalar_tensor_tensor(
                            out=xp[:, t, oq * 256:(oq + 1) * 256], in0=xp[:, t, oq * 256:(oq + 1) * 256], scalar=ALPHA,
                            in1=tmp[:, tb, :], op0=ALU.mult, op1=ALU.add),
                            reads=[f"tmpf{tb}", f"xp{t}"], writes=[f"xp{t}"])
                if CUT <= 4:
                    S.barrier()
                    return
                run_staged(8, ln_out_stages(lambda t: xp[:, t, :], lambda t: f"xp{t}", T, V,
                                            lambda t, p=p: [xo[(p * 8 + t) * 128:(p * 8 + t + 1) * 128, :] for xo in xouts]))
        S.barrier()

    class _Cut(Exception):
        pass

    def mcut(n):
        import os
        if int(os.environ.get("KDBG_MCUT", "99")) <= n:
            raise _Cut()

    def mixer(l, xin, xouts):
        try:
            mixer_(l, xin, xouts)
        except _Cut:
            pass
        S.barrier()

    def mixer_(l, xin, xouts):
        win = w_in.ap()[l].rearrange("(k p) n -> p k n", p=128)
        with ExitStack() as st:
            V, T = alloc_ln(st)
            hT = st.enter_context(_sbuf_tensor("hTm", [128, KC, NT], BF16))
            specT = st.enter_context(_sbuf_tensor("specT", [128, 4, NT], BF16))
            omT = st.enter_context(_sbuf_tensor("omT", [128, 4, NT], BF16))
            onT = st.enter_context(_sbuf_tensor("onT", [128, 4, NT], BF16))
            load_mod(l, 3, 4, V)
            with _sbuf_tensor("xt2", [128, 6, D], F32) as xt2, _sbuf_tensor("xn", [128, NTAG, D], BF16) as xn_:
                T["xn"] = xn_

                def m0_load(t):
                    S.dma("sp", xt2[:, t % 6, :], xin[t * 128:(t + 1) * 128, :], reads=["xdram"], writes=[f"xt2{t % 6}"])
                run_staged(NTILE, [m0_load] + ln_in_stages(lambda t: xt2[:, t % 6, :], lambda t: f"xt2{t % 6}", T, V, hT, "hTm",
                                                           lambda t: t * 128))
            S.barrier()
            mcut(0)

            with ExitStack() as s2:
                wf = s2.enter_context(_sbuf_tensor("wf", [128, KC, 512], BF16))
                cs = s2.enter_context(_sbuf_tensor("cs", [128, 256], BF16))
                AB = s2.enter_context(_sbuf_tensor("AB", [128, NTILE, 4, 256], BF16))
                ufT = s2.enter_context(_sbuf_tensor("ufT", [128, 2, 512], BF16))
                dbuf = s2.enter_context(_sbuf_tensor("dbuf", [128, 2, 2, 8, 512], BF16))
                S.dma("pool", wf[:], win[:, :, C_F:C_F + 512], writes=["wf"])
                S.dma("pool", cs[:], dftCS.ap(), writes=["cs"])
                it = 0
                for c in range(4):
                    for g in range(4):
                        b = it % 2
                        it += 1
                        for k in range(KC):
                            S.mm(lambda e, k=k, g=g, c=c, b=b: e.matmul(PS[b][:, :], lhsT=wf[:, k, g * 128:(g + 1) * 128],
                                                                   rhs=hT[:, k, c * 512:(c + 1) * 512], start=(k == 0), stop=(k == KC - 1)),
                                 reads=["wf", "hTm"], writes=[PK[b]], first=(k == 0))
                        S.op("act", copy_op("act", ufT[:, b, :], PS[b][:, :]), reads=[PK[b]], writes=[f"ufT{b}"])
                        for tt in range(4):
                            t = c * 4 + tt
                            S.mm(lambda e, tt=tt, b=b: e.matmul(PS[2][:, tt * 256:(tt + 1) * 256] if tt < 2 else PS[3][:, (tt - 2) * 256:(tt - 1) * 256],
                                                           lhsT=ufT[:, b, tt * 128:(tt + 1) * 128], rhs=cs[:], start=True, stop=True),
                                 reads=[f"ufT{b}", "cs"], writes=[PK[2] if tt < 2 else PK[3]])
                        for hh in range(2):
                            S.op("dve", lambda e, hh=hh, c=c, g=g: e.tensor_copy(
                                out=AB[:, c * 4 + hh * 2:c * 4 + hh * 2 + 2, g, :],
                                in_=PS[2 + hh][:, :].rearrange("p (t n) -> p t n", n=256)),
                                reads=[PK[2 + hh]], writes=["AB"])
                dC = dftC.ap().rearrange("(t p) n -> p t n", p=128)
                dS = dftS.ap().rearrange("(t p) n -> p t n", p=128)
                dit = 0
                for c in range(4):
                    for half in range(2):
                        b = dit % 2
                        dit += 1
                        S.dma("sp", dbuf[:, b, 0], dC[:, half * 8:(half + 1) * 8, c * 512:(c + 1) * 512], writes=[f"dC{b}"])
                        S.dma("sp", dbuf[:, b, 1], dS[:, half * 8:(half + 1) * 8, c * 512:(c + 1) * 512], writes=[f"dS{b}"])
                        for g in range(4):
                            for lt in range(8):
                                tl = half * 8 + lt
                                S.mm(lambda e, g=g, lt=lt, tl=tl, b=b, half=half: e.matmul(
                                    PS[g][:, :], lhsT=AB[:, tl, g, 0:128], rhs=dbuf[:, b, 0, lt, :],
                                    start=(half == 0 and lt == 0), stop=False),
                                    reads=["AB", f"dC{b}"], writes=[PK[g]], first=(half == 0 and lt == 0))
                                S.mm(lambda e, g=g, lt=lt, tl=tl, b=b, half=half: e.matmul(
                                    PS[g][:, :], lhsT=AB[:, tl, g, 128:256], rhs=dbuf[:, b, 1, lt, :],
                                    start=False, stop=(half == 1 and lt == 7)),
                                    reads=["AB", f"dS{b}"], writes=[PK[g]], first=False)
                    for g in range(4):
                        eng = evac_eng()
                        S.op(eng, copy_op(eng, specT[:, g, c * 512:(c + 1) * 512], PS[g][:, :]), reads=[PK[g]], writes=["specT"])
            S.barrier()

            mcut(1)
            with ExitStack() as s2:
                knT = s2.enter_context(_sbuf_tensor("knT", [128, 4, NK], BF16))
                qnT = s2.enter_context(_sbuf_tensor("qnT", [128, 4, NT], BF16))
                Vn = s2.enter_context(_sbuf_tensor("Vn", [128, NKT, 8, 65], BF16))
                with ExitStack() as s3:
                    wq = s3.enter_context(_sbuf_tensor("wq", [128, KC, 512], BF16))
                    wk = s3.enter_context(_sbuf_tensor("wk", [128, KC, 512], BF16))
                    wv = s3.enter_context(_sbuf_tensor("wv", [128, KC, 512], BF16))
                    ck = s3.enter_context(_sbuf_tensor("ck", [128, 4, 512], BF16))
                    of32 = s3.enter_context(_sbuf_tensor("of32", [128, 2, 512], F32))
                    S.dma("pool", wq[:], win[:, :, C_NQ:C_NQ + 512], writes=["wq"])
                    S.dma("pool", wk[:], win[:, :, C_NK:C_NK + 512], writes=["wk"])
                    S.dma("pool", wv[:], win[:, :, C_NV:C_NV + 512], writes=["wv"])
                    S.dma("pool", ck[:], c_nk.ap()[l].rearrange("(t p) n -> p t n", p=128), writes=["ck"])
                    for j in range(4):
                        S.dma("pool", Vn[:, NTILE + j, :, 0:64], c_nv.ap()[l, j * 128:(j + 1) * 128, :].rearrange("p (h d) -> p h d", d=64), writes=["Vnc"])
                    S.op("pool", lambda e: e.memset(Vn[:, :, :, 64:65], 1.0), writes=["Vn1"])
                    it = 0
                    for t in range(NTILE):
                        for (wsb, wkey, odst, isv) in ((wk, "wk", o_nk, False), (wv, "wv", o_nv, True)):
                            b = it % 2
                            it += 1
                            for k in range(KC):
                                S.mm(lambda e, k=k, t=t, b=b, wsb=wsb: e.matmul(PS[b][:, :], lhsT=hT[:, k, t * 128:(t + 1) * 128], rhs=wsb[:, k, :],
                                                                               start=(k == 0), stop=(k == KC - 1)),
                                     reads=["hTm", wkey], writes=[PK[b]], first=(k == 0))
                            S.op("act", copy_op("act", of32[:, b, :], PS[b][:, :]), reads=[PK[b]], writes=[f"of32{b}"])
                            if isv:
                                S.op("dve", lambda e, t=t, b=b: e.tensor_copy(out=Vn[:, t, :, 0:64], in_=PS[b][:, :].rearrange("p (h d) -> p h d", d=64)),
                                     reads=[PK[b]], writes=["Vn"])
                            S.dma("sp", odst.ap()[l, t * 128:(t + 1) * 128, :], of32[:, b, :], reads=[f"of32{b}"])
                    for pr in range(4):
                        for c in range(4):
                            for (wsb, wkey, dstT, dkey) in ((wk, "wk", knT, "knT"), (wq, "wq", qnT, "qnT")):
                                b = it % 2
                                it += 1
                                for k in range(KC):
                                    S.mm(lambda e, k=k, pr=pr, c=c, b=b, wsb=wsb: e.matmul(
                                        PS[b][:, :], lhsT=wsb[:, k, pr * 128:(pr + 1) * 128], rhs=hT[:, k, c * 512:(c + 1) * 512],
                                        start=(k == 0), stop=(k == KC - 1)),
                                        reads=[wkey, "hTm"], writes=[PK[b]], first=(k == 0))
                                eng = evac_eng()
                                S.op(eng, copy_op(eng, dstT[:, pr, c * 512:(c + 1) * 512], PS[b][:, :]), reads=[PK[b]], writes=[dkey])
                        for j in range(4):
                            S.mm(lambda e, j=j, pr=pr: e.transpose(out=PSB[:, j * 128:(j + 1) * 128], in_=ck[:, j, pr * 128:(pr + 1) * 128], identity=ident[:]),
                                 reads=["ck", "ident"], writes=["ps7"], first=(j == 0))
                        S.op("dve", lambda e, pr=pr: e.tensor_copy(out=knT[:, pr, NT:NK], in_=PSB[:, 0:512]), reads=["ps7"], writes=["knT"])
                S.barrier()
                mcut(2)
                s3 = s2
                iq = s3.enter_context(_sbuf_tensor("iq", [72, NT], BF16))
                ik = s3.enter_context(_sbuf_tensor("ik", [72, NK], BF16))
                m1 = s3.enter_context(_sbuf_tensor("m1", [128, NPAT, 128], BF16))
                m2 = s3.enter_context(_sbuf_tensor("m2", [128, NPAT, 128], BF16))
                btf = s3.enter_context(_sbuf_tensor("btf", [128, NPAT, 128], F32))
                BT = s3.enter_context(_sbuf_tensor("BT", [128, 2, NPAT, 128], BF16))
                pT = s3.enter_context(_sbuf_tensor("pT", [128, 2, 9, 128], BF16))
                osb = s3.enter_context(_sbuf_tensor("osb", [65, 2, 512], F32))
                otmp = s3.enter_context(_sbuf_tensor("otmp", [64, 2, 512], BF16))
                for pb_ in (0, 64):
                    S.dma("pool", iq[pb_:pb_ + 8, :], indq_n.ap(), writes=["iq"])
                    S.dma("pool", ik[pb_:pb_ + 8, :], indk.ap(), writes=["ik"])
                S.dma("pool", m1[:], m1d.ap(), writes=["m1"])
                S.dma("pool", m2[:], m2d.ap(), writes=["m2"])
                items = [(h, c, qi) for h in range(8) for c in range(4) for qi in range(4)]

                def na_bt(h):
                    hb = h % 2
                    S.dma("sp", btf[:], AP(btd, ((l * 8 + h) * NPAT) * 16384, [[128, 128], [16384, NPAT], [1, 128]]),
                          reads=["btd"], writes=["btf"])
                    S.op("dve", lambda e: e.tensor_tensor(out=btf[:], in0=btf[:], in1=m1[:], op=ALU.mult),
                         reads=["btf", "m1"], writes=["btf"])
                    S.op("dve", lambda e: e.tensor_tensor(out=BT[:, hb], in0=btf[:], in1=m2[:], op=ALU.add),
                         reads=["btf", "m2"], writes=[f"BT{hb}"])

                def na_slots(i):
                    return [(j, pat) for (j, pat) in na_window(i)] + [(NTILE + j, None) for j in range(4)]

                def na_scores(k):
                    h, c, qi = items[k]
                    pr, pb, hb = h // 2, 64 * (h % 2), h % 2
                    i = c * 4 + qi
                    ab = k % 2
                    banks = [ab * 3 + 0, ab * 3 + 1, ab * 3 + 2]
                    for si, (j, pat) in enumerate(na_slots(i)):
                        bk = banks[si // 4]
                        col = (si % 4) * 128
                        S.mm(lambda e, bk=bk, col=col, j=j: e.matmul(
                            PS[bk][:, col:col + 128], lhsT=knT[pb:pb + 64, pr, j * 128:(j + 1) * 128],
                            rhs=qnT[pb:pb + 64, pr, i * 128:(i + 1) * 128], start=True, stop=False),
                            reads=["knT", "qnT"], writes=[PK[bk]], first=(si % 4 == 0))
                        S.mm(lambda e, bk=bk, col=col, j=j, pat=pat: e.matmul(
                            PS[bk][:, col:col + 128], lhsT=ik[pb:pb + 8, j * 128:(j + 1) * 128], rhs=iq[pb:pb + 8, i * 128:(i + 1) * 128],
                            start=False, stop=(pat is None)),
                            reads=["ik", "iq"], writes=[PK[bk]], first=False)
                        if pat is not None:
                            S.mm(lambda e, bk=bk, col=col, pat=pat: e.matmul(
                                PS[bk][:, col:col + 128], lhsT=ident[:], rhs=BT[:, hb, pat, :], start=False, stop=True),
                                reads=["ident", f"BT{hb}"], writes=[PK[bk]], first=False)

                def na_exp(k):
                    h, c, qi = items[k]
                    i = c * 4 + qi
                    ab = k % 2
                    ns = len(na_slots(i))
                    for g in range(3):
                        n_in = min(4, ns - g * 4)
                        if n_in <= 0:
                            continue
                        bk = ab * 3 + g
                        S.op("act", lambda e, bk=bk, g=g, n_in=n_in: e.activation(
                            out=pT[:, ab, g * 4:g * 4 + n_in, :], in_=PS[bk][:, 0:n_in * 128].rearrange("p (s n) -> p s n", n=128),
                            func=AF.Exp, scale=NA_SCALE),
                            reads=[PK[bk]], writes=[f"pT{ab}"])

                def na_pv(k):
                    h, c, qi = items[k]
                    i = c * 4 + qi
                    ab = k % 2
                    po = 6
                    slots = na_slots(i)
                    ns = len(slots)
                    for si, (j, pat) in enumerate(slots):
                        S.mm(lambda e, si=si, j=j: e.matmul(
                            PS[po][0:65, qi * 128:(qi + 1) * 128], lhsT=Vn[:, j, h, :], rhs=pT[:, ab, si, :],
                            start=(si == 0), stop=(si == ns - 1)),
                            reads=["Vn", "Vnc", "Vn1", f"pT{ab}"], writes=[PK[po]], first=(si == 0 and qi == 0))

                def na_norm(k):
                    h, c, qi = items[k]
                    pr, hb = h // 2, h % 2
                    ob = (h * 4 + c) % 2
                    po = 6
                    S.op("act", copy_op("act", osb[:, ob, :], PS[po][0:65, :]), reads=[PK[po]], writes=[f"osb{ob}"])
                    S.op("dve", lambda e: e.reciprocal(out=osb[64:65, ob, :], in_=osb[64:65, ob, :]), reads=[f"osb{ob}"], writes=[f"osb{ob}"])
                    S.mm(lambda e: e.matmul(PS[po][0:64, :], lhsT=onesf[64:65, 0:64], rhs=osb[64:65, ob, :], start=True, stop=True),
                         reads=["ones", f"osb{ob}"], writes=[PK[po]])
                    if hb == 0:
                        S.op("dve", lambda e: e.tensor_tensor(
                            out=onT[0:64, pr, c * 512:(c + 1) * 512], in0=PS[po][0:64, :], in1=osb[0:64, ob, :], op=ALU.mult),
                            reads=[PK[po], f"osb{ob}"], writes=["onT"])
                    else:
                        S.op("dve", lambda e: e.tensor_tensor(
                            out=otmp[:, ob, :], in0=PS[po][0:64, :], in1=osb[0:64, ob, :], op=ALU.mult),
                            reads=[PK[po], f"osb{ob}"], writes=[f"otmp{ob}"])
                        S.dma("sp", onT[64:128, pr, c * 512:(c + 1) * 512], otmp[:, ob, :], reads=[f"otmp{ob}"], writes=["onT"])

                na_bt(0)
                na_scores(0)
                for k in range(len(items)):
                    na_exp(k)
                    if k + 1 < len(items):
                        if items[k + 1][0] != items[k][0]:
                            na_bt(items[k + 1][0])
                        na_scores(k + 1)
                    na_pv(k)
                    if items[k][2] == 3:
                        na_norm(k)
            S.barrier()

            mcut(3)
            with ExitStack() as s2:
                wuq = s2.enter_context(_sbuf_tensor("wuq", [128, 3, 768], BF16))
                wukv = s2.enter_context(_sbuf_tensor("wukv", [128, 2, 2, 8, 64], BF16))
                qn = s2.enter_context(_sbuf_tensor("qn", [128, 3, NT], BF16))
                ckvT = s2.enter_context(_sbuf_tensor("ckvT", [128, 2, NK], BF16))
                kall = s2.enter_context(_sbuf_tensor("kall", [104, 2, NK], BF16))
                qall = s2.enter_context(_sbuf_tensor("qall", [104, 2, NT], BF16))
                wuqr = s2.enter_context(_sbuf_tensor("wuqr", [128, 3, 8, 32], BF16))
                Vm = s2.enter_context(_sbuf_tensor("Vm", [128, NKT, 8, 65], BF16))
                rC = s2.enter_context(_sbuf_tensor("rC", [96, NT], BF16))
                rS = s2.enter_context(_sbuf_tensor("rS", [96, NT], BF16))
                t1 = s2.enter_context(_sbuf_tensor("t1", [96, 2, 512], F32))
                t2 = s2.enter_context(_sbuf_tensor("t2", [96, 2, 512], F32))
                S.dma("pool", wuq[:], w_uq.ap()[l].rearrange("(k p) n -> p k n", p=128), writes=["wuq"])
                for k2 in range(2):
                    for tt in range(2):
                        S.dma("pool", wukv[:, k2, tt],
                              w_ukv.ap()[l, k2 * 128:(k2 + 1) * 128, :].rearrange("p (h t d) -> p t h d", h=8, t=2)[:, tt], writes=["wukv"])
                S.dma("pool", rC[64:96, :], ropeC.ap(), writes=["rC"])
                S.dma("pool", rS[64:96, :], ropeS.ap(), writes=["rS"])
                for hb_ in range(2):
                    S.dma("pool", kall[96:104, hb_, :], indk.ap(), writes=["krTi"])
                    S.dma("pool", qall[96:104, hb_, :], indq_m.ap(), writes=["qrTi"])

                def rot_weights(dst4, src4, keys_r, key_w):
                    S.op("dve", lambda e: e.tensor_scalar(out=dst4[:, :, :, 0:8], in0=src4[:, :, :, 8:16], scalar1=-1.0, scalar2=None, op0=ALU.mult),
                         reads=keys_r, writes=[key_w])
                    S.op("dve", lambda e: e.tensor_copy(out=dst4[:, :, :, 8:16], in_=src4[:, :, :, 0:8]), reads=keys_r, writes=[key_w])

                for j in range(3):
                    rot_weights(wuqr[:, j].rearrange("p h (a d) -> p h a d", d=16),
                                wuq[:, j, :].rearrange("p (h x) -> p h x", x=96)[:, :, 64:96].rearrange("p h (a d) -> p h a d", d=16),
                                ["wuq"], "wuqr")
                S.op("pool", lambda e: e.memset(Vm[:, :, :, 64:65], 1.0), writes=["Vm1"])

                def rope2(ps_x, pkx, ps_r, pkr, b, dsts, dkey, c):
                    S.op("dve", lambda e: e.tensor_tensor(out=t1[64:96, b, :], in0=ps_x, in1=rC[64:96, c * 512:(c + 1) * 512], op=ALU.mult),
                         reads=[pkx, "rC"], writes=[f"t1{b}"])
                    S.op("dve", lambda e: e.tensor_tensor(out=t2[64:96, b, :], in0=ps_r, in1=rS[64:96, c * 512:(c + 1) * 512], op=ALU.mult),
                         reads=[pkr, "rS"], writes=[f"t2{b}"])
                    for dst in dsts:
                        S.op("pool", lambda e, dst=dst: e.tensor_tensor(out=dst, in0=t1[64:96, b, :], in1=t2[64:96, b, :], op=ALU.add),
                             reads=[f"t1{b}", f"t2{b}"], writes=[dkey])

                with ExitStack() as s3:
                    wqa = s3.enter_context(_sbuf_tensor("wqa", [128, KC, 384], BF16))
                    wkr = s3.enter_context(_sbuf_tensor("wkr", [128, KC, 288], BF16))
                    wkrr = s3.enter_context(_sbuf_tensor("wkrr", [128, KC, 32], BF16))
                    gq = s3.enter_context(_sbuf_tensor("gq", [128, 3], F32))
                    gkv = s3.enter_context(_sbuf_tensor("gkv", [128, 256], F32))
                    cc = s3.enter_context(_sbuf_tensor("cc", [128, 4, 256], BF16))
                    ckr = s3.enter_context(_sbuf_tensor("ckr", [128, 4, 32], BF16))
                    uq = s3.enter_context(_sbuf_tensor("uq", [128, 3, 512], F32))
                    sq = s3.enter_context(_sbuf_tensor("sq", [128, 3, 512], BF16))
                    rq = s3.enter_context(_sbuf_tensor("rq", [128, 512], F32))
                    ukv = s3.enter_context(_sbuf_tensor("ukv", [128, 2, 288], F32))
                    kst = s3.enter_context(_sbuf_tensor("kst", [128, 2, 6], F32))
                    kmv = s3.enter_context(_sbuf_tensor("kmv", [128, 2, 2], F32))
                    ssk = s3.enter_context(_sbuf_tensor("ssk", [128, 2], F32))
                    ckf = s3.enter_context(_sbuf_tensor("ckf", [128, 2, 256], F32))
                    ckb = s3.enter_context(_sbuf_tensor("ckb", [128, 2, 256], BF16))
                    S.dma("pool", wqa[:], win[:, :, C_Q:C_Q + 384], writes=["wqa"])
                    S.dma("pool", wkr[:], win[:, :, C_KV:C_KV + 288], writes=["wkr"])
                    rot_weights(wkrr[:].rearrange("p k (a d) -> p k a d", d=16),
                                wkr[:, :, 256:288].rearrange("p k (a d) -> p k a d", d=16), ["wkr"], "wkrr")
                    load_colvec(q_norm, l * 384, 3, gq[:], "gq")
                    S.dma("sp", gkv[:], AP(kv_norm, l * 256, [[0, 128], [1, 256]]), writes=["gkv"])
                    S.dma("pool", cc[:], c_ckv.ap()[l].rearrange("(t p) n -> p t n", p=128), writes=["cc"])
                    S.dma("pool", ckr[:], c_kr.ap()[l].rearrange("(t p) n -> p t n", p=128), writes=["ckr"])
                    it = 0
                    for c in range(4):
                        for j in range(3):
                            b = it % 2
                            it += 1
                            for k in range(KC):
                                S.mm(lambda e, k=k, j=j, c=c, b=b: e.matmul(PS[b][:, :], lhsT=wqa[:, k, j * 128:(j + 1) * 128],
                                                                       rhs=hT[:, k, c * 512:(c + 1) * 512], start=(k == 0), stop=(k == KC - 1)),
                                     reads=["wqa", "hTm"], writes=[PK[b]], first=(k == 0))
                            S.op("act", lambda e, j=j, b=b: e.activation(out=sq[:, j, :], in_=PS[b][:, :], func=AF.Square), reads=[PK[b]], writes=[f"sq{j}"])
                            S.op("dve", lambda e, j=j, b=b: e.tensor_copy(out=uq[:, j, :], in_=PS[b][:, :]), reads=[PK[b]], writes=[f"uq{j}"])
                        for j in range(3):
                            S.mm(lambda e, j=j: e.matmul(PS[2][:, :], lhsT=ones[:], rhs=sq[:, j, :], start=(j == 0), stop=(j == 2)),
                                 reads=["ones", f"sq{j}"], writes=[PK[2]], first=(j == 0))
                        S.op("act", lambda e: e.activation(out=rq[:], in_=PS[2][:, :], func=AF.Sqrt, scale=1.0 / 384, bias=epsc[:, 0:1]),
                             reads=[PK[2], "epsc"], writes=["rq"])
                        S.op("dve", lambda e: e.reciprocal(out=rq[:], in_=rq[:]), reads=["rq"], writes=["rq"])
                        for j in range(3):
                            S.op("dve", lambda e, j=j, c=c: e.scalar_tensor_tensor(out=qn[:, j, c * 512:(c + 1) * 512], in0=uq[:, j, :], scalar=gq[:, j:j + 1],
                                                                                  in1=rq[:], op0=ALU.mult, op1=ALU.mult),
                                 reads=[f"uq{j}", "gq", "rq"], writes=["qn"])
                    for t in range(NTILE):
                        b = t % 2
                        for k in range(KC):
                            S.mm(lambda e, k=k, t=t, b=b: e.matmul(PS[b][:, 0:288], lhsT=hT[:, k, t * 128:(t + 1) * 128], rhs=wkr[:, k, :],
                                                                  start=(k == 0), stop=(k == KC - 1)),
                                 reads=["hTm", "wkr"], writes=[PK[b]], first=(k == 0))
                        S.op("act", copy_op("act", ukv[:, b, :], PS[b][:, 0:288]), reads=[PK[b]], writes=[f"ukv{b}"])
                        S.dma("sp", o_kr.ap()[l, t * 128:(t + 1) * 128, :], ukv[:, b, 256:288], reads=[f"ukv{b}"])
                        S.op("dve", lambda e, b=b: e.bn_stats(out=kst[:, b, :], in_=ukv[:, b, 0:256]), reads=[f"ukv{b}"], writes=[f"kst{b}"])
                        S.op("dve", lambda e, b=b: e.bn_aggr(out=kmv[:, b, :], in_=kst[:, b, :]), reads=[f"kst{b}"], writes=[f"kmv{b}"])
                        S.op("dve", lambda e, b=b: e.scalar_tensor_tensor(out=ssk[:, b:b + 1], in0=kmv[:, b, 0:1], scalar=kmv[:, b, 0:1], in1=kmv[:, b, 1:2],
                                                                         op0=ALU.mult, op1=ALU.add),
                             reads=[f"kmv{b}"], writes=[f"ssk{b}"])
                        S.op("act", lambda e, b=b: e.activation(out=ssk[:, b:b + 1], in_=ssk[:, b:b + 1], func=AF.Sqrt, scale=1.0, bias=epsc[:, 0:1]),
                             reads=[f"ssk{b}", "epsc"], writes=[f"ssk{b}"])
                        S.op("dve", lambda e, b=b: e.reciprocal(out=ssk[:, b:b + 1], in_=ssk[:, b:b + 1]), reads=[f"ssk{b}"], writes=[f"ssk{b}"])
                        S.op("dve", lambda e, b=b: e.scalar_tensor_tensor(out=ckf[:, b, :], in0=ukv[:, b, 0:256], scalar=ssk[:, b:b + 1], in1=gkv[:],
                                                                         op0=ALU.mult, op1=ALU.mult),
                             reads=[f"ukv{b}", f"ssk{b}", "gkv"], writes=[f"ckf{b}"])
                        S.dma("sp", o_ckv.ap()[l, t * 128:(t + 1) * 128, :], ckf[:, b, :], reads=[f"ckf{b}"])
                        S.op("pool", lambda e, b=b: e.tensor_copy(out=ckb[:, b, :], in_=ckf[:, b, :]), reads=[f"ckf{b}"], writes=[f"ckb{b}"])
                        for k2 in range(2):
                            S.mm(lambda e, k2=k2, b=b: e.transpose(out=PSB[:, k2 * 128:(k2 + 1) * 128], in_=ckb[:, b, k2 * 128:(k2 + 1) * 128], identity=ident[:]),
                                 reads=[f"ckb{b}", "ident"], writes=["ps7"], first=(k2 == 0))
                        S.op("dve", lambda e, t=t: e.tensor_copy(out=ckvT[:, :, t * 128:(t + 1) * 128], in_=PSB[:, 0:256].rearrange("p (k n) -> p k n", n=128)),
                             reads=["ps7"], writes=["ckvT"])
                    for j in range(4):
                        for k2 in range(2):
                            S.mm(lambda e, k2=k2, j=j: e.transpose(out=PSB[:, k2 * 128:(k2 + 1) * 128], in_=cc[:, j, k2 * 128:(k2 + 1) * 128], identity=ident[:]),
                                 reads=["cc", "ident"], writes=["ps7"], first=(k2 == 0))
                        S.op("dve", lambda e, j=j: e.tensor_copy(out=ckvT[:, :, NT + j * 128:NT + (j + 1) * 128],
                                                                in_=PSB[:, 0:256].rearrange("p (k n) -> p k n", n=128)),
                             reads=["ps7"], writes=["ckvT"])
                    for j in range(4):
                        S.mm(lambda e, j=j: e.matmul(PS[3][64:96, j * 128:(j + 1) * 128], lhsT=ckr[:, j, :], rhs=ident[:], start=True, stop=True),
                             reads=["ckr", "ident"], writes=[PK[3]], first=(j == 0))
                    for hb_ in range(2):
                        S.op("dve", lambda e, hb_=hb_: e.tensor_copy(out=kall[64:96, hb_, NT:NK], in_=PS[3][64:96, :]), reads=[PK[3]], writes=["krT"])
                    for c in range(4):
                        b = c % 2
                        for k in range(KC):
                            S.mm(lambda e, k=k, c=c, b=b: e.matmul(PS[b][64:96, :], lhsT=wkr[:, k, 256:288], rhs=hT[:, k, c * 512:(c + 1) * 512],
                                                                  start=(k == 0), stop=(k == KC - 1)),
                                 reads=["wkr", "hTm"], writes=[PK[b]], first=(k == 0))
                        for k in range(KC):
                            S.mm(lambda e, k=k, c=c, b=b: e.matmul(PS[2 + b][64:96, :], lhsT=wkrr[:, k, :], rhs=hT[:, k, c * 512:(c + 1) * 512],
                                                                  start=(k == 0), stop=(k == KC - 1)),
                                 reads=["wkrr", "hTm"], writes=[PK[2 + b]], first=(k == 0))
                        rope2(PS[b][64:96, :], PK[b], PS[2 + b][64:96, :], PK[2 + b], b,
                              [kall[64:96, 0, c * 512:(c + 1) * 512], kall[64:96, 1, c * 512:(c + 1) * 512]], "krT", c)
                    for t in range(NKT):
                        b = t % 2
                        for k2 in range(2):
                            S.mm(lambda e, k2=k2, t=t, b=b: e.matmul(PS[b][:, :], lhsT=ckvT[:, k2, t * 128:(t + 1) * 128],
                                                                    rhs=wukv[:, k2, 1].rearrange("p h d -> p (h d)"),
                                                                    start=(k2 == 0), stop=(k2 == 1)),
                                 reads=["ckvT", "wukv"], writes=[PK[b]], first=(k2 == 0))
                        eng = evac_eng()
                        S.op(eng, copy_op(eng, Vm[:, t, :, 0:64], PS[b][:, :].rearrange("p (h d) -> p h d", d=64)), reads=[PK[b]], writes=["Vm"])
                S.barrier()
                mcut(4)
                s3 = s2
                pTm = s3.enter_context(_sbuf_tensor("pTm", [128, 4, 512], BF16))
                osb = s3.enter_context(_sbuf_tensor("osbm", [65, 2, 512], F32))
                otmp = s3.enter_context(_sbuf_tensor("otmpm", [64, 2, 512], BF16))

                def head_pieces(h):
                    hb_ = h % 2
                    pcs = []
                    for c5 in range(5):
                        def pk_(c5=c5):
                            for k2 in range(2):
                                S.mm(lambda e, k2=k2: e.matmul(PS[6][0:64, :], lhsT=wukv[:, k2, 0, h, :], rhs=ckvT[:, k2, c5 * 512:(c5 + 1) * 512],
                                                               start=(k2 == 0), stop=(k2 == 1)),
                                     reads=["wukv", "ckvT"], writes=[PK[6]], first=(k2 == 0))
                            S.op("dve", lambda e: e.tensor_copy(out=kall[0:64, hb_, c5 * 512:(c5 + 1) * 512], in_=PS[6][0:64, :]),
                                 reads=[PK[6]], writes=[f"knp{hb_}"])
                        pcs.append(pk_)
                    for c in range(4):
                        def pq_(c=c):
                            for j in range(3):
                                S.mm(lambda e, j=j: e.matmul(PS[6][0:64, :], lhsT=wuq[:, j, h * 96:h * 96 + 64], rhs=qn[:, j, c * 512:(c + 1) * 512],
                                                             start=(j == 0), stop=(j == 2)),
                                     reads=["wuq", "qn"], writes=[PK[6]], first=(j == 0))
                            for j in range(3):
                                S.mm(lambda e, j=j: e.matmul(PS[6][64:96, :], lhsT=wuq[:, j, h * 96 + 64:h * 96 + 96], rhs=qn[:, j, c * 512:(c + 1) * 512],
                                                             start=(j == 0), stop=(j == 2)),
                                     reads=["wuq", "qn"], writes=[PK[6]], first=False)
                            for j in range(3):
                                S.mm(lambda e, j=j: e.matmul(PS[7][64:96, :], lhsT=wuqr[:, j, h, :], rhs=qn[:, j, c * 512:(c + 1) * 512],
                                                             start=(j == 0), stop=(j == 2)),
                                     reads=["wuqr", "qn"], writes=[PK[7]], first=(j == 0))
                            S.op("dve", lambda e: e.tensor_copy(out=qall[0:64, hb_, c * 512:(c + 1) * 512], in_=PS[6][0:64, :]),
                                 reads=[PK[6]], writes=[f"qnp{hb_}"])
                            rope2(PS[6][64:96, :], PK[6], PS[7][64:96, :], PK[7], c % 2, [qall[64:96, hb_, c * 512:(c + 1) * 512]], f"qrT{hb_}", c)
                        pcs.append(pq_)
                    return pcs

                for pc in head_pieces(0):
                    pc()
                sit = 0
                import os
                mdbg = os.environ.get("KDBG_MLA", "")
                for h in range(8):
                    pr, hh = h // 2, h % 2
                    hb = h % 2
                    nxt = head_pieces(h + 1) if h + 1 < 8 else []
                    LA = 2
                    items = [(c, j) for c in range(4) for j in range(NKT)]
                    pend = []

                    def mla_scores(c, j, sidx, hb=hb):
                        bk = sidx % 4
                        S.mm(lambda e: e.matmul(PS[bk][:, :], lhsT=kall[:, hb, j * 128:(j + 1) * 128], rhs=qall[:, hb, c * 512:(c + 1) * 512],
                                                start=True, stop=True),
                             reads=[f"knp{hb}", f"qnp{hb}", "krT", "krTi", f"qrT{hb}", "qrTi"], writes=[PK[bk]], first=True)

                    def mla_exp_pv(c, j, sidx, h=h):
                        bk = sidx % 4
                        po = 4 + (c % 2)
                        S.op("act", lambda e: e.activation(out=pTm[:, bk, :], in_=PS[bk][:, :], func=AF.Exp, scale=MLA_SCALE),
                             reads=[PK[bk]], writes=[f"pTm{bk}"])
                        S.mm(lambda e: e.matmul(PS[po][0:65, :], lhsT=Vm[:, j, h, :], rhs=pTm[:, bk, :], start=(j == 0), stop=(j == NKT - 1)),
                             reads=["Vm", "Vm1", f"pTm{bk}"], writes=[PK[po]], first=(j == 0))

                    def mla_norm_a(c, h=h):
                        po = 4 + (c % 2)
                        ob = c % 2
                        S.op("act", copy_op("act", osb[:, ob, :], PS[po][0:65, :]), reads=[PK[po]], writes=[f"osb{ob}"])
                        S.op("dve", lambda e: e.reciprocal(out=osb[64:65, ob, :], in_=osb[64:65, ob, :]), reads=[f"osb{ob}"], writes=[f"osb{ob}"])

                    def mla_norm_b(c, h=h, pr=pr, hh=hh):
                        po = 4 + (c % 2)
                        ob = c % 2
                        S.mm(lambda e: e.matmul(PS[po][0:64, :], lhsT=onesf[64:65, 0:64], rhs=osb[64:65, ob, :], start=True, stop=True),
                             reads=["ones", f"osb{ob}"], writes=[PK[po]])
                        if hh == 0:
                            S.op("dve", lambda e: e.tensor_tensor(
                                out=omT[0:64, pr, c * 512:(c + 1) * 512], in0=PS[po][0:64, :], in1=osb[0:64, ob, :], op=ALU.mult),
                                reads=[PK[po], f"osb{ob}"], writes=["omT"])
                        else:
                            S.op("dve", lambda e: e.tensor_tensor(
                                out=otmp[:, ob, :], in0=PS[po][0:64, :], in1=osb[0:64, ob, :], op=ALU.mult),
                                reads=[PK[po], f"osb{ob}"], writes=[f"otmp{ob}"])
                            S.dma("sp", omT[64:128, pr, c * 512:(c + 1) * 512], otmp[:, ob, :], reads=[f"otmp{ob}"], writes=["omT"])

                    n_it = len(items)
                    for k in range(n_it + LA):
                        if k < n_it:
                            mla_scores(items[k][0], items[k][1], sit + k)
                        for pd in list(pend):
                            if k >= pd[0]:
                                mla_norm_b(pd[1])
                                pend.remove(pd)
                        if k >= LA:
                            c_, j_ = items[k - LA]
                            mla_exp_pv(c_, j_, sit + k - LA)
                            if j_ == NKT - 1:
                                mla_norm_a(c_)
                                pend.append((k + 3, c_))
                        if nxt and k % 6 == 3:
                            nxt.pop(0)()
                    for pd in pend:
                        mla_norm_b(pd[1])
                    for pc in nxt:
                        pc()
                    sit += n_it
            S.barrier()

            mcut(5)
            with ExitStack() as s2:
                mT = s2.enter_context(_sbuf_tensor("mT", [128, KC, NT], BF16))
                wg = s2.enter_context(_sbuf_tensor("wg", [128, 2, 3, KC, 128], BF16))
                wb_ = s2.enter_context(_sbuf_tensor("wbr", [128, 2, 3, 4, 128], BF16))
                bgT = s2.enter_context(_sbuf_tensor("bgT", [128, 24], F32))
                gsb = s2.enter_context(_sbuf_tensor("gsb", [128, 2, 512], BF16))
                acc = s2.enter_context(_sbuf_tensor("acc", [128, 2, 512], F32))
                tm = s2.enter_context(_sbuf_tensor("tm", [128, 2, 512], F32))
                wo = s2.enter_context(_sbuf_tensor("wo", [128, KC, D], BF16))
                tmp = s2.enter_context(_sbuf_tensor("tmpm", [128, 5, D], F32))
                xt2 = s2.enter_context(_sbuf_tensor("xt2m", [128, 2, D], F32))
                alloc_vec(s2, V)
                load_epi(l, 5, 1, 1.0, V)
                load_colvec(b_gate, l * 3 * D, 24, bgT[:], "bgT")
                S.dma("pool", wo[:], w_out.ap()[l].rearrange("(k p) n -> p k n", p=128), writes=["wo"])
                wgv = w_gate.ap()[l].rearrange("(k p) (g n) -> p g k n", p=128, g=3)
                brs = [w.ap()[l].rearrange("(k p) n -> p k n", p=128) for w in (w_bf, w_bm, w_bn)]
                bins = [(specT, "specT"), (omT, "omT"), (onT, "onT")]
                git = 0
                for oc in range(KC):
                    wbuf = oc % 2
                    S.dma("pool", wg[:, wbuf], wgv[:, :, :, oc * 128:(oc + 1) * 128], writes=[f"wg{wbuf}"])
                    for gi in range(3):
                        S.dma("pool", wb_[:, wbuf, gi], brs[gi][:, :, oc * 128:(oc + 1) * 128], writes=[f"wbr{wbuf}"])
                    for c in range(4):
                        ab = (oc * 4 + c) % 2
                        for gi in range(3):
                            pg = (git % 2)
                            py = 2 + (git % 2)
                            git += 1
                            for k in range(KC):
                                S.mm(lambda e, k=k, gi=gi, c=c, pg=pg, wbuf=wbuf: e.matmul(
                                    PS[pg][:, :], lhsT=wg[:, wbuf, gi, k, :], rhs=hT[:, k, c * 512:(c + 1) * 512], start=(k == 0), stop=(k == KC - 1)),
                                    reads=[f"wg{wbuf}", "hTm"], writes=[PK[pg]], first=(k == 0))
                            bsrc, bkey = bins[gi]
                            for k4 in range(4):
                                S.mm(lambda e, k4=k4, gi=gi, c=c, py=py, wbuf=wbuf, bsrc=bsrc: e.matmul(
                                    PS[py][:, :], lhsT=wb_[:, wbuf, gi, k4, :], rhs=bsrc[:, k4, c * 512:(c + 1) * 512], start=(k4 == 0), stop=(k4 == 3)),
                                    reads=[f"wbr{wbuf}", bkey], writes=[PK[py]], first=(k4 == 0))
                            gb = git % 2
                            S.op("act", lambda e, pg=pg, gi=gi, oc=oc, gb=gb: e.activation(
                                out=gsb[:, gb, :], in_=PS[pg][:, :], func=AF.Sigmoid, bias=bgT[:, gi * 8 + oc:gi * 8 + oc + 1], scale=1.0),
                                reads=[PK[pg], "bgT"], writes=[f"gsb{gb}"])
                            if gi == 0:
                                S.op("dve", lambda e, py=py, gb=gb, ab=ab: e.tensor_tensor(out=acc[:, ab, :], in0=PS[py][:, :], in1=gsb[:, gb, :], op=ALU.mult),
                                     reads=[PK[py], f"gsb{gb}"], writes=[f"acc{ab}"])
                            else:
                                S.op("dve", lambda e, py=py, gb=gb, ab=ab: e.tensor_tensor(out=tm[:, ab, :], in0=PS[py][:, :], in1=gsb[:, gb, :], op=ALU.mult),
                                     reads=[PK[py], f"gsb{gb}"], writes=[f"tm{ab}"])
                                if gi == 1:
                                    S.op("pool", lambda e, ab=ab: e.tensor_tensor(out=acc[:, ab, :], in0=acc[:, ab, :], in1=tm[:, ab, :], op=ALU.add),
                                         reads=[f"acc{ab}", f"tm{ab}"], writes=[f"acc{ab}"])
                                else:
                                    S.op("pool", lambda e, ab=ab, oc=oc, c=c: e.tensor_tensor(out=mT[:, oc, c * 512:(c + 1) * 512], in0=acc[:, ab, :], in1=tm[:, ab, :], op=ALU.add),
                                         reads=[f"acc{ab}", f"tm{ab}"], writes=["mT"])
                NB = 5

                def mg_load(t):
                    b = t % 2
                    S.dma("sp", xt2[:, b, :], xin[t * 128:(t + 1) * 128, :], reads=["xdram"], writes=[f"xt2{b}"])

                def mg_mm(t):
                    b = t % 2
                    zb = t % NB
                    for half in range(2):
                        pi = 4 + half
                        for k in range(KC):
                            S.mm(lambda e, k=k, half=half, pi=pi: e.matmul(
                                PS[pi][:, :], lhsT=mT[:, k, t * 128:(t + 1) * 128], rhs=wo[:, k, half * 512:(half + 1) * 512],
                                start=(k == 0), stop=(k == KC - 1)),
                                reads=["mT", "wo"], writes=[PK[pi]], first=(k == 0))
                        S.op("dve", lambda e, pi=pi, half=half: e.tensor_tensor(
                            out=tmp[:, zb, half * 512:(half + 1) * 512], in0=PS[pi][:, :], in1=V["gate_bc"][:, half * 512:(half + 1) * 512], op=ALU.mult),
                            reads=[PK[pi], "gate_bc"], writes=[f"tmpm{zb}"])
                    S.op("dve", lambda e: e.scalar_tensor_tensor(out=tmp[:, zb, :], in0=xt2[:, b, :], scalar=ALPHA, in1=tmp[:, zb, :],
                                                                 op0=ALU.mult, op1=ALU.add),
                         reads=[f"xt2{b}", f"tmpm{zb}"], writes=[f"tmpm{zb}"])

                lo = ln_out_stages(lambda t: tmp[:, t % NB, :], lambda t: f"tmpm{t % NB}", T, V,
                                   lambda t: [xo[t * 128:(t + 1) * 128, :] for xo in xouts])

                def mg_mm_stats(t):
                    mg_mm(t)
                    lo[0](t)

                run_staged(NTILE, [mg_load, mg_mm_stats] + lo[1:])
        S.barrier()

    prologue()
    bufs = [xa.ap(), xb.ap()]
    cur = x0.ap()
    stage = 0
    for l in range(DEPTH):
        for kind in ("ffn1", "mix", "ffn2"):
            if stage >= nstages:
                break
            last = (stage == nstages - 1)
            dst = y.ap() if last else bufs[stage % 2]
            if kind == "ffn1":
                ffn(l, 1, cur, [dst])
            elif kind == "mix":
                mixer(l, cur, [dst])
            else:
                ffn(l, 2, cur, [dst])
            cur = dst
            stage += 1
    for e in ("sp", "pool", "act", "dve", "pe"):
        S.wait_all_dma(e)
    import os
    if os.environ.get("KDBG_STATS"):
        print("SIGVALS", S.sigval, "POS", S.pos, "DMA", max(S.dcount), flush=True)
    return nc


def _bf16(a):
    return np.asarray(a, dtype=np.float32).astype(ml_dtypes.bfloat16)


def _role_consts(role):
    c = {}
    t = np.arange(NT)
    if role == "sample":
        pos = np.stack([t // 64, t % 64], -1).astype(np.float32)
        inv = (10000.0 ** (-np.arange(8, dtype=np.float32) / 8)).astype(np.float32)
        ang = pos[:, :, None] * inv
        ang = np.concatenate([ang, ang], -1)
        cos = np.cos(ang).reshape(NT, 32).T
        sin = np.sin(ang).reshape(NT, 32).T
        c["ropeC"] = np.ascontiguousarray(cos, dtype=np.float32)
        c["ropeS"] = np.ascontiguousarray(sin, dtype=np.float32)
        c["indq_m"] = np.zeros((8, NT), np.float32)
        c["indq_n"] = np.zeros((8, NT), np.float32)
        c["indk"] = np.zeros((8, NK), np.float32)
        L = NT
        blk = np.zeros(NT, np.int64)
        loc = t
    else:
        c["ropeC"] = np.ones((32, NT), np.float32)
        c["ropeS"] = np.zeros((32, NT), np.float32)
        oh = (t[None, :] // 256 == np.arange(8)[:, None]).astype(np.float32)
        c["indq_m"] = oh * BIG_MLA
        c["indq_n"] = oh * BIG_NA
        ik = np.zeros((8, NK), np.float32)
        ik[:, :NT] = oh
        c["indk"] = ik
        L = 256
        blk = t // 256
        loc = t % 256
    norm = 1.0 / math.sqrt(L * 128.0)
    same = (blk[:, None] == blk[None, :])
    ph = (2.0 * np.pi / L) * ((loc[:, None] * loc[None, :]) % L).astype(np.float64)
    c["dftC"] = _bf16(np.where(same, np.cos(ph) * norm, 0.0))
    c["dftS"] = _bf16(np.where(same, -np.sin(ph) * norm, 0.0))
    cc = np.arange(128)
    ph2 = (2.0 * np.pi / 128) * ((cc[:, None] * cc[None, :]) % 128).astype(np.float64)
    c["dftCS"] = np.concatenate([np.cos(ph2), np.sin(ph2)], 1).astype(np.float32)
    m1 = np.zeros((128, NPAT, 128), np.float32)
    m2 = np.zeros((128, NPAT, 128), np.float32)
    if role == "sample":
        kk = np.arange(128)
        kr, kc = kk // 64, kk % 64
        qr, qc = kk // 64, kk % 64
        cstart = np.clip(qc - 8, 0, 48)
        col_ok = (kc[:, None] >= cstart[None, :]) & (kc[:, None] < cstart[None, :] + 16)
        for p, (dl, typ) in enumerate(PAT_DELTA):
            rel = 2 * dl + kr[:, None] - qr[None, :]
            row_ok = ((rel >= -4) & (rel <= 3)) if typ == 0 else np.ones_like(rel, bool)
            ok = col_ok & row_ok
            m1[:, p, :] = np.where(ok, 1.0 / NA_SCALE, 0.0)
            m2[:, p, :] = np.where(ok, 0.0, NEG)
    c["m1d"] = m1
    c["m2d"] = m2
    pr = np.zeros((32, 32), np.float32)
    for a in range(2):
        for j in range(16):
            d = a * 16 + j
            if j < 8:
                pr[d, d + 8] = -1.0
            else:
                pr[d, d - 8] = 1.0
    c["protT"] = np.ascontiguousarray(pr.T)
    c["identd"] = np.eye(128, dtype=np.float32)
    return c


_CACHE = {}


def _get_nc(nstages):
    if nstages not in _CACHE:
        _CACHE[nstages] = build(nstages)
    return _CACHE[nstages]


def run_units(inputs, nstages=3 * DEPTH, cores=None):
    f32 = lambda a: np.ascontiguousarray(np.asarray(a), dtype=np.float32)
    xp = f32(inputs["x_prompt"])
    xs = f32(inputs["x_sample"])
    shared = {}
    for nm in ("w_ada", "b_ada", "ffn1_w1", "ffn1_w3", "ffn1_w2", "ffn2_w1", "ffn2_w3", "ffn2_w2", "w_in",
               "mla_q_norm", "mla_w_uq", "mla_kv_norm", "mla_w_ukv", "w_branch_f", "w_branch_m", "w_branch_n",
               "w_gate", "b_gate", "w_out", "ln_g", "ln_b"):
        shared[nm] = f32(inputs[nm])
    rp = f32(inputs["na_rpb"])[..., ::-1].reshape(-1)
    shared["rpbr"] = np.concatenate([np.zeros(RPAD, np.float32), rp, np.zeros(RPAD, np.float32)])
    cp = _role_consts("prompt")
    cs = _role_consts("sample")
    zc = {"c_ckv": np.zeros((DEPTH, NCTX, 256), np.float32), "c_kr": np.zeros((DEPTH, NCTX, 32), np.float32),
          "c_nk": np.zeros((DEPTH, NCTX, 512), np.float32), "c_nv": np.zeros((DEPTH, NCTX, 512), np.float32)}
    in_maps = []
    for core in range(8):
        m = dict(shared)
        if core < 4 or core >= 6:
            u = core if core < 4 else core - 6
            m["x0"] = xp[u * 8:(u + 1) * 8].reshape(NT, D)
            m["cvec"] = f32(inputs["c_ctx"]).reshape(1, D)
            m.update(zc)
            m.update(cp)
        else:
            b = core - 4
            m["x0"] = xs[b]
            m["cvec"] = f32(inputs["c"])[b].reshape(1, D)
            m["c_ckv"] = f32(inputs["cache_mla_ckv"])[b]
            m["c_kr"] = f32(inputs["cache_mla_krope"])[b]
            m["c_nk"] = f32(inputs["cache_na_k"])[b].reshape(DEPTH, NCTX, 512)
            m["c_nv"] = f32(inputs["cache_na_v"])[b].reshape(DEPTH, NCTX, 512)
            m.update(cs)
        in_maps.append(m)
    nc = _get_nc(nstages)
    if cores is not None:
        res = run_bass_kernel_spmd(nc, [in_maps[c] for c in cores], core_ids=list(range(len(cores))))
        return {c: res.results[i] for i, c in enumerate(cores)}
    res = run_bass_kernel_spmd(nc, in_maps, core_ids=list(range(8)))
    return res.results


def kernel(**inputs):
    r = run_units(inputs)
    yp = np.concatenate([r[u]["y"].reshape(8, 256, D) for u in range(4)], 0)
    ys = np.stack([r[4]["y"], r[5]["y"]], 0)

    def gather(name, tail):
        a = np.concatenate([r[u][name].reshape(DEPTH, 8, 256, -1).transpose(1, 0, 2, 3) for u in range(4)], 0)
        return np.ascontiguousarray(a.reshape((32, DEPTH, 256) + tail), dtype=np.float32)

    return (np.ascontiguousarray(yp, dtype=np.float32), np.ascontiguousarray(ys, dtype=np.float32),
            gather("o_ckv", (256,)), gather("o_kr", (32,)), gather("o_nk", (8, 64)), gather("o_nv", (8, 64)))
```

```python
import math
from collections import defaultdict

import numpy as np
import ml_dtypes

import concourse.bass as bass
import concourse.mybir as mybir
from concourse.bass_utils import run_bass_kernel_spmd

F32 = mybir.dt.float32
BF16 = mybir.dt.bfloat16
AF = mybir.ActivationFunctionType
ALU = mybir.AluOpType

D = 1024
KC = 8
DEPTH = 4
NT = 2048
NTILE = 16
NCTX = 512
NK = NT + NCTX
NKT = NK // 128
FF = 2816
FC = FF // 128
IN_W = 2720
C_F, C_Q, C_KV, C_R, C_NQ, C_NK, C_NV = 0, 512, 896, 1152, 1184, 1696, 2208
ALPHA = (2.0 * DEPTH) ** 0.25
MLA_SCALE = 96 ** -0.5
NA_SCALE = 0.125
BIG_MLA = 576.0
BIG_NA = 480.0
NEG = -30000.0
NPAT = 12
RPAD = 64


class _PEProxy:
    def __init__(self, pe):
        self.pe = pe
        self.last_stop = None

    def matmul(self, *a, **kw):
        self.last_stop = kw.get("stop", None)
        return self.pe.matmul(*a, **kw)

    def transpose(self, *a, **kw):
        self.last_stop = True
        return self.pe.transpose(*a, **kw)


class Sched:
    ENGS = ("pe", "act", "dve", "pool", "sp")

    def __init__(self, nc, n_dma_sems=56):
        self.nc = nc
        self.eng = {"pe": nc.tensor, "act": nc.scalar, "dve": nc.vector, "pool": nc.gpsimd, "sp": nc.sync}
        self.esem = {e: nc.alloc_semaphore(f"es_{e}") for e in self.ENGS}
        self.pos = {e: 0 for e in self.ENGS}
        self.sigs = {e: [] for e in self.ENGS}
        self.sigval = {e: 0 for e in self.ENGS}
        self.last = {e: None for e in self.ENGS}
        self.waited = defaultdict(int)
        self.dsems = [nc.alloc_semaphore(f"ds_{i}") for i in range(n_dma_sems)]
        self.dcount = [0] * n_dma_sems
        self.dnext = 0
        self.dnext_pool = 0
        self.W = defaultdict(dict)
        self.R = defaultdict(dict)
        self.peproxy = _PEProxy(nc.tensor)

    def _need(self, eng, tok, raw):
        if tok[0] == "d":
            _, idx, val = tok
            return (("d", idx), self.dsems[idx], val)
        _, f, p = tok
        if f == eng and eng == "pe":
            return None
        val = None
        for (sp_, sv) in reversed(self.sigs[f]):
            if sp_ >= p:
                val = sv
            else:
                break
        if val is None:
            ins, lp = self.last[f]
            assert lp >= p
            self.sigval[f] += 1
            ins.then_inc(self.esem[f], 1)
            self.sigs[f].append((lp, self.sigval[f]))
            val = self.sigval[f]
        return (("e", f), self.esem[f], val)

    def _waits(self, eng, reads, writes):
        needs = {}

        def add(tok, raw):
            n = self._need(eng, tok, raw)
            if n:
                key, sem, val = n
                if key not in needs or needs[key][0] < val:
                    needs[key] = (val, sem)

        for k in reads:
            for t in self.W[k].values():
                add(t, True)
            if k.startswith("ps"):
                for rk, r in self.R[k].items():
                    if rk != eng:
                        add(r, False)
        for k in writes:
            for r in self.R[k].values():
                add(r, False)
        for key, (val, sem) in needs.items():
            if self.waited[(eng, key)] < val:
                self.eng[eng].wait_ge(sem, val)
                self.waited[(eng, key)] = val

    def _record(self, tok, reads, writes, rkey):
        for k in writes:
            self.W[k][rkey] = tok
        for k in reads:
            self.R[k][rkey] = tok

    def op(self, eng, fn, reads=(), writes=(), check_writes=True):
        self._waits(eng, reads, writes if check_writes else ())
        if eng == "pe":
            self.peproxy.last_stop = None
            ins = fn(self.peproxy)
            sig = bool(self.peproxy.last_stop)
        else:
            ins = fn(self.eng[eng])
            sig = True
        self.pos[eng] += 1
        p = self.pos[eng]
        self.last[eng] = (ins, p)
        if sig:
            self.sigval[eng] += 1
            ins.then_inc(self.esem[eng], 1)
            self.sigs[eng].append((p, self.sigval[eng]))
        self._record(("e", eng, p), reads, writes, eng)
        return ins

    def mm(self, fn, reads=(), writes=(), first=True):
        return self.op("pe", fn, reads, writes, check_writes=first)

    def dma(self, q, out, in_, reads=(), writes=(), **kw):
        half = len(self.dsems) // 2
        if q == "pool":
            idx = self.dnext_pool
            self.dnext_pool = (self.dnext_pool + 1) % half
        else:
            idx = half + self.dnext
            self.dnext = (self.dnext + 1) % (len(self.dsems) - half)
        if self.dcount[idx] and self.waited[(q, ("d", idx))] < self.dcount[idx]:
            self.eng[q].wait_ge(self.dsems[idx], self.dcount[idx])
            self.waited[(q, ("d", idx))] = self.dcount[idx]
        self._waits(q, reads, writes)
        self.eng[q].dma_start(out=out, in_=in_, **kw).then_inc(self.dsems[idx], 16)
        self.dcount[idx] += 16
        tok = ("d", idx, self.dcount[idx])
        self._record(tok, reads, writes, ("d", idx))
        return tok

    def wait_all_dma(self, eng="sp"):
        for idx, c in enumerate(self.dcount):
            if c and self.waited[(eng, ("d", idx))] < c:
                self.eng[eng].wait_ge(self.dsems[idx], c)
                self.waited[(eng, ("d", idx))] = c

    def barrier(self):
        toks = []
        for f in self.ENGS:
            if self.last[f] is not None:
                toks.append(("e", f, self.last[f][1]))
        for e in self.ENGS:
            needs = {}
            for t in toks:
                n = self._need(e, t, True)
                if n:
                    key, sem, val = n
                    if key not in needs or needs[key][0] < val:
                        needs[key] = (val, sem)
            for key, (val, sem) in needs.items():
                if self.waited[(e, key)] < val:
                    self.eng[e].wait_ge(sem, val)
                    self.waited[(e, key)] = val
            self.wait_all_dma(e)
        self.W = defaultdict(dict)
        self.R = defaultdict(dict)


def na_window(i):
    if i <= 1:
        tiles, typ = [0, 1, 2, 3], 1
    elif i >= 14:
        tiles, typ = [12, 13, 14, 15], 1
    else:
        tiles, typ = [i - 2, i - 1, i, i + 1, i + 2], 0
    out = []
    for j in tiles:
        dl = j - i
        pat = (dl + 2) if typ == 0 else (5 + dl + 3)
        out.append((j, pat))
    return out


PAT_DELTA = [(-2, 0), (-1, 0), (0, 0), (1, 0), (2, 0)] + [(d, 1) for d in range(-3, 4)]


def build(nstages=3 * DEPTH):
    nc = bass.Bass("TRN2", target_bir_lowering=False)
    S = Sched(nc)

    def din(name, shape, dt=F32):
        return nc.dram_tensor(name, list(shape), dt, kind="ExternalInput")

    x0 = din("x0", [NT, D])
    cvec = din("cvec", [1, D])
    c_ckv = din("c_ckv", [DEPTH, NCTX, 256])
    c_kr = din("c_kr", [DEPTH, NCTX, 32])
    c_nk = din("c_nk", [DEPTH, NCTX, 512])
    c_nv = din("c_nv", [DEPTH, NCTX, 512])
    w_ada = din("w_ada", [DEPTH, D, 9 * D])
    b_ada = din("b_ada", [DEPTH, 9 * D])
    fw = {}
    for nm in ("ffn1_w1", "ffn1_w3", "ffn2_w1", "ffn2_w3"):
        fw[nm] = din(nm, [DEPTH, D, FF])
    for nm in ("ffn1_w2", "ffn2_w2"):
        fw[nm] = din(nm, [DEPTH, FF, D])
    w_in = din("w_in", [DEPTH, D, IN_W])
    q_norm = din("mla_q_norm", [DEPTH, 384])
    w_uq = din("mla_w_uq", [DEPTH, 384, 768])
    kv_norm = din("mla_kv_norm", [DEPTH, 256])
    w_ukv = din("mla_w_ukv", [DEPTH, 256, 1024])
    rpbr = din("rpbr", [2 * RPAD + DEPTH * 8 * 15 * 31])
    w_bf = din("w_branch_f", [DEPTH, 512, D])
    w_bm = din("w_branch_m", [DEPTH, 512, D])
    w_bn = din("w_branch_n", [DEPTH, 512, D])
    w_gate = din("w_gate", [DEPTH, D, 3 * D])
    b_gate = din("b_gate", [DEPTH, 3 * D])
    w_out = din("w_out", [DEPTH, D, D])
    ln_g = din("ln_g", [DEPTH, 3, D])
    ln_b = din("ln_b", [DEPTH, 3, D])
    ropeC = din("ropeC", [32, NT])
    ropeS = din("ropeS", [32, NT])
    protT = din("protT", [32, 32])
    indq_m = din("indq_m", [8, NT])
    indq_n = din("indq_n", [8, NT])
    indk = din("indk", [8, NK])
    dftC = din("dftC", [NT, NT], BF16)
    dftS = din("dftS", [NT, NT], BF16)
    dftCS = din("dftCS", [128, 256])
    m1d = din("m1d", [128, NPAT, 128])
    m2d = din("m2d", [128, NPAT, 128])
    identd = din("identd", [128, 128])

    def dout(name, shape):
        return nc.dram_tensor(name, list(shape), F32, kind="ExternalOutput")

    y = dout("y", [NT, D])
    o_ckv = dout("o_ckv", [DEPTH, NT, 256])
    o_kr = dout("o_kr", [DEPTH, NT, 32])
    o_nk = dout("o_nk", [DEPTH, NT, 512])
    o_nv = dout("o_nv", [DEPTH, NT, 512])

    xa = nc.dram_tensor("xa", [NT, D], F32, kind="Internal")
    xb = nc.dram_tensor("xb", [NT, D], F32, kind="Internal")
    ada_d = nc.dram_tensor("ada_d", [DEPTH, 9 * D], F32, kind="Internal")
    btd = nc.dram_tensor("btd", [DEPTH, 8, NPAT, 128, 128], F32, kind="Internal")

    def AP(t, off, dims):
        return bass.AP(t, off, [list(d) for d in dims])

    _uid = [0]
    _orig_sbuf_tensor = nc.sbuf_tensor

    def _sbuf_tensor(name, shape, dt):
        _uid[0] += 1
        return _orig_sbuf_tensor(f"{name}_{_uid[0]}", shape, dt)

    sb = nc.alloc_sbuf_tensor
    ident = sb("ident", [128, 128], BF16)
    ones = sb("ones", [128, 128], BF16)
    epsc = sb("epsc", [128, 4], F32)
    identf = sb("identf", [128, 128], F32)
    onesf = sb("onesf", [128, 64], F32)
    PS = [nc.alloc_psum_tensor(f"ps{i}", [128, 512], F32) for i in range(8)]
    PSB = PS[7].bitcast(BF16)
    PK = [f"ps{i}" for i in range(8)]

    S.dma("sp", identf[:], identd.ap(), writes=["identf"])
    S.op("dve", lambda e: e.tensor_copy(out=ident[:], in_=identf[:]), reads=["identf"], writes=["ident"])
    S.op("dve", lambda e: e.memset(ones[:], 1.0), writes=["ones"])
    S.op("dve", lambda e: e.memset(onesf[:], 1.0), writes=["ones"])
    S.op("dve", lambda e: e.memset(epsc[:, 0:1], 1e-6), writes=["epsc"])
    S.op("dve", lambda e: e.memset(epsc[:, 1:2], 1e-5), writes=["epsc"])

    evac_rr = [0]

    def evac_eng():
        evac_rr[0] += 1
        return "act" if evac_rr[0] % 2 else "dve"

    def copy_op(eng, out, in_):
        if eng == "act":
            return lambda e: e.activation(out=out, in_=in_, func=AF.Copy)
        return lambda e: e.tensor_copy(out=out, in_=in_)

    def prologue():
        with _sbuf_tensor("crow", [8, 128], F32) as crow, \
                _sbuf_tensor("srow", [8, 128], BF16) as srow, \
                _sbuf_tensor("scT", [128, 8], BF16) as scT, \
                _sbuf_tensor("wada", [128, 2, 8, 512], BF16) as wada, \
                _sbuf_tensor("brow", [1, 2, 512], F32) as brow, \
                _sbuf_tensor("orow", [1, 2, 512], F32) as orow:
            S.dma("sp", crow[:], cvec.ap().rearrange("o (k p) -> (o k) p", p=128), writes=["crow"])
            S.op("act", lambda e: e.activation(out=srow[:], in_=crow[:], func=AF.Silu), reads=["crow"], writes=["srow"])
            S.mm(lambda e: e.matmul(PS[0][:, 0:8], lhsT=srow[:], rhs=ident[0:8, 0:8], start=True, stop=True),
                 reads=["srow", "ident"], writes=[PK[0]])
            S.op("dve", lambda e: e.tensor_copy(out=scT[:], in_=PS[0][:, 0:8]), reads=[PK[0]], writes=["scT"])
            it = 0
            for l in range(DEPTH):
                wv = w_ada.ap()[l].rearrange("(k p) n -> p k n", p=128)
                for j in range(18):
                    b = it % 2
                    it += 1
                    S.dma("pool", wada[:, b], wv[:, :, j * 512:(j + 1) * 512], writes=[f"wada{b}"])
                    S.dma("sp", brow[:, b], b_ada.ap()[l:l + 1, j * 512:(j + 1) * 512], writes=[f"brow{b}"])
                    pk = PK[b]
                    for k in range(KC):
                        S.mm(lambda e, k=k, b=b: e.matmul(PS[b][0:1, :], lhsT=scT[:, k:k + 1], rhs=wada[:, b, k, :],
                                                          start=(k == 0), stop=(k == KC - 1)),
                             reads=["scT", f"wada{b}"], writes=[pk], first=(k == 0))
                    S.op("dve", lambda e, b=b: e.tensor_tensor(out=orow[:, b], in0=PS[b][0:1, :], in1=brow[:, b], op=ALU.add),
                         reads=[pk, f"brow{b}"], writes=[f"orow{b}"])
                    S.dma("sp", ada_d.ap()[l:l + 1, j * 512:(j + 1) * 512], orow[:, b], reads=[f"orow{b}"], writes=["ada_d"])
        for l in range(DEPTH):
            for p, (dl, typ) in enumerate(PAT_DELTA):
                for kr in range(2):
                    for qr in range(2):
                        dr = 2 * dl + kr - qr + 7
                        drc = min(max(dr, 0), 14)
                        src = AP(rpbr, RPAD + ((l * 8) * 15 + drc) * 31 + 15, [[465, 8], [-1, 64], [1, 64]])
                        dst = AP(btd, (l * 8 * NPAT + p) * 16384 + kr * 64 * 128 + qr * 64,
                                 [[NPAT * 16384, 8], [128, 64], [1, 64]])
                        S.dma("sp", dst, src, writes=["btd"])
        S.barrier()

    def load_colvec(src_t, off, n, dst, dkey):
        with _sbuf_tensor("cvrow", [32, 128], F32) as row:
            S.dma("sp", row[0:n, :], AP(src_t, off, [[128, n], [1, 128]]), writes=["cvrow"])
            S.mm(lambda e: e.transpose(out=PS[6][:, 0:n], in_=row[0:n, :], identity=identf[0:n, 0:n]),
                 reads=["cvrow", "identf"], writes=[PK[6]])
            S.op("dve", lambda e: e.tensor_copy(out=dst, in_=PS[6][:, 0:n]), reads=[PK[6]], writes=[dkey])
            S.barrier()

    def load_mod(l, shift_idx, scale_idx, V):
        load_colvec(ada_d, l * 9 * D + shift_idx * D, 8, V["shT"][:], "shT")
        load_colvec(ada_d, l * 9 * D + scale_idx * D, 8, V["scT"][:], "scT")
        S.op("dve", lambda e: e.tensor_scalar(out=V["scT"][:], in0=V["scT"][:], scalar1=1.0, scalar2=None, op0=ALU.add),
             reads=["scT"], writes=["scT"])

    def load_epi(l, gate_idx, ln_idx, gate_coef, V):
        S.dma("sp", V["gate_bc"][:], AP(ada_d, l * 9 * D + gate_idx * D, [[0, 128], [1, D]]), reads=["ada_d"], writes=["gate_bc"])
        S.dma("sp", V["lng_bc"][:], AP(ln_g, (l * 3 + ln_idx) * D, [[0, 128], [1, D]]), writes=["lng_bc"])
        S.dma("sp", V["lnb_bc"][:], AP(ln_b, (l * 3 + ln_idx) * D, [[0, 128], [1, D]]), writes=["lnb_bc"])
        if gate_coef != 1.0:
            S.op("pool", lambda e: e.tensor_scalar(out=V["gate_bc"][:], in0=V["gate_bc"][:], scalar1=gate_coef, scalar2=None, op0=ALU.mult),
                 reads=["gate_bc"], writes=["gate_bc"])

    NTAG = 5

    def run_staged(n, stages):
        k = len(stages)
        for step in range(n + k - 1):
            for s_ in reversed(range(k)):
                t = step - s_
                if 0 <= t < n:
                    stages[s_](t)

    def ln_stage_fns(x_of, xkey_of, T, eps_col):
        st, mv, rs, nb = T["st"], T["mv"], T["rstd"], T["nb"]

        def A(t):
            g = t % NTAG
            xt, xkey = x_of(t), xkey_of(t)
            S.op("dve", lambda e: e.bn_stats(out=st[:, g, 0:6], in_=xt[:, 0:512]), reads=[xkey], writes=[f"lnsa{g}"])
            S.op("dve", lambda e: e.bn_stats(out=st[:, g, 6:12], in_=xt[:, 512:1024]), reads=[xkey], writes=[f"lnsb{g}"])
            S.op("dve", lambda e: e.bn_aggr(out=mv[:, g, :], in_=st[:, g, :]), reads=[f"lnsa{g}", f"lnsb{g}"], writes=[f"lnmv{g}"])

        def B(t):
            g = t % NTAG
            S.op("act", lambda e: e.activation(out=rs[:, g:g + 1], in_=mv[:, g, 1:2], func=AF.Sqrt, bias=epsc[:, eps_col:eps_col + 1], scale=1.0),
                 reads=[f"lnmv{g}", "epsc"], writes=[f"lnrs{g}"])

        def C(t):
            g = t % NTAG
            S.op("dve", lambda e: e.reciprocal(out=rs[:, g:g + 1], in_=rs[:, g:g + 1]), reads=[f"lnrs{g}"], writes=[f"lnrs{g}"])
            S.op("dve", lambda e: e.scalar_tensor_tensor(out=nb[:, g:g + 1], in0=mv[:, g, 0:1], scalar=-1.0, in1=rs[:, g:g + 1],
                                                         op0=ALU.mult, op1=ALU.mult),
                 reads=[f"lnmv{g}", f"lnrs{g}"], writes=[f"lnnb{g}"])

        return [A, B, C]

    def ln_in_stages(x_of, xkey_of, T, V, hT, hkey, tcol_of):
        xn = T["xn"]

        def D(t):
            g = t % NTAG
            S.op("act", lambda e: e.activation(out=xn[:, g, :], in_=x_of(t), func=AF.Identity, scale=T["rstd"][:, g:g + 1], bias=T["nb"][:, g:g + 1]),
                 reads=[xkey_of(t), f"lnrs{g}", f"lnnb{g}"], writes=[f"xn{g}"])

        def E(t):
            g = t % NTAG
            tcol = tcol_of(t)
            for kk in range(KC):
                S.mm(lambda e, kk=kk: e.transpose(out=PSB[:, kk * 128:(kk + 1) * 128], in_=xn[:, g, kk * 128:(kk + 1) * 128], identity=ident[:]),
                     reads=[f"xn{g}", "ident"], writes=["ps7"], first=(kk == 0))
            evac_rr[0] += 1
            teng = "dve" if evac_rr[0] % 2 == 0 else "act"
            for kk in range(KC):
                if teng == "dve":
                    fn = lambda e, kk=kk: e.tensor_scalar(out=hT[:, kk, tcol:tcol + 128], in0=PSB[:, kk * 128:(kk + 1) * 128],
                                                          scalar1=V["scT"][:, kk:kk + 1], scalar2=V["shT"][:, kk:kk + 1], op0=ALU.mult, op1=ALU.add)
                else:
                    fn = lambda e, kk=kk: e.activation(out=hT[:, kk, tcol:tcol + 128], in_=PSB[:, kk * 128:(kk + 1) * 128], func=AF.Identity,
                                                       scale=V["scT"][:, kk:kk + 1], bias=V["shT"][:, kk:kk + 1])
                S.op(teng, fn, reads=["ps7", "scT", "shT"], writes=[hkey])

        return ln_stage_fns(x_of, xkey_of, T, 0) + [D, E]

    def ln_out_stages(z_of, zkey_of, T, V, dst_of):
        def D(t):
            g = t % NTAG
            z, zk = z_of(t), zkey_of(t)
            S.op("act", lambda e: e.activation(out=z, in_=z, func=AF.Identity, scale=T["rstd"][:, g:g + 1], bias=T["nb"][:, g:g + 1]),
                 reads=[zk, f"lnrs{g}", f"lnnb{g}"], writes=[zk])

        def E(t):
            z, zk = z_of(t), zkey_of(t)
            S.op("dve", lambda e: e.tensor_tensor(out=z, in0=z, in1=V["lng_bc"][:], op=ALU.mult), reads=[zk, "lng_bc"], writes=[zk])
            S.op("pool", lambda e: e.tensor_tensor(out=z, in0=z, in1=V["lnb_bc"][:], op=ALU.add), reads=[zk, "lnb_bc"], writes=[zk])
            for d_ in dst_of(t):
                S.dma("sp", d_, z, reads=[zk], writes=["xdram"])

        return ln_stage_fns(z_of, zkey_of, T, 1) + [D, E]

    def alloc_ln(stack):
        V = {}
        V["shT"] = stack.enter_context(_sbuf_tensor("shT", [128, 8], F32))
        V["scT"] = stack.enter_context(_sbuf_tensor("scT1", [128, 8], F32))
        T = {}
        T["st"] = stack.enter_context(_sbuf_tensor("st", [128, NTAG, 12], F32))
        T["mv"] = stack.enter_context(_sbuf_tensor("mv", [128, NTAG, 2], F32))
        T["rstd"] = stack.enter_context(_sbuf_tensor("rstd", [128, NTAG], F32))
        T["nb"] = stack.enter_context(_sbuf_tensor("nb", [128, NTAG], F32))
        return V, T

    def alloc_vec(stack, V):
        for nm in ("gate_bc", "lng_bc", "lnb_bc"):
            V[nm] = stack.enter_context(_sbuf_tensor(nm, [128, D], F32))

    from contextlib import ExitStack

    def ffn(l, which, xin, xouts):
        w1 = fw[f"ffn{which}_w1"].ap()[l].rearrange("(k p) n -> p k n", p=128)
        w3 = fw[f"ffn{which}_w3"].ap()[l].rearrange("(k p) n -> p k n", p=128)
        w2 = fw[f"ffn{which}_w2"].ap()[l].rearrange("(f p) n -> p f n", p=128)
        base = 0 if which == 1 else 6
        with ExitStack() as st:
            V, T = alloc_ln(st)
            T["xn"] = st.enter_context(_sbuf_tensor("xn", [128, NTAG, D], BF16))
            alloc_vec(st, V)
            xp = st.enter_context(_sbuf_tensor("xp", [128, 8, D], F32))
            hT = st.enter_context(_sbuf_tensor("hTf", [128, KC, 1024], BF16))
            gT = st.enter_context(_sbuf_tensor("gT", [128, FC, 1024], BF16))
            w13 = st.enter_context(_sbuf_tensor("w13", [128, 2, 2, KC, 256], BF16))
            w2b = st.enter_context(_sbuf_tensor("w2b", [128, 2, FC, 256], BF16))
            sg = st.enter_context(_sbuf_tensor("sg", [128, 2, 512], BF16))
            tmp = st.enter_context(_sbuf_tensor("tmpf", [128, 2, 256], F32))
            import os
            CUT = int(os.environ.get("KDBG_CUT", "99"))
            load_mod(l, base + 0, base + 1, V)
            load_epi(l, base + 2, 0 if which == 1 else 2, 0.5, V)
            if CUT <= 0:
                S.barrier()
                return
            wit = 0
            w2it = 0
            for p in range(2):
                for t in range(8):
                    S.dma("sp", xp[:, t, :], xin[(p * 8 + t) * 128:(p * 8 + t + 1) * 128, :], reads=["xdram"], writes=[f"xp{t}"])
                if CUT <= 1:
                    S.barrier()
                    return
                run_staged(8, ln_in_stages(lambda t: xp[:, t, :], lambda t: f"xp{t}", T, V, hT, "hTf", lambda t: t * 128))
                if CUT <= 2:
                    S.barrier()
                    return
                for f2 in range(FC // 2):
                    b = wit % 2
                    wit += 1
                    S.dma("pool", w13[:, b, 0], w1[:, :, f2 * 256:(f2 + 1) * 256], writes=[f"w1b{b}"])
                    S.dma("pool", w13[:, b, 1], w3[:, :, f2 * 256:(f2 + 1) * 256], writes=[f"w3b{b}"])
                    for fi in range(2):
                        f = f2 * 2 + fi
                        for half in range(2):
                            pa, pb_ = (0, 1) if half == 0 else (2, 3)
                            for k in range(KC):
                                S.mm(lambda e, k=k, b=b, fi=fi, half=half, pa=pa: e.matmul(
                                    PS[pa][:, :], lhsT=w13[:, b, 0, k, fi * 128:(fi + 1) * 128], rhs=hT[:, k, half * 512:(half + 1) * 512],
                                    start=(k == 0), stop=(k == KC - 1)),
                                    reads=[f"w1b{b}", "hTf"], writes=[PK[pa]], first=(k == 0))
                            for k in range(KC):
                                S.mm(lambda e, k=k, b=b, fi=fi, half=half, pb_=pb_: e.matmul(
                                    PS[pb_][:, :], lhsT=w13[:, b, 1, k, fi * 128:(fi + 1) * 128], rhs=hT[:, k, half * 512:(half + 1) * 512],
                                    start=(k == 0), stop=(k == KC - 1)),
                                    reads=[f"w3b{b}", "hTf"], writes=[PK[pb_]], first=(k == 0))
                            S.op("act", lambda e, half=half, pa=pa: e.activation(out=sg[:, half, :], in_=PS[pa][:, :], func=AF.Silu),
                                 reads=[PK[pa]], writes=[f"sg{half}"])
                            S.op("dve", lambda e, half=half, pb_=pb_, f=f: e.tensor_tensor(
                                out=gT[:, f, half * 512:(half + 1) * 512], in0=PS[pb_][:, :], in1=sg[:, half, :], op=ALU.mult),
                                reads=[PK[pb_], f"sg{half}"], writes=["gT"])
                if CUT <= 3:
                    S.barrier()
                    return
                for oq in range(4):
                    b = w2it % 2
                    w2it += 1
                    S.dma("pool", w2b[:, b], w2[:, :, oq * 256:(oq + 1) * 256], writes=[f"w2b{b}"])
                    for t in range(8):
                        pi = 4 + (t % 2)
                        for f in range(FC):
                            S.mm(lambda e, f=f, t=t, b=b, pi=pi: e.matmul(
                                PS[pi][:, 0:256], lhsT=gT[:, f, t * 128:(t + 1) * 128], rhs=w2b[:, b, f, :],
                                start=(f == 0), stop=(f == FC - 1)),
                                reads=["gT", f"w2b{b}"], writes=[PK[pi]], first=(f == 0))
                        tb = t % 2
                        S.op("dve", lambda e, pi=pi, tb=tb, oq=oq: e.tensor_tensor(
                            out=tmp[:, tb, :], in0=PS[pi][:, 0:256], in1=V["gate_bc"][:, oq * 256:(oq + 1) * 256], op=ALU.mult),
                            reads=[PK[pi], "gate_bc"], writes=[f"tmpf{tb}"])
                        S.op("dve", lambda e, t=t, tb=tb, oq=oq: e.scalar_tensor_tensor(
                            out=xp[:, t, oq * 256:(oq + 1) * 256], in0=xp[:, t, oq * 256:(oq + 1) * 256], scalar=ALPHA,
                            in1=tmp[:, tb, :], op0=ALU.mult, op1=ALU.add),
                            reads=[f"tmpf{tb}", f"xp{t}"], writes=[f"xp{t}"])
                if CUT <= 4:
                    S.barrier()
                    return
                run_staged(8, ln_out_stages(lambda t: xp[:, t, :], lambda t: f"xp{t}", T, V,
                                            lambda t, p=p: [xo[(p * 8 + t) * 128:(p * 8 + t + 1) * 128, :] for xo in xouts]))
        S.barrier()

    class _Cut(Exception):
        pass

    def mcut(n):
        import os
        if int(os.environ.get("KDBG_MCUT", "99")) <= n:
            raise _Cut()

    def mixer(l, xin, xouts):
        try:
            mixer_(l, xin, xouts)
        except _Cut:
            pass
        S.barrier()

    def mixer_(l, xin, xouts):
        win = w_in.ap()[l].rearrange("(k p) n -> p k n", p=128)
        with ExitStack() as st:
            V, T = alloc_ln(st)
            hT = st.enter_context(_sbuf_tensor("hTm", [128, KC, NT], BF16))
            specT = st.enter_context(_sbuf_tensor("specT", [128, 4, NT], BF16))
            omT = st.enter_context(_sbuf_tensor("omT", [128, 4, NT], BF16))
            onT = st.enter_context(_sbuf_tensor("onT", [128, 4, NT], BF16))
            load_mod(l, 3, 4, V)
            with _sbuf_tensor("xt2", [128, 6, D], F32) as xt2, _sbuf_tensor("xn", [128, NTAG, D], BF16) as xn_:
                T["xn"] = xn_

                def m0_load(t):
                    S.dma("sp", xt2[:, t % 6, :], xin[t * 128:(t + 1) * 128, :], reads=["xdram"], writes=[f"xt2{t % 6}"])
                run_staged(NTILE, [m0_load] + ln_in_stages(lambda t: xt2[:, t % 6, :], lambda t: f"xt2{t % 6}", T, V, hT, "hTm",
                                                           lambda t: t * 128))
            S.barrier()
            mcut(0)

            with ExitStack() as s2:
                wf = s2.enter_context(_sbuf_tensor("wf", [128, KC, 512], BF16))
                cs = s2.enter_context(_sbuf_tensor("cs", [128, 256], BF16))
                AB = s2.enter_context(_sbuf_tensor("AB", [128, NTILE, 4, 256], BF16))
                ufT = s2.enter_context(_sbuf_tensor("ufT", [128, 2, 512], BF16))
                dbuf = s2.enter_context(_sbuf_tensor("dbuf", [128, 2, 2, 8, 512], BF16))
                S.dma("pool", wf[:], win[:, :, C_F:C_F + 512], writes=["wf"])
                S.dma("pool", cs[:], dftCS.ap(), writes=["cs"])
                it = 0
                for c in range(4):
                    for g in range(4):
                        b = it % 2
                        it += 1
                        for k in range(KC):
                            S.mm(lambda e, k=k, g=g, c=c, b=b: e.matmul(PS[b][:, :], lhsT=wf[:, k, g * 128:(g + 1) * 128],
                                                                   rhs=hT[:, k, c * 512:(c + 1) * 512], start=(k == 0), stop=(k == KC - 1)),
                                 reads=["wf", "hTm"], writes=[PK[b]], first=(k == 0))
                        S.op("act", copy_op("act", ufT[:, b, :], PS[b][:, :]), reads=[PK[b]], writes=[f"ufT{b}"])
                        for tt in range(4):
                            t = c * 4 + tt
                            S.mm(lambda e, tt=tt, b=b: e.matmul(PS[2][:, tt * 256:(tt + 1) * 256] if tt < 2 else PS[3][:, (tt - 2) * 256:(tt - 1) * 256],
                                                           lhsT=ufT[:, b, tt * 128:(tt + 1) * 128], rhs=cs[:], start=True, stop=True),
                                 reads=[f"ufT{b}", "cs"], writes=[PK[2] if tt < 2 else PK[3]])
                        for hh in range(2):
                            S.op("dve", lambda e, hh=hh, c=c, g=g: e.tensor_copy(
                                out=AB[:, c * 4 + hh * 2:c * 4 + hh * 2 + 2, g, :],
                                in_=PS[2 + hh][:, :].rearrange("p (t n) -> p t n", n=256)),
                                reads=[PK[2 + hh]], writes=["AB"])
                dC = dftC.ap().rearrange("(t p) n -> p t n", p=128)
                dS = dftS.ap().rearrange("(t p) n -> p t n", p=128)
                dit = 0
                for c in range(4):
                    for half in range(2):
                        b = dit % 2
                        dit += 1
                        S.dma("sp", dbuf[:, b, 0], dC[:, half * 8:(half + 1) * 8, c * 512:(c + 1) * 512], writes=[f"dC{b}"])
                        S.dma("sp", dbuf[:, b, 1], dS[:, half * 8:(half + 1) * 8, c * 512:(c + 1) * 512], writes=[f"dS{b}"])
                        for g in range(4):
                            for lt in range(8):
                                tl = half * 8 + lt
                                S.mm(lambda e, g=g, lt=lt, tl=tl, b=b, half=half: e.matmul(
                                    PS[g][:, :], lhsT=AB[:, tl, g, 0:128], rhs=dbuf[:, b, 0, lt, :],
                                    start=(half == 0 and lt == 0), stop=False),
                                    reads=["AB", f"dC{b}"], writes=[PK[g]], first=(half == 0 and lt == 0))
                                S.mm(lambda e, g=g, lt=lt, tl=tl, b=b, half=half: e.matmul(
                                    PS[g][:, :], lhsT=AB[:, tl, g, 128:256], rhs=dbuf[:, b, 1, lt, :],
                                    start=False, stop=(half == 1 and lt == 7)),
                                    reads=["AB", f"dS{b}"], writes=[PK[g]], first=False)
                    for g in range(4):
                        eng = evac_eng()
                        S.op(eng, copy_op(eng, specT[:, g, c * 512:(c + 1) * 512], PS[g][:, :]), reads=[PK[g]], writes=["specT"])
            S.barrier()

            mcut(1)
            with ExitStack() as s2:
                knT = s2.enter_context(_sbuf_tensor("knT", [128, 4, NK], BF16))
                qnT = s2.enter_context(_sbuf_tensor("qnT", [128, 4, NT], BF16))
                Vn = s2.enter_context(_sbuf_tensor("Vn", [128, NKT, 8, 65], BF16))
                with ExitStack() as s3:
                    wq = s3.enter_context(_sbuf_tensor("wq", [128, KC, 512], BF16))
                    wk = s3.enter_context(_sbuf_tensor("wk", [128, KC, 512], BF16))
                    wv = s3.enter_context(_sbuf_tensor("wv", [128, KC, 512], BF16))
                    ck = s3.enter_context(_sbuf_tensor("ck", [128, 4, 512], BF16))
                    of32 = s3.enter_context(_sbuf_tensor("of32", [128, 2, 512], F32))
                    S.dma("pool", wq[:], win[:, :, C_NQ:C_NQ + 512], writes=["wq"])
                    S.dma("pool", wk[:], win[:, :, C_NK:C_NK + 512], writes=["wk"])
                    S.dma("pool", wv[:], win[:, :, C_NV:C_NV + 512], writes=["wv"])
                    S.dma("pool", ck[:], c_nk.ap()[l].rearrange("(t p) n -> p t n", p=128), writes=["ck"])
                    for j in range(4):
                        S.dma("pool", Vn[:, NTILE + j, :, 0:64], c_nv.ap()[l, j * 128:(j + 1) * 128, :].rearrange("p (h d) -> p h d", d=64), writes=["Vnc"])
                    S.op("pool", lambda e: e.memset(Vn[:, :, :, 64:65], 1.0), writes=["Vn1"])
                    it = 0
                    for t in range(NTILE):
                        for (wsb, wkey, odst, isv) in ((wk, "wk", o_nk, False), (wv, "wv", o_nv, True)):
                            b = it % 2
                            it += 1
                            for k in range(KC):
                                S.mm(lambda e, k=k, t=t, b=b, wsb=wsb: e.matmul(PS[b][:, :], lhsT=hT[:, k, t * 128:(t + 1) * 128], rhs=wsb[:, k, :],
                                                                               start=(k == 0), stop=(k == KC - 1)),
                                     reads=["hTm", wkey], writes=[PK[b]], first=(k == 0))
                            S.op("act", copy_op("act", of32[:, b, :], PS[b][:, :]), reads=[PK[b]], writes=[f"of32{b}"])
                            if isv:
                                S.op("dve", lambda e, t=t, b=b: e.tensor_copy(out=Vn[:, t, :, 0:64], in_=PS[b][:, :].rearrange("p (h d) -> p h d", d=64)),
                                     reads=[PK[b]], writes=["Vn"])
                            S.dma("sp", odst.ap()[l, t * 128:(t + 1) * 128, :], of32[:, b, :], reads=[f"of32{b}"])
                    for pr in range(4):
                        for c in range(4):
                            for (wsb, wkey, dstT, dkey) in ((wk, "wk", knT, "knT"), (wq, "wq", qnT, "qnT")):
                                b = it % 2
                                it += 1
                                for k in range(KC):
                                    S.mm(lambda e, k=k, pr=pr, c=c, b=b, wsb=wsb: e.matmul(
                                        PS[b][:, :], lhsT=wsb[:, k, pr * 128:(pr + 1) * 128], rhs=hT[:, k, c * 512:(c + 1) * 512],
                                        start=(k == 0), stop=(k == KC - 1)),
                                        reads=[wkey, "hTm"], writes=[PK[b]], first=(k == 0))
                                eng = evac_eng()
                                S.op(eng, copy_op(eng, dstT[:, pr, c * 512:(c + 1) * 512], PS[b][:, :]), reads=[PK[b]], writes=[dkey])
                        for j in range(4):
                            S.mm(lambda e, j=j, pr=pr: e.transpose(out=PSB[:, j * 128:(j + 1) * 128], in_=ck[:, j, pr * 128:(pr + 1) * 128], identity=ident[:]),
                                 reads=["ck", "ident"], writes=["ps7"], first=(j == 0))
                        S.op("dve", lambda e, pr=pr: e.tensor_copy(out=knT[:, pr, NT:NK], in_=PSB[:, 0:512]), reads=["ps7"], writes=["knT"])
                S.barrier()
                mcut(2)
                s3 = s2
                iq = s3.enter_context(_sbuf_tensor("iq", [72, NT], BF16))
                ik = s3.enter_context(_sbuf_tensor("ik", [72, NK], BF16))
                m1 = s3.enter_context(_sbuf_tensor("m1", [128, NPAT, 128], BF16))
                m2 = s3.enter_context(_sbuf_tensor("m2", [128, NPAT, 128], BF16))
                btf = s3.enter_context(_sbuf_tensor("btf", [128, NPAT, 128], F32))
                BT = s3.enter_context(_sbuf_tensor("BT", [128, 2, NPAT, 128], BF16))
                pT = s3.enter_context(_sbuf_tensor("pT", [128, 2, 9, 128], BF16))
                osb = s3.enter_context(_sbuf_tensor("osb", [65, 2, 512], F32))
                otmp = s3.enter_context(_sbuf_tensor("otmp", [64, 2, 512], BF16))
                for pb_ in (0, 64):
                    S.dma("pool", iq[pb_:pb_ + 8, :], indq_n.ap(), writes=["iq"])
                    S.dma("pool", ik[pb_:pb_ + 8, :], indk.ap(), writes=["ik"])
                S.dma("pool", m1[:], m1d.ap(), writes=["m1"])
                S.dma("pool", m2[:], m2d.ap(), writes=["m2"])
                items = [(h, c, qi) for h in range(8) for c in range(4) for qi in range(4)]

                def na_bt(h):
                    hb = h % 2
                    S.dma("sp", btf[:], AP(btd, ((l * 8 + h) * NPAT) * 16384, [[128, 128], [16384, NPAT], [1, 128]]),
                          reads=["btd"], writes=["btf"])
                    S.op("dve", lambda e: e.tensor_tensor(out=btf[:], in0=btf[:], in1=m1[:], op=ALU.mult),
                         reads=["btf", "m1"], writes=["btf"])
                    S.op("dve", lambda e: e.tensor_tensor(out=BT[:, hb], in0=btf[:], in1=m2[:], op=ALU.add),
                         reads=["btf", "m2"], writes=[f"BT{hb}"])

                def na_slots(i):
                    return [(j, pat) for (j, pat) in na_window(i)] + [(NTILE + j, None) for j in range(4)]

                def na_scores(k):
                    h, c, qi = items[k]
                    pr, pb, hb = h // 2, 64 * (h % 2), h % 2
                    i = c * 4 + qi
                    ab = k % 2
                    banks = [ab * 3 + 0, ab * 3 + 1, ab * 3 + 2]
                    for si, (j, pat) in enumerate(na_slots(i)):
                        bk = banks[si // 4]
                        col = (si % 4) * 128
                        S.mm(lambda e, bk=bk, col=col, j=j: e.matmul(
                            PS[bk][:, col:col + 128], lhsT=knT[pb:pb + 64, pr, j * 128:(j + 1) * 128],
                            rhs=qnT[pb:pb + 64, pr, i * 128:(i + 1) * 128], start=True, stop=False),
                            reads=["knT", "qnT"], writes=[PK[bk]], first=(si % 4 == 0))
                        S.mm(lambda e, bk=bk, col=col, j=j, pat=pat: e.matmul(
                            PS[bk][:, col:col + 128], lhsT=ik[pb:pb + 8, j * 128:(j + 1) * 128], rhs=iq[pb:pb + 8, i * 128:(i + 1) * 128],
                            start=False, stop=(pat is None)),
                            reads=["ik", "iq"], writes=[PK[bk]], first=False)
                        if pat is not None:
                            S.mm(lambda e, bk=bk, col=col, pat=pat: e.matmul(
                                PS[bk][:, col:col + 128], lhsT=ident[:], rhs=BT[:, hb, pat, :], start=False, stop=True),
                                reads=["ident", f"BT{hb}"], writes=[PK[bk]], first=False)

                def na_exp(k):
                    h, c, qi = items[k]
                    i = c * 4 + qi
                    ab = k % 2
                    ns = len(na_slots(i))
                    for g in range(3):
                        n_in = min(4, ns - g * 4)
                        if n_in <= 0:
                            continue
                        bk = ab * 3 + g
                        S.op("act", lambda e, bk=bk, g=g, n_in=n_in: e.activation(
                            out=pT[:, ab, g * 4:g * 4 + n_in, :], in_=PS[bk][:, 0:n_in * 128].rearrange("p (s n) -> p s n", n=128),
                            func=AF.Exp, scale=NA_SCALE),
                            reads=[PK[bk]], writes=[f"pT{ab}"])

                def na_pv(k):
                    h, c, qi = items[k]
                    i = c * 4 + qi
                    ab = k % 2
                    po = 6 + ((h * 4 + c) % 2)
                    slots = na_slots(i)
                    ns = len(slots)
                    for si, (j, pat) in enumerate(slots):
                        S.mm(lambda e, si=si, j=j: e.matmul(
                            PS[po][0:65, qi * 128:(qi + 1) * 128], lhsT=Vn[:, j, h, :], rhs=pT[:, ab, si, :],
                            start=(si == 0), stop=(si == ns - 1)),
                            reads=["Vn", "Vnc", "Vn1", f"pT{ab}"], writes=[PK[po]], first=(si == 0 and qi == 0))

                def na_norm_a(k):
                    h, c, qi = items[k]
                    ob = (h * 4 + c) % 2
                    po = 6 + ob
                    S.op("act", copy_op("act", osb[:, ob, :], PS[po][0:65, :]), reads=[PK[po]], writes=[f"osb{ob}"])
                    S.op("dve", lambda e: e.reciprocal(out=osb[64:65, ob, :], in_=osb[64:65, ob, :]), reads=[f"osb{ob}"], writes=[f"osb{ob}"])

                def na_norm_b(k):
                    h, c, qi = items[k]
                    pr, hb = h // 2, h % 2
                    ob = (h * 4 + c) % 2
                    po = 6 + ob
                    S.mm(lambda e: e.matmul(PS[po][0:64, :], lhsT=onesf[64:65, 0:64], rhs=osb[64:65, ob, :], start=True, stop=True),
                         reads=["ones", f"osb{ob}"], writes=[PK[po]])
                    if hb == 0:
                        S.op("dve", lambda e: e.tensor_tensor(
                            out=onT[0:64, pr, c * 512:(c + 1) * 512], in0=PS[po][0:64, :], in1=osb[0:64, ob, :], op=ALU.mult),
                            reads=[PK[po], f"osb{ob}"], writes=["onT"])
                    else:
                        S.op("dve", lambda e: e.tensor_tensor(
                            out=otmp[:, ob, :], in0=PS[po][0:64, :], in1=osb[0:64, ob, :], op=ALU.mult),
                            reads=[PK[po], f"osb{ob}"], writes=[f"otmp{ob}"])
                        S.dma("sp", onT[64:128, pr, c * 512:(c + 1) * 512], otmp[:, ob, :], reads=[f"otmp{ob}"], writes=["onT"])

                na_bt(0)
                na_scores(0)
                npend = []
                for k in range(len(items)):
                    na_exp(k)
                    if k + 1 < len(items):
                        if items[k + 1][0] != items[k][0]:
                            na_bt(items[k + 1][0])
                        na_scores(k + 1)
                    for pd in list(npend):
                        if k >= pd[0]:
                            na_norm_b(pd[1])
                            npend.remove(pd)
                    na_pv(k)
                    if items[k][2] == 3:
                        na_norm_a(k)
                        npend.append((k + 1, k))
                for pd in npend:
                    na_norm_b(pd[1])
            S.barrier()

            mcut(3)
            with ExitStack() as s2:
                wuq = s2.enter_context(_sbuf_tensor("wuq", [128, 3, 768], BF16))
                wukv = s2.enter_context(_sbuf_tensor("wukv", [128, 2, 2, 8, 64], BF16))
                qn = s2.enter_context(_sbuf_tensor("qn", [128, 3, NT], BF16))
                ckvT = s2.enter_context(_sbuf_tensor("ckvT", [128, 2, NK], BF16))
                kall = s2.enter_context(_sbuf_tensor("kall", [104, 2, NK], BF16))
                qall = s2.enter_context(_sbuf_tensor("qall", [104, 2, NT], BF16))
                wuqr = s2.enter_context(_sbuf_tensor("wuqr", [128, 3, 8, 32], BF16))
                Vm = s2.enter_context(_sbuf_tensor("Vm", [128, NKT, 8, 65], BF16))
                rC = s2.enter_context(_sbuf_tensor("rC", [96, NT], BF16))
                rS = s2.enter_context(_sbuf_tensor("rS", [96, NT], BF16))
                t1 = s2.enter_context(_sbuf_tensor("t1", [96, 2, 512], F32))
                t2 = s2.enter_context(_sbuf_tensor("t2", [96, 2, 512], F32))
                S.dma("pool", wuq[:], w_uq.ap()[l].rearrange("(k p) n -> p k n", p=128), writes=["wuq"])
                for k2 in range(2):
                    for tt in range(2):
                        S.dma("pool", wukv[:, k2, tt],
                              w_ukv.ap()[l, k2 * 128:(k2 + 1) * 128, :].rearrange("p (h t d) -> p t h d", h=8, t=2)[:, tt], writes=["wukv"])
                S.dma("pool", rC[64:96, :], ropeC.ap(), writes=["rC"])
                S.dma("pool", rS[64:96, :], ropeS.ap(), writes=["rS"])
                for hb_ in range(2):
                    S.dma("pool", kall[96:104, hb_, :], indk.ap(), writes=["krTi"])
                    S.dma("pool", qall[96:104, hb_, :], indq_m.ap(), writes=["qrTi"])

                def rot_weights(dst4, src4, keys_r, key_w):
                    S.op("dve", lambda e: e.tensor_scalar(out=dst4[:, :, :, 0:8], in0=src4[:, :, :, 8:16], scalar1=-1.0, scalar2=None, op0=ALU.mult),
                         reads=keys_r, writes=[key_w])
                    S.op("dve", lambda e: e.tensor_copy(out=dst4[:, :, :, 8:16], in_=src4[:, :, :, 0:8]), reads=keys_r, writes=[key_w])

                for j in range(3):
                    rot_weights(wuqr[:, j].rearrange("p h (a d) -> p h a d", d=16),
                                wuq[:, j, :].rearrange("p (h x) -> p h x", x=96)[:, :, 64:96].rearrange("p h (a d) -> p h a d", d=16),
                                ["wuq"], "wuqr")
                S.op("pool", lambda e: e.memset(Vm[:, :, :, 64:65], 1.0), writes=["Vm1"])

                def rope2(ps_x, pkx, ps_r, pkr, b, dsts, dkey, c):
                    S.op("dve", lambda e: e.tensor_tensor(out=t1[64:96, b, :], in0=ps_x, in1=rC[64:96, c * 512:(c + 1) * 512], op=ALU.mult),
                         reads=[pkx, "rC"], writes=[f"t1{b}"])
                    S.op("dve", lambda e: e.tensor_tensor(out=t2[64:96, b, :], in0=ps_r, in1=rS[64:96, c * 512:(c + 1) * 512], op=ALU.mult),
                         reads=[pkr, "rS"], writes=[f"t2{b}"])
                    for dst in dsts:
                        S.op("pool", lambda e, dst=dst: e.tensor_tensor(out=dst, in0=t1[64:96, b, :], in1=t2[64:96, b, :], op=ALU.add),
                             reads=[f"t1{b}", f"t2{b}"], writes=[dkey])

                with ExitStack() as s3:
                    wqa = s3.enter_context(_sbuf_tensor("wqa", [128, KC, 384], BF16))
                    wkr = s3.enter_context(_sbuf_tensor("wkr", [128, KC, 288], BF16))
                    wkrr = s3.enter_context(_sbuf_tensor("wkrr", [128, KC, 32], BF16))
                    gq = s3.enter_context(_sbuf_tensor("gq", [128, 3], F32))
                    gkv = s3.enter_context(_sbuf_tensor("gkv", [128, 256], F32))
                    cc = s3.enter_context(_sbuf_tensor("cc", [128, 4, 256], BF16))
                    ckr = s3.enter_context(_sbuf_tensor("ckr", [128, 4, 32], BF16))
                    uq = s3.enter_context(_sbuf_tensor("uq", [128, 3, 512], F32))
                    sq = s3.enter_context(_sbuf_tensor("sq", [128, 3, 512], BF16))
                    rq = s3.enter_context(_sbuf_tensor("rq", [128, 512], F32))
                    ukv = s3.enter_context(_sbuf_tensor("ukv", [128, 2, 288], F32))
                    kst = s3.enter_context(_sbuf_tensor("kst", [128, 2, 6], F32))
                    kmv = s3.enter_context(_sbuf_tensor("kmv", [128, 2, 2], F32))
                    ssk = s3.enter_context(_sbuf_tensor("ssk", [128, 2], F32))
                    ckf = s3.enter_context(_sbuf_tensor("ckf", [128, 2, 256], F32))
                    ckb = s3.enter_context(_sbuf_tensor("ckb", [128, 2, 256], BF16))
                    S.dma("pool", wqa[:], win[:, :, C_Q:C_Q + 384], writes=["wqa"])
                    S.dma("pool", wkr[:], win[:, :, C_KV:C_KV + 288], writes=["wkr"])
                    rot_weights(wkrr[:].rearrange("p k (a d) -> p k a d", d=16),
                                wkr[:, :, 256:288].rearrange("p k (a d) -> p k a d", d=16), ["wkr"], "wkrr")
                    load_colvec(q_norm, l * 384, 3, gq[:], "gq")
                    S.dma("sp", gkv[:], AP(kv_norm, l * 256, [[0, 128], [1, 256]]), writes=["gkv"])
                    S.dma("pool", cc[:], c_ckv.ap()[l].rearrange("(t p) n -> p t n", p=128), writes=["cc"])
                    S.dma("pool", ckr[:], c_kr.ap()[l].rearrange("(t p) n -> p t n", p=128), writes=["ckr"])
                    it = 0
                    for c in range(4):
                        for j in range(3):
                            b = it % 2
                            it += 1
                            for k in range(KC):
                                S.mm(lambda e, k=k, j=j, c=c, b=b: e.matmul(PS[b][:, :], lhsT=wqa[:, k, j * 128:(j + 1) * 128],
                                                                       rhs=hT[:, k, c * 512:(c + 1) * 512], start=(k == 0), stop=(k == KC - 1)),
                                     reads=["wqa", "hTm"], writes=[PK[b]], first=(k == 0))
                            S.op("act", lambda e, j=j, b=b: e.activation(out=sq[:, j, :], in_=PS[b][:, :], func=AF.Square), reads=[PK[b]], writes=[f"sq{j}"])
                            S.op("dve", lambda e, j=j, b=b: e.tensor_copy(out=uq[:, j, :], in_=PS[b][:, :]), reads=[PK[b]], writes=[f"uq{j}"])
                        for j in range(3):
                            S.mm(lambda e, j=j: e.matmul(PS[2][:, :], lhsT=ones[:], rhs=sq[:, j, :], start=(j == 0), stop=(j == 2)),
                                 reads=["ones", f"sq{j}"], writes=[PK[2]], first=(j == 0))
                        S.op("act", lambda e: e.activation(out=rq[:], in_=PS[2][:, :], func=AF.Sqrt, scale=1.0 / 384, bias=epsc[:, 0:1]),
                             reads=[PK[2], "epsc"], writes=["rq"])
                        S.op("dve", lambda e: e.reciprocal(out=rq[:], in_=rq[:]), reads=["rq"], writes=["rq"])
                        for j in range(3):
                            S.op("dve", lambda e, j=j, c=c: e.scalar_tensor_tensor(out=qn[:, j, c * 512:(c + 1) * 512], in0=uq[:, j, :], scalar=gq[:, j:j + 1],
                                                                                  in1=rq[:], op0=ALU.mult, op1=ALU.mult),
                                 reads=[f"uq{j}", "gq", "rq"], writes=["qn"])
                    for t in range(NTILE):
                        b = t % 2
                        for k in range(KC):
                            S.mm(lambda e, k=k, t=t, b=b: e.matmul(PS[b][:, 0:288], lhsT=hT[:, k, t * 128:(t + 1) * 128], rhs=wkr[:, k, :],
                                                                  start=(k == 0), stop=(k == KC - 1)),
                                 reads=["hTm", "wkr"], writes=[PK[b]], first=(k == 0))
                        S.op("act", copy_op("act", ukv[:, b, :], PS[b][:, 0:288]), reads=[PK[b]], writes=[f"ukv{b}"])
                        S.dma("sp", o_kr.ap()[l, t * 128:(t + 1) * 128, :], ukv[:, b, 256:288], reads=[f"ukv{b}"])
                        S.op("dve", lambda e, b=b: e.bn_stats(out=kst[:, b, :], in_=ukv[:, b, 0:256]), reads=[f"ukv{b}"], writes=[f"kst{b}"])
                        S.op("dve", lambda e, b=b: e.bn_aggr(out=kmv[:, b, :], in_=kst[:, b, :]), reads=[f"kst{b}"], writes=[f"kmv{b}"])
                        S.op("dve", lambda e, b=b: e.scalar_tensor_tensor(out=ssk[:, b:b + 1], in0=kmv[:, b, 0:1], scalar=kmv[:, b, 0:1], in1=kmv[:, b, 1:2],
                                                                         op0=ALU.mult, op1=ALU.add),
                             reads=[f"kmv{b}"], writes=[f"ssk{b}"])
                        S.op("act", lambda e, b=b: e.activation(out=ssk[:, b:b + 1], in_=ssk[:, b:b + 1], func=AF.Sqrt, scale=1.0, bias=epsc[:, 0:1]),
                             reads=[f"ssk{b}", "epsc"], writes=[f"ssk{b}"])
                        S.op("dve", lambda e, b=b: e.reciprocal(out=ssk[:, b:b + 1], in_=ssk[:, b:b + 1]), reads=[f"ssk{b}"], writes=[f"ssk{b}"])
                        S.op("dve", lambda e, b=b: e.scalar_tensor_tensor(out=ckf[:, b, :], in0=ukv[:, b, 0:256], scalar=ssk[:, b:b + 1], in1=gkv[:],
                                                                         op0=ALU.mult, op1=ALU.mult),
                             reads=[f"ukv{b}", f"ssk{b}", "gkv"], writes=[f"ckf{b}"])
                        S.dma("sp", o_ckv.ap()[l, t * 128:(t + 1) * 128, :], ckf[:, b, :], reads=[f"ckf{b}"])
                        S.op("pool", lambda e, b=b: e.tensor_copy(out=ckb[:, b, :], in_=ckf[:, b, :]), reads=[f"ckf{b}"], writes=[f"ckb{b}"])
                        for k2 in range(2):
                            S.mm(lambda e, k2=k2, b=b: e.transpose(out=PSB[:, k2 * 128:(k2 + 1) * 128], in_=ckb[:, b, k2 * 128:(k2 + 1) * 128], identity=ident[:]),
                                 reads=[f"ckb{b}", "ident"], writes=["ps7"], first=(k2 == 0))
                        S.op("dve", lambda e, t=t: e.tensor_copy(out=ckvT[:, :, t * 128:(t + 1) * 128], in_=PSB[:, 0:256].rearrange("p (k n) -> p k n", n=128)),
                             reads=["ps7"], writes=["ckvT"])
                    for j in range(4):
                        for k2 in range(2):
                            S.mm(lambda e, k2=k2, j=j: e.transpose(out=PSB[:, k2 * 128:(k2 + 1) * 128], in_=cc[:, j, k2 * 128:(k2 + 1) * 128], identity=ident[:]),
                                 reads=["cc", "ident"], writes=["ps7"], first=(k2 == 0))
                        S.op("dve", lambda e, j=j: e.tensor_copy(out=ckvT[:, :, NT + j * 128:NT + (j + 1) * 128],
                                                                in_=PSB[:, 0:256].rearrange("p (k n) -> p k n", n=128)),
                             reads=["ps7"], writes=["ckvT"])
                    for j in range(4):
                        S.mm(lambda e, j=j: e.matmul(PS[3][64:96, j * 128:(j + 1) * 128], lhsT=ckr[:, j, :], rhs=ident[:], start=True, stop=True),
                             reads=["ckr", "ident"], writes=[PK[3]], first=(j == 0))
                    for hb_ in range(2):
                        S.op("dve", lambda e, hb_=hb_: e.tensor_copy(out=kall[64:96, hb_, NT:NK], in_=PS[3][64:96, :]), reads=[PK[3]], writes=["krT"])
                    for c in range(4):
                        b = c % 2
                        for k in range(KC):
                            S.mm(lambda e, k=k, c=c, b=b: e.matmul(PS[b][64:96, :], lhsT=wkr[:, k, 256:288], rhs=hT[:, k, c * 512:(c + 1) * 512],
                                                                  start=(k == 0), stop=(k == KC - 1)),
                                 reads=["wkr", "hTm"], writes=[PK[b]], first=(k == 0))
                        for k in range(KC):
                            S.mm(lambda e, k=k, c=c, b=b: e.matmul(PS[2 + b][64:96, :], lhsT=wkrr[:, k, :], rhs=hT[:, k, c * 512:(c + 1) * 512],
                                                                  start=(k == 0), stop=(k == KC - 1)),
                                 reads=["wkrr", "hTm"], writes=[PK[2 + b]], first=(k == 0))
                        rope2(PS[b][64:96, :], PK[b], PS[2 + b][64:96, :], PK[2 + b], b,
                              [kall[64:96, 0, c * 512:(c + 1) * 512], kall[64:96, 1, c * 512:(c + 1) * 512]], "krT", c)
                    for t in range(NKT):
                        b = t % 2
                        for k2 in range(2):
                            S.mm(lambda e, k2=k2, t=t, b=b: e.matmul(PS[b][:, :], lhsT=ckvT[:, k2, t * 128:(t + 1) * 128],
                                                                    rhs=wukv[:, k2, 1].rearrange("p h d -> p (h d)"),
                                                                    start=(k2 == 0), stop=(k2 == 1)),
                                 reads=["ckvT", "wukv"], writes=[PK[b]], first=(k2 == 0))
                        eng = evac_eng()
                        S.op(eng, copy_op(eng, Vm[:, t, :, 0:64], PS[b][:, :].rearrange("p (h d) -> p h d", d=64)), reads=[PK[b]], writes=["Vm"])
                S.barrier()
                mcut(4)
                s3 = s2
                pTm = s3.enter_context(_sbuf_tensor("pTm", [128, 4, 512], BF16))
                osb = s3.enter_context(_sbuf_tensor("osbm", [65, 2, 512], F32))
                otmp = s3.enter_context(_sbuf_tensor("otmpm", [64, 2, 512], BF16))

                def head_pieces(h):
                    hb_ = h % 2
                    pcs = []
                    for c5 in range(5):
                        def pk_(c5=c5):
                            for k2 in range(2):
                                S.mm(lambda e, k2=k2: e.matmul(PS[6][0:64, :], lhsT=wukv[:, k2, 0, h, :], rhs=ckvT[:, k2, c5 * 512:(c5 + 1) * 512],
                                                               start=(k2 == 0), stop=(k2 == 1)),
                                     reads=["wukv", "ckvT"], writes=[PK[6]], first=(k2 == 0))
                            S.op("dve", lambda e: e.tensor_copy(out=kall[0:64, hb_, c5 * 512:(c5 + 1) * 512], in_=PS[6][0:64, :]),
                                 reads=[PK[6]], writes=[f"knp{hb_}"])
                        pcs.append(pk_)
                    for c in range(4):
                        def pq_(c=c):
                            for j in range(3):
                                S.mm(lambda e, j=j: e.matmul(PS[6][0:64, :], lhsT=wuq[:, j, h * 96:h * 96 + 64], rhs=qn[:, j, c * 512:(c + 1) * 512],
                                                             start=(j == 0), stop=(j == 2)),
                                     reads=["wuq", "qn"], writes=[PK[6]], first=(j == 0))
                            for j in range(3):
                                S.mm(lambda e, j=j: e.matmul(PS[6][64:96, :], lhsT=wuq[:, j, h * 96 + 64:h * 96 + 96], rhs=qn[:, j, c * 512:(c + 1) * 512],
                                                             start=(j == 0), stop=(j == 2)),
                                     reads=["wuq", "qn"], writes=[PK[6]], first=False)
                            for j in range(3):
                                S.mm(lambda e, j=j: e.matmul(PS[7][64:96, :], lhsT=wuqr[:, j, h, :], rhs=qn[:, j, c * 512:(c + 1) * 512],
                                                             start=(j == 0), stop=(j == 2)),
                                     reads=["wuqr", "qn"], writes=[PK[7]], first=(j == 0))
                            S.op("dve", lambda e: e.tensor_copy(out=qall[0:64, hb_, c * 512:(c + 1) * 512], in_=PS[6][0:64, :]),
                                 reads=[PK[6]], writes=[f"qnp{hb_}"])
                            rope2(PS[6][64:96, :], PK[6], PS[7][64:96, :], PK[7], c % 2, [qall[64:96, hb_, c * 512:(c + 1) * 512]], f"qrT{hb_}", c)
                        pcs.append(pq_)
                    return pcs

                for pc in head_pieces(0):
                    pc()
                sit = 0
                import os
                mdbg = os.environ.get("KDBG_MLA", "")
                for h in range(8):
                    pr, hh = h // 2, h % 2
                    hb = h % 2
                    nxt = head_pieces(h + 1) if h + 1 < 8 else []
                    LA = 2
                    items = [(c, j) for c in range(4) for j in range(NKT)]
                    pend = []

                    def mla_scores(c, j, sidx, hb=hb):
                        bk = sidx % 4
                        S.mm(lambda e: e.matmul(PS[bk][:, :], lhsT=kall[:, hb, j * 128:(j + 1) * 128], rhs=qall[:, hb, c * 512:(c + 1) * 512],
                                                start=True, stop=True),
                             reads=[f"knp{hb}", f"qnp{hb}", "krT", "krTi", f"qrT{hb}", "qrTi"], writes=[PK[bk]], first=True)

                    def mla_exp_pv(c, j, sidx, h=h):
                        bk = sidx % 4
                        po = 4 + (c % 2)
                        S.op("act", lambda e: e.activation(out=pTm[:, bk, :], in_=PS[bk][:, :], func=AF.Exp, scale=MLA_SCALE),
                             reads=[PK[bk]], writes=[f"pTm{bk}"])
                        S.mm(lambda e: e.matmul(PS[po][0:65, :], lhsT=Vm[:, j, h, :], rhs=pTm[:, bk, :], start=(j == 0), stop=(j == NKT - 1)),
                             reads=["Vm", "Vm1", f"pTm{bk}"], writes=[PK[po]], first=(j == 0))

                    def mla_norm_a(c, h=h):
                        po = 4 + (c % 2)
                        ob = c % 2
                        S.op("act", copy_op("act", osb[:, ob, :], PS[po][0:65, :]), reads=[PK[po]], writes=[f"osb{ob}"])
                        S.op("dve", lambda e: e.reciprocal(out=osb[64:65, ob, :], in_=osb[64:65, ob, :]), reads=[f"osb{ob}"], writes=[f"osb{ob}"])

                    def mla_norm_b(c, h=h, pr=pr, hh=hh):
                        po = 4 + (c % 2)
                        ob = c % 2
                        S.mm(lambda e: e.matmul(PS[po][0:64, :], lhsT=onesf[64:65, 0:64], rhs=osb[64:65, ob, :], start=True, stop=True),
                             reads=["ones", f"osb{ob}"], writes=[PK[po]])
                        if hh == 0:
                            S.op("dve", lambda e: e.tensor_tensor(
                                out=omT[0:64, pr, c * 512:(c + 1) * 512], in0=PS[po][0:64, :], in1=osb[0:64, ob, :], op=ALU.mult),
                                reads=[PK[po], f"osb{ob}"], writes=["omT"])
                        else:
                            S.op("dve", lambda e: e.tensor_tensor(
                                out=otmp[:, ob, :], in0=PS[po][0:64, :], in1=osb[0:64, ob, :], op=ALU.mult),
                                reads=[PK[po], f"osb{ob}"], writes=[f"otmp{ob}"])
                            S.dma("sp", omT[64:128, pr, c * 512:(c + 1) * 512], otmp[:, ob, :], reads=[f"otmp{ob}"], writes=["omT"])

                    n_it = len(items)
                    for k in range(n_it + LA):
                        if k < n_it:
                            mla_scores(items[k][0], items[k][1], sit + k)
                        for pd in list(pend):
                            if k >= pd[0]:
                                mla_norm_b(pd[1])
                                pend.remove(pd)
                        if k >= LA:
                            c_, j_ = items[k - LA]
                            mla_exp_pv(c_, j_, sit + k - LA)
                            if j_ == NKT - 1:
                                mla_norm_a(c_)
                                pend.append((k + 3, c_))
                        if nxt and k % 6 == 3:
                            nxt.pop(0)()
                    for pd in pend:
                        mla_norm_b(pd[1])
                    for pc in nxt:
                        pc()
                    sit += n_it
            S.barrier()

            mcut(5)
            with ExitStack() as s2:
                mT = s2.enter_context(_sbuf_tensor("mT", [128, KC, NT], BF16))
                wg = s2.enter_context(_sbuf_tensor("wg", [128, 2, 3, KC, 128], BF16))
                wb_ = s2.enter_context(_sbuf_tensor("wbr", [128, 2, 3, 4, 128], BF16))
                bgT = s2.enter_context(_sbuf_tensor("bgT", [128, 24], F32))
                gsb = s2.enter_context(_sbuf_tensor("gsb", [128, 2, 512], BF16))
                acc = s2.enter_context(_sbuf_tensor("acc", [128, 2, 512], F32))
                tm = s2.enter_context(_sbuf_tensor("tm", [128, 2, 512], F32))
                wo = s2.enter_context(_sbuf_tensor("wo", [128, KC, D], BF16))
                tmp = s2.enter_context(_sbuf_tensor("tmpm", [128, 5, D], F32))
                xt2 = s2.enter_context(_sbuf_tensor("xt2m", [128, 2, D], F32))
                alloc_vec(s2, V)
                load_epi(l, 5, 1, 1.0, V)
                load_colvec(b_gate, l * 3 * D, 24, bgT[:], "bgT")
                S.dma("pool", wo[:], w_out.ap()[l].rearrange("(k p) n -> p k n", p=128), writes=["wo"])
                wgv = w_gate.ap()[l].rearrange("(k p) (g n) -> p g k n", p=128, g=3)
                brs = [w.ap()[l].rearrange("(k p) n -> p k n", p=128) for w in (w_bf, w_bm, w_bn)]
                bins = [(specT, "specT"), (omT, "omT"), (onT, "onT")]
                git = 0
                for oc in range(KC):
                    wbuf = oc % 2
                    S.dma("pool", wg[:, wbuf], wgv[:, :, :, oc * 128:(oc + 1) * 128], writes=[f"wg{wbuf}"])
                    for gi in range(3):
                        S.dma("pool", wb_[:, wbuf, gi], brs[gi][:, :, oc * 128:(oc + 1) * 128], writes=[f"wbr{wbuf}"])
                    for c in range(4):
                        ab = (oc * 4 + c) % 2
                        for gi in range(3):
                            pg = (git % 2)
                            py = 2 + (git % 2)
                            git += 1
                            for k in range(KC):
                                S.mm(lambda e, k=k, gi=gi, c=c, pg=pg, wbuf=wbuf: e.matmul(
                                    PS[pg][:, :], lhsT=wg[:, wbuf, gi, k, :], rhs=hT[:, k, c * 512:(c + 1) * 512], start=(k == 0), stop=(k == KC - 1)),
                                    reads=[f"wg{wbuf}", "hTm"], writes=[PK[pg]], first=(k == 0))
                            bsrc, bkey = bins[gi]
                            for k4 in range(4):
                                S.mm(lambda e, k4=k4, gi=gi, c=c, py=py, wbuf=wbuf, bsrc=bsrc: e.matmul(
                                    PS[py][:, :], lhsT=wb_[:, wbuf, gi, k4, :], rhs=bsrc[:, k4, c * 512:(c + 1) * 512], start=(k4 == 0), stop=(k4 == 3)),
                                    reads=[f"wbr{wbuf}", bkey], writes=[PK[py]], first=(k4 == 0))
                            gb = git % 2
                            S.op("act", lambda e, pg=pg, gi=gi, oc=oc, gb=gb: e.activation(
                                out=gsb[:, gb, :], in_=PS[pg][:, :], func=AF.Sigmoid, bias=bgT[:, gi * 8 + oc:gi * 8 + oc + 1], scale=1.0),
                                reads=[PK[pg], "bgT"], writes=[f"gsb{gb}"])
                            if gi == 0:
                                S.op("dve", lambda e, py=py, gb=gb, ab=ab: e.tensor_tensor(out=acc[:, ab, :], in0=PS[py][:, :], in1=gsb[:, gb, :], op=ALU.mult),
                                     reads=[PK[py], f"gsb{gb}"], writes=[f"acc{ab}"])
                            else:
                                S.op("dve", lambda e, py=py, gb=gb, ab=ab: e.tensor_tensor(out=tm[:, ab, :], in0=PS[py][:, :], in1=gsb[:, gb, :], op=ALU.mult),
                                     reads=[PK[py], f"gsb{gb}"], writes=[f"tm{ab}"])
                                if gi == 1:
                                    S.op("pool", lambda e, ab=ab: e.tensor_tensor(out=acc[:, ab, :], in0=acc[:, ab, :], in1=tm[:, ab, :], op=ALU.add),
                                         reads=[f"acc{ab}", f"tm{ab}"], writes=[f"acc{ab}"])
                                else:
                                    S.op("pool", lambda e, ab=ab, oc=oc, c=c: e.tensor_tensor(out=mT[:, oc, c * 512:(c + 1) * 512], in0=acc[:, ab, :], in1=tm[:, ab, :], op=ALU.add),
                                         reads=[f"acc{ab}", f"tm{ab}"], writes=["mT"])
                NB = 5

                def mg_load(t):
                    b = t % 2
                    S.dma("sp", xt2[:, b, :], xin[t * 128:(t + 1) * 128, :], reads=["xdram"], writes=[f"xt2{b}"])

                def mg_mm(t):
                    b = t % 2
                    zb = t % NB
                    for half in range(2):
                        pi = 4 + half
                        for k in range(KC):
                            S.mm(lambda e, k=k, half=half, pi=pi: e.matmul(
                                PS[pi][:, :], lhsT=mT[:, k, t * 128:(t + 1) * 128], rhs=wo[:, k, half * 512:(half + 1) * 512],
                                start=(k == 0), stop=(k == KC - 1)),
                                reads=["mT", "wo"], writes=[PK[pi]], first=(k == 0))
                        S.op("dve", lambda e, pi=pi, half=half: e.tensor_tensor(
                            out=tmp[:, zb, half * 512:(half + 1) * 512], in0=PS[pi][:, :], in1=V["gate_bc"][:, half * 512:(half + 1) * 512], op=ALU.mult),
                            reads=[PK[pi], "gate_bc"], writes=[f"tmpm{zb}"])
                    S.op("dve", lambda e: e.scalar_tensor_tensor(out=tmp[:, zb, :], in0=xt2[:, b, :], scalar=ALPHA, in1=tmp[:, zb, :],
                                                                 op0=ALU.mult, op1=ALU.add),
                         reads=[f"xt2{b}", f"tmpm{zb}"], writes=[f"tmpm{zb}"])

                lo = ln_out_stages(lambda t: tmp[:, t % NB, :], lambda t: f"tmpm{t % NB}", T, V,
                                   lambda t: [xo[t * 128:(t + 1) * 128, :] for xo in xouts])

                def mg_mm_stats(t):
                    mg_mm(t)
                    lo[0](t)

                run_staged(NTILE, [mg_load, mg_mm_stats] + lo[1:])
        S.barrier()

    prologue()
    bufs = [xa.ap(), xb.ap()]
    cur = x0.ap()
    stage = 0
    for l in range(DEPTH):
        for kind in ("ffn1", "mix", "ffn2"):
            if stage >= nstages:
                break
            last = (stage == nstages - 1)
            dst = y.ap() if last else bufs[stage % 2]
            if kind == "ffn1":
                ffn(l, 1, cur, [dst])
            elif kind == "mix":
                mixer(l, cur, [dst])
            else:
                ffn(l, 2, cur, [dst])
            cur = dst
            stage += 1
    for e in ("sp", "pool", "act", "dve", "pe"):
        S.wait_all_dma(e)
    import os
    if os.environ.get("KDBG_STATS"):
        print("SIGVALS", S.sigval, "POS", S.pos, "DMA", max(S.dcount), flush=True)
    return nc


def _bf16(a):
    return np.asarray(a, dtype=np.float32).astype(ml_dtypes.bfloat16)


def _role_consts(role):
    c = {}
    t = np.arange(NT)
    if role == "sample":
        pos = np.stack([t // 64, t % 64], -1).astype(np.float32)
        inv = (10000.0 ** (-np.arange(8, dtype=np.float32) / 8)).astype(np.float32)
        ang = pos[:, :, None] * inv
        ang = np.concatenate([ang, ang], -1)
        cos = np.cos(ang).reshape(NT, 32).T
        sin = np.sin(ang).reshape(NT, 32).T
        c["ropeC"] = np.ascontiguousarray(cos, dtype=np.float32)
        c["ropeS"] = np.ascontiguousarray(sin, dtype=np.float32)
        c["indq_m"] = np.zeros((8, NT), np.float32)
        c["indq_n"] = np.zeros((8, NT), np.float32)
        c["indk"] = np.zeros((8, NK), np.float32)
        L = NT
        blk = np.zeros(NT, np.int64)
        loc = t
    else:
        c["ropeC"] = np.ones((32, NT), np.float32)
        c["ropeS"] = np.zeros((32, NT), np.float32)
        oh = (t[None, :] // 256 == np.arange(8)[:, None]).astype(np.float32)
        c["indq_m"] = oh * BIG_MLA
        c["indq_n"] = oh * BIG_NA
        ik = np.zeros((8, NK), np.float32)
        ik[:, :NT] = oh
        c["indk"] = ik
        L = 256
        blk = t // 256
        loc = t % 256
    norm = 1.0 / math.sqrt(L * 128.0)
    same = (blk[:, None] == blk[None, :])
    ph = (2.0 * np.pi / L) * ((loc[:, None] * loc[None, :]) % L).astype(np.float64)
    c["dftC"] = _bf16(np.where(same, np.cos(ph) * norm, 0.0))
    c["dftS"] = _bf16(np.where(same, -np.sin(ph) * norm, 0.0))
    cc = np.arange(128)
    ph2 = (2.0 * np.pi / 128) * ((cc[:, None] * cc[None, :]) % 128).astype(np.float64)
    c["dftCS"] = np.concatenate([np.cos(ph2), np.sin(ph2)], 1).astype(np.float32)
    m1 = np.zeros((128, NPAT, 128), np.float32)
    m2 = np.zeros((128, NPAT, 128), np.float32)
    if role == "sample":
        kk = np.arange(128)
        kr, kc = kk // 64, kk % 64
        qr, qc = kk // 64, kk % 64
        cstart = np.clip(qc - 8, 0, 48)
        col_ok = (kc[:, None] >= cstart[None, :]) & (kc[:, None] < cstart[None, :] + 16)
        for p, (dl, typ) in enumerate(PAT_DELTA):
            rel = 2 * dl + kr[:, None] - qr[None, :]
            row_ok = ((rel >= -4) & (rel <= 3)) if typ == 0 else np.ones_like(rel, bool)
            ok = col_ok & row_ok
            m1[:, p, :] = np.where(ok, 1.0 / NA_SCALE, 0.0)
            m2[:, p, :] = np.where(ok, 0.0, NEG)
    c["m1d"] = m1
    c["m2d"] = m2
    pr = np.zeros((32, 32), np.float32)
    for a in range(2):
        for j in range(16):
            d = a * 16 + j
            if j < 8:
                pr[d, d + 8] = -1.0
            else:
                pr[d, d - 8] = 1.0
    c["protT"] = np.ascontiguousarray(pr.T)
    c["identd"] = np.eye(128, dtype=np.float32)
    return c


_CACHE = {}


def _get_nc(nstages):
    if nstages not in _CACHE:
        _CACHE[nstages] = build(nstages)
    return _CACHE[nstages]


def run_units(inputs, nstages=3 * DEPTH, cores=None):
    f32 = lambda a: np.ascontiguousarray(np.asarray(a), dtype=np.float32)
    xp = f32(inputs["x_prompt"])
    xs = f32(inputs["x_sample"])
    shared = {}
    for nm in ("w_ada", "b_ada", "ffn1_w1", "ffn1_w3", "ffn1_w2", "ffn2_w1", "ffn2_w3", "ffn2_w2", "w_in",
               "mla_q_norm", "mla_w_uq", "mla_kv_norm", "mla_w_ukv", "w_branch_f", "w_branch_m", "w_branch_n",
               "w_gate", "b_gate", "w_out", "ln_g", "ln_b"):
        shared[nm] = f32(inputs[nm])
    rp = f32(inputs["na_rpb"])[..., ::-1].reshape(-1)
    shared["rpbr"] = np.concatenate([np.zeros(RPAD, np.float32), rp, np.zeros(RPAD, np.float32)])
    cp = _role_consts("prompt")
    cs = _role_consts("sample")
    zc = {"c_ckv": np.zeros((DEPTH, NCTX, 256), np.float32), "c_kr": np.zeros((DEPTH, NCTX, 32), np.float32),
          "c_nk": np.zeros((DEPTH, NCTX, 512), np.float32), "c_nv": np.zeros((DEPTH, NCTX, 512), np.float32)}
    in_maps = []
    for core in range(8):
        m = dict(shared)
        if core < 4 or core >= 6:
            u = core if core < 4 else core - 6
            m["x0"] = xp[u * 8:(u + 1) * 8].reshape(NT, D)
            m["cvec"] = f32(inputs["c_ctx"]).reshape(1, D)
            m.update(zc)
            m.update(cp)
        else:
            b = core - 4
            m["x0"] = xs[b]
            m["cvec"] = f32(inputs["c"])[b].reshape(1, D)
            m["c_ckv"] = f32(inputs["cache_mla_ckv"])[b]
            m["c_kr"] = f32(inputs["cache_mla_krope"])[b]
            m["c_nk"] = f32(inputs["cache_na_k"])[b].reshape(DEPTH, NCTX, 512)
            m["c_nv"] = f32(inputs["cache_na_v"])[b].reshape(DEPTH, NCTX, 512)
            m.update(cs)
        in_maps.append(m)
    nc = _get_nc(nstages)
    if cores is not None:
        res = run_bass_kernel_spmd(nc, [in_maps[c] for c in cores], core_ids=list(range(len(cores))))
        return {c: res.results[i] for i, c in enumerate(cores)}
    res = run_bass_kernel_spmd(nc, in_maps, core_ids=list(range(8)))
    return res.results


def kernel(**inputs):
    r = run_units(inputs)
    yp = np.concatenate([r[u]["y"].reshape(8, 256, D) for u in range(4)], 0)
    ys = np.stack([r[4]["y"], r[5]["y"]], 0)

    def gather(name, tail):
        a = np.concatenate([r[u][name].reshape(DEPTH, 8, 256, -1).transpose(1, 0, 2, 3) for u in range(4)], 0)
        return np.ascontiguousarray(a.reshape((32, DEPTH, 256) + tail), dtype=np.float32)

    return (np.ascontiguousarray(yp, dtype=np.float32), np.ascontiguousarray(ys, dtype=np.float32),
            gather("o_ckv", (256,)), gather("o_kr", (32,)), gather("o_nk", (8, 64)), gather("o_nv", (8, 64)))
```

```python
import math
from collections import defaultdict

import numpy as np
import ml_dtypes

import concourse.bass as bass
import concourse.mybir as mybir
from concourse.bass_utils import run_bass_kernel_spmd

F32 = mybir.dt.float32
BF16 = mybir.dt.bfloat16
AF = mybir.ActivationFunctionType
ALU = mybir.AluOpType

D = 1024
KC = 8
DEPTH = 4
NT = 2048
NTILE = 16
NCTX = 512
NK = NT + NCTX
NKT = NK // 128
FF = 2816
FC = FF // 128
IN_W = 2720
C_F, C_Q, C_KV, C_R, C_NQ, C_NK, C_NV = 0, 512, 896, 1152, 1184, 1696, 2208
ALPHA = (2.0 * DEPTH) ** 0.25
MLA_SCALE = 96 ** -0.5
NA_SCALE = 0.125
BIG_MLA = 576.0
BIG_NA = 480.0
NEG = -30000.0
NPAT = 12
RPAD = 64


class _PEProxy:
    def __init__(self, pe):
        self.pe = pe
        self.last_stop = None

    def matmul(self, *a, **kw):
        self.last_stop = kw.get("stop", None)
        return self.pe.matmul(*a, **kw)

    def transpose(self, *a, **kw):
        self.last_stop = True
        return self.pe.transpose(*a, **kw)


class Sched:
    ENGS = ("pe", "act", "dve", "pool", "sp")

    def __init__(self, nc, n_dma_sems=56):
        self.nc = nc
        self.eng = {"pe": nc.tensor, "act": nc.scalar, "dve": nc.vector, "pool": nc.gpsimd, "sp": nc.sync}
        self.esem = {e: nc.alloc_semaphore(f"es_{e}") for e in self.ENGS}
        self.pos = {e: 0 for e in self.ENGS}
        self.sigs = {e: [] for e in self.ENGS}
        self.sigval = {e: 0 for e in self.ENGS}
        self.last = {e: None for e in self.ENGS}
        self.waited = defaultdict(int)
        self.dsems = [nc.alloc_semaphore(f"ds_{i}") for i in range(n_dma_sems)]
        self.dcount = [0] * n_dma_sems
        self.dnext = 0
        self.dnext_pool = 0
        self.W = defaultdict(dict)
        self.R = defaultdict(dict)
        self.peproxy = _PEProxy(nc.tensor)

    def _need(self, eng, tok, raw):
        if tok[0] == "d":
            _, idx, val = tok
            return (("d", idx), self.dsems[idx], val)
        _, f, p = tok
        if f == eng and eng == "pe":
            return None
        val = None
        for (sp_, sv) in reversed(self.sigs[f]):
            if sp_ >= p:
                val = sv
            else:
                break
        if val is None:
            ins, lp = self.last[f]
            assert lp >= p
            self.sigval[f] += 1
            ins.then_inc(self.esem[f], 1)
            self.sigs[f].append((lp, self.sigval[f]))
            val = self.sigval[f]
        return (("e", f), self.esem[f], val)

    def _waits(self, eng, reads, writes):
        needs = {}

        def add(tok, raw):
            n = self._need(eng, tok, raw)
            if n:
                key, sem, val = n
                if key not in needs or needs[key][0] < val:
                    needs[key] = (val, sem)

        for k in reads:
            for t in self.W[k].values():
                add(t, True)
            if k.startswith("ps"):
                for rk, r in self.R[k].items():
                    if rk != eng:
                        add(r, False)
        for k in writes:
            for r in self.R[k].values():
                add(r, False)
        for key, (val, sem) in needs.items():
            if self.waited[(eng, key)] < val:
                self.eng[eng].wait_ge(sem, val)
                self.waited[(eng, key)] = val

    def _record(self, tok, reads, writes, rkey):
        for k in writes:
            self.W[k][rkey] = tok
        for k in reads:
            self.R[k][rkey] = tok

    def op(self, eng, fn, reads=(), writes=(), check_writes=True):
        self._waits(eng, reads, writes if check_writes else ())
        if eng == "pe":
            self.peproxy.last_stop = None
            ins = fn(self.peproxy)
            sig = bool(self.peproxy.last_stop)
        else:
            ins = fn(self.eng[eng])
            sig = True
        self.pos[eng] += 1
        p = self.pos[eng]
        self.last[eng] = (ins, p)
        if sig:
            self.sigval[eng] += 1
            ins.then_inc(self.esem[eng], 1)
            self.sigs[eng].append((p, self.sigval[eng]))
        self._record(("e", eng, p), reads, writes, eng)
        return ins

    def mm(self, fn, reads=(), writes=(), first=True):
        return self.op("pe", fn, reads, writes, check_writes=first)

    def dma(self, q, out, in_, reads=(), writes=(), **kw):
        half = len(self.dsems) // 2
        if q == "pool":
            idx = self.dnext_pool
            self.dnext_pool = (self.dnext_pool + 1) % half
        else:
            idx = half + self.dnext
            self.dnext = (self.dnext + 1) % (len(self.dsems) - half)
        if self.dcount[idx] and self.waited[(q, ("d", idx))] < self.dcount[idx]:
            self.eng[q].wait_ge(self.dsems[idx], self.dcount[idx])
            self.waited[(q, ("d", idx))] = self.dcount[idx]
        self._waits(q, reads, writes)
        self.eng[q].dma_start(out=out, in_=in_, **kw).then_inc(self.dsems[idx], 16)
        self.dcount[idx] += 16
        tok = ("d", idx, self.dcount[idx])
        self._record(tok, reads, writes, ("d", idx))
        return tok

    def wait_all_dma(self, eng="sp"):
        for idx, c in enumerate(self.dcount):
            if c and self.waited[(eng, ("d", idx))] < c:
                self.eng[eng].wait_ge(self.dsems[idx], c)
                self.waited[(eng, ("d", idx))] = c

    def barrier(self):
        toks = []
        for f in self.ENGS:
            if self.last[f] is not None:
                toks.append(("e", f, self.last[f][1]))
        for e in self.ENGS:
            needs = {}
            for t in toks:
                n = self._need(e, t, True)
                if n:
                    key, sem, val = n
                    if key not in needs or needs[key][0] < val:
                        needs[key] = (val, sem)
            for key, (val, sem) in needs.items():
                if self.waited[(e, key)] < val:
                    self.eng[e].wait_ge(sem, val)
                    self.waited[(e, key)] = val
            self.wait_all_dma(e)
        self.W = defaultdict(dict)
        self.R = defaultdict(dict)


def na_window(i):
    if i <= 1:
        tiles, typ = [0, 1, 2, 3], 1
    elif i >= 14:
        tiles, typ = [12, 13, 14, 15], 1
    else:
        tiles, typ = [i - 2, i - 1, i, i + 1, i + 2], 0
    out = []
    for j in tiles:
        dl = j - i
        pat = (dl + 2) if typ == 0 else (5 + dl + 3)
        out.append((j, pat))
    return out


PAT_DELTA = [(-2, 0), (-1, 0), (0, 0), (1, 0), (2, 0)] + [(d, 1) for d in range(-3, 4)]


def build(nstages=3 * DEPTH):
    nc = bass.Bass("TRN2", target_bir_lowering=False)
    S = Sched(nc)

    def din(name, shape, dt=F32):
        return nc.dram_tensor(name, list(shape), dt, kind="ExternalInput")

    x0 = din("x0", [NT, D])
    cvec = din("cvec", [1, D])
    c_ckv = din("c_ckv", [DEPTH, NCTX, 256])
    c_kr = din("c_kr", [DEPTH, NCTX, 32])
    c_nk = din("c_nk", [DEPTH, NCTX, 512])
    c_nv = din("c_nv", [DEPTH, NCTX, 512])
    w_ada = din("w_ada", [DEPTH, D, 9 * D])
    b_ada = din("b_ada", [DEPTH, 9 * D])
    fw = {}
    for nm in ("ffn1_w1", "ffn1_w3", "ffn2_w1", "ffn2_w3"):
        fw[nm] = din(nm, [DEPTH, D, FF])
    for nm in ("ffn1_w2", "ffn2_w2"):
        fw[nm] = din(nm, [DEPTH, FF, D])
    w_in = din("w_in", [DEPTH, D, IN_W])
    q_norm = din("mla_q_norm", [DEPTH, 384])
    w_uq = din("mla_w_uq", [DEPTH, 384, 768])
    kv_norm = din("mla_kv_norm", [DEPTH, 256])
    w_ukv = din("mla_w_ukv", [DEPTH, 256, 1024])
    rpbr = din("rpbr", [2 * RPAD + DEPTH * 8 * 15 * 31])
    w_bf = din("w_branch_f", [DEPTH, 512, D])
    w_bm = din("w_branch_m", [DEPTH, 512, D])
    w_bn = din("w_branch_n", [DEPTH, 512, D])
    w_gate = din("w_gate", [DEPTH, D, 3 * D])
    b_gate = din("b_gate", [DEPTH, 3 * D])
    w_out = din("w_out", [DEPTH, D, D])
    ln_g = din("ln_g", [DEPTH, 3, D])
    ln_b = din("ln_b", [DEPTH, 3, D])
    ropeC = din("ropeC", [32, NT])
    ropeS = din("ropeS", [32, NT])
    protT = din("protT", [32, 32])
    indq_m = din("indq_m", [8, NT])
    indq_n = din("indq_n", [8, NT])
    indk = din("indk", [8, NK])
    dftC = din("dftC", [NT, NT], BF16)
    dftS = din("dftS", [NT, NT], BF16)
    dftCS = din("dftCS", [128, 256])
    m1d = din("m1d", [128, NPAT, 128])
    m2d = din("m2d", [128, NPAT, 128])
    identd = din("identd", [128, 128])

    def dout(name, shape):
        return nc.dram_tensor(name, list(shape), F32, kind="ExternalOutput")

    y = dout("y", [NT, D])
    o_ckv = dout("o_ckv", [DEPTH, NT, 256])
    o_kr = dout("o_kr", [DEPTH, NT, 32])
    o_nk = dout("o_nk", [DEPTH, NT, 512])
    o_nv = dout("o_nv", [DEPTH, NT, 512])

    xa = nc.dram_tensor("xa", [NT, D], F32, kind="Internal")
    xb = nc.dram_tensor("xb", [NT, D], F32, kind="Internal")
    ada_d = nc.dram_tensor("ada_d", [DEPTH, 9 * D], F32, kind="Internal")
    btd = nc.dram_tensor("btd", [DEPTH, 8, NPAT, 128, 128], F32, kind="Internal")

    def AP(t, off, dims):
        return bass.AP(t, off, [list(d) for d in dims])

    _uid = [0]
    _orig_sbuf_tensor = nc.sbuf_tensor

    def _sbuf_tensor(name, shape, dt):
        _uid[0] += 1
        return _orig_sbuf_tensor(f"{name}_{_uid[0]}", shape, dt)

    sb = nc.alloc_sbuf_tensor
    ident = sb("ident", [128, 128], BF16)
    ones = sb("ones", [128, 128], BF16)
    epsc = sb("epsc", [128, 4], F32)
    identf = sb("identf", [128, 128], F32)
    onesf = sb("onesf", [128, 64], F32)
    PS = [nc.alloc_psum_tensor(f"ps{i}", [128, 512], F32) for i in range(8)]
    PSB = PS[7].bitcast(BF16)
    PK = [f"ps{i}" for i in range(8)]

    S.dma("sp", identf[:], identd.ap(), writes=["identf"])
    S.op("dve", lambda e: e.tensor_copy(out=ident[:], in_=identf[:]), reads=["identf"], writes=["ident"])
    S.op("dve", lambda e: e.memset(ones[:], 1.0), writes=["ones"])
    S.op("dve", lambda e: e.memset(onesf[:], 1.0), writes=["ones"])
    S.op("dve", lambda e: e.memset(epsc[:, 0:1], 1e-6), writes=["epsc"])
    S.op("dve", lambda e: e.memset(epsc[:, 1:2], 1e-5), writes=["epsc"])

    evac_rr = [0]

    def evac_eng():
        evac_rr[0] += 1
        return "act" if evac_rr[0] % 2 else "dve"

    def copy_op(eng, out, in_):
        if eng == "act":
            return lambda e: e.activation(out=out, in_=in_, func=AF.Copy)
        return lambda e: e.tensor_copy(out=out, in_=in_)

    def prologue():
        with _sbuf_tensor("crow", [8, 128], F32) as crow, \
                _sbuf_tensor("srow", [8, 128], BF16) as srow, \
                _sbuf_tensor("scT", [128, 8], BF16) as scT, \
                _sbuf_tensor("wada", [128, 2, 8, 512], BF16) as wada, \
                _sbuf_tensor("brow", [1, 2, 512], F32) as brow, \
                _sbuf_tensor("orow", [1, 2, 512], F32) as orow:
            S.dma("sp", crow[:], cvec.ap().rearrange("o (k p) -> (o k) p", p=128), writes=["crow"])
            S.op("act", lambda e: e.activation(out=srow[:], in_=crow[:], func=AF.Silu), reads=["crow"], writes=["srow"])
            S.mm(lambda e: e.matmul(PS[0][:, 0:8], lhsT=srow[:], rhs=ident[0:8, 0:8], start=True, stop=True),
                 reads=["srow", "ident"], writes=[PK[0]])
            S.op("dve", lambda e: e.tensor_copy(out=scT[:], in_=PS[0][:, 0:8]), reads=[PK[0]], writes=["scT"])
            it = 0
            for l in range(DEPTH):
                wv = w_ada.ap()[l].rearrange("(k p) n -> p k n", p=128)
                for j in range(18):
                    b = it % 2
                    it += 1
                    S.dma("pool", wada[:, b], wv[:, :, j * 512:(j + 1) * 512], writes=[f"wada{b}"])
                    S.dma("sp", brow[:, b], b_ada.ap()[l:l + 1, j * 512:(j + 1) * 512], writes=[f"brow{b}"])
                    pk = PK[b]
                    for k in range(KC):
                        S.mm(lambda e, k=k, b=b: e.matmul(PS[b][0:1, :], lhsT=scT[:, k:k + 1], rhs=wada[:, b, k, :],
                                                          start=(k == 0), stop=(k == KC - 1)),
                             reads=["scT", f"wada{b}"], writes=[pk], first=(k == 0))
                    S.op("dve", lambda e, b=b: e.tensor_tensor(out=orow[:, b], in0=PS[b][0:1, :], in1=brow[:, b], op=ALU.add),
                         reads=[pk, f"brow{b}"], writes=[f"orow{b}"])
                    S.dma("sp", ada_d.ap()[l:l + 1, j * 512:(j + 1) * 512], orow[:, b], reads=[f"orow{b}"], writes=["ada_d"])
        for l in range(DEPTH):
            for p, (dl, typ) in enumerate(PAT_DELTA):
                for kr in range(2):
                    for qr in range(2):
                        dr = 2 * dl + kr - qr + 7
                        drc = min(max(dr, 0), 14)
                        src = AP(rpbr, RPAD + ((l * 8) * 15 + drc) * 31 + 15, [[465, 8], [-1, 64], [1, 64]])
                        dst = AP(btd, (l * 8 * NPAT + p) * 16384 + kr * 64 * 128 + qr * 64,
                                 [[NPAT * 16384, 8], [128, 64], [1, 64]])
                        S.dma("sp", dst, src, writes=["btd"])
        S.barrier()

    def load_colvec(src_t, off, n, dst, dkey):
        with _sbuf_tensor("cvrow", [32, 128], F32) as row:
            S.dma("sp", row[0:n, :], AP(src_t, off, [[128, n], [1, 128]]), writes=["cvrow"])
            S.mm(lambda e: e.transpose(out=PS[6][:, 0:n], in_=row[0:n, :], identity=identf[0:n, 0:n]),
                 reads=["cvrow", "identf"], writes=[PK[6]])
            S.op("dve", lambda e: e.tensor_copy(out=dst, in_=PS[6][:, 0:n]), reads=[PK[6]], writes=[dkey])
            S.barrier()

    def load_mod(l, shift_idx, scale_idx, V):
        load_colvec(ada_d, l * 9 * D + shift_idx * D, 8, V["shT"][:], "shT")
        load_colvec(ada_d, l * 9 * D + scale_idx * D, 8, V["scT"][:], "scT")
        S.op("dve", lambda e: e.tensor_scalar(out=V["scT"][:], in0=V["scT"][:], scalar1=1.0, scalar2=None, op0=ALU.add),
             reads=["scT"], writes=["scT"])

    def load_epi(l, gate_idx, ln_idx, gate_coef, V):
        S.dma("sp", V["gate_bc"][:], AP(ada_d, l * 9 * D + gate_idx * D, [[0, 128], [1, D]]), reads=["ada_d"], writes=["gate_bc"])
        S.dma("sp", V["lng_bc"][:], AP(ln_g, (l * 3 + ln_idx) * D, [[0, 128], [1, D]]), writes=["lng_bc"])
        S.dma("sp", V["lnb_bc"][:], AP(ln_b, (l * 3 + ln_idx) * D, [[0, 128], [1, D]]), writes=["lnb_bc"])
        if gate_coef != 1.0:
            S.op("pool", lambda e: e.tensor_scalar(out=V["gate_bc"][:], in0=V["gate_bc"][:], scalar1=gate_coef, scalar2=None, op0=ALU.mult),
                 reads=["gate_bc"], writes=["gate_bc"])

    NTAG = 5

    def run_staged(n, stages):
        k = len(stages)
        for step in range(n + k - 1):
            for s_ in reversed(range(k)):
                t = step - s_
                if 0 <= t < n:
                    stages[s_](t)

    def ln_stage_fns(x_of, xkey_of, T, eps_col):
        st, mv, rs, nb = T["st"], T["mv"], T["rstd"], T["nb"]

        def A(t):
            g = t % NTAG
            xt, xkey = x_of(t), xkey_of(t)
            S.op("dve", lambda e: e.bn_stats(out=st[:, g, 0:6], in_=xt[:, 0:512]), reads=[xkey], writes=[f"lnsa{g}"])
            S.op("dve", lambda e: e.bn_stats(out=st[:, g, 6:12], in_=xt[:, 512:1024]), reads=[xkey], writes=[f"lnsb{g}"])
            S.op("dve", lambda e: e.bn_aggr(out=mv[:, g, :], in_=st[:, g, :]), reads=[f"lnsa{g}", f"lnsb{g}"], writes=[f"lnmv{g}"])

        def B(t):
            g = t % NTAG
            S.op("act", lambda e: e.activation(out=rs[:, g:g + 1], in_=mv[:, g, 1:2], func=AF.Sqrt, bias=epsc[:, eps_col:eps_col + 1], scale=1.0),
                 reads=[f"lnmv{g}", "epsc"], writes=[f"lnrs{g}"])

        def C(t):
            g = t % NTAG
            S.op("dve", lambda e: e.reciprocal(out=rs[:, g:g + 1], in_=rs[:, g:g + 1]), reads=[f"lnrs{g}"], writes=[f"lnrs{g}"])
            S.op("dve", lambda e: e.scalar_tensor_tensor(out=nb[:, g:g + 1], in0=mv[:, g, 0:1], scalar=-1.0, in1=rs[:, g:g + 1],
                                                         op0=ALU.mult, op1=ALU.mult),
                 reads=[f"lnmv{g}", f"lnrs{g}"], writes=[f"lnnb{g}"])

        return [A, B, C]

    def ln_in_stages(x_of, xkey_of, T, V, hT, hkey, tcol_of):
        xn = T["xn"]

        def D(t):
            g = t % NTAG
            S.op("act", lambda e: e.activation(out=xn[:, g, :], in_=x_of(t), func=AF.Identity, scale=T["rstd"][:, g:g + 1], bias=T["nb"][:, g:g + 1]),
                 reads=[xkey_of(t), f"lnrs{g}", f"lnnb{g}"], writes=[f"xn{g}"])

        def E(t):
            g = t % NTAG
            tcol = tcol_of(t)
            for kk in range(KC):
                S.mm(lambda e, kk=kk: e.transpose(out=PSB[:, kk * 128:(kk + 1) * 128], in_=xn[:, g, kk * 128:(kk + 1) * 128], identity=ident[:]),
                     reads=[f"xn{g}", "ident"], writes=["ps7"], first=(kk == 0))
            evac_rr[0] += 1
            teng = "dve" if evac_rr[0] % 2 == 0 else "act"
            for kk in range(KC):
                if teng == "dve":
                    fn = lambda e, kk=kk: e.tensor_scalar(out=hT[:, kk, tcol:tcol + 128], in0=PSB[:, kk * 128:(kk + 1) * 128],
                                                          scalar1=V["scT"][:, kk:kk + 1], scalar2=V["shT"][:, kk:kk + 1], op0=ALU.mult, op1=ALU.add)
                else:
                    fn = lambda e, kk=kk: e.activation(out=hT[:, kk, tcol:tcol + 128], in_=PSB[:, kk * 128:(kk + 1) * 128], func=AF.Identity,
                                                       scale=V["scT"][:, kk:kk + 1], bias=V["shT"][:, kk:kk + 1])
                S.op(teng, fn, reads=["ps7", "scT", "shT"], writes=[hkey])

        return ln_stage_fns(x_of, xkey_of, T, 0) + [D, E]

    def ln_out_stages(z_of, zkey_of, T, V, dst_of):
        def D(t):
            g = t % NTAG
            z, zk = z_of(t), zkey_of(t)
            S.op("act", lambda e: e.activation(out=z, in_=z, func=AF.Identity, scale=T["rstd"][:, g:g + 1], bias=T["nb"][:, g:g + 1]),
                 reads=[zk, f"lnrs{g}", f"lnnb{g}"], writes=[zk])

        def E(t):
            z, zk = z_of(t), zkey_of(t)
            S.op("dve", lambda e: e.tensor_tensor(out=z, in0=z, in1=V["lng_bc"][:], op=ALU.mult), reads=[zk, "lng_bc"], writes=[zk])
            S.op("pool", lambda e: e.tensor_tensor(out=z, in0=z, in1=V["lnb_bc"][:], op=ALU.add), reads=[zk, "lnb_bc"], writes=[zk])
            for d_ in dst_of(t):
                S.dma("sp", d_, z, reads=[zk], writes=["xdram"])

        return ln_stage_fns(z_of, zkey_of, T, 1) + [D, E]

    def alloc_ln(stack):
        V = {}
        V["shT"] = stack.enter_context(_sbuf_tensor("shT", [128, 8], F32))
        V["scT"] = stack.enter_context(_sbuf_tensor("scT1", [128, 8], F32))
        T = {}
        T["st"] = stack.enter_context(_sbuf_tensor("st", [128, NTAG, 12], F32))
        T["mv"] = stack.enter_context(_sbuf_tensor("mv", [128, NTAG, 2], F32))
        T["rstd"] = stack.enter_context(_sbuf_tensor("rstd", [128, NTAG], F32))
        T["nb"] = stack.enter_context(_sbuf_tensor("nb", [128, NTAG], F32))
        return V, T

    def alloc_vec(stack, V):
        for nm in ("gate_bc", "lng_bc", "lnb_bc"):
            V[nm] = stack.enter_context(_sbuf_tensor(nm, [128, D], F32))

    from contextlib import ExitStack

    def ffn(l, which, xin, xouts):
        w1 = fw[f"ffn{which}_w1"].ap()[l].rearrange("(k p) n -> p k n", p=128)
        w3 = fw[f"ffn{which}_w3"].ap()[l].rearrange("(k p) n -> p k n", p=128)
        w2 = fw[f"ffn{which}_w2"].ap()[l].rearrange("(f p) n -> p f n", p=128)
        base = 0 if which == 1 else 6
        with ExitStack() as st:
            V, T = alloc_ln(st)
            T["xn"] = st.enter_context(_sbuf_tensor("xn", [128, NTAG, D], BF16))
            alloc_vec(st, V)
            xp = st.enter_context(_sbuf_tensor("xp", [128, 8, D], F32))
            hT = st.enter_context(_sbuf_tensor("hTf", [128, KC, 1024], BF16))
            gT = st.enter_context(_sbuf_tensor("gT", [128, FC, 1024], BF16))
            w13 = st.enter_context(_sbuf_tensor("w13", [128, 2, 2, KC, 256], BF16))
            w2b = st.enter_context(_sbuf_tensor("w2b", [128, 2, FC, 256], BF16))
            sg = st.enter_context(_sbuf_tensor("sg", [128, 2, 512], BF16))
            tmp = st.enter_context(_sbuf_tensor("tmpf", [128, 2, 256], F32))
            import os
            CUT = int(os.environ.get("KDBG_CUT", "99"))
            load_mod(l, base + 0, base + 1, V)
            load_epi(l, base + 2, 0 if which == 1 else 2, 0.5, V)
            if CUT <= 0:
                S.barrier()
                return
            wit = 0
            w2it = 0
            for p in range(2):
                for t in range(8):
                    S.dma("sp", xp[:, t, :], xin[(p * 8 + t) * 128:(p * 8 + t + 1) * 128, :], reads=["xdram"], writes=[f"xp{t}"])
                if CUT <= 1:
                    S.barrier()
                    return
                run_staged(8, ln_in_stages(lambda t: xp[:, t, :], lambda t: f"xp{t}", T, V, hT, "hTf", lambda t: t * 128))
                if CUT <= 2:
                    S.barrier()
                    return
                for f2 in range(FC // 2):
                    b = wit % 2
                    wit += 1
                    S.dma("pool", w13[:, b, 0], w1[:, :, f2 * 256:(f2 + 1) * 256], writes=[f"w1b{b}"])
                    S.dma("pool", w13[:, b, 1], w3[:, :, f2 * 256:(f2 + 1) * 256], writes=[f"w3b{b}"])
                    for fi in range(2):
                        f = f2 * 2 + fi
                        for half in range(2):
                            pa, pb_ = (0, 1) if half == 0 else (2, 3)
                            for k in range(KC):
                                S.mm(lambda e, k=k, b=b, fi=fi, half=half, pa=pa: e.matmul(
                                    PS[pa][:, :], lhsT=w13[:, b, 0, k, fi * 128:(fi + 1) * 128], rhs=hT[:, k, half * 512:(half + 1) * 512],
                                    start=(k == 0), stop=(k == KC - 1)),
                                    reads=[f"w1b{b}", "hTf"], writes=[PK[pa]], first=(k == 0))
                            for k in range(KC):
                                S.mm(lambda e, k=k, b=b, fi=fi, half=half, pb_=pb_: e.matmul(
                                    PS[pb_][:, :], lhsT=w13[:, b, 1, k, fi * 128:(fi + 1) * 128], rhs=hT[:, k, half * 512:(half + 1) * 512],
                                    start=(k == 0), stop=(k == KC - 1)),
                                    reads=[f"w3b{b}", "hTf"], writes=[PK[pb_]], first=(k == 0))
                            S.op("act", lambda e, half=half, pa=pa: e.activation(out=sg[:, half, :], in_=PS[pa][:, :], func=AF.Silu),
                                 reads=[PK[pa]], writes=[f"sg{half}"])
                            S.op("dve", lambda e, half=half, pb_=pb_, f=f: e.tensor_tensor(
                                out=gT[:, f, half * 512:(half + 1) * 512], in0=PS[pb_][:, :], in1=sg[:, half, :], op=ALU.mult),
                                reads=[PK[pb_], f"sg{half}"], writes=["gT"])
                if CUT <= 3:
                    S.barrier()
                    return
                for oq in range(4):
                    b = w2it % 2
                    w2it += 1
                    S.dma("pool", w2b[:, b], w2[:, :, oq * 256:(oq + 1) * 256], writes=[f"w2b{b}"])
                    for t in range(8):
                        pi = 4 + (t % 2)
                        for f in range(FC):
                            S.mm(lambda e, f=f, t=t, b=b, pi=pi: e.matmul(
                                PS[pi][:, 0:256], lhsT=gT[:, f, t * 128:(t + 1) * 128], rhs=w2b[:, b, f, :],
                                start=(f == 0), stop=(f == FC - 1)),
                                reads=["gT", f"w2b{b}"], writes=[PK[pi]], first=(f == 0))
                        tb = t % 2
                        S.op("dve", lambda e, pi=pi, tb=tb, oq=oq: e.tensor_tensor(
                            out=tmp[:, tb, :], in0=PS[pi][:, 0:256], in1=V["gate_bc"][:, oq * 256:(oq + 1) * 256], op=ALU.mult),
                            reads=[PK[pi], "gate_bc"], writes=[f"tmpf{tb}"])
                        S.op("dve", lambda e, t=t, tb=tb, oq=oq: e.scalar_tensor_tensor(
                            out=xp[:, t, oq * 256:(oq + 1) * 256], in0=xp[:, t, oq * 256:(oq + 1) * 256], scalar=ALPHA,
                            in1=tmp[:, tb, :], op0=ALU.mult, op1=ALU.add),
                            reads=[f"tmpf{tb}", f"xp{t}"], writes=[f"xp{t}"])
                if CUT <= 4:
                    S.barrier()
                    return
                run_staged(8, ln_out_stages(lambda t: xp[:, t, :], lambda t: f"xp{t}", T, V,
                                            lambda t, p=p: [xo[(p * 8 + t) * 128:(p * 8 + t + 1) * 128, :] for xo in xouts]))
        S.barrier()

    class _Cut(Exception):
        pass

    def mcut(n):
        import os
        if int(os.environ.get("KDBG_MCUT", "99")) <= n:
            raise _Cut()

    def mixer(l, xin, xouts):
        try:
            mixer_(l, xin, xouts)
        except _Cut:
            pass
        S.barrier()

    def mixer_(l, xin, xouts):
        win = w_in.ap()[l].rearrange("(k p) n -> p k n", p=128)
        with ExitStack() as st:
            V, T = alloc_ln(st)
            hT = st.enter_context(_sbuf_tensor("hTm", [128, KC, NT], BF16))
            specT = st.enter_context(_sbuf_tensor("specT", [128, 4, NT], BF16))
            omT = st.enter_context(_sbuf_tensor("omT", [128, 4, NT], BF16))
            onT = st.enter_context(_sbuf_tensor("onT", [128, 4, NT], BF16))
            load_mod(l, 3, 4, V)
            with _sbuf_tensor("xt2", [128, 6, D], F32) as xt2, _sbuf_tensor("xn", [128, NTAG, D], BF16) as xn_:
                T["xn"] = xn_

                def m0_load(t):
                    S.dma("sp", xt2[:, t % 6, :], xin[t * 128:(t + 1) * 128, :], reads=["xdram"], writes=[f"xt2{t % 6}"])
                run_staged(NTILE, [m0_load] + ln_in_stages(lambda t: xt2[:, t % 6, :], lambda t: f"xt2{t % 6}", T, V, hT, "hTm",
                                                           lambda t: t * 128))
            S.barrier()
            mcut(0)

            with ExitStack() as s2:
                wf = s2.enter_context(_sbuf_tensor("wf", [128, KC, 512], BF16))
                cs = s2.enter_context(_sbuf_tensor("cs", [128, 256], BF16))
                AB = s2.enter_context(_sbuf_tensor("AB", [128, NTILE, 4, 256], BF16))
                ufT = s2.enter_context(_sbuf_tensor("ufT", [128, 2, 512], BF16))
                dbuf = s2.enter_context(_sbuf_tensor("dbuf", [128, 2, 2, 8, 512], BF16))
                S.dma("pool", wf[:], win[:, :, C_F:C_F + 512], writes=["wf"])
                S.dma("pool", cs[:], dftCS.ap(), writes=["cs"])
                it = 0
                for c in range(4):
                    for g in range(4):
                        b = it % 2
                        it += 1
                        for k in range(KC):
                            S.mm(lambda e, k=k, g=g, c=c, b=b: e.matmul(PS[b][:, :], lhsT=wf[:, k, g * 128:(g + 1) * 128],
                                                                   rhs=hT[:, k, c * 512:(c + 1) * 512], start=(k == 0), stop=(k == KC - 1)),
                                 reads=["wf", "hTm"], writes=[PK[b]], first=(k == 0))
                        S.op("act", copy_op("act", ufT[:, b, :], PS[b][:, :]), reads=[PK[b]], writes=[f"ufT{b}"])
                        for tt in range(4):
                            t = c * 4 + tt
                            S.mm(lambda e, tt=tt, b=b: e.matmul(PS[2][:, tt * 256:(tt + 1) * 256] if tt < 2 else PS[3][:, (tt - 2) * 256:(tt - 1) * 256],
                                                           lhsT=ufT[:, b, tt * 128:(tt + 1) * 128], rhs=cs[:], start=True, stop=True),
                                 reads=[f"ufT{b}", "cs"], writes=[PK[2] if tt < 2 else PK[3]])
                        for hh in range(2):
                            S.op("dve", lambda e, hh=hh, c=c, g=g: e.tensor_copy(
                                out=AB[:, c * 4 + hh * 2:c * 4 + hh * 2 + 2, g, :],
                                in_=PS[2 + hh][:, :].rearrange("p (t n) -> p t n", n=256)),
                                reads=[PK[2 + hh]], writes=["AB"])
                dC = dftC.ap().rearrange("(t p) n -> p t n", p=128)
                dS = dftS.ap().rearrange("(t p) n -> p t n", p=128)
                dit = 0
                for c in range(4):
                    for half in range(2):
                        b = dit % 2
                        dit += 1
                        S.dma("sp", dbuf[:, b, 0], dC[:, half * 8:(half + 1) * 8, c * 512:(c + 1) * 512], writes=[f"dC{b}"])
                        S.dma("sp", dbuf[:, b, 1], dS[:, half * 8:(half + 1) * 8, c * 512:(c + 1) * 512], writes=[f"dS{b}"])
                        for g in range(4):
                            for lt in range(8):
                                tl = half * 8 + lt
                                S.mm(lambda e, g=g, lt=lt, tl=tl, b=b, half=half: e.matmul(
                                    PS[g][:, :], lhsT=AB[:, tl, g, 0:128], rhs=dbuf[:, b, 0, lt, :],
                                    start=(half == 0 and lt == 0), stop=False),
                                    reads=["AB", f"dC{b}"], writes=[PK[g]], first=(half == 0 and lt == 0))
                                S.mm(lambda e, g=g, lt=lt, tl=tl, b=b, half=half: e.matmul(
                                    PS[g][:, :], lhsT=AB[:, tl, g, 128:256], rhs=dbuf[:, b, 1, lt, :],
                                    start=False, stop=(half == 1 and lt == 7)),
                                    reads=["AB", f"dS{b}"], writes=[PK[g]], first=False)
                    for g in range(4):
                        eng = evac_eng()
                        S.op(eng, copy_op(eng, specT[:, g, c * 512:(c + 1) * 512], PS[g][:, :]), reads=[PK[g]], writes=["specT"])
            S.barrier()

            mcut(1)
            with ExitStack() as s2:
                knT = s2.enter_context(_sbuf_tensor("knT", [128, 4, NK], BF16))
                qnT = s2.enter_context(_sbuf_tensor("qnT", [128, 4, NT], BF16))
                Vn = s2.enter_context(_sbuf_tensor("Vn", [128, NKT, 8, 65], BF16))
                with ExitStack() as s3:
                    wq = s3.enter_context(_sbuf_tensor("wq", [128, KC, 512], BF16))
                    wk = s3.enter_context(_sbuf_tensor("wk", [128, KC, 512], BF16))
                    wv = s3.enter_context(_sbuf_tensor("wv", [128, KC, 512], BF16))
                    ck = s3.enter_context(_sbuf_tensor("ck", [128, 4, 512], BF16))
                    of32 = s3.enter_context(_sbuf_tensor("of32", [128, 2, 512], F32))
                    S.dma("pool", wq[:], win[:, :, C_NQ:C_NQ + 512], writes=["wq"])
                    S.dma("pool", wk[:], win[:, :, C_NK:C_NK + 512], writes=["wk"])
                    S.dma("pool", wv[:], win[:, :, C_NV:C_NV + 512], writes=["wv"])
                    S.dma("pool", ck[:], c_nk.ap()[l].rearrange("(t p) n -> p t n", p=128), writes=["ck"])
                    for j in range(4):
                        S.dma("pool", Vn[:, NTILE + j, :, 0:64], c_nv.ap()[l, j * 128:(j + 1) * 128, :].rearrange("p (h d) -> p h d", d=64), writes=["Vnc"])
                    S.op("pool", lambda e: e.memset(Vn[:, :, :, 64:65], 1.0), writes=["Vn1"])
                    it = 0
                    for t in range(NTILE):
                        for (wsb, wkey, odst, isv) in ((wk, "wk", o_nk, False), (wv, "wv", o_nv, True)):
                            b = it % 2
                            it += 1
                            for k in range(KC):
                                S.mm(lambda e, k=k, t=t, b=b, wsb=wsb: e.matmul(PS[b][:, :], lhsT=hT[:, k, t * 128:(t + 1) * 128], rhs=wsb[:, k, :],
                                                                               start=(k == 0), stop=(k == KC - 1)),
                                     reads=["hTm", wkey], writes=[PK[b]], first=(k == 0))
                            S.op("act", copy_op("act", of32[:, b, :], PS[b][:, :]), reads=[PK[b]], writes=[f"of32{b}"])
                            if isv:
                                S.op("dve", lambda e, t=t, b=b: e.tensor_copy(out=Vn[:, t, :, 0:64], in_=PS[b][:, :].rearrange("p (h d) -> p h d", d=64)),
                                     reads=[PK[b]], writes=["Vn"])
                            S.dma("sp", odst.ap()[l, t * 128:(t + 1) * 128, :], of32[:, b, :], reads=[f"of32{b}"])
                    for pr in range(4):
                        for c in range(4):
                            for (wsb, wkey, dstT, dkey) in ((wk, "wk", knT, "knT"), (wq, "wq", qnT, "qnT")):
                                b = it % 2
                                it += 1
                                for k in range(KC):
                                    S.mm(lambda e, k=k, pr=pr, c=c, b=b, wsb=wsb: e.matmul(
                                        PS[b][:, :], lhsT=wsb[:, k, pr * 128:(pr + 1) * 128], rhs=hT[:, k, c * 512:(c + 1) * 512],
                                        start=(k == 0), stop=(k == KC - 1)),
                                        reads=[wkey, "hTm"], writes=[PK[b]], first=(k == 0))
                                eng = evac_eng()
                                S.op(eng, copy_op(eng, dstT[:, pr, c * 512:(c + 1) * 512], PS[b][:, :]), reads=[PK[b]], writes=[dkey])
                        for j in range(4):
                            S.mm(lambda e, j=j, pr=pr: e.transpose(out=PSB[:, j * 128:(j + 1) * 128], in_=ck[:, j, pr * 128:(pr + 1) * 128], identity=ident[:]),
                                 reads=["ck", "ident"], writes=["ps7"], first=(j == 0))
                        S.op("dve", lambda e, pr=pr: e.tensor_copy(out=knT[:, pr, NT:NK], in_=PSB[:, 0:512]), reads=["ps7"], writes=["knT"])
                S.barrier()
                mcut(2)
                s3 = s2
                iq = s3.enter_context(_sbuf_tensor("iq", [72, NT], BF16))
                ik = s3.enter_context(_sbuf_tensor("ik", [72, NK], BF16))
                m1 = s3.enter_context(_sbuf_tensor("m1", [128, NPAT, 128], BF16))
                m2 = s3.enter_context(_sbuf_tensor("m2", [128, NPAT, 128], BF16))
                btf = s3.enter_context(_sbuf_tensor("btf", [128, NPAT, 128], F32))
                BT = s3.enter_context(_sbuf_tensor("BT", [128, 2, NPAT, 128], BF16))
                pT = s3.enter_context(_sbuf_tensor("pT", [128, 2, 9, 128], BF16))
                osb = s3.enter_context(_sbuf_tensor("osb", [65, 2, 512], F32))
                otmp = s3.enter_context(_sbuf_tensor("otmp", [64, 2, 512], BF16))
                for pb_ in (0, 64):
                    S.dma("pool", iq[pb_:pb_ + 8, :], indq_n.ap(), writes=["iq"])
                    S.dma("pool", ik[pb_:pb_ + 8, :], indk.ap(), writes=["ik"])
                S.dma("pool", m1[:], m1d.ap(), writes=["m1"])
                S.dma("pool", m2[:], m2d.ap(), writes=["m2"])
                items = [(h, c, qi) for h in range(8) for c in range(4) for qi in range(4)]

                def na_bt(h):
                    hb = h % 2
                    S.dma("sp", btf[:], AP(btd, ((l * 8 + h) * NPAT) * 16384, [[128, 128], [16384, NPAT], [1, 128]]),
                          reads=["btd"], writes=["btf"])
                    S.op("dve", lambda e: e.tensor_tensor(out=btf[:], in0=btf[:], in1=m1[:], op=ALU.mult),
                         reads=["btf", "m1"], writes=["btf"])
                    S.op("dve", lambda e: e.tensor_tensor(out=BT[:, hb], in0=btf[:], in1=m2[:], op=ALU.add),
                         reads=["btf", "m2"], writes=[f"BT{hb}"])

                def na_slots(i):
                    return [(j, pat) for (j, pat) in na_window(i)] + [(NTILE + j, None) for j in range(4)]

                def na_scores(k):
                    h, c, qi = items[k]
                    pr, pb, hb = h // 2, 64 * (h % 2), h % 2
                    i = c * 4 + qi
                    ab = k % 2
                    banks = [ab * 3 + 0, ab * 3 + 1, ab * 3 + 2]
                    for si, (j, pat) in enumerate(na_slots(i)):
                        bk = banks[si // 4]
                        col = (si % 4) * 128
                        S.mm(lambda e, bk=bk, col=col, j=j: e.matmul(
                            PS[bk][:, col:col + 128], lhsT=knT[pb:pb + 64, pr, j * 128:(j + 1) * 128],
                            rhs=qnT[pb:pb + 64, pr, i * 128:(i + 1) * 128], start=True, stop=False),
                            reads=["knT", "qnT"], writes=[PK[bk]], first=(si % 4 == 0))
                        S.mm(lambda e, bk=bk, col=col, j=j, pat=pat: e.matmul(
                            PS[bk][:, col:col + 128], lhsT=ik[pb:pb + 8, j * 128:(j + 1) * 128], rhs=iq[pb:pb + 8, i * 128:(i + 1) * 128],
                            start=False, stop=(pat is None)),
                            reads=["ik", "iq"], writes=[PK[bk]], first=False)
                        if pat is not None:
                            S.mm(lambda e, bk=bk, col=col, pat=pat: e.matmul(
                                PS[bk][:, col:col + 128], lhsT=ident[:], rhs=BT[:, hb, pat, :], start=False, stop=True),
                                reads=["ident", f"BT{hb}"], writes=[PK[bk]], first=False)

                def na_exp(k):
                    h, c, qi = items[k]
                    i = c * 4 + qi
                    ab = k % 2
                    ns = len(na_slots(i))
                    for g in range(3):
                        n_in = min(4, ns - g * 4)
                        if n_in <= 0:
                            continue
                        bk = ab * 3 + g
                        S.op("act", lambda e, bk=bk, g=g, n_in=n_in: e.activation(
                            out=pT[:, ab, g * 4:g * 4 + n_in, :], in_=PS[bk][:, 0:n_in * 128].rearrange("p (s n) -> p s n", n=128),
                            func=AF.Exp, scale=NA_SCALE),
                            reads=[PK[bk]], writes=[f"pT{ab}"])

                def na_pv(k):
                    h, c, qi = items[k]
                    i = c * 4 + qi
                    ab = k % 2
                    po = 6 + ((h * 4 + c) % 2)
                    slots = na_slots(i)
                    ns = len(slots)
                    for si, (j, pat) in enumerate(slots):
                        S.mm(lambda e, si=si, j=j: e.matmul(
                            PS[po][0:65, qi * 128:(qi + 1) * 128], lhsT=Vn[:, j, h, :], rhs=pT[:, ab, si, :],
                            start=(si == 0), stop=(si == ns - 1)),
                            reads=["Vn", "Vnc", "Vn1", f"pT{ab}"], writes=[PK[po]], first=(si == 0 and qi == 0))

                def na_norm_a(k):
                    h, c, qi = items[k]
                    ob = (h * 4 + c) % 2
                    po = 6 + ob
                    S.op("act", copy_op("act", osb[:, ob, :], PS[po][0:65, :]), reads=[PK[po]], writes=[f"osb{ob}"])
                    S.op("dve", lambda e: e.reciprocal(out=osb[64:65, ob, :], in_=osb[64:65, ob, :]), reads=[f"osb{ob}"], writes=[f"osb{ob}"])

                def na_norm_b(k):
                    h, c, qi = items[k]
                    pr, hb = h // 2, h % 2
                    ob = (h * 4 + c) % 2
                    po = 6 + ob
                    S.mm(lambda e: e.matmul(PS[po][0:64, :], lhsT=onesf[64:65, 0:64], rhs=osb[64:65, ob, :], start=True, stop=True),
                         reads=["ones", f"osb{ob}"], writes=[PK[po]])
                    if hb == 0:
                        S.op("dve", lambda e: e.tensor_tensor(
                            out=onT[0:64, pr, c * 512:(c + 1) * 512], in0=PS[po][0:64, :], in1=osb[0:64, ob, :], op=ALU.mult),
                            reads=[PK[po], f"osb{ob}"], writes=["onT"])
                    else:
                        S.op("dve", lambda e: e.tensor_tensor(
                            out=otmp[:, ob, :], in0=PS[po][0:64, :], in1=osb[0:64, ob, :], op=ALU.mult),
                            reads=[PK[po], f"osb{ob}"], writes=[f"otmp{ob}"])
                        S.dma("sp", onT[64:128, pr, c * 512:(c + 1) * 512], otmp[:, ob, :], reads=[f"otmp{ob}"], writes=["onT"])

                na_bt(0)
                na_scores(0)
                npend = []
                for k in range(len(items)):
                    na_exp(k)
                    if k + 1 < len(items):
                        if items[k + 1][0] != items[k][0]:
                            na_bt(items[k + 1][0])
                        na_scores(k + 1)
                    for pd in list(npend):
                        if k >= pd[0]:
                            na_norm_b(pd[1])
                            npend.remove(pd)
                    na_pv(k)
                    if items[k][2] == 3:
                        na_norm_a(k)
                        npend.append((k + 1, k))
                for pd in npend:
                    na_norm_b(pd[1])
            S.barrier()

            mcut(3)
            with ExitStack() as s2:
                wuq = s2.enter_context(_sbuf_tensor("wuq", [128, 3, 768], BF16))
                wukv = s2.enter_context(_sbuf_tensor("wukv", [128, 2, 2, 8, 64], BF16))
                qn = s2.enter_context(_sbuf_tensor("qn", [128, 3, NT], BF16))
                ckvT = s2.enter_context(_sbuf_tensor("ckvT", [128, 2, NK], BF16))
                kall = s2.enter_context(_sbuf_tensor("kall", [104, 2, NK], BF16))
                qall = s2.enter_context(_sbuf_tensor("qall", [104, 2, NT], BF16))
                wuqr = s2.enter_context(_sbuf_tensor("wuqr", [128, 3, 8, 32], BF16))
                Vm = s2.enter_context(_sbuf_tensor("Vm", [128, NKT, 8, 65], BF16))
                rC = s2.enter_context(_sbuf_tensor("rC", [96, NT], BF16))
                rS = s2.enter_context(_sbuf_tensor("rS", [96, NT], BF16))
                t1 = s2.enter_context(_sbuf_tensor("t1", [96, 2, 512], F32))
                t2 = s2.enter_context(_sbuf_tensor("t2", [96, 2, 512], F32))
                S.dma("pool", wuq[:], w_uq.ap()[l].rearrange("(k p) n -> p k n", p=128), writes=["wuq"])
                for k2 in range(2):
                    for tt in range(2):
                        S.dma("pool", wukv[:, k2, tt],
                              w_ukv.ap()[l, k2 * 128:(k2 + 1) * 128, :].rearrange("p (h t d) -> p t h d", h=8, t=2)[:, tt], writes=["wukv"])
                S.dma("pool", rC[64:96, :], ropeC.ap(), writes=["rC"])
                S.dma("pool", rS[64:96, :], ropeS.ap(), writes=["rS"])
                for hb_ in range(2):
                    S.dma("pool", kall[96:104, hb_, :], indk.ap(), writes=["krTi"])
                    S.dma("pool", qall[96:104, hb_, :], indq_m.ap(), writes=["qrTi"])

                def rot_weights(dst4, src4, keys_r, key_w):
                    S.op("dve", lambda e: e.tensor_scalar(out=dst4[:, :, :, 0:8], in0=src4[:, :, :, 8:16], scalar1=-1.0, scalar2=None, op0=ALU.mult),
                         reads=keys_r, writes=[key_w])
                    S.op("dve", lambda e: e.tensor_copy(out=dst4[:, :, :, 8:16], in_=src4[:, :, :, 0:8]), reads=keys_r, writes=[key_w])

                for j in range(3):
                    rot_weights(wuqr[:, j].rearrange("p h (a d) -> p h a d", d=16),
                                wuq[:, j, :].rearrange("p (h x) -> p h x", x=96)[:, :, 64:96].rearrange("p h (a d) -> p h a d", d=16),
                                ["wuq"], "wuqr")
                S.op("pool", lambda e: e.memset(Vm[:, :, :, 64:65], 1.0), writes=["Vm1"])

                def rope2(ps_x, pkx, ps_r, pkr, b, dsts, dkey, c):
                    S.op("dve", lambda e: e.tensor_tensor(out=t1[64:96, b, :], in0=ps_x, in1=rC[64:96, c * 512:(c + 1) * 512], op=ALU.mult),
                         reads=[pkx, "rC"], writes=[f"t1{b}"])
                    S.op("dve", lambda e: e.tensor_tensor(out=t2[64:96, b, :], in0=ps_r, in1=rS[64:96, c * 512:(c + 1) * 512], op=ALU.mult),
                         reads=[pkr, "rS"], writes=[f"t2{b}"])
                    for dst in dsts:
                        S.op("pool", lambda e, dst=dst: e.tensor_tensor(out=dst, in0=t1[64:96, b, :], in1=t2[64:96, b, :], op=ALU.add),
                             reads=[f"t1{b}", f"t2{b}"], writes=[dkey])

                with ExitStack() as s3:
                    wqa = s3.enter_context(_sbuf_tensor("wqa", [128, KC, 384], BF16))
                    wkr = s3.enter_context(_sbuf_tensor("wkr", [128, KC, 288], BF16))
                    wkrr = s3.enter_context(_sbuf_tensor("wkrr", [128, KC, 32], BF16))
                    gq = s3.enter_context(_sbuf_tensor("gq", [128, 3], F32))
                    gkv = s3.enter_context(_sbuf_tensor("gkv", [128, 256], F32))
                    cc = s3.enter_context(_sbuf_tensor("cc", [128, 4, 256], BF16))
                    ckr = s3.enter_context(_sbuf_tensor("ckr", [128, 4, 32], BF16))
                    uq = s3.enter_context(_sbuf_tensor("uq", [128, 3, 512], F32))
                    sq = s3.enter_context(_sbuf_tensor("sq", [128, 3, 512], BF16))
                    rq = s3.enter_context(_sbuf_tensor("rq", [128, 512], F32))
                    ukv = s3.enter_context(_sbuf_tensor("ukv", [128, 4, 288], F32))
                    kst = s3.enter_context(_sbuf_tensor("kst", [128, 5, 6], F32))
                    kmv = s3.enter_context(_sbuf_tensor("kmv", [128, 5, 2], F32))
                    ssk = s3.enter_context(_sbuf_tensor("ssk", [128, 5], F32))
                    ckf = s3.enter_context(_sbuf_tensor("ckf", [128, 2, 256], F32))
                    ckb = s3.enter_context(_sbuf_tensor("ckb", [128, 2, 256], BF16))
                    S.dma("pool", wqa[:], win[:, :, C_Q:C_Q + 384], writes=["wqa"])
                    S.dma("pool", wkr[:], win[:, :, C_KV:C_KV + 288], writes=["wkr"])
                    rot_weights(wkrr[:].rearrange("p k (a d) -> p k a d", d=16),
                                wkr[:, :, 256:288].rearrange("p k (a d) -> p k a d", d=16), ["wkr"], "wkrr")
                    load_colvec(q_norm, l * 384, 3, gq[:], "gq")
                    S.dma("sp", gkv[:], AP(kv_norm, l * 256, [[0, 128], [1, 256]]), writes=["gkv"])
                    S.dma("pool", cc[:], c_ckv.ap()[l].rearrange("(t p) n -> p t n", p=128), writes=["cc"])
                    S.dma("pool", ckr[:], c_kr.ap()[l].rearrange("(t p) n -> p t n", p=128), writes=["ckr"])
                    it = 0
                    for c in range(4):
                        for j in range(3):
                            b = it % 2
                            it += 1
                            for k in range(KC):
                                S.mm(lambda e, k=k, j=j, c=c, b=b: e.matmul(PS[b][:, :], lhsT=wqa[:, k, j * 128:(j + 1) * 128],
                                                                       rhs=hT[:, k, c * 512:(c + 1) * 512], start=(k == 0), stop=(k == KC - 1)),
                                     reads=["wqa", "hTm"], writes=[PK[b]], first=(k == 0))
                            S.op("act", lambda e, j=j, b=b: e.activation(out=sq[:, j, :], in_=PS[b][:, :], func=AF.Square), reads=[PK[b]], writes=[f"sq{j}"])
                            S.op("dve", lambda e, j=j, b=b: e.tensor_copy(out=uq[:, j, :], in_=PS[b][:, :]), reads=[PK[b]], writes=[f"uq{j}"])
                        for j in range(3):
                            S.mm(lambda e, j=j: e.matmul(PS[2][:, :], lhsT=ones[:], rhs=sq[:, j, :], start=(j == 0), stop=(j == 2)),
                                 reads=["ones", f"sq{j}"], writes=[PK[2]], first=(j == 0))
                        S.op("act", lambda e: e.activation(out=rq[:], in_=PS[2][:, :], func=AF.Sqrt, scale=1.0 / 384, bias=epsc[:, 0:1]),
                             reads=[PK[2], "epsc"], writes=["rq"])
                        S.op("dve", lambda e: e.reciprocal(out=rq[:], in_=rq[:]), reads=["rq"], writes=["rq"])
                        for j in range(3):
                            S.op("dve", lambda e, j=j, c=c: e.scalar_tensor_tensor(out=qn[:, j, c * 512:(c + 1) * 512], in0=uq[:, j, :], scalar=gq[:, j:j + 1],
                                                                                  in1=rq[:], op0=ALU.mult, op1=ALU.mult),
                                 reads=[f"uq{j}", "gq", "rq"], writes=["qn"])
                    def kv0(t):
                        b, ub = t % 2, t % 4
                        for k in range(KC):
                            S.mm(lambda e, k=k: e.matmul(PS[b][:, 0:288], lhsT=hT[:, k, t * 128:(t + 1) * 128], rhs=wkr[:, k, :],
                                                         start=(k == 0), stop=(k == KC - 1)),
                                 reads=["hTm", "wkr"], writes=[PK[b]], first=(k == 0))
                        S.op("act", copy_op("act", ukv[:, ub, :], PS[b][:, 0:288]), reads=[PK[b]], writes=[f"ukv{ub}"])
                        S.dma("sp", o_kr.ap()[l, t * 128:(t + 1) * 128, :], ukv[:, ub, 256:288], reads=[f"ukv{ub}"])

                    def kv1(t):
                        ub, g = t % 4, t % 5
                        S.op("dve", lambda e: e.bn_stats(out=kst[:, g, :], in_=ukv[:, ub, 0:256]), reads=[f"ukv{ub}"], writes=[f"kst{g}"])
                        S.op("dve", lambda e: e.bn_aggr(out=kmv[:, g, :], in_=kst[:, g, :]), reads=[f"kst{g}"], writes=[f"kmv{g}"])
                        S.op("dve", lambda e: e.scalar_tensor_tensor(out=ssk[:, g:g + 1], in0=kmv[:, g, 0:1], scalar=kmv[:, g, 0:1], in1=kmv[:, g, 1:2],
                                                                     op0=ALU.mult, op1=ALU.add),
                             reads=[f"kmv{g}"], writes=[f"ssk{g}"])

                    def kv2(t):
                        g = t % 5
                        S.op("act", lambda e: e.activation(out=ssk[:, g:g + 1], in_=ssk[:, g:g + 1], func=AF.Sqrt, scale=1.0, bias=epsc[:, 0:1]),
                             reads=[f"ssk{g}", "epsc"], writes=[f"ssk{g}"])

                    def kv3(t):
                        b, ub, g = t % 2, t % 4, t % 5
                        S.op("dve", lambda e: e.reciprocal(out=ssk[:, g:g + 1], in_=ssk[:, g:g + 1]), reads=[f"ssk{g}"], writes=[f"ssk{g}"])
                        S.op("dve", lambda e: e.scalar_tensor_tensor(out=ckf[:, b, :], in0=ukv[:, ub, 0:256], scalar=ssk[:, g:g + 1], in1=gkv[:],
                                                                     op0=ALU.mult, op1=ALU.mult),
                             reads=[f"ukv{ub}", f"ssk{g}", "gkv"], writes=[f"ckf{b}"])
                        S.dma("sp", o_ckv.ap()[l, t * 128:(t + 1) * 128, :], ckf[:, b, :], reads=[f"ckf{b}"])
                        S.op("pool", lambda e: e.tensor_copy(out=ckb[:, b, :], in_=ckf[:, b, :]), reads=[f"ckf{b}"], writes=[f"ckb{b}"])

                    def kv4(t):
                        b = t % 2
                        for k2 in range(2):
                            S.mm(lambda e, k2=k2: e.transpose(out=PSB[:, k2 * 128:(k2 + 1) * 128], in_=ckb[:, b, k2 * 128:(k2 + 1) * 128], identity=ident[:]),
                                 reads=[f"ckb{b}", "ident"], writes=["ps7"], first=(k2 == 0))
                        S.op("dve", lambda e: e.tensor_copy(out=ckvT[:, :, t * 128:(t + 1) * 128], in_=PSB[:, 0:256].rearrange("p (k n) -> p k n", n=128)),
                             reads=["ps7"], writes=["ckvT"])

                    run_staged(NTILE, [kv0, kv1, kv2, kv3, kv4])
                    for j in range(4):
                        for k2 in range(2):
                            S.mm(lambda e, k2=k2, j=j: e.transpose(out=PSB[:, k2 * 128:(k2 + 1) * 128], in_=cc[:, j, k2 * 128:(k2 + 1) * 128], identity=ident[:]),
                                 reads=["cc", "ident"], writes=["ps7"], first=(k2 == 0))
                        S.op("dve", lambda e, j=j: e.tensor_copy(out=ckvT[:, :, NT + j * 128:NT + (j + 1) * 128],
                                                                in_=PSB[:, 0:256].rearrange("p (k n) -> p k n", n=128)),
                             reads=["ps7"], writes=["ckvT"])
                    for j in range(4):
                        S.mm(lambda e, j=j: e.matmul(PS[3][64:96, j * 128:(j + 1) * 128], lhsT=ckr[:, j, :], rhs=ident[:], start=True, stop=True),
                             reads=["ckr", "ident"], writes=[PK[3]], first=(j == 0))
                    for hb_ in range(2):
                        S.op("dve", lambda e, hb_=hb_: e.tensor_copy(out=kall[64:96, hb_, NT:NK], in_=PS[3][64:96, :]), reads=[PK[3]], writes=["krT"])
                    for c in range(4):
                        b = c % 2
                        for k in range(KC):
                            S.mm(lambda e, k=k, c=c, b=b: e.matmul(PS[b][64:96, :], lhsT=wkr[:, k, 256:288], rhs=hT[:, k, c * 512:(c + 1) * 512],
                                                                  start=(k == 0), stop=(k == KC - 1)),
                                 reads=["wkr", "hTm"], writes=[PK[b]], first=(k == 0))
                        for k in range(KC):
                            S.mm(lambda e, k=k, c=c, b=b: e.matmul(PS[2 + b][64:96, :], lhsT=wkrr[:, k, :], rhs=hT[:, k, c * 512:(c + 1) * 512],
                                                                  start=(k == 0), stop=(k == KC - 1)),
                                 reads=["wkrr", "hTm"], writes=[PK[2 + b]], first=(k == 0))
                        rope2(PS[b][64:96, :], PK[b], PS[2 + b][64:96, :], PK[2 + b], b,
                              [kall[64:96, 0, c * 512:(c + 1) * 512], kall[64:96, 1, c * 512:(c + 1) * 512]], "krT", c)
                    for t in range(NKT):
                        b = t % 2
                        for k2 in range(2):
                            S.mm(lambda e, k2=k2, t=t, b=b: e.matmul(PS[b][:, :], lhsT=ckvT[:, k2, t * 128:(t + 1) * 128],
                                                                    rhs=wukv[:, k2, 1].rearrange("p h d -> p (h d)"),
                                                                    start=(k2 == 0), stop=(k2 == 1)),
                                 reads=["ckvT", "wukv"], writes=[PK[b]], first=(k2 == 0))
                        eng = evac_eng()
                        S.op(eng, copy_op(eng, Vm[:, t, :, 0:64], PS[b][:, :].rearrange("p (h d) -> p h d", d=64)), reads=[PK[b]], writes=["Vm"])
                S.barrier()
                mcut(4)
                s3 = s2
                pTm = s3.enter_context(_sbuf_tensor("pTm", [128, 4, 512], BF16))
                osb = s3.enter_context(_sbuf_tensor("osbm", [65, 2, 512], F32))
                otmp = s3.enter_context(_sbuf_tensor("otmpm", [64, 2, 512], BF16))

                def head_pieces(h):
                    hb_ = h % 2
                    pcs = []
                    for c5 in range(5):
                        def pk_(c5=c5):
                            for k2 in range(2):
                                S.mm(lambda e, k2=k2: e.matmul(PS[6][0:64, :], lhsT=wukv[:, k2, 0, h, :], rhs=ckvT[:, k2, c5 * 512:(c5 + 1) * 512],
                                                               start=(k2 == 0), stop=(k2 == 1)),
                                     reads=["wukv", "ckvT"], writes=[PK[6]], first=(k2 == 0))
                            S.op("dve", lambda e: e.tensor_copy(out=kall[0:64, hb_, c5 * 512:(c5 + 1) * 512], in_=PS[6][0:64, :]),
                                 reads=[PK[6]], writes=[f"knp{hb_}"])
                        pcs.append(pk_)
                    for c in range(4):
                        def pq_(c=c):
                            for j in range(3):
                                S.mm(lambda e, j=j: e.matmul(PS[6][0:64, :], lhsT=wuq[:, j, h * 96:h * 96 + 64], rhs=qn[:, j, c * 512:(c + 1) * 512],
                                                             start=(j == 0), stop=(j == 2)),
                                     reads=["wuq", "qn"], writes=[PK[6]], first=(j == 0))
                            for j in range(3):
                                S.mm(lambda e, j=j: e.matmul(PS[6][64:96, :], lhsT=wuq[:, j, h * 96 + 64:h * 96 + 96], rhs=qn[:, j, c * 512:(c + 1) * 512],
                                                             start=(j == 0), stop=(j == 2)),
                                     reads=["wuq", "qn"], writes=[PK[6]], first=False)
                            for j in range(3):
                                S.mm(lambda e, j=j: e.matmul(PS[7][64:96, :], lhsT=wuqr[:, j, h, :], rhs=qn[:, j, c * 512:(c + 1) * 512],
                                                             start=(j == 0), stop=(j == 2)),
                                     reads=["wuqr", "qn"], writes=[PK[7]], first=(j == 0))
                            S.op("dve", lambda e: e.tensor_copy(out=qall[0:64, hb_, c * 512:(c + 1) * 512], in_=PS[6][0:64, :]),
                                 reads=[PK[6]], writes=[f"qnp{hb_}"])
                            rope2(PS[6][64:96, :], PK[6], PS[7][64:96, :], PK[7], c % 2, [qall[64:96, hb_, c * 512:(c + 1) * 512]], f"qrT{hb_}", c)
                        pcs.append(pq_)
                    return pcs

                for pc in head_pieces(0):
                    pc()
                sit = 0
                import os
                mdbg = os.environ.get("KDBG_MLA", "")
                for h in range(8):
                    pr, hh = h // 2, h % 2
                    hb = h % 2
                    nxt = head_pieces(h + 1) if h + 1 < 8 else []
                    LA = 2
                    items = [(c, j) for c in range(4) for j in range(NKT)]
                    pend = []

                    def mla_scores(c, j, sidx, hb=hb):
                        bk = sidx % 4
                        S.mm(lambda e: e.matmul(PS[bk][:, :], lhsT=kall[:, hb, j * 128:(j + 1) * 128], rhs=qall[:, hb, c * 512:(c + 1) * 512],
                                                start=True, stop=True),
                             reads=[f"knp{hb}", f"qnp{hb}", "krT", "krTi", f"qrT{hb}", "qrTi"], writes=[PK[bk]], first=True)

                    def mla_exp_pv(c, j, sidx, h=h):
                        bk = sidx % 4
                        po = 4 + (c % 2)
                        S.op("act", lambda e: e.activation(out=pTm[:, bk, :], in_=PS[bk][:, :], func=AF.Exp, scale=MLA_SCALE),
                             reads=[PK[bk]], writes=[f"pTm{bk}"])
                        S.mm(lambda e: e.matmul(PS[po][0:65, :], lhsT=Vm[:, j, h, :], rhs=pTm[:, bk, :], start=(j == 0), stop=(j == NKT - 1)),
                             reads=["Vm", "Vm1", f"pTm{bk}"], writes=[PK[po]], first=(j == 0))

                    def mla_norm_a(c, h=h):
                        po = 4 + (c % 2)
                        ob = c % 2
                        S.op("act", copy_op("act", osb[:, ob, :], PS[po][0:65, :]), reads=[PK[po]], writes=[f"osb{ob}"])
                        S.op("dve", lambda e: e.reciprocal(out=osb[64:65, ob, :], in_=osb[64:65, ob, :]), reads=[f"osb{ob}"], writes=[f"osb{ob}"])

                    def mla_norm_b(c, h=h, pr=pr, hh=hh):
                        po = 4 + (c % 2)
                        ob = c % 2
                        S.mm(lambda e: e.matmul(PS[po][0:64, :], lhsT=onesf[64:65, 0:64], rhs=osb[64:65, ob, :], start=True, stop=True),
                             reads=["ones", f"osb{ob}"], writes=[PK[po]])
                        if hh == 0:
                            S.op("dve", lambda e: e.tensor_tensor(
                                out=omT[0:64, pr, c * 512:(c + 1) * 512], in0=PS[po][0:64, :], in1=osb[0:64, ob, :], op=ALU.mult),
                                reads=[PK[po], f"osb{ob}"], writes=["omT"])
                        else:
                            S.op("dve", lambda e: e.tensor_tensor(
                                out=otmp[:, ob, :], in0=PS[po][0:64, :], in1=osb[0:64, ob, :], op=ALU.mult),
                                reads=[PK[po], f"osb{ob}"], writes=[f"otmp{ob}"])
                            S.dma("sp", omT[64:128, pr, c * 512:(c + 1) * 512], otmp[:, ob, :], reads=[f"otmp{ob}"], writes=["omT"])

                    n_it = len(items)
                    for k in range(n_it + LA):
                        if k < n_it:
                            mla_scores(items[k][0], items[k][1], sit + k)
                        for pd in list(pend):
                            if k >= pd[0]:
                                mla_norm_b(pd[1])
                                pend.remove(pd)
                        if k >= LA:
                            c_, j_ = items[k - LA]
                            mla_exp_pv(c_, j_, sit + k - LA)
                            if j_ == NKT - 1:
                                mla_norm_a(c_)
                                pend.append((k + 3, c_))
                        if nxt and k % 6 == 3:
                            nxt.pop(0)()
                    for pd in pend:
                        mla_norm_b(pd[1])
                    for pc in nxt:
                        pc()
                    sit += n_it
            S.barrier()

            mcut(5)
            with ExitStack() as s2:
                mT = s2.enter_context(_sbuf_tensor("mT", [128, KC, NT], BF16))
                wg = s2.enter_context(_sbuf_tensor("wg", [128, 2, 3, KC, 128], BF16))
                wb_ = s2.enter_context(_sbuf_tensor("wbr", [128, 2, 3, 4, 128], BF16))
                bgT = s2.enter_context(_sbuf_tensor("bgT", [128, 24], F32))
                gsb = s2.enter_context(_sbuf_tensor("gsb", [128, 2, 512], BF16))
                acc = s2.enter_context(_sbuf_tensor("acc", [128, 2, 512], F32))
                tm = s2.enter_context(_sbuf_tensor("tm", [128, 2, 512], F32))
                wo = s2.enter_context(_sbuf_tensor("wo", [128, KC, D], BF16))
                tmp = s2.enter_context(_sbuf_tensor("tmpm", [128, 5, D], F32))
                xt2 = s2.enter_context(_sbuf_tensor("xt2m", [128, 2, D], F32))
                alloc_vec(s2, V)
                load_epi(l, 5, 1, 1.0, V)
                load_colvec(b_gate, l * 3 * D, 24, bgT[:], "bgT")
                S.dma("pool", wo[:], w_out.ap()[l].rearrange("(k p) n -> p k n", p=128), writes=["wo"])
                wgv = w_gate.ap()[l].rearrange("(k p) (g n) -> p g k n", p=128, g=3)
                brs = [w.ap()[l].rearrange("(k p) n -> p k n", p=128) for w in (w_bf, w_bm, w_bn)]
                bins = [(specT, "specT"), (omT, "omT"), (onT, "onT")]
                git = 0
                for oc in range(KC):
                    wbuf = oc % 2
                    S.dma("pool", wg[:, wbuf], wgv[:, :, :, oc * 128:(oc + 1) * 128], writes=[f"wg{wbuf}"])
                    for gi in range(3):
                        S.dma("pool", wb_[:, wbuf, gi], brs[gi][:, :, oc * 128:(oc + 1) * 128], writes=[f"wbr{wbuf}"])
                    for c in range(4):
                        ab = (oc * 4 + c) % 2
                        for gi in range(3):
                            pg = (git % 2)
                            py = 2 + (git % 2)
                            git += 1
                            for k in range(KC):
                                S.mm(lambda e, k=k, gi=gi, c=c, pg=pg, wbuf=wbuf: e.matmul(
                                    PS[pg][:, :], lhsT=wg[:, wbuf, gi, k, :], rhs=hT[:, k, c * 512:(c + 1) * 512], start=(k == 0), stop=(k == KC - 1)),
                                    reads=[f"wg{wbuf}", "hTm"], writes=[PK[pg]], first=(k == 0))
                            bsrc, bkey = bins[gi]
                            for k4 in range(4):
                                S.mm(lambda e, k4=k4, gi=gi, c=c, py=py, wbuf=wbuf, bsrc=bsrc: e.matmul(
                                    PS[py][:, :], lhsT=wb_[:, wbuf, gi, k4, :], rhs=bsrc[:, k4, c * 512:(c + 1) * 512], start=(k4 == 0), stop=(k4 == 3)),
                                    reads=[f"wbr{wbuf}", bkey], writes=[PK[py]], first=(k4 == 0))
                            gb = git % 2
                            S.op("act", lambda e, pg=pg, gi=gi, oc=oc, gb=gb: e.activation(
                                out=gsb[:, gb, :], in_=PS[pg][:, :], func=AF.Sigmoid, bias=bgT[:, gi * 8 + oc:gi * 8 + oc + 1], scale=1.0),
                                reads=[PK[pg], "bgT"], writes=[f"gsb{gb}"])
                            if gi == 0:
                                S.op("dve", lambda e, py=py, gb=gb, ab=ab: e.tensor_tensor(out=acc[:, ab, :], in0=PS[py][:, :], in1=gsb[:, gb, :], op=ALU.mult),
                                     reads=[PK[py], f"gsb{gb}"], writes=[f"acc{ab}"])
                            else:
                                S.op("dve", lambda e, py=py, gb=gb, ab=ab: e.tensor_tensor(out=tm[:, ab, :], in0=PS[py][:, :], in1=gsb[:, gb, :], op=ALU.mult),
                                     reads=[PK[py], f"gsb{gb}"], writes=[f"tm{ab}"])
                                if gi == 1:
                                    S.op("pool", lambda e, ab=ab: e.tensor_tensor(out=acc[:, ab, :], in0=acc[:, ab, :], in1=tm[:, ab, :], op=ALU.add),
                                         reads=[f"acc{ab}", f"tm{ab}"], writes=[f"acc{ab}"])
                                else:
                                    S.op("pool", lambda e, ab=ab, oc=oc, c=c: e.tensor_tensor(out=mT[:, oc, c * 512:(c + 1) * 512], in0=acc[:, ab, :], in1=tm[:, ab, :], op=ALU.add),
                                         reads=[f"acc{ab}", f"tm{ab}"], writes=["mT"])
                NB = 5

                def mg_load(t):
                    b = t % 2
                    S.dma("sp", xt2[:, b, :], xin[t * 128:(t + 1) * 128, :], reads=["xdram"], writes=[f"xt2{b}"])

                def mg_mm(t):
                    b = t % 2
                    zb = t % NB
                    for half in range(2):
                        pi = 4 + half
                        for k in range(KC):
                            S.mm(lambda e, k=k, half=half, pi=pi: e.matmul(
                                PS[pi][:, :], lhsT=mT[:, k, t * 128:(t + 1) * 128], rhs=wo[:, k, half * 512:(half + 1) * 512],
                                start=(k == 0), stop=(k == KC - 1)),
                                reads=["mT", "wo"], writes=[PK[pi]], first=(k == 0))
                        S.op("dve", lambda e, pi=pi, half=half: e.tensor_tensor(
                            out=tmp[:, zb, half * 512:(half + 1) * 512], in0=PS[pi][:, :], in1=V["gate_bc"][:, half * 512:(half + 1) * 512], op=ALU.mult),
                            reads=[PK[pi], "gate_bc"], writes=[f"tmpm{zb}"])
                    S.op("dve", lambda e: e.scalar_tensor_tensor(out=tmp[:, zb, :], in0=xt2[:, b, :], scalar=ALPHA, in1=tmp[:, zb, :],
                                                                 op0=ALU.mult, op1=ALU.add),
                         reads=[f"xt2{b}", f"tmpm{zb}"], writes=[f"tmpm{zb}"])

                lo = ln_out_stages(lambda t: tmp[:, t % NB, :], lambda t: f"tmpm{t % NB}", T, V,
                                   lambda t: [xo[t * 128:(t + 1) * 128, :] for xo in xouts])

                def mg_mm_stats(t):
                    mg_mm(t)
                    lo[0](t)

                run_staged(NTILE, [mg_load, mg_mm_stats] + lo[1:])
        S.barrier()

    prologue()
    bufs = [xa.ap(), xb.ap()]
    cur = x0.ap()
    stage = 0
    for l in range(DEPTH):
        for kind in ("ffn1", "mix", "ffn2"):
            if stage >= nstages:
                break
            last = (stage == nstages - 1)
            dst = y.ap() if last else bufs[stage % 2]
            if kind == "ffn1":
                ffn(l, 1, cur, [dst])
            elif kind == "mix":
                mixer(l, cur, [dst])
            else:
                ffn(l, 2, cur, [dst])
            cur = dst
            stage += 1
    for e in ("sp", "pool", "act", "dve", "pe"):
        S.wait_all_dma(e)
    import os
    if os.environ.get("KDBG_STATS"):
        print("SIGVALS", S.sigval, "POS", S.pos, "DMA", max(S.dcount), flush=True)
    return nc


def _bf16(a):
    return np.asarray(a, dtype=np.float32).astype(ml_dtypes.bfloat16)


def _role_consts(role):
    c = {}
    t = np.arange(NT)
    if role == "sample":
        pos = np.stack([t // 64, t % 64], -1).astype(np.float32)
        inv = (10000.0 ** (-np.arange(8, dtype=np.float32) / 8)).astype(np.float32)
        ang = pos[:, :, None] * inv
        ang = np.concatenate([ang, ang], -1)
        cos = np.cos(ang).reshape(NT, 32).T
        sin = np.sin(ang).reshape(NT, 32).T
        c["ropeC"] = np.ascontiguousarray(cos, dtype=np.float32)
        c["ropeS"] = np.ascontiguousarray(sin, dtype=np.float32)
        c["indq_m"] = np.zeros((8, NT), np.float32)
        c["indq_n"] = np.zeros((8, NT), np.float32)
        c["indk"] = np.zeros((8, NK), np.float32)
        L = NT
        blk = np.zeros(NT, np.int64)
        loc = t
    else:
        c["ropeC"] = np.ones((32, NT), np.float32)
        c["ropeS"] = np.zeros((32, NT), np.float32)
        oh = (t[None, :] // 256 == np.arange(8)[:, None]).astype(np.float32)
        c["indq_m"] = oh * BIG_MLA
        c["indq_n"] = oh * BIG_NA
        ik = np.zeros((8, NK), np.float32)
        ik[:, :NT] = oh
        c["indk"] = ik
        L = 256
        blk = t // 256
        loc = t % 256
    norm = 1.0 / math.sqrt(L * 128.0)
    same = (blk[:, None] == blk[None, :])
    ph = (2.0 * np.pi / L) * ((loc[:, None] * loc[None, :]) % L).astype(np.float64)
    c["dftC"] = _bf16(np.where(same, np.cos(ph) * norm, 0.0))
    c["dftS"] = _bf16(np.where(same, -np.sin(ph) * norm, 0.0))
    cc = np.arange(128)
    ph2 = (2.0 * np.pi / 128) * ((cc[:, None] * cc[None, :]) % 128).astype(np.float64)
    c["dftCS"] = np.concatenate([np.cos(ph2), np.sin(ph2)], 1).astype(np.float32)
    m1 = np.zeros((128, NPAT, 128), np.float32)
    m2 = np.zeros((128, NPAT, 128), np.float32)
    if role == "sample":
        kk = np.arange(128)
        kr, kc = kk // 64, kk % 64
        qr, qc = kk // 64, kk % 64
        cstart = np.clip(qc - 8, 0, 48)
        col_ok = (kc[:, None] >= cstart[None, :]) & (kc[:, None] < cstart[None, :] + 16)
        for p, (dl, typ) in enumerate(PAT_DELTA):
            rel = 2 * dl + kr[:, None] - qr[None, :]
            row_ok = ((rel >= -4) & (rel <= 3)) if typ == 0 else np.ones_like(rel, bool)
            ok = col_ok & row_ok
            m1[:, p, :] = np.where(ok, 1.0 / NA_SCALE, 0.0)
            m2[:, p, :] = np.where(ok, 0.0, NEG)
    c["m1d"] = m1
    c["m2d"] = m2
    pr = np.zeros((32, 32), np.float32)
    for a in range(2):
        for j in range(16):
            d = a * 16 + j
            if j < 8:
                pr[d, d + 8] = -1.0
            else:
                pr[d, d - 8] = 1.0
    c["protT"] = np.ascontiguousarray(pr.T)
    c["identd"] = np.eye(128, dtype=np.float32)
    return c


_CACHE = {}


def _get_nc(nstages):
    if nstages not in _CACHE:
        _CACHE[nstages] = build(nstages)
    return _CACHE[nstages]


def run_units(inputs, nstages=3 * DEPTH, cores=None):
    f32 = lambda a: np.ascontiguousarray(np.asarray(a), dtype=np.float32)
    xp = f32(inputs["x_prompt"])
    xs = f32(inputs["x_sample"])
    shared = {}
    for nm in ("w_ada", "b_ada", "ffn1_w1", "ffn1_w3", "ffn1_w2", "ffn2_w1", "ffn2_w3", "ffn2_w2", "w_in",
               "mla_q_norm", "mla_w_uq", "mla_kv_norm", "mla_w_ukv", "w_branch_f", "w_branch_m", "w_branch_n",
               "w_gate", "b_gate", "w_out", "ln_g", "ln_b"):
        shared[nm] = f32(inputs[nm])
    rp = f32(inputs["na_rpb"])[..., ::-1].reshape(-1)
    shared["rpbr"] = np.concatenate([np.zeros(RPAD, np.float32), rp, np.zeros(RPAD, np.float32)])
    cp = _role_consts("prompt")
    cs = _role_consts("sample")
    zc = {"c_ckv": np.zeros((DEPTH, NCTX, 256), np.float32), "c_kr": np.zeros((DEPTH, NCTX, 32), np.float32),
          "c_nk": np.zeros((DEPTH, NCTX, 512), np.float32), "c_nv": np.zeros((DEPTH, NCTX, 512), np.float32)}
    in_maps = []
    for core in range(8):
        m = dict(shared)
        if core < 4 or core >= 6:
            u = core if core < 4 else core - 6
            m["x0"] = xp[u * 8:(u + 1) * 8].reshape(NT, D)
            m["cvec"] = f32(inputs["c_ctx"]).reshape(1, D)
            m.update(zc)
            m.update(cp)
        else:
            b = core - 4
            m["x0"] = xs[b]
            m["cvec"] = f32(inputs["c"])[b].reshape(1, D)
            m["c_ckv"] = f32(inputs["cache_mla_ckv"])[b]
            m["c_kr"] = f32(inputs["cache_mla_krope"])[b]
            m["c_nk"] = f32(inputs["cache_na_k"])[b].reshape(DEPTH, NCTX, 512)
            m["c_nv"] = f32(inputs["cache_na_v"])[b].reshape(DEPTH, NCTX, 512)
            m.update(cs)
        in_maps.append(m)
    nc = _get_nc(nstages)
    if cores is not None:
        res = run_bass_kernel_spmd(nc, [in_maps[c] for c in cores], core_ids=list(range(len(cores))))
        return {c: res.results[i] for i, c in enumerate(cores)}
    res = run_bass_kernel_spmd(nc, in_maps, core_ids=list(range(8)))
    return res.results


def kernel(**inputs):
    r = run_units(inputs)
    yp = np.concatenate([r[u]["y"].reshape(8, 256, D) for u in range(4)], 0)
    ys = np.stack([r[4]["y"], r[5]["y"]], 0)

    def gather(name, tail):
        a = np.concatenate([r[u][name].reshape(DEPTH, 8, 256, -1).transpose(1, 0, 2, 3) for u in range(4)], 0)
        return np.ascontiguousarray(a.reshape((32, DEPTH, 256) + tail), dtype=np.float32)

    return (np.ascontiguousarray(yp, dtype=np.float32), np.ascontiguousarray(ys, dtype=np.float32),
            gather("o_ckv", (256,)), gather("o_kr", (32,)), gather("o_nk", (8, 64)), gather("o_nv", (8, 64)))
```

```python
import math
from collections import defaultdict

import numpy as np
import ml_dtypes

import concourse.bass as bass
import concourse.mybir as mybir
from concourse.bass_utils import run_bass_kernel_spmd

F32 = mybir.dt.float32
BF16 = mybir.dt.bfloat16
AF = mybir.ActivationFunctionType
ALU = mybir.AluOpType

D = 1024
KC = 8
DEPTH = 4
NT = 2048
NTILE = 16
NCTX = 512
NK = NT + NCTX
NKT = NK // 128
FF = 2816
FC = FF // 128
IN_W = 2720
C_F, C_Q, C_KV, C_R, C_NQ, C_NK, C_NV = 0, 512, 896, 1152, 1184, 1696, 2208
ALPHA = (2.0 * DEPTH) ** 0.25
MLA_SCALE = 96 ** -0.5
NA_SCALE = 0.125
BIG_MLA = 576.0
BIG_NA = 480.0
NEG = -30000.0
NPAT = 12
RPAD = 64


class _PEProxy:
    def __init__(self, pe):
        self.pe = pe
        self.last_stop = None

    def matmul(self, *a, **kw):
        self.last_stop = kw.get("stop", None)
        return self.pe.matmul(*a, **kw)

    def transpose(self, *a, **kw):
        self.last_stop = True
        return self.pe.transpose(*a, **kw)


class Sched:
    ENGS = ("pe", "act", "dve", "pool", "sp")

    def __init__(self, nc, n_dma_sems=56):
        self.nc = nc
        self.eng = {"pe": nc.tensor, "act": nc.scalar, "dve": nc.vector, "pool": nc.gpsimd, "sp": nc.sync}
        self.esem = {e: nc.alloc_semaphore(f"es_{e}") for e in self.ENGS}
        self.pos = {e: 0 for e in self.ENGS}
        self.sigs = {e: [] for e in self.ENGS}
        self.sigval = {e: 0 for e in self.ENGS}
        self.last = {e: None for e in self.ENGS}
        self.waited = defaultdict(int)
        self.dsems = [nc.alloc_semaphore(f"ds_{i}") for i in range(n_dma_sems)]
        self.dcount = [0] * n_dma_sems
        self.dnext = 0
        self.dnext_pool = 0
        self.W = defaultdict(dict)
        self.R = defaultdict(dict)
        self.peproxy = _PEProxy(nc.tensor)

    def _need(self, eng, tok, raw):
        if tok[0] == "d":
            _, idx, val = tok
            return (("d", idx), self.dsems[idx], val)
        _, f, p = tok
        if f == eng and eng == "pe":
            return None
        val = None
        for (sp_, sv) in reversed(self.sigs[f]):
            if sp_ >= p:
                val = sv
            else:
                break
        if val is None:
            ins, lp = self.last[f]
            assert lp >= p
            self.sigval[f] += 1
            ins.then_inc(self.esem[f], 1)
            self.sigs[f].append((lp, self.sigval[f]))
            val = self.sigval[f]
        return (("e", f), self.esem[f], val)

    def _waits(self, eng, reads, writes):
        needs = {}

        def add(tok, raw):
            n = self._need(eng, tok, raw)
            if n:
                key, sem, val = n
                if key not in needs or needs[key][0] < val:
                    needs[key] = (val, sem)

        for k in reads:
            for t in self.W[k].values():
                add(t, True)
            if k.startswith("ps"):
                for rk, r in self.R[k].items():
                    if rk != eng:
                        add(r, False)
        for k in writes:
            for r in self.R[k].values():
                add(r, False)
        for key, (val, sem) in needs.items():
            if self.waited[(eng, key)] < val:
                self.eng[eng].wait_ge(sem, val)
                self.waited[(eng, key)] = val

    def _record(self, tok, reads, writes, rkey):
        for k in writes:
            self.W[k][rkey] = tok
        for k in reads:
            self.R[k][rkey] = tok

    def op(self, eng, fn, reads=(), writes=(), check_writes=True):
        self._waits(eng, reads, writes if check_writes else ())
        if eng == "pe":
            self.peproxy.last_stop = None
            ins = fn(self.peproxy)
            sig = bool(self.peproxy.last_stop)
        else:
            ins = fn(self.eng[eng])
            sig = True
        self.pos[eng] += 1
        p = self.pos[eng]
        self.last[eng] = (ins, p)
        if sig:
            self.sigval[eng] += 1
            ins.then_inc(self.esem[eng], 1)
            self.sigs[eng].append((p, self.sigval[eng]))
        self._record(("e", eng, p), reads, writes, eng)
        return ins

    def mm(self, fn, reads=(), writes=(), first=True):
        return self.op("pe", fn, reads, writes, check_writes=first)

    def dma(self, q, out, in_, reads=(), writes=(), **kw):
        half = len(self.dsems) // 2
        if q == "pool":
            idx = self.dnext_pool
            self.dnext_pool = (self.dnext_pool + 1) % half
        else:
            idx = half + self.dnext
            self.dnext = (self.dnext + 1) % (len(self.dsems) - half)
        if self.dcount[idx] and self.waited[(q, ("d", idx))] < self.dcount[idx]:
            self.eng[q].wait_ge(self.dsems[idx], self.dcount[idx])
            self.waited[(q, ("d", idx))] = self.dcount[idx]
        self._waits(q, reads, writes)
        self.eng[q].dma_start(out=out, in_=in_, **kw).then_inc(self.dsems[idx], 16)
        self.dcount[idx] += 16
        tok = ("d", idx, self.dcount[idx])
        self._record(tok, reads, writes, ("d", idx))
        return tok

    def wait_all_dma(self, eng="sp"):
        for idx, c in enumerate(self.dcount):
            if c and self.waited[(eng, ("d", idx))] < c:
                self.eng[eng].wait_ge(self.dsems[idx], c)
                self.waited[(eng, ("d", idx))] = c

    def barrier(self):
        toks = []
        for f in self.ENGS:
            if self.last[f] is not None:
                toks.append(("e", f, self.last[f][1]))
        for e in self.ENGS:
            needs = {}
            for t in toks:
                n = self._need(e, t, True)
                if n:
                    key, sem, val = n
                    if key not in needs or needs[key][0] < val:
                        needs[key] = (val, sem)
            for key, (val, sem) in needs.items():
                if self.waited[(e, key)] < val:
                    self.eng[e].wait_ge(sem, val)
                    self.waited[(e, key)] = val
            self.wait_all_dma(e)
        self.W = defaultdict(dict)
        self.R = defaultdict(dict)


def na_window(i):
    if i <= 1:
        tiles, typ = [0, 1, 2, 3], 1
    elif i >= 14:
        tiles, typ = [12, 13, 14, 15], 1
    else:
        tiles, typ = [i - 2, i - 1, i, i + 1, i + 2], 0
    out = []
    for j in tiles:
        dl = j - i
        pat = (dl + 2) if typ == 0 else (5 + dl + 3)
        out.append((j, pat))
    return out


PAT_DELTA = [(-2, 0), (-1, 0), (0, 0), (1, 0), (2, 0)] + [(d, 1) for d in range(-3, 4)]


def build(nstages=3 * DEPTH):
    nc = bass.Bass("TRN2", target_bir_lowering=False)
    S = Sched(nc)

    def din(name, shape, dt=F32):
        return nc.dram_tensor(name, list(shape), dt, kind="ExternalInput")

    x0 = din("x0", [NT, D])
    cvec = din("cvec", [1, D])
    c_ckv = din("c_ckv", [DEPTH, NCTX, 256])
    c_kr = din("c_kr", [DEPTH, NCTX, 32])
    c_nk = din("c_nk", [DEPTH, NCTX, 512])
    c_nv = din("c_nv", [DEPTH, NCTX, 512])
    w_ada = din("w_ada", [DEPTH, D, 9 * D])
    b_ada = din("b_ada", [DEPTH, 9 * D])
    fw = {}
    for nm in ("ffn1_w1", "ffn1_w3", "ffn2_w1", "ffn2_w3"):
        fw[nm] = din(nm, [DEPTH, D, FF])
    for nm in ("ffn1_w2", "ffn2_w2"):
        fw[nm] = din(nm, [DEPTH, FF, D])
    w_in = din("w_in", [DEPTH, D, IN_W])
    q_norm = din("mla_q_norm", [DEPTH, 384])
    w_uq = din("mla_w_uq", [DEPTH, 384, 768])
    kv_norm = din("mla_kv_norm", [DEPTH, 256])
    w_ukv = din("mla_w_ukv", [DEPTH, 256, 1024])
    rpbr = din("rpbr", [2 * RPAD + DEPTH * 8 * 15 * 31])
    w_bf = din("w_branch_f", [DEPTH, 512, D])
    w_bm = din("w_branch_m", [DEPTH, 512, D])
    w_bn = din("w_branch_n", [DEPTH, 512, D])
    w_gate = din("w_gate", [DEPTH, D, 3 * D])
    b_gate = din("b_gate", [DEPTH, 3 * D])
    w_out = din("w_out", [DEPTH, D, D])
    ln_g = din("ln_g", [DEPTH, 3, D])
    ln_b = din("ln_b", [DEPTH, 3, D])
    ropeC = din("ropeC", [32, NT])
    ropeS = din("ropeS", [32, NT])
    protT = din("protT", [32, 32])
    indq_m = din("indq_m", [8, NT])
    indq_n = din("indq_n", [8, NT])
    indk = din("indk", [8, NK])
    dftC = din("dftC", [NT, NT], BF16)
    dftS = din("dftS", [NT, NT], BF16)
    dftCS = din("dftCS", [128, 256])
    m1d = din("m1d", [128, NPAT, 128])
    m2d = din("m2d", [128, NPAT, 128])
    identd = din("identd", [128, 128])

    def dout(name, shape):
        return nc.dram_tensor(name, list(shape), F32, kind="ExternalOutput")

    y = dout("y", [NT, D])
    o_ckv = dout("o_ckv", [DEPTH, NT, 256])
    o_kr = dout("o_kr", [DEPTH, NT, 32])
    o_nk = dout("o_nk", [DEPTH, NT, 512])
    o_nv = dout("o_nv", [DEPTH, NT, 512])

    xa = nc.dram_tensor("xa", [NT, D], F32, kind="Internal")
    xb = nc.dram_tensor("xb", [NT, D], F32, kind="Internal")
    ada_d = nc.dram_tensor("ada_d", [DEPTH, 9 * D], F32, kind="Internal")
    btd = nc.dram_tensor("btd", [DEPTH, 8, NPAT, 128, 128], F32, kind="Internal")

    def AP(t, off, dims):
        return bass.AP(t, off, [list(d) for d in dims])

    _uid = [0]
    _orig_sbuf_tensor = nc.sbuf_tensor

    def _sbuf_tensor(name, shape, dt):
        _uid[0] += 1
        return _orig_sbuf_tensor(f"{name}_{_uid[0]}", shape, dt)

    sb = nc.alloc_sbuf_tensor
    ident = sb("ident", [128, 128], BF16)
    ones = sb("ones", [128, 128], BF16)
    epsc = sb("epsc", [128, 4], F32)
    identf = sb("identf", [128, 128], F32)
    onesf = sb("onesf", [128, 64], F32)
    PS = [nc.alloc_psum_tensor(f"ps{i}", [128, 512], F32) for i in range(8)]
    PSB = PS[7].bitcast(BF16)
    PK = [f"ps{i}" for i in range(8)]

    S.dma("sp", identf[:], identd.ap(), writes=["identf"])
    S.op("dve", lambda e: e.tensor_copy(out=ident[:], in_=identf[:]), reads=["identf"], writes=["ident"])
    S.op("dve", lambda e: e.memset(ones[:], 1.0), writes=["ones"])
    S.op("dve", lambda e: e.memset(onesf[:], 1.0), writes=["ones"])
    S.op("dve", lambda e: e.memset(epsc[:, 0:1], 1e-6), writes=["epsc"])
    S.op("dve", lambda e: e.memset(epsc[:, 1:2], 1e-5), writes=["epsc"])

    evac_rr = [0]

    def evac_eng():
        evac_rr[0] += 1
        return "act" if evac_rr[0] % 2 else "dve"

    def copy_op(eng, out, in_):
        if eng == "act":
            return lambda e: e.activation(out=out, in_=in_, func=AF.Copy)
        return lambda e: e.tensor_copy(out=out, in_=in_)

    def prologue():
        with _sbuf_tensor("crow", [8, 128], F32) as crow, \
                _sbuf_tensor("srow", [8, 128], BF16) as srow, \
                _sbuf_tensor("scT", [128, 8], BF16) as scT, \
                _sbuf_tensor("wada", [128, 2, 8, 512], BF16) as wada, \
                _sbuf_tensor("brow", [1, 2, 512], F32) as brow, \
                _sbuf_tensor("orow", [1, 2, 512], F32) as orow:
            S.dma("sp", crow[:], cvec.ap().rearrange("o (k p) -> (o k) p", p=128), writes=["crow"])
            S.op("act", lambda e: e.activation(out=srow[:], in_=crow[:], func=AF.Silu), reads=["crow"], writes=["srow"])
            S.mm(lambda e: e.matmul(PS[0][:, 0:8], lhsT=srow[:], rhs=ident[0:8, 0:8], start=True, stop=True),
                 reads=["srow", "ident"], writes=[PK[0]])
            S.op("dve", lambda e: e.tensor_copy(out=scT[:], in_=PS[0][:, 0:8]), reads=[PK[0]], writes=["scT"])
            it = 0
            for l in range(DEPTH):
                wv = w_ada.ap()[l].rearrange("(k p) n -> p k n", p=128)
                for j in range(18):
                    b = it % 2
                    it += 1
                    S.dma("pool", wada[:, b], wv[:, :, j * 512:(j + 1) * 512], writes=[f"wada{b}"])
                    S.dma("sp", brow[:, b], b_ada.ap()[l:l + 1, j * 512:(j + 1) * 512], writes=[f"brow{b}"])
                    pk = PK[b]
                    for k in range(KC):
                        S.mm(lambda e, k=k, b=b: e.matmul(PS[b][0:1, :], lhsT=scT[:, k:k + 1], rhs=wada[:, b, k, :],
                                                          start=(k == 0), stop=(k == KC - 1)),
                             reads=["scT", f"wada{b}"], writes=[pk], first=(k == 0))
                    S.op("dve", lambda e, b=b: e.tensor_tensor(out=orow[:, b], in0=PS[b][0:1, :], in1=brow[:, b], op=ALU.add),
                         reads=[pk, f"brow{b}"], writes=[f"orow{b}"])
                    S.dma("sp", ada_d.ap()[l:l + 1, j * 512:(j + 1) * 512], orow[:, b], reads=[f"orow{b}"], writes=["ada_d"])
        for l in range(DEPTH):
            for p, (dl, typ) in enumerate(PAT_DELTA):
                for kr in range(2):
                    for qr in range(2):
                        dr = 2 * dl + kr - qr + 7
                        drc = min(max(dr, 0), 14)
                        src = AP(rpbr, RPAD + ((l * 8) * 15 + drc) * 31 + 15, [[465, 8], [-1, 64], [1, 64]])
                        dst = AP(btd, (l * 8 * NPAT + p) * 16384 + kr * 64 * 128 + qr * 64,
                                 [[NPAT * 16384, 8], [128, 64], [1, 64]])
                        S.dma("sp", dst, src, writes=["btd"])
        S.barrier()

    cvrow = sb("cvrow", [32, 4, 128], F32)
    cv_rr = [0]

    def load_colvec(src_t, off, n, dst, dkey):
        r = cv_rr[0] % 4
        cv_rr[0] += 1
        S.dma("sp", cvrow[0:n, r, :], AP(src_t, off, [[128, n], [1, 128]]), writes=[f"cvrow{r}"])
        S.mm(lambda e: e.transpose(out=PS[6][:, 0:n], in_=cvrow[0:n, r, :], identity=identf[0:n, 0:n]),
             reads=[f"cvrow{r}", "identf"], writes=[PK[6]])
        S.op("dve", lambda e: e.tensor_copy(out=dst, in_=PS[6][:, 0:n]), reads=[PK[6]], writes=[dkey])

    def load_mod(l, shift_idx, scale_idx, V):
        load_colvec(ada_d, l * 9 * D + shift_idx * D, 8, V["shT"][:], "shT")
        load_colvec(ada_d, l * 9 * D + scale_idx * D, 8, V["scT"][:], "scT")
        S.op("dve", lambda e: e.tensor_scalar(out=V["scT"][:], in0=V["scT"][:], scalar1=1.0, scalar2=None, op0=ALU.add),
             reads=["scT"], writes=["scT"])

    def load_epi(l, gate_idx, ln_idx, gate_coef, V):
        S.dma("sp", V["gate_bc"][:], AP(ada_d, l * 9 * D + gate_idx * D, [[0, 128], [1, D]]), reads=["ada_d"], writes=["gate_bc"])
        S.dma("sp", V["lng_bc"][:], AP(ln_g, (l * 3 + ln_idx) * D, [[0, 128], [1, D]]), writes=["lng_bc"])
        S.dma("sp", V["lnb_bc"][:], AP(ln_b, (l * 3 + ln_idx) * D, [[0, 128], [1, D]]), writes=["lnb_bc"])
        if gate_coef != 1.0:
            S.op("pool", lambda e: e.tensor_scalar(out=V["gate_bc"][:], in0=V["gate_bc"][:], scalar1=gate_coef, scalar2=None, op0=ALU.mult),
                 reads=["gate_bc"], writes=["gate_bc"])

    NTAG = 5

    def run_staged(n, stages):
        k = len(stages)
        for step in range(n + k - 1):
            for s_ in reversed(range(k)):
                t = step - s_
                if 0 <= t < n:
                    stages[s_](t)

    def ln_stage_fns(x_of, xkey_of, T, eps_col):
        st, mv, rs, nb = T["st"], T["mv"], T["rstd"], T["nb"]

        def A(t):
            g = t % NTAG
            xt, xkey = x_of(t), xkey_of(t)
            S.op("dve", lambda e: e.bn_stats(out=st[:, g, 0:6], in_=xt[:, 0:512]), reads=[xkey], writes=[f"lnsa{g}"])
            S.op("dve", lambda e: e.bn_stats(out=st[:, g, 6:12], in_=xt[:, 512:1024]), reads=[xkey], writes=[f"lnsb{g}"])
            S.op("dve", lambda e: e.bn_aggr(out=mv[:, g, :], in_=st[:, g, :]), reads=[f"lnsa{g}", f"lnsb{g}"], writes=[f"lnmv{g}"])

        def B(t):
            g = t % NTAG
            S.op("act", lambda e: e.activation(out=rs[:, g:g + 1], in_=mv[:, g, 1:2], func=AF.Sqrt, bias=epsc[:, eps_col:eps_col + 1], scale=1.0),
                 reads=[f"lnmv{g}", "epsc"], writes=[f"lnrs{g}"])

        def C(t):
            g = t % NTAG
            S.op("dve", lambda e: e.reciprocal(out=rs[:, g:g + 1], in_=rs[:, g:g + 1]), reads=[f"lnrs{g}"], writes=[f"lnrs{g}"])
            S.op("dve", lambda e: e.scalar_tensor_tensor(out=nb[:, g:g + 1], in0=mv[:, g, 0:1], scalar=-1.0, in1=rs[:, g:g + 1],
                                                         op0=ALU.mult, op1=ALU.mult),
                 reads=[f"lnmv{g}", f"lnrs{g}"], writes=[f"lnnb{g}"])

        return [A, B, C]

    def ln_in_stages(x_of, xkey_of, T, V, hT, hkey, tcol_of):
        xn = T["xn"]

        def D(t):
            g = t % NTAG
            S.op("act", lambda e: e.activation(out=xn[:, g, :], in_=x_of(t), func=AF.Identity, scale=T["rstd"][:, g:g + 1], bias=T["nb"][:, g:g + 1]),
                 reads=[xkey_of(t), f"lnrs{g}", f"lnnb{g}"], writes=[f"xn{g}"])

        def E(t):
            g = t % NTAG
            tcol = tcol_of(t)
            for kk in range(KC):
                S.mm(lambda e, kk=kk: e.transpose(out=PSB[:, kk * 128:(kk + 1) * 128], in_=xn[:, g, kk * 128:(kk + 1) * 128], identity=ident[:]),
                     reads=[f"xn{g}", "ident"], writes=["ps7"], first=(kk == 0))
            evac_rr[0] += 1
            teng = "dve" if evac_rr[0] % 2 == 0 else "act"
            for kk in range(KC):
                if teng == "dve":
                    fn = lambda e, kk=kk: e.tensor_scalar(out=hT[:, kk, tcol:tcol + 128], in0=PSB[:, kk * 128:(kk + 1) * 128],
                                                          scalar1=V["scT"][:, kk:kk + 1], scalar2=V["shT"][:, kk:kk + 1], op0=ALU.mult, op1=ALU.add)
                else:
                    fn = lambda e, kk=kk: e.activation(out=hT[:, kk, tcol:tcol + 128], in_=PSB[:, kk * 128:(kk + 1) * 128], func=AF.Identity,
                                                       scale=V["scT"][:, kk:kk + 1], bias=V["shT"][:, kk:kk + 1])
                S.op(teng, fn, reads=["ps7", "scT", "shT"], writes=[hkey])

        return ln_stage_fns(x_of, xkey_of, T, 0) + [D, E]

    def ln_out_stages(z_of, zkey_of, T, V, dst_of):
        def D(t):
            g = t % NTAG
            z, zk = z_of(t), zkey_of(t)
            S.op("act", lambda e: e.activation(out=z, in_=z, func=AF.Identity, scale=T["rstd"][:, g:g + 1], bias=T["nb"][:, g:g + 1]),
                 reads=[zk, f"lnrs{g}", f"lnnb{g}"], writes=[zk])

        def E(t):
            z, zk = z_of(t), zkey_of(t)
            S.op("dve", lambda e: e.tensor_tensor(out=z, in0=z, in1=V["lng_bc"][:], op=ALU.mult), reads=[zk, "lng_bc"], writes=[zk])
            S.op("pool", lambda e: e.tensor_tensor(out=z, in0=z, in1=V["lnb_bc"][:], op=ALU.add), reads=[zk, "lnb_bc"], writes=[zk])
            for d_ in dst_of(t):
                S.dma("sp", d_, z, reads=[zk], writes=["xdram"])

        return ln_stage_fns(z_of, zkey_of, T, 1) + [D, E]

    def alloc_ln(stack):
        V = {}
        V["shT"] = stack.enter_context(_sbuf_tensor("shT", [128, 8], F32))
        V["scT"] = stack.enter_context(_sbuf_tensor("scT1", [128, 8], F32))
        T = {}
        T["st"] = stack.enter_context(_sbuf_tensor("st", [128, NTAG, 12], F32))
        T["mv"] = stack.enter_context(_sbuf_tensor("mv", [128, NTAG, 2], F32))
        T["rstd"] = stack.enter_context(_sbuf_tensor("rstd", [128, NTAG], F32))
        T["nb"] = stack.enter_context(_sbuf_tensor("nb", [128, NTAG], F32))
        return V, T

    def alloc_vec(stack, V):
        for nm in ("gate_bc", "lng_bc", "lnb_bc"):
            V[nm] = stack.enter_context(_sbuf_tensor(nm, [128, D], F32))

    from contextlib import ExitStack

    def ffn(l, which, xin, xouts):
        w1 = fw[f"ffn{which}_w1"].ap()[l].rearrange("(k p) n -> p k n", p=128)
        w3 = fw[f"ffn{which}_w3"].ap()[l].rearrange("(k p) n -> p k n", p=128)
        w2 = fw[f"ffn{which}_w2"].ap()[l].rearrange("(f p) n -> p f n", p=128)
        base = 0 if which == 1 else 6
        with ExitStack() as st:
            V, T = alloc_ln(st)
            T["xn"] = st.enter_context(_sbuf_tensor("xn", [128, NTAG, D], BF16))
            alloc_vec(st, V)
            xp = st.enter_context(_sbuf_tensor("xp", [128, 8, D], F32))
            hT = st.enter_context(_sbuf_tensor("hTf", [128, KC, 1024], BF16))
            gT = st.enter_context(_sbuf_tensor("gT", [128, FC, 1024], BF16))
            w13 = st.enter_context(_sbuf_tensor("w13", [128, 2, 2, KC, 256], BF16))
            w2b = st.enter_context(_sbuf_tensor("w2b", [128, 2, FC, 256], BF16))
            sg = st.enter_context(_sbuf_tensor("sg", [128, 2, 512], BF16))
            tmp = st.enter_context(_sbuf_tensor("tmpf", [128, 2, 256], F32))
            import os
            CUT = int(os.environ.get("KDBG_CUT", "99"))
            load_mod(l, base + 0, base + 1, V)
            load_epi(l, base + 2, 0 if which == 1 else 2, 0.5, V)
            if CUT <= 0:
                S.barrier()
                return
            wit = 0
            w2it = 0
            for p in range(2):
                for t in range(8):
                    S.dma("sp", xp[:, t, :], xin[(p * 8 + t) * 128:(p * 8 + t + 1) * 128, :], reads=["xdram"], writes=[f"xp{t}"])
                if CUT <= 1:
                    S.barrier()
                    return
                run_staged(8, ln_in_stages(lambda t: xp[:, t, :], lambda t: f"xp{t}", T, V, hT, "hTf", lambda t: t * 128))
                if CUT <= 2:
                    S.barrier()
                    return
                for f2 in range(FC // 2):
                    b = wit % 2
                    wit += 1
                    S.dma("pool", w13[:, b, 0], w1[:, :, f2 * 256:(f2 + 1) * 256], writes=[f"w1b{b}"])
                    S.dma("pool", w13[:, b, 1], w3[:, :, f2 * 256:(f2 + 1) * 256], writes=[f"w3b{b}"])
                    for fi in range(2):
                        f = f2 * 2 + fi
                        for half in range(2):
                            pa, pb_ = (0, 1) if half == 0 else (2, 3)
                            for k in range(KC):
                                S.mm(lambda e, k=k, b=b, fi=fi, half=half, pa=pa: e.matmul(
                                    PS[pa][:, :], lhsT=w13[:, b, 0, k, fi * 128:(fi + 1) * 128], rhs=hT[:, k, half * 512:(half + 1) * 512],
                                    start=(k == 0), stop=(k == KC - 1)),
                                    reads=[f"w1b{b}", "hTf"], writes=[PK[pa]], first=(k == 0))
                            for k in range(KC):
                                S.mm(lambda e, k=k, b=b, fi=fi, half=half, pb_=pb_: e.matmul(
                                    PS[pb_][:, :], lhsT=w13[:, b, 1, k, fi * 128:(fi + 1) * 128], rhs=hT[:, k, half * 512:(half + 1) * 512],
                                    start=(k == 0), stop=(k == KC - 1)),
                                    reads=[f"w3b{b}", "hTf"], writes=[PK[pb_]], first=(k == 0))
                            S.op("act", lambda e, half=half, pa=pa: e.activation(out=sg[:, half, :], in_=PS[pa][:, :], func=AF.Silu),
                                 reads=[PK[pa]], writes=[f"sg{half}"])
                            S.op("dve", lambda e, half=half, pb_=pb_, f=f: e.tensor_tensor(
                                out=gT[:, f, half * 512:(half + 1) * 512], in0=PS[pb_][:, :], in1=sg[:, half, :], op=ALU.mult),
                                reads=[PK[pb_], f"sg{half}"], writes=["gT"])
                if CUT <= 3:
                    S.barrier()
                    return
                for oq in range(4):
                    b = w2it % 2
                    w2it += 1
                    S.dma("pool", w2b[:, b], w2[:, :, oq * 256:(oq + 1) * 256], writes=[f"w2b{b}"])
                    for t in range(8):
                        pi = 4 + (t % 2)
                        for f in range(FC):
                            S.mm(lambda e, f=f, t=t, b=b, pi=pi: e.matmul(
                                PS[pi][:, 0:256], lhsT=gT[:, f, t * 128:(t + 1) * 128], rhs=w2b[:, b, f, :],
                                start=(f == 0), stop=(f == FC - 1)),
                                reads=["gT", f"w2b{b}"], writes=[PK[pi]], first=(f == 0))
                        tb = t % 2
                        S.op("dve", lambda e, pi=pi, tb=tb, oq=oq: e.tensor_tensor(
                            out=tmp[:, tb, :], in0=PS[pi][:, 0:256], in1=V["gate_bc"][:, oq * 256:(oq + 1) * 256], op=ALU.mult),
                            reads=[PK[pi], "gate_bc"], writes=[f"tmpf{tb}"])
                        S.op("dve", lambda e, t=t, tb=tb, oq=oq: e.scalar_tensor_tensor(
                            out=xp[:, t, oq * 256:(oq + 1) * 256], in0=xp[:, t, oq * 256:(oq + 1) * 256], scalar=ALPHA,
                            in1=tmp[:, tb, :], op0=ALU.mult, op1=ALU.add),
                            reads=[f"tmpf{tb}", f"xp{t}"], writes=[f"xp{t}"])
                if CUT <= 4:
                    S.barrier()
                    return
                run_staged(8, ln_out_stages(lambda t: xp[:, t, :], lambda t: f"xp{t}", T, V,
                                            lambda t, p=p: [xo[(p * 8 + t) * 128:(p * 8 + t + 1) * 128, :] for xo in xouts]))
        S.barrier()

    class _Cut(Exception):
        pass

    def mcut(n):
        import os
        if int(os.environ.get("KDBG_MCUT", "99")) <= n:
            raise _Cut()

    def mixer(l, xin, xouts):
        try:
            mixer_(l, xin, xouts)
        except _Cut:
            pass
        S.barrier()

    def mixer_(l, xin, xouts):
        win = w_in.ap()[l].rearrange("(k p) n -> p k n", p=128)
        with ExitStack() as st:
            V, T = alloc_ln(st)
            hT = st.enter_context(_sbuf_tensor("hTm", [128, KC, NT], BF16))
            specT = st.enter_context(_sbuf_tensor("specT", [128, 4, NT], BF16))
            omT = st.enter_context(_sbuf_tensor("omT", [128, 4, NT], BF16))
            onT = st.enter_context(_sbuf_tensor("onT", [128, 4, NT], BF16))
            load_mod(l, 3, 4, V)
            with _sbuf_tensor("xt2", [128, 6, D], F32) as xt2, _sbuf_tensor("xn", [128, NTAG, D], BF16) as xn_:
                T["xn"] = xn_

                def m0_load(t):
                    S.dma("sp", xt2[:, t % 6, :], xin[t * 128:(t + 1) * 128, :], reads=["xdram"], writes=[f"xt2{t % 6}"])
                run_staged(NTILE, [m0_load] + ln_in_stages(lambda t: xt2[:, t % 6, :], lambda t: f"xt2{t % 6}", T, V, hT, "hTm",
                                                           lambda t: t * 128))
            S.barrier()
            mcut(0)

            with ExitStack() as s2:
                wf = s2.enter_context(_sbuf_tensor("wf", [128, KC, 512], BF16))
                cs = s2.enter_context(_sbuf_tensor("cs", [128, 256], BF16))
                AB = s2.enter_context(_sbuf_tensor("AB", [128, NTILE, 4, 256], BF16))
                ufT = s2.enter_context(_sbuf_tensor("ufT", [128, 2, 512], BF16))
                dbuf = s2.enter_context(_sbuf_tensor("dbuf", [128, 2, 2, 8, 512], BF16))
                S.dma("pool", wf[:], win[:, :, C_F:C_F + 512], writes=["wf"])
                S.dma("pool", cs[:], dftCS.ap(), writes=["cs"])
                it = 0
                for c in range(4):
                    for g in range(4):
                        b = it % 2
                        it += 1
                        for k in range(KC):
                            S.mm(lambda e, k=k, g=g, c=c, b=b: e.matmul(PS[b][:, :], lhsT=wf[:, k, g * 128:(g + 1) * 128],
                                                                   rhs=hT[:, k, c * 512:(c + 1) * 512], start=(k == 0), stop=(k == KC - 1)),
                                 reads=["wf", "hTm"], writes=[PK[b]], first=(k == 0))
                        S.op("act", copy_op("act", ufT[:, b, :], PS[b][:, :]), reads=[PK[b]], writes=[f"ufT{b}"])
                        for tt in range(4):
                            t = c * 4 + tt
                            S.mm(lambda e, tt=tt, b=b: e.matmul(PS[2][:, tt * 256:(tt + 1) * 256] if tt < 2 else PS[3][:, (tt - 2) * 256:(tt - 1) * 256],
                                                           lhsT=ufT[:, b, tt * 128:(tt + 1) * 128], rhs=cs[:], start=True, stop=True),
                                 reads=[f"ufT{b}", "cs"], writes=[PK[2] if tt < 2 else PK[3]])
                        for hh in range(2):
                            S.op("dve", lambda e, hh=hh, c=c, g=g: e.tensor_copy(
                                out=AB[:, c * 4 + hh * 2:c * 4 + hh * 2 + 2, g, :],
                                in_=PS[2 + hh][:, :].rearrange("p (t n) -> p t n", n=256)),
                                reads=[PK[2 + hh]], writes=["AB"])
                dC = dftC.ap().rearrange("(t p) n -> p t n", p=128)
                dS = dftS.ap().rearrange("(t p) n -> p t n", p=128)
                dit = 0
                for c in range(4):
                    for half in range(2):
                        b = dit % 2
                        dit += 1
                        S.dma("sp", dbuf[:, b, 0], dC[:, half * 8:(half + 1) * 8, c * 512:(c + 1) * 512], writes=[f"dC{b}"])
                        S.dma("sp", dbuf[:, b, 1], dS[:, half * 8:(half + 1) * 8, c * 512:(c + 1) * 512], writes=[f"dS{b}"])
                        for g in range(4):
                            for lt in range(8):
                                tl = half * 8 + lt
                                S.mm(lambda e, g=g, lt=lt, tl=tl, b=b, half=half: e.matmul(
                                    PS[g][:, :], lhsT=AB[:, tl, g, 0:128], rhs=dbuf[:, b, 0, lt, :],
                                    start=(half == 0 and lt == 0), stop=False),
                                    reads=["AB", f"dC{b}"], writes=[PK[g]], first=(half == 0 and lt == 0))
                                S.mm(lambda e, g=g, lt=lt, tl=tl, b=b, half=half: e.matmul(
                                    PS[g][:, :], lhsT=AB[:, tl, g, 128:256], rhs=dbuf[:, b, 1, lt, :],
                                    start=False, stop=(half == 1 and lt == 7)),
                                    reads=["AB", f"dS{b}"], writes=[PK[g]], first=False)
                    for g in range(4):
                        eng = evac_eng()
                        S.op(eng, copy_op(eng, specT[:, g, c * 512:(c + 1) * 512], PS[g][:, :]), reads=[PK[g]], writes=["specT"])
            S.barrier()

            mcut(1)
            with ExitStack() as s2:
                knT = s2.enter_context(_sbuf_tensor("knT", [128, 4, NK], BF16))
                qnT = s2.enter_context(_sbuf_tensor("qnT", [128, 4, NT], BF16))
                Vn = s2.enter_context(_sbuf_tensor("Vn", [128, NKT, 8, 65], BF16))
                with ExitStack() as s3:
                    wq = s3.enter_context(_sbuf_tensor("wq", [128, KC, 512], BF16))
                    wk = s3.enter_context(_sbuf_tensor("wk", [128, KC, 512], BF16))
                    wv = s3.enter_context(_sbuf_tensor("wv", [128, KC, 512], BF16))
                    ck = s3.enter_context(_sbuf_tensor("ck", [128, 4, 512], BF16))
                    of32 = s3.enter_context(_sbuf_tensor("of32", [128, 2, 512], F32))
                    S.dma("pool", wq[:], win[:, :, C_NQ:C_NQ + 512], writes=["wq"])
                    S.dma("pool", wk[:], win[:, :, C_NK:C_NK + 512], writes=["wk"])
                    S.dma("pool", wv[:], win[:, :, C_NV:C_NV + 512], writes=["wv"])
                    S.dma("pool", ck[:], c_nk.ap()[l].rearrange("(t p) n -> p t n", p=128), writes=["ck"])
                    for j in range(4):
                        S.dma("pool", Vn[:, NTILE + j, :, 0:64], c_nv.ap()[l, j * 128:(j + 1) * 128, :].rearrange("p (h d) -> p h d", d=64), writes=["Vnc"])
                    S.op("pool", lambda e: e.memset(Vn[:, :, :, 64:65], 1.0), writes=["Vn1"])
                    it = 0
                    for t in range(NTILE):
                        for (wsb, wkey, odst, isv) in ((wk, "wk", o_nk, False), (wv, "wv", o_nv, True)):
                            b = it % 2
                            it += 1
                            for k in range(KC):
                                S.mm(lambda e, k=k, t=t, b=b, wsb=wsb: e.matmul(PS[b][:, :], lhsT=hT[:, k, t * 128:(t + 1) * 128], rhs=wsb[:, k, :],
                                                                               start=(k == 0), stop=(k == KC - 1)),
                                     reads=["hTm", wkey], writes=[PK[b]], first=(k == 0))
                            S.op("act", copy_op("act", of32[:, b, :], PS[b][:, :]), reads=[PK[b]], writes=[f"of32{b}"])
                            if isv:
                                S.op("dve", lambda e, t=t, b=b: e.tensor_copy(out=Vn[:, t, :, 0:64], in_=PS[b][:, :].rearrange("p (h d) -> p h d", d=64)),
                                     reads=[PK[b]], writes=["Vn"])
                            S.dma("sp", odst.ap()[l, t * 128:(t + 1) * 128, :], of32[:, b, :], reads=[f"of32{b}"])
                    for pr in range(4):
                        for c in range(4):
                            for (wsb, wkey, dstT, dkey) in ((wk, "wk", knT, "knT"), (wq, "wq", qnT, "qnT")):
                                b = it % 2
                                it += 1
                                for k in range(KC):
                                    S.mm(lambda e, k=k, pr=pr, c=c, b=b, wsb=wsb: e.matmul(
                                        PS[b][:, :], lhsT=wsb[:, k, pr * 128:(pr + 1) * 128], rhs=hT[:, k, c * 512:(c + 1) * 512],
                                        start=(k == 0), stop=(k == KC - 1)),
                                        reads=[wkey, "hTm"], writes=[PK[b]], first=(k == 0))
                                eng = evac_eng()
                                S.op(eng, copy_op(eng, dstT[:, pr, c * 512:(c + 1) * 512], PS[b][:, :]), reads=[PK[b]], writes=[dkey])
                        for j in range(4):
                            S.mm(lambda e, j=j, pr=pr: e.transpose(out=PSB[:, j * 128:(j + 1) * 128], in_=ck[:, j, pr * 128:(pr + 1) * 128], identity=ident[:]),
                                 reads=["ck", "ident"], writes=["ps7"], first=(j == 0))
                        S.op("dve", lambda e, pr=pr: e.tensor_copy(out=knT[:, pr, NT:NK], in_=PSB[:, 0:512]), reads=["ps7"], writes=["knT"])
                S.barrier()
                mcut(2)
                s3 = s2
                iq = s3.enter_context(_sbuf_tensor("iq", [72, NT], BF16))
                ik = s3.enter_context(_sbuf_tensor("ik", [72, NK], BF16))
                m1 = s3.enter_context(_sbuf_tensor("m1", [128, NPAT, 128], BF16))
                m2 = s3.enter_context(_sbuf_tensor("m2", [128, NPAT, 128], BF16))
                btf = s3.enter_context(_sbuf_tensor("btf", [128, NPAT, 128], F32))
                BT = s3.enter_context(_sbuf_tensor("BT", [128, 2, NPAT, 128], BF16))
                pT = s3.enter_context(_sbuf_tensor("pT", [128, 2, 9, 128], BF16))
                osb = s3.enter_context(_sbuf_tensor("osb", [65, 2, 512], F32))
                otmp = s3.enter_context(_sbuf_tensor("otmp", [64, 2, 512], BF16))
                for pb_ in (0, 64):
                    S.dma("pool", iq[pb_:pb_ + 8, :], indq_n.ap(), writes=["iq"])
                    S.dma("pool", ik[pb_:pb_ + 8, :], indk.ap(), writes=["ik"])
                S.dma("pool", m1[:], m1d.ap(), writes=["m1"])
                S.dma("pool", m2[:], m2d.ap(), writes=["m2"])
                items = [(h, c, qi) for h in range(8) for c in range(4) for qi in range(4)]

                def na_bt(h):
                    hb = h % 2
                    S.dma("sp", btf[:], AP(btd, ((l * 8 + h) * NPAT) * 16384, [[128, 128], [16384, NPAT], [1, 128]]),
                          reads=["btd"], writes=["btf"])
                    S.op("dve", lambda e: e.tensor_tensor(out=btf[:], in0=btf[:], in1=m1[:], op=ALU.mult),
                         reads=["btf", "m1"], writes=["btf"])
                    S.op("dve", lambda e: e.tensor_tensor(out=BT[:, hb], in0=btf[:], in1=m2[:], op=ALU.add),
                         reads=["btf", "m2"], writes=[f"BT{hb}"])

                def na_slots(i):
                    return [(j, pat) for (j, pat) in na_window(i)] + [(NTILE + j, None) for j in range(4)]

                def na_scores(k):
                    h, c, qi = items[k]
                    pr, pb, hb = h // 2, 64 * (h % 2), h % 2
                    i = c * 4 + qi
                    ab = k % 2
                    banks = [ab * 3 + 0, ab * 3 + 1, ab * 3 + 2]
                    for si, (j, pat) in enumerate(na_slots(i)):
                        bk = banks[si // 4]
                        col = (si % 4) * 128
                        S.mm(lambda e, bk=bk, col=col, j=j: e.matmul(
                            PS[bk][:, col:col + 128], lhsT=knT[pb:pb + 64, pr, j * 128:(j + 1) * 128],
                            rhs=qnT[pb:pb + 64, pr, i * 128:(i + 1) * 128], start=True, stop=False),
                            reads=["knT", "qnT"], writes=[PK[bk]], first=(si % 4 == 0))
                        S.mm(lambda e, bk=bk, col=col, j=j, pat=pat: e.matmul(
                            PS[bk][:, col:col + 128], lhsT=ik[pb:pb + 8, j * 128:(j + 1) * 128], rhs=iq[pb:pb + 8, i * 128:(i + 1) * 128],
                            start=False, stop=(pat is None)),
                            reads=["ik", "iq"], writes=[PK[bk]], first=False)
                        if pat is not None:
                            S.mm(lambda e, bk=bk, col=col, pat=pat: e.matmul(
                                PS[bk][:, col:col + 128], lhsT=ident[:], rhs=BT[:, hb, pat, :], start=False, stop=True),
                                reads=["ident", f"BT{hb}"], writes=[PK[bk]], first=False)

                def na_exp(k):
                    h, c, qi = items[k]
                    i = c * 4 + qi
                    ab = k % 2
                    ns = len(na_slots(i))
                    for g in range(3):
                        n_in = min(4, ns - g * 4)
                        if n_in <= 0:
                            continue
                        bk = ab * 3 + g
                        S.op("act", lambda e, bk=bk, g=g, n_in=n_in: e.activation(
                            out=pT[:, ab, g * 4:g * 4 + n_in, :], in_=PS[bk][:, 0:n_in * 128].rearrange("p (s n) -> p s n", n=128),
                            func=AF.Exp, scale=NA_SCALE),
                            reads=[PK[bk]], writes=[f"pT{ab}"])

                def na_pv(k):
                    h, c, qi = items[k]
                    i = c * 4 + qi
                    ab = k % 2
                    po = 6 + ((h * 4 + c) % 2)
                    slots = na_slots(i)
                    ns = len(slots)
                    for si, (j, pat) in enumerate(slots):
                        S.mm(lambda e, si=si, j=j: e.matmul(
                            PS[po][0:65, qi * 128:(qi + 1) * 128], lhsT=Vn[:, j, h, :], rhs=pT[:, ab, si, :],
                            start=(si == 0), stop=(si == ns - 1)),
                            reads=["Vn", "Vnc", "Vn1", f"pT{ab}"], writes=[PK[po]], first=(si == 0 and qi == 0))

                def na_norm_a(k):
                    h, c, qi = items[k]
                    ob = (h * 4 + c) % 2
                    po = 6 + ob
                    S.op("act", copy_op("act", osb[:, ob, :], PS[po][0:65, :]), reads=[PK[po]], writes=[f"osb{ob}"])
                    S.op("dve", lambda e: e.reciprocal(out=osb[64:65, ob, :], in_=osb[64:65, ob, :]), reads=[f"osb{ob}"], writes=[f"osb{ob}"])

                def na_norm_b(k):
                    h, c, qi = items[k]
                    pr, hb = h // 2, h % 2
                    ob = (h * 4 + c) % 2
                    po = 6 + ob
                    S.mm(lambda e: e.matmul(PS[po][0:64, :], lhsT=onesf[64:65, 0:64], rhs=osb[64:65, ob, :], start=True, stop=True),
                         reads=["ones", f"osb{ob}"], writes=[PK[po]])
                    if hb == 0:
                        S.op("dve", lambda e: e.tensor_tensor(
                            out=onT[0:64, pr, c * 512:(c + 1) * 512], in0=PS[po][0:64, :], in1=osb[0:64, ob, :], op=ALU.mult),
                            reads=[PK[po], f"osb{ob}"], writes=["onT"])
                    else:
                        S.op("dve", lambda e: e.tensor_tensor(
                            out=otmp[:, ob, :], in0=PS[po][0:64, :], in1=osb[0:64, ob, :], op=ALU.mult),
                            reads=[PK[po], f"osb{ob}"], writes=[f"otmp{ob}"])
                        S.dma("sp", onT[64:128, pr, c * 512:(c + 1) * 512], otmp[:, ob, :], reads=[f"otmp{ob}"], writes=["onT"])

                na_bt(0)
                na_scores(0)
                npend = []
                for k in range(len(items)):
                    na_exp(k)
                    if k + 1 < len(items):
                        if items[k + 1][0] != items[k][0]:
                            na_bt(items[k + 1][0])
                        na_scores(k + 1)
                    for pd in list(npend):
                        if k >= pd[0]:
                            na_norm_b(pd[1])
                            npend.remove(pd)
                    na_pv(k)
                    if items[k][2] == 3:
                        na_norm_a(k)
                        npend.append((k + 1, k))
                for pd in npend:
                    na_norm_b(pd[1])
            S.barrier()

            mcut(3)
            with ExitStack() as s2:
                wuq = s2.enter_context(_sbuf_tensor("wuq", [128, 3, 768], BF16))
                wukv = s2.enter_context(_sbuf_tensor("wukv", [128, 2, 2, 8, 64], BF16))
                qn = s2.enter_context(_sbuf_tensor("qn", [128, 3, NT], BF16))
                ckvT = s2.enter_context(_sbuf_tensor("ckvT", [128, 2, NK], BF16))
                kall = s2.enter_context(_sbuf_tensor("kall", [104, 2, NK], BF16))
                qall = s2.enter_context(_sbuf_tensor("qall", [104, 2, NT], BF16))
                wuqr = s2.enter_context(_sbuf_tensor("wuqr", [128, 3, 8, 32], BF16))
                Vm = s2.enter_context(_sbuf_tensor("Vm", [128, NKT, 8, 65], BF16))
                rC = s2.enter_context(_sbuf_tensor("rC", [96, NT], BF16))
                rS = s2.enter_context(_sbuf_tensor("rS", [96, NT], BF16))
                t1 = s2.enter_context(_sbuf_tensor("t1", [96, 2, 512], F32))
                t2 = s2.enter_context(_sbuf_tensor("t2", [96, 2, 512], F32))
                S.dma("pool", wuq[:], w_uq.ap()[l].rearrange("(k p) n -> p k n", p=128), writes=["wuq"])
                for k2 in range(2):
                    for tt in range(2):
                        S.dma("pool", wukv[:, k2, tt],
                              w_ukv.ap()[l, k2 * 128:(k2 + 1) * 128, :].rearrange("p (h t d) -> p t h d", h=8, t=2)[:, tt], writes=["wukv"])
                S.dma("pool", rC[64:96, :], ropeC.ap(), writes=["rC"])
                S.dma("pool", rS[64:96, :], ropeS.ap(), writes=["rS"])
                for hb_ in range(2):
                    S.dma("pool", kall[96:104, hb_, :], indk.ap(), writes=["krTi"])
                    S.dma("pool", qall[96:104, hb_, :], indq_m.ap(), writes=["qrTi"])

                def rot_weights(dst4, src4, keys_r, key_w):
                    S.op("dve", lambda e: e.tensor_scalar(out=dst4[:, :, :, 0:8], in0=src4[:, :, :, 8:16], scalar1=-1.0, scalar2=None, op0=ALU.mult),
                         reads=keys_r, writes=[key_w])
                    S.op("dve", lambda e: e.tensor_copy(out=dst4[:, :, :, 8:16], in_=src4[:, :, :, 0:8]), reads=keys_r, writes=[key_w])

                for j in range(3):
                    rot_weights(wuqr[:, j].rearrange("p h (a d) -> p h a d", d=16),
                                wuq[:, j, :].rearrange("p (h x) -> p h x", x=96)[:, :, 64:96].rearrange("p h (a d) -> p h a d", d=16),
                                ["wuq"], "wuqr")
                S.op("pool", lambda e: e.memset(Vm[:, :, :, 64:65], 1.0), writes=["Vm1"])

                def rope2(ps_x, pkx, ps_r, pkr, b, dsts, dkey, c):
                    S.op("dve", lambda e: e.tensor_tensor(out=t1[64:96, b, :], in0=ps_x, in1=rC[64:96, c * 512:(c + 1) * 512], op=ALU.mult),
                         reads=[pkx, "rC"], writes=[f"t1{b}"])
                    S.op("dve", lambda e: e.tensor_tensor(out=t2[64:96, b, :], in0=ps_r, in1=rS[64:96, c * 512:(c + 1) * 512], op=ALU.mult),
                         reads=[pkr, "rS"], writes=[f"t2{b}"])
                    for dst in dsts:
                        S.op("pool", lambda e, dst=dst: e.tensor_tensor(out=dst, in0=t1[64:96, b, :], in1=t2[64:96, b, :], op=ALU.add),
                             reads=[f"t1{b}", f"t2{b}"], writes=[dkey])

                with ExitStack() as s3:
                    wqa = s3.enter_context(_sbuf_tensor("wqa", [128, KC, 384], BF16))
                    wkr = s3.enter_context(_sbuf_tensor("wkr", [128, KC, 288], BF16))
                    wkrr = s3.enter_context(_sbuf_tensor("wkrr", [128, KC, 32], BF16))
                    gq = s3.enter_context(_sbuf_tensor("gq", [128, 3], F32))
                    gkv = s3.enter_context(_sbuf_tensor("gkv", [128, 256], F32))
                    cc = s3.enter_context(_sbuf_tensor("cc", [128, 4, 256], BF16))
                    ckr = s3.enter_context(_sbuf_tensor("ckr", [128, 4, 32], BF16))
                    uq = s3.enter_context(_sbuf_tensor("uq", [128, 3, 512], F32))
                    sq = s3.enter_context(_sbuf_tensor("sq", [128, 3, 512], BF16))
                    rq = s3.enter_context(_sbuf_tensor("rq", [128, 512], F32))
                    ukv = s3.enter_context(_sbuf_tensor("ukv", [128, 4, 288], F32))
                    kst = s3.enter_context(_sbuf_tensor("kst", [128, 5, 6], F32))
                    kmv = s3.enter_context(_sbuf_tensor("kmv", [128, 5, 2], F32))
                    ssk = s3.enter_context(_sbuf_tensor("ssk", [128, 5], F32))
                    ckf = s3.enter_context(_sbuf_tensor("ckf", [128, 2, 256], F32))
                    ckb = s3.enter_context(_sbuf_tensor("ckb", [128, 2, 256], BF16))
                    S.dma("pool", wqa[:], win[:, :, C_Q:C_Q + 384], writes=["wqa"])
                    S.dma("pool", wkr[:], win[:, :, C_KV:C_KV + 288], writes=["wkr"])
                    rot_weights(wkrr[:].rearrange("p k (a d) -> p k a d", d=16),
                                wkr[:, :, 256:288].rearrange("p k (a d) -> p k a d", d=16), ["wkr"], "wkrr")
                    load_colvec(q_norm, l * 384, 3, gq[:], "gq")
                    S.dma("sp", gkv[:], AP(kv_norm, l * 256, [[0, 128], [1, 256]]), writes=["gkv"])
                    S.dma("pool", cc[:], c_ckv.ap()[l].rearrange("(t p) n -> p t n", p=128), writes=["cc"])
                    S.dma("pool", ckr[:], c_kr.ap()[l].rearrange("(t p) n -> p t n", p=128), writes=["ckr"])
                    it = 0
                    for c in range(4):
                        for j in range(3):
                            b = it % 2
                            it += 1
                            for k in range(KC):
                                S.mm(lambda e, k=k, j=j, c=c, b=b: e.matmul(PS[b][:, :], lhsT=wqa[:, k, j * 128:(j + 1) * 128],
                                                                       rhs=hT[:, k, c * 512:(c + 1) * 512], start=(k == 0), stop=(k == KC - 1)),
                                     reads=["wqa", "hTm"], writes=[PK[b]], first=(k == 0))
                            S.op("act", lambda e, j=j, b=b: e.activation(out=sq[:, j, :], in_=PS[b][:, :], func=AF.Square), reads=[PK[b]], writes=[f"sq{j}"])
                            S.op("dve", lambda e, j=j, b=b: e.tensor_copy(out=uq[:, j, :], in_=PS[b][:, :]), reads=[PK[b]], writes=[f"uq{j}"])
                        for j in range(3):
                            S.mm(lambda e, j=j: e.matmul(PS[2][:, :], lhsT=ones[:], rhs=sq[:, j, :], start=(j == 0), stop=(j == 2)),
                                 reads=["ones", f"sq{j}"], writes=[PK[2]], first=(j == 0))
                        S.op("act", lambda e: e.activation(out=rq[:], in_=PS[2][:, :], func=AF.Sqrt, scale=1.0 / 384, bias=epsc[:, 0:1]),
                             reads=[PK[2], "epsc"], writes=["rq"])
                        S.op("dve", lambda e: e.reciprocal(out=rq[:], in_=rq[:]), reads=["rq"], writes=["rq"])
                        for j in range(3):
                            S.op("dve", lambda e, j=j, c=c: e.scalar_tensor_tensor(out=qn[:, j, c * 512:(c + 1) * 512], in0=uq[:, j, :], scalar=gq[:, j:j + 1],
                                                                                  in1=rq[:], op0=ALU.mult, op1=ALU.mult),
                                 reads=[f"uq{j}", "gq", "rq"], writes=["qn"])
                    def kv0(t):
                        b, ub = t % 2, t % 4
                        for k in range(KC):
                            S.mm(lambda e, k=k: e.matmul(PS[b][:, 0:288], lhsT=hT[:, k, t * 128:(t + 1) * 128], rhs=wkr[:, k, :],
                                                         start=(k == 0), stop=(k == KC - 1)),
                                 reads=["hTm", "wkr"], writes=[PK[b]], first=(k == 0))
                        S.op("act", copy_op("act", ukv[:, ub, :], PS[b][:, 0:288]), reads=[PK[b]], writes=[f"ukv{ub}"])
                        S.dma("sp", o_kr.ap()[l, t * 128:(t + 1) * 128, :], ukv[:, ub, 256:288], reads=[f"ukv{ub}"])

                    def kv1(t):
                        ub, g = t % 4, t % 5
                        S.op("dve", lambda e: e.bn_stats(out=kst[:, g, :], in_=ukv[:, ub, 0:256]), reads=[f"ukv{ub}"], writes=[f"kst{g}"])
                        S.op("dve", lambda e: e.bn_aggr(out=kmv[:, g, :], in_=kst[:, g, :]), reads=[f"kst{g}"], writes=[f"kmv{g}"])
                        S.op("dve", lambda e: e.scalar_tensor_tensor(out=ssk[:, g:g + 1], in0=kmv[:, g, 0:1], scalar=kmv[:, g, 0:1], in1=kmv[:, g, 1:2],
                                                                     op0=ALU.mult, op1=ALU.add),
                             reads=[f"kmv{g}"], writes=[f"ssk{g}"])

                    def kv2(t):
                        g = t % 5
                        S.op("act", lambda e: e.activation(out=ssk[:, g:g + 1], in_=ssk[:, g:g + 1], func=AF.Sqrt, scale=1.0, bias=epsc[:, 0:1]),
                             reads=[f"ssk{g}", "epsc"], writes=[f"ssk{g}"])

                    def kv3(t):
                        b, ub, g = t % 2, t % 4, t % 5
                        S.op("dve", lambda e: e.reciprocal(out=ssk[:, g:g + 1], in_=ssk[:, g:g + 1]), reads=[f"ssk{g}"], writes=[f"ssk{g}"])
                        S.op("dve", lambda e: e.scalar_tensor_tensor(out=ckf[:, b, :], in0=ukv[:, ub, 0:256], scalar=ssk[:, g:g + 1], in1=gkv[:],
                                                                     op0=ALU.mult, op1=ALU.mult),
                             reads=[f"ukv{ub}", f"ssk{g}", "gkv"], writes=[f"ckf{b}"])
                        S.dma("sp", o_ckv.ap()[l, t * 128:(t + 1) * 128, :], ckf[:, b, :], reads=[f"ckf{b}"])
                        S.op("pool", lambda e: e.tensor_copy(out=ckb[:, b, :], in_=ckf[:, b, :]), reads=[f"ckf{b}"], writes=[f"ckb{b}"])

                    def kv4(t):
                        b = t % 2
                        for k2 in range(2):
                            S.mm(lambda e, k2=k2: e.transpose(out=PSB[:, k2 * 128:(k2 + 1) * 128], in_=ckb[:, b, k2 * 128:(k2 + 1) * 128], identity=ident[:]),
                                 reads=[f"ckb{b}", "ident"], writes=["ps7"], first=(k2 == 0))
                        S.op("dve", lambda e: e.tensor_copy(out=ckvT[:, :, t * 128:(t + 1) * 128], in_=PSB[:, 0:256].rearrange("p (k n) -> p k n", n=128)),
                             reads=["ps7"], writes=["ckvT"])

                    run_staged(NTILE, [kv0, kv1, kv2, kv3, kv4])
                    for j in range(4):
                        for k2 in range(2):
                            S.mm(lambda e, k2=k2, j=j: e.transpose(out=PSB[:, k2 * 128:(k2 + 1) * 128], in_=cc[:, j, k2 * 128:(k2 + 1) * 128], identity=ident[:]),
                                 reads=["cc", "ident"], writes=["ps7"], first=(k2 == 0))
                        S.op("dve", lambda e, j=j: e.tensor_copy(out=ckvT[:, :, NT + j * 128:NT + (j + 1) * 128],
                                                                in_=PSB[:, 0:256].rearrange("p (k n) -> p k n", n=128)),
                             reads=["ps7"], writes=["ckvT"])
                    for j in range(4):
                        S.mm(lambda e, j=j: e.matmul(PS[3][64:96, j * 128:(j + 1) * 128], lhsT=ckr[:, j, :], rhs=ident[:], start=True, stop=True),
                             reads=["ckr", "ident"], writes=[PK[3]], first=(j == 0))
                    for hb_ in range(2):
                        S.op("dve", lambda e, hb_=hb_: e.tensor_copy(out=kall[64:96, hb_, NT:NK], in_=PS[3][64:96, :]), reads=[PK[3]], writes=["krT"])
                    for c in range(4):
                        b = c % 2
                        for k in range(KC):
                            S.mm(lambda e, k=k, c=c, b=b: e.matmul(PS[b][64:96, :], lhsT=wkr[:, k, 256:288], rhs=hT[:, k, c * 512:(c + 1) * 512],
                                                                  start=(k == 0), stop=(k == KC - 1)),
                                 reads=["wkr", "hTm"], writes=[PK[b]], first=(k == 0))
                        for k in range(KC):
                            S.mm(lambda e, k=k, c=c, b=b: e.matmul(PS[2 + b][64:96, :], lhsT=wkrr[:, k, :], rhs=hT[:, k, c * 512:(c + 1) * 512],
                                                                  start=(k == 0), stop=(k == KC - 1)),
                                 reads=["wkrr", "hTm"], writes=[PK[2 + b]], first=(k == 0))
                        rope2(PS[b][64:96, :], PK[b], PS[2 + b][64:96, :], PK[2 + b], b,
                              [kall[64:96, 0, c * 512:(c + 1) * 512], kall[64:96, 1, c * 512:(c + 1) * 512]], "krT", c)
                    for t in range(NKT):
                        b = t % 2
                        for k2 in range(2):
                            S.mm(lambda e, k2=k2, t=t, b=b: e.matmul(PS[b][:, :], lhsT=ckvT[:, k2, t * 128:(t + 1) * 128],
                                                                    rhs=wukv[:, k2, 1].rearrange("p h d -> p (h d)"),
                                                                    start=(k2 == 0), stop=(k2 == 1)),
                                 reads=["ckvT", "wukv"], writes=[PK[b]], first=(k2 == 0))
                        eng = evac_eng()
                        S.op(eng, copy_op(eng, Vm[:, t, :, 0:64], PS[b][:, :].rearrange("p (h d) -> p h d", d=64)), reads=[PK[b]], writes=["Vm"])
                S.barrier()
                mcut(4)
                s3 = s2
                pTm = s3.enter_context(_sbuf_tensor("pTm", [128, 4, 512], BF16))
                osb = s3.enter_context(_sbuf_tensor("osbm", [65, 2, 512], F32))
                otmp = s3.enter_context(_sbuf_tensor("otmpm", [64, 2, 512], BF16))

                def head_pieces(h):
                    hb_ = h % 2
                    pcs = []
                    for c5 in range(5):
                        def pk_(c5=c5):
                            for k2 in range(2):
                                S.mm(lambda e, k2=k2: e.matmul(PS[6][0:64, :], lhsT=wukv[:, k2, 0, h, :], rhs=ckvT[:, k2, c5 * 512:(c5 + 1) * 512],
                                                               start=(k2 == 0), stop=(k2 == 1)),
                                     reads=["wukv", "ckvT"], writes=[PK[6]], first=(k2 == 0))
                            S.op("dve", lambda e: e.tensor_copy(out=kall[0:64, hb_, c5 * 512:(c5 + 1) * 512], in_=PS[6][0:64, :]),
                                 reads=[PK[6]], writes=[f"knp{hb_}"])
                        pcs.append(pk_)
                    for c in range(4):
                        def pq_(c=c):
                            for j in range(3):
                                S.mm(lambda e, j=j: e.matmul(PS[6][0:64, :], lhsT=wuq[:, j, h * 96:h * 96 + 64], rhs=qn[:, j, c * 512:(c + 1) * 512],
                                                             start=(j == 0), stop=(j == 2)),
                                     reads=["wuq", "qn"], writes=[PK[6]], first=(j == 0))
                            for j in range(3):
                                S.mm(lambda e, j=j: e.matmul(PS[6][64:96, :], lhsT=wuq[:, j, h * 96 + 64:h * 96 + 96], rhs=qn[:, j, c * 512:(c + 1) * 512],
                                                             start=(j == 0), stop=(j == 2)),
                                     reads=["wuq", "qn"], writes=[PK[6]], first=False)
                            for j in range(3):
                                S.mm(lambda e, j=j: e.matmul(PS[7][64:96, :], lhsT=wuqr[:, j, h, :], rhs=qn[:, j, c * 512:(c + 1) * 512],
                                                             start=(j == 0), stop=(j == 2)),
                                     reads=["wuqr", "qn"], writes=[PK[7]], first=(j == 0))
                            S.op("dve", lambda e: e.tensor_copy(out=qall[0:64, hb_, c * 512:(c + 1) * 512], in_=PS[6][0:64, :]),
                                 reads=[PK[6]], writes=[f"qnp{hb_}"])
                            rope2(PS[6][64:96, :], PK[6], PS[7][64:96, :], PK[7], c % 2, [qall[64:96, hb_, c * 512:(c + 1) * 512]], f"qrT{hb_}", c)
                        pcs.append(pq_)
                    return pcs

                for pc in head_pieces(0):
                    pc()
                sit = 0
                import os
                mdbg = os.environ.get("KDBG_MLA", "")
                for h in range(8):
                    pr, hh = h // 2, h % 2
                    hb = h % 2
                    nxt = head_pieces(h + 1) if h + 1 < 8 else []
                    LA = 2
                    items = [(c, j) for c in range(4) for j in range(NKT)]
                    pend = []

                    def mla_scores(c, j, sidx, hb=hb):
                        bk = sidx % 4
                        S.mm(lambda e: e.matmul(PS[bk][:, :], lhsT=kall[:, hb, j * 128:(j + 1) * 128], rhs=qall[:, hb, c * 512:(c + 1) * 512],
                                                start=True, stop=True),
                             reads=[f"knp{hb}", f"qnp{hb}", "krT", "krTi", f"qrT{hb}", "qrTi"], writes=[PK[bk]], first=True)

                    def mla_exp_pv(c, j, sidx, h=h):
                        bk = sidx % 4
                        po = 4 + (c % 2)
                        S.op("act", lambda e: e.activation(out=pTm[:, bk, :], in_=PS[bk][:, :], func=AF.Exp, scale=MLA_SCALE),
                             reads=[PK[bk]], writes=[f"pTm{bk}"])
                        S.mm(lambda e: e.matmul(PS[po][0:65, :], lhsT=Vm[:, j, h, :], rhs=pTm[:, bk, :], start=(j == 0), stop=(j == NKT - 1)),
                             reads=["Vm", "Vm1", f"pTm{bk}"], writes=[PK[po]], first=(j == 0))

                    def mla_norm_a(c, h=h):
                        po = 4 + (c % 2)
                        ob = c % 2
                        S.op("act", copy_op("act", osb[:, ob, :], PS[po][0:65, :]), reads=[PK[po]], writes=[f"osb{ob}"])
                        S.op("dve", lambda e: e.reciprocal(out=osb[64:65, ob, :], in_=osb[64:65, ob, :]), reads=[f"osb{ob}"], writes=[f"osb{ob}"])

                    def mla_norm_b(c, h=h, pr=pr, hh=hh):
                        po = 4 + (c % 2)
                        ob = c % 2
                        S.mm(lambda e: e.matmul(PS[po][0:64, :], lhsT=onesf[64:65, 0:64], rhs=osb[64:65, ob, :], start=True, stop=True),
                             reads=["ones", f"osb{ob}"], writes=[PK[po]])
                        if hh == 0:
                            S.op("dve", lambda e: e.tensor_tensor(
                                out=omT[0:64, pr, c * 512:(c + 1) * 512], in0=PS[po][0:64, :], in1=osb[0:64, ob, :], op=ALU.mult),
                                reads=[PK[po], f"osb{ob}"], writes=["omT"])
                        else:
                            S.op("dve", lambda e: e.tensor_tensor(
                                out=otmp[:, ob, :], in0=PS[po][0:64, :], in1=osb[0:64, ob, :], op=ALU.mult),
                                reads=[PK[po], f"osb{ob}"], writes=[f"otmp{ob}"])
                            S.dma("sp", omT[64:128, pr, c * 512:(c + 1) * 512], otmp[:, ob, :], reads=[f"otmp{ob}"], writes=["omT"])

                    n_it = len(items)
                    for k in range(n_it + LA):
                        if k < n_it:
                            mla_scores(items[k][0], items[k][1], sit + k)
                        for pd in list(pend):
                            if k >= pd[0]:
                                mla_norm_b(pd[1])
                                pend.remove(pd)
                        if k >= LA:
                            c_, j_ = items[k - LA]
                            mla_exp_pv(c_, j_, sit + k - LA)
                            if j_ == NKT - 1:
                                mla_norm_a(c_)
                                pend.append((k + 3, c_))
                        if nxt and k % 6 == 3:
                            nxt.pop(0)()
                    for pd in pend:
                        mla_norm_b(pd[1])
                    for pc in nxt:
                        pc()
                    sit += n_it
            S.barrier()

            mcut(5)
            with ExitStack() as s2:
                mT = s2.enter_context(_sbuf_tensor("mT", [128, KC, NT], BF16))
                wg = s2.enter_context(_sbuf_tensor("wg", [128, 2, 3, KC, 128], BF16))
                wb_ = s2.enter_context(_sbuf_tensor("wbr", [128, 2, 3, 4, 128], BF16))
                bgT = s2.enter_context(_sbuf_tensor("bgT", [128, 24], F32))
                gsb = s2.enter_context(_sbuf_tensor("gsb", [128, 2, 512], BF16))
                acc = s2.enter_context(_sbuf_tensor("acc", [128, 2, 512], F32))
                tm = s2.enter_context(_sbuf_tensor("tm", [128, 2, 512], F32))
                wo = s2.enter_context(_sbuf_tensor("wo", [128, KC, D], BF16))
                tmp = s2.enter_context(_sbuf_tensor("tmpm", [128, 5, D], F32))
                xt2 = s2.enter_context(_sbuf_tensor("xt2m", [128, 2, D], F32))
                alloc_vec(s2, V)
                load_epi(l, 5, 1, 1.0, V)
                load_colvec(b_gate, l * 3 * D, 24, bgT[:], "bgT")
                S.dma("pool", wo[:], w_out.ap()[l].rearrange("(k p) n -> p k n", p=128), writes=["wo"])
                wgv = w_gate.ap()[l].rearrange("(k p) (g n) -> p g k n", p=128, g=3)
                brs = [w.ap()[l].rearrange("(k p) n -> p k n", p=128) for w in (w_bf, w_bm, w_bn)]
                bins = [(specT, "specT"), (omT, "omT"), (onT, "onT")]
                git = 0
                for oc in range(KC):
                    wbuf = oc % 2
                    S.dma("pool", wg[:, wbuf], wgv[:, :, :, oc * 128:(oc + 1) * 128], writes=[f"wg{wbuf}"])
                    for gi in range(3):
                        S.dma("pool", wb_[:, wbuf, gi], brs[gi][:, :, oc * 128:(oc + 1) * 128], writes=[f"wbr{wbuf}"])
                    for c in range(4):
                        ab = (oc * 4 + c) % 2
                        for gi in range(3):
                            pg = (git % 2)
                            py = 2 + (git % 2)
                            git += 1
                            for k in range(KC):
                                S.mm(lambda e, k=k, gi=gi, c=c, pg=pg, wbuf=wbuf: e.matmul(
                                    PS[pg][:, :], lhsT=wg[:, wbuf, gi, k, :], rhs=hT[:, k, c * 512:(c + 1) * 512], start=(k == 0), stop=(k == KC - 1)),
                                    reads=[f"wg{wbuf}", "hTm"], writes=[PK[pg]], first=(k == 0))
                            bsrc, bkey = bins[gi]
                            for k4 in range(4):
                                S.mm(lambda e, k4=k4, gi=gi, c=c, py=py, wbuf=wbuf, bsrc=bsrc: e.matmul(
                                    PS[py][:, :], lhsT=wb_[:, wbuf, gi, k4, :], rhs=bsrc[:, k4, c * 512:(c + 1) * 512], start=(k4 == 0), stop=(k4 == 3)),
                                    reads=[f"wbr{wbuf}", bkey], writes=[PK[py]], first=(k4 == 0))
                            gb = git % 2
                            S.op("act", lambda e, pg=pg, gi=gi, oc=oc, gb=gb: e.activation(
                                out=gsb[:, gb, :], in_=PS[pg][:, :], func=AF.Sigmoid, bias=bgT[:, gi * 8 + oc:gi * 8 + oc + 1], scale=1.0),
                                reads=[PK[pg], "bgT"], writes=[f"gsb{gb}"])
                            if gi == 0:
                                S.op("dve", lambda e, py=py, gb=gb, ab=ab: e.tensor_tensor(out=acc[:, ab, :], in0=PS[py][:, :], in1=gsb[:, gb, :], op=ALU.mult),
                                     reads=[PK[py], f"gsb{gb}"], writes=[f"acc{ab}"])
                            else:
                                S.op("dve", lambda e, py=py, gb=gb, ab=ab: e.tensor_tensor(out=tm[:, ab, :], in0=PS[py][:, :], in1=gsb[:, gb, :], op=ALU.mult),
                                     reads=[PK[py], f"gsb{gb}"], writes=[f"tm{ab}"])
                                if gi == 1:
                                    S.op("pool", lambda e, ab=ab: e.tensor_tensor(out=acc[:, ab, :], in0=acc[:, ab, :], in1=tm[:, ab, :], op=ALU.add),
                                         reads=[f"acc{ab}", f"tm{ab}"], writes=[f"acc{ab}"])
                                else:
                                    S.op("pool", lambda e, ab=ab, oc=oc, c=c: e.tensor_tensor(out=mT[:, oc, c * 512:(c + 1) * 512], in0=acc[:, ab, :], in1=tm[:, ab, :], op=ALU.add),
                                         reads=[f"acc{ab}", f"tm{ab}"], writes=["mT"])
                NB = 5

                def mg_load(t):
                    b = t % 2
                    S.dma("sp", xt2[:, b, :], xin[t * 128:(t + 1) * 128, :], reads=["xdram"], writes=[f"xt2{b}"])

                def mg_mm(t):
                    b = t % 2
                    zb = t % NB
                    for half in range(2):
                        pi = 4 + half
                        for k in range(KC):
                            S.mm(lambda e, k=k, half=half, pi=pi: e.matmul(
                                PS[pi][:, :], lhsT=mT[:, k, t * 128:(t + 1) * 128], rhs=wo[:, k, half * 512:(half + 1) * 512],
                                start=(k == 0), stop=(k == KC - 1)),
                                reads=["mT", "wo"], writes=[PK[pi]], first=(k == 0))
                        S.op("dve", lambda e, pi=pi, half=half: e.tensor_tensor(
                            out=tmp[:, zb, half * 512:(half + 1) * 512], in0=PS[pi][:, :], in1=V["gate_bc"][:, half * 512:(half + 1) * 512], op=ALU.mult),
                            reads=[PK[pi], "gate_bc"], writes=[f"tmpm{zb}"])
                    S.op("dve", lambda e: e.scalar_tensor_tensor(out=tmp[:, zb, :], in0=xt2[:, b, :], scalar=ALPHA, in1=tmp[:, zb, :],
                                                                 op0=ALU.mult, op1=ALU.add),
                         reads=[f"xt2{b}", f"tmpm{zb}"], writes=[f"tmpm{zb}"])

                lo = ln_out_stages(lambda t: tmp[:, t % NB, :], lambda t: f"tmpm{t % NB}", T, V,
                                   lambda t: [xo[t * 128:(t + 1) * 128, :] for xo in xouts])

                def mg_mm_stats(t):
                    mg_mm(t)
                    lo[0](t)

                run_staged(NTILE, [mg_load, mg_mm_stats] + lo[1:])
        S.barrier()

    prologue()
    bufs = [xa.ap(), xb.ap()]
    cur = x0.ap()
    stage = 0
    for l in range(DEPTH):
        for kind in ("ffn1", "mix", "ffn2"):
            if stage >= nstages:
                break
            last = (stage == nstages - 1)
            dst = y.ap() if last else bufs[stage % 2]
            if kind == "ffn1":
                ffn(l, 1, cur, [dst])
            elif kind == "mix":
                mixer(l, cur, [dst])
            else:
                ffn(l, 2, cur, [dst])
            cur = dst
            stage += 1
    for e in ("sp", "pool", "act", "dve", "pe"):
        S.wait_all_dma(e)
    import os
    if os.environ.get("KDBG_STATS"):
        print("SIGVALS", S.sigval, "POS", S.pos, "DMA", max(S.dcount), flush=True)
    return nc


def _bf16(a):
    return np.asarray(a, dtype=np.float32).astype(ml_dtypes.bfloat16)


def _role_consts(role):
    c = {}
    t = np.arange(NT)
    if role == "sample":
        pos = np.stack([t // 64, t % 64], -1).astype(np.float32)
        inv = (10000.0 ** (-np.arange(8, dtype=np.float32) / 8)).astype(np.float32)
        ang = pos[:, :, None] * inv
        ang = np.concatenate([ang, ang], -1)
        cos = np.cos(ang).reshape(NT, 32).T
        sin = np.sin(ang).reshape(NT, 32).T
        c["ropeC"] = np.ascontiguousarray(cos, dtype=np.float32)
        c["ropeS"] = np.ascontiguousarray(sin, dtype=np.float32)
        c["indq_m"] = np.zeros((8, NT), np.float32)
        c["indq_n"] = np.zeros((8, NT), np.float32)
        c["indk"] = np.zeros((8, NK), np.float32)
        L = NT
        blk = np.zeros(NT, np.int64)
        loc = t
    else:
        c["ropeC"] = np.ones((32, NT), np.float32)
        c["ropeS"] = np.zeros((32, NT), np.float32)
        oh = (t[None, :] // 256 == np.arange(8)[:, None]).astype(np.float32)
        c["indq_m"] = oh * BIG_MLA
        c["indq_n"] = oh * BIG_NA
        ik = np.zeros((8, NK), np.float32)
        ik[:, :NT] = oh
        c["indk"] = ik
        L = 256
        blk = t // 256
        loc = t % 256
    norm = 1.0 / math.sqrt(L * 128.0)
    same = (blk[:, None] == blk[None, :])
    ph = (2.0 * np.pi / L) * ((loc[:, None] * loc[None, :]) % L).astype(np.float64)
    c["dftC"] = _bf16(np.where(same, np.cos(ph) * norm, 0.0))
    c["dftS"] = _bf16(np.where(same, -np.sin(ph) * norm, 0.0))
    cc = np.arange(128)
    ph2 = (2.0 * np.pi / 128) * ((cc[:, None] * cc[None, :]) % 128).astype(np.float64)
    c["dftCS"] = np.concatenate([np.cos(ph2), np.sin(ph2)], 1).astype(np.float32)
    m1 = np.zeros((128, NPAT, 128), np.float32)
    m2 = np.zeros((128, NPAT, 128), np.float32)
    if role == "sample":
        kk = np.arange(128)
        kr, kc = kk // 64, kk % 64
        qr, qc = kk // 64, kk % 64
        cstart = np.clip(qc - 8, 0, 48)
        col_ok = (kc[:, None] >= cstart[None, :]) & (kc[:, None] < cstart[None, :] + 16)
        for p, (dl, typ) in enumerate(PAT_DELTA):
            rel = 2 * dl + kr[:, None] - qr[None, :]
            row_ok = ((rel >= -4) & (rel <= 3)) if typ == 0 else np.ones_like(rel, bool)
            ok = col_ok & row_ok
            m1[:, p, :] = np.where(ok, 1.0 / NA_SCALE, 0.0)
            m2[:, p, :] = np.where(ok, 0.0, NEG)
    c["m1d"] = m1
    c["m2d"] = m2
    pr = np.zeros((32, 32), np.float32)
    for a in range(2):
        for j in range(16):
            d = a * 16 + j
            if j < 8:
                pr[d, d + 8] = -1.0
            else:
                pr[d, d - 8] = 1.0
    c["protT"] = np.ascontiguousarray(pr.T)
    c["identd"] = np.eye(128, dtype=np.float32)
    return c


_CACHE = {}


def _get_nc(nstages):
    if nstages not in _CACHE:
        _CACHE[nstages] = build(nstages)
    return _CACHE[nstages]


def run_units(inputs, nstages=3 * DEPTH, cores=None):
    f32 = lambda a: np.ascontiguousarray(np.asarray(a), dtype=np.float32)
    xp = f32(inputs["x_prompt"])
    xs = f32(inputs["x_sample"])
    shared = {}
    for nm in ("w_ada", "b_ada", "ffn1_w1", "ffn1_w3", "ffn1_w2", "ffn2_w1", "ffn2_w3", "ffn2_w2", "w_in",
               "mla_q_norm", "mla_w_uq", "mla_kv_norm", "mla_w_ukv", "w_branch_f", "w_branch_m", "w_branch_n",
               "w_gate", "b_gate", "w_out", "ln_g", "ln_b"):
        shared[nm] = f32(inputs[nm])
    rp = f32(inputs["na_rpb"])[..., ::-1].reshape(-1)
    shared["rpbr"] = np.concatenate([np.zeros(RPAD, np.float32), rp, np.zeros(RPAD, np.float32)])
    cp = _role_consts("prompt")
    cs = _role_consts("sample")
    zc = {"c_ckv": np.zeros((DEPTH, NCTX, 256), np.float32), "c_kr": np.zeros((DEPTH, NCTX, 32), np.float32),
          "c_nk": np.zeros((DEPTH, NCTX, 512), np.float32), "c_nv": np.zeros((DEPTH, NCTX, 512), np.float32)}
    in_maps = []
    for core in range(8):
        m = dict(shared)
        if core < 4 or core >= 6:
            u = core if core < 4 else core - 6
            m["x0"] = xp[u * 8:(u + 1) * 8].reshape(NT, D)
            m["cvec"] = f32(inputs["c_ctx"]).reshape(1, D)
            m.update(zc)
            m.update(cp)
        else:
            b = core - 4
            m["x0"] = xs[b]
            m["cvec"] = f32(inputs["c"])[b].reshape(1, D)
            m["c_ckv"] = f32(inputs["cache_mla_ckv"])[b]
            m["c_kr"] = f32(inputs["cache_mla_krope"])[b]
            m["c_nk"] = f32(inputs["cache_na_k"])[b].reshape(DEPTH, NCTX, 512)
            m["c_nv"] = f32(inputs["cache_na_v"])[b].reshape(DEPTH, NCTX, 512)
            m.update(cs)
        in_maps.append(m)
    nc = _get_nc(nstages)
    if cores is not None:
        res = run_bass_kernel_spmd(nc, [in_maps[c] for c in cores], core_ids=list(range(len(cores))))
        return {c: res.results[i] for i, c in enumerate(cores)}
    res = run_bass_kernel_spmd(nc, in_maps, core_ids=list(range(8)))
    return res.results


def kernel(**inputs):
    r = run_units(inputs)
    yp = np.concatenate([r[u]["y"].reshape(8, 256, D) for u in range(4)], 0)
    ys = np.stack([r[4]["y"], r[5]["y"]], 0)

    def gather(name, tail):
        a = np.concatenate([r[u][name].reshape(DEPTH, 8, 256, -1).transpose(1, 0, 2, 3) for u in range(4)], 0)
        return np.ascontiguousarray(a.reshape((32, DEPTH, 256) + tail), dtype=np.float32)

    return (np.ascontiguousarray(yp, dtype=np.float32), np.ascontiguousarray(ys, dtype=np.float32),
            gather("o_ckv", (256,)), gather("o_kr", (32,)), gather("o_nk", (8, 64)), gather("o_nv", (8, 64)))
```
